# Optimizing a Trainium2 kernel written in Bass

```python
import math
import jax
import jax.numpy as jnp
from jax import lax
import numpy as np

D_MODEL = 2048
BATCH = 1
SEQ = 8192
DEPTH = 2

MEM_LEN = 256
HEAD_DIM = 128
N_MIX_HEADS = D_MODEL // HEAD_DIM
MEM_HEADS = 4
MEM_W = MEM_HEADS * HEAD_DIM
ATTN_GROUPS = ((128, 1), (512, 4), (2048, 16))
ATTN_HEADS = N_MIX_HEADS - MEM_HEADS
HEADS_PER_GROUP = ATTN_HEADS // len(ATTN_GROUPS)
ATTN_W = ATTN_HEADS * HEAD_DIM
ATTN_OUT_W = HEADS_PER_GROUP * HEAD_DIM
BLK = 128
SGU_GROUPS = ATTN_HEADS
SGU_GROUP_DIM = HEAD_DIM
SGU_W = SGU_GROUPS * SGU_GROUP_DIM
SGU_CHUNK = 128
ROT_DIM = HEAD_DIM // 4
ROPE_THETA = 500000.0
FFN_HIDDEN = ((8 * D_MODEL // 3 + 255) // 256) * 256
NORM_EPS = 1e-6
LN_EPS = 1e-5
NEG_INF = -1e30

kernel_name = 'hybrid_dilated_attn_gmlp_memory_trunk'


def _rms_norm(x, g):
    xf = x.astype(jnp.float32)
    y = xf * lax.rsqrt(jnp.mean(xf * xf, axis=-1, keepdims=True) + NORM_EPS)
    return (y * g.astype(jnp.float32)).astype(x.dtype)


def _layer_norm(x, g, b):
    xf = x.astype(jnp.float32)
    mu = jnp.mean(xf, axis=-1, keepdims=True)
    var = jnp.mean(jnp.square(xf - mu), axis=-1, keepdims=True)
    y = (xf - mu) * lax.rsqrt(var + LN_EPS)
    return (y * g.astype(jnp.float32) + b.astype(jnp.float32)).astype(x.dtype)


def _partial_rotary(t, positions):
    half = ROT_DIM // 2
    inv_freq = ROPE_THETA ** (-jnp.arange(half, dtype=jnp.float32) / half)
    ang = positions.astype(jnp.float32)[:, :, None] * inv_freq
    cos = jnp.cos(ang)[:, :, None, :]
    sin = jnp.sin(ang)[:, :, None, :]
    tf = t.astype(jnp.float32)
    x1 = tf[..., :half]
    x2 = tf[..., half:ROT_DIM]
    rot = jnp.concatenate([x1 * cos - x2 * sin, x2 * cos + x1 * sin, tf[..., ROT_DIM:]], axis=-1)
    return rot.astype(t.dtype)


def _dilated_group(q, k, v, window, dilation):
    b, s, h, dh = q.shape
    n_back = window // dilation
    span = dilation * BLK
    sp = -(-s // span) * span
    length = sp // dilation
    nb = length // BLK

    def to_blocks(t):
        t = jnp.pad(t, ((0, 0), (0, sp - s), (0, 0), (0, 0)))
        t = t.reshape(b, length, dilation, h, dh).transpose(0, 2, 3, 1, 4)
        return t.reshape(b, dilation, h, nb, BLK, dh)

    def with_prev(t):
        prev = jnp.pad(t[:, :, :, :-1], ((0, 0), (0, 0), (0, 0), (1, 0), (0, 0), (0, 0)))
        return jnp.concatenate([prev, t], axis=4)

    qb = to_blocks(q)
    kb = with_prev(to_blocks(k))
    vb = with_prev(to_blocks(v))
    logits = jnp.einsum('brhnqd,brhnkd->brhnqk', qb, kb,
                        preferred_element_type=jnp.float32) * (dh ** -0.5)
    qi = jnp.arange(BLK)[:, None]
    ki = jnp.arange(2 * BLK)[None, :]
    dist = BLK + qi - ki
    band = (dist >= 0) & (dist <= n_back)
    first = (jnp.arange(nb) == 0)[:, None, None]
    mask = band[None] & (jnp.logical_not(first) | (ki >= BLK)[None])
    logits = jnp.where(mask, logits, NEG_INF)
    lse = jax.nn.logsumexp(logits, axis=-1)
    p = jnp.exp(logits - lse[..., None])
    out = jnp.einsum('brhnqk,brhnkd->brhnqd', p.astype(v.dtype), vb)
    out = out.reshape(b, dilation, h, length, dh).transpose(0, 3, 1, 2, 4).reshape(b, sp, h, dh)[:, :s]
    lse = lse.reshape(b, dilation, h, length).transpose(0, 3, 1, 2).reshape(b, sp, h)[:, :s]
    return out, lse


def _dilated_attention_mixer(h, positions, w_in):
    b, s, _ = h.shape
    proj = h @ w_in
    q, k, v, q_mem = jnp.split(proj, [ATTN_W, 2 * ATTN_W, 3 * ATTN_W], axis=-1)
    q = _partial_rotary(q.reshape(b, s, ATTN_HEADS, HEAD_DIM), positions)
    k = _partial_rotary(k.reshape(b, s, ATTN_HEADS, HEAD_DIM), positions)
    v = v.reshape(b, s, ATTN_HEADS, HEAD_DIM)
    outs, lses = [], []
    for g, (window, dilation) in enumerate(ATTN_GROUPS):
        sl = slice(g * HEADS_PER_GROUP, (g + 1) * HEADS_PER_GROUP)
        o, l = _dilated_group(q[:, :, sl], k[:, :, sl], v[:, :, sl], window, dilation)
        outs.append(o)
        lses.append(l)
    w = jax.nn.softmax(jnp.stack(lses, axis=0), axis=0)
    merged = jnp.einsum('gbsh,gbshd->bshd', w.astype(v.dtype), jnp.stack(outs, axis=0))
    return merged.reshape(b, s, ATTN_OUT_W), q_mem


def _spatial_gating_mixer(h, w_in, ln_g, ln_b, w_spatial, b_spatial):
    b, s, _ = h.shape
    proj = h @ w_in
    u, v, q_mem = jnp.split(proj, [SGU_W, 2 * SGU_W], axis=-1)
    u = jax.nn.gelu(u)
    v = _layer_norm(jax.nn.gelu(v), ln_g, ln_b)
    v = v.reshape(b, s // SGU_CHUNK, SGU_CHUNK, SGU_GROUPS, SGU_GROUP_DIM)
    causal = jnp.tril(jnp.ones((SGU_CHUNK, SGU_CHUNK), dtype=bool))
    w_s = jnp.where(causal[None], w_spatial, 0.0).astype(v.dtype)
    mixed = jnp.einsum('gts,bnsgc->bntgc', w_s, v) + b_spatial.T[None, None, :, :, None]
    return u * mixed.reshape(b, s, SGU_W), q_mem


def _memory_attention(q_mem, mem_n, w_mem_kv):
    b, s, _ = q_mem.shape
    kv = mem_n @ w_mem_kv
    k, v = jnp.split(kv, 2, axis=-1)
    q = q_mem.reshape(b, s, MEM_HEADS, HEAD_DIM)
    k = k.reshape(b, -1, MEM_HEADS, HEAD_DIM)
    v = v.reshape(b, -1, MEM_HEADS, HEAD_DIM)
    logits = jnp.einsum('bshd,bmhd->bhsm', q, k,
                        preferred_element_type=jnp.float32) * (HEAD_DIM ** -0.5)
    p = jax.nn.softmax(logits, axis=-1)
    out = jnp.einsum('bhsm,bmhd->bshd', p.astype(v.dtype), v)
    return out.reshape(b, s, MEM_W)


def _swiglu(h, w_gate, w_up, w_down):
    return (jax.nn.silu(h @ w_gate) * (h @ w_up)) @ w_down


def setup_inputs(seed: int = 0) -> dict:
    key = jax.random.key(seed)
    ks = jax.random.split(key, 24)
    n_a = (DEPTH + 1) // 2
    n_b = DEPTH // 2

    def dense(k, shape, fan_in):
        return jax.random.normal(k, shape, jnp.float32) * (fan_in ** -0.5)

    def gain(k, shape):
        return 1.0 + 0.02 * jax.random.normal(k, shape, jnp.float32)

    def small(k, shape):
        return 0.02 * jax.random.normal(k, shape, jnp.float32)

    x = jax.random.normal(ks[0], (BATCH, SEQ, D_MODEL), jnp.float32)
    mem = jax.random.normal(ks[1], (BATCH, MEM_LEN, D_MODEL), jnp.float32)
    offset = jax.random.randint(ks[2], (BATCH, 1), 0, 4096, dtype=jnp.int32)
    positions = offset + jnp.arange(SEQ, dtype=jnp.int32)[None, :]
    return {
        'x': x,
        'mem': mem,
        'positions': positions,
        'mix_norm': gain(ks[3], (DEPTH, D_MODEL)),
        'mem_norm': gain(ks[4], (DEPTH, D_MODEL)),
        'w_mem_kv': dense(ks[5], (DEPTH, D_MODEL, 2 * MEM_W), D_MODEL),
        'ffn_norm': gain(ks[6], (DEPTH, D_MODEL)),
        'w_gate': dense(ks[7], (DEPTH, D_MODEL, FFN_HIDDEN), D_MODEL),
        'w_up': dense(ks[8], (DEPTH, D_MODEL, FFN_HIDDEN), D_MODEL),
        'w_down': dense(ks[9], (DEPTH, FFN_HIDDEN, D_MODEL), FFN_HIDDEN),
        'attn_w_in': dense(ks[10], (n_a, D_MODEL, 3 * ATTN_W + MEM_W), D_MODEL),
        'attn_w_out': dense(ks[11], (n_a, ATTN_OUT_W + MEM_W, D_MODEL), ATTN_OUT_W + MEM_W),
        'sgu_w_in': dense(ks[12], (n_b, D_MODEL, 2 * SGU_W + MEM_W), D_MODEL),
        'sgu_ln_g': gain(ks[13], (n_b, SGU_W)),
        'sgu_ln_b': small(ks[14], (n_b, SGU_W)),
        'sgu_w_spatial': dense(ks[15], (n_b, SGU_GROUPS, SGU_CHUNK, SGU_CHUNK), SGU_CHUNK),
        'sgu_b_spatial': gain(ks[16], (n_b, SGU_GROUPS, SGU_CHUNK)),
        'sgu_w_out': dense(ks[17], (n_b, SGU_W + MEM_W, D_MODEL), SGU_W + MEM_W),
        'final_norm': gain(ks[18], (D_MODEL,)),
    }


def reference(x, mem, positions, mix_norm, mem_norm, w_mem_kv, ffn_norm, w_gate, w_up, w_down,
              attn_w_in, attn_w_out, sgu_w_in, sgu_ln_g, sgu_ln_b, sgu_w_spatial, sgu_b_spatial,
              sgu_w_out, final_norm):
    for i in range(DEPTH):
        j = i // 2
        h = _rms_norm(x, mix_norm[i])
        if i % 2 == 0:
            mix_out, q_mem = _dilated_attention_mixer(h, positions, attn_w_in[j])
            w_out = attn_w_out[j]
        else:
            mix_out, q_mem = _spatial_gating_mixer(h, sgu_w_in[j], sgu_ln_g[j], sgu_ln_b[j],
                                                   sgu_w_spatial[j], sgu_b_spatial[j])
            w_out = sgu_w_out[j]
        mem_out = _memory_attention(q_mem, _rms_norm(mem, mem_norm[i]), w_mem_kv[i])
        x = x + jnp.concatenate([mix_out, mem_out], axis=-1) @ w_out
        x = x + _swiglu(_rms_norm(x, ffn_norm[i]), w_gate[i], w_up[i], w_down[i])
    return _rms_norm(x, final_norm)
```

```python
import numpy as np
from contextlib import ExitStack
import concourse.bass as bass
import concourse.mybir as mybir
from concourse.bass_utils import run_bass_kernel_spmd

F32 = mybir.dt.float32
BF16 = mybir.dt.bfloat16
I32 = mybir.dt.int32
AF = mybir.ActivationFunctionType
ALU = mybir.AluOpType

NCORES = 8
TOK = 1024
DM = 2048
FF = 5632
NS = 4
HALO = 2688
ENGS = ("pe", "act", "dve", "pool", "sp")
HALO_BATCHES = [(0, 4, 2), (4, 4, 2), (8, 4, 2), (12, 4, 2), (16, 4, 1), (20, 1, 0)]
HALO_COL0 = {2: 0, 1: 2048, 0: 2560}


class Prog:
    def __init__(self, nc):
        self.nc = nc
        self.q = {e: [] for e in ENGS}
        self.semcnt = {}
        self.waited = {e: {} for e in ENGS}

    def op(self, eng, fn):
        ent = {"fn": fn, "inc": None}
        self.q[eng].append(ent)
        return ent

    def inc(self, ent, sem, amt=1):
        assert ent["inc"] is None
        self.semcnt[sem] = self.semcnt.get(sem, 0) + amt
        ent["inc"] = (sem, amt)
        return (sem, self.semcnt[sem])

    def wait(self, eng, ev):
        if ev is None:
            return
        if isinstance(ev, list):
            for e in ev:
                self.wait(eng, e)
            return
        sem, val = ev
        w = self.waited[eng]
        if w.get(sem, 0) >= val:
            return
        w[sem] = val
        self.q[eng].append({"wait": (sem, val)})

    def build(self):
        nc = self.nc
        with ExitStack() as es:
            sems = {}
            for name in self.semcnt:
                sems[name] = es.enter_context(nc.semaphore(name))
            block = es.enter_context(nc.Block())

            def replay(engname):
                def body(eng):
                    for ent in self.q[engname]:
                        if "wait" in ent:
                            s, v = ent["wait"]
                            eng.wait_ge(sems[s], v)
                        else:
                            ins = ent["fn"](eng)
                            if ent["inc"] is not None:
                                s, a = ent["inc"]
                                ins.then_inc(sems[s], a)
                return body

            for name, meth in (("sp", block.sync), ("pe", block.tensor), ("act", block.scalar),
                               ("dve", block.vector), ("pool", block.gpsimd)):
                if self.q[name]:
                    meth(replay(name))


def build(layers=(0, 1), final=True, stop=None):
    nc = bass.Bass("TRN2", target_bir_lowering=False)
    P = Prog(nc)

    def din(name, shape, dt=F32):
        return nc.dram_tensor(name, list(shape), dt, kind="ExternalInput").ap()

    xo = din("xo", [TOK, DM])
    if 0 in layers:
        xh = din("xh", [HALO, DM])
        posr = din("posr", [32, HALO + TOK], I32)
        valid = din("valid", [128, 21])
        w0 = din("w0", [DM, 8192])
        wo0 = din("wo0", [1024, DM])
        invf = din("invf", [32, 2])
    if 1 in layers:
        w1 = din("w1", [DM, 3584])
        wo1 = din("wo1", [DM, DM])
        lng_d = din("lng", [128, 1536])
        lnb_d = din("lnb", [128, 1536])
        bsp_d = din("bsp", [128, 1536])
        wst_d = din("wst", [128, 1536])
    memd = din("mem", [256, DM])
    cmat_d = din("cmat", [128, 672])
    gT_d = din("gT", [128, 96])
    gfin_d = din("gfin", [128, DM])
    wkv = din("wkv", [2, DM, 1024])
    wg = din("wg", [2, DM, FF])
    wu = din("wu", [2, DM, FF])
    wd = din("wd", [2, FF, DM])
    y = nc.dram_tensor("y", [TOK, DM], F32, kind="ExternalOutput").ap()

    es = ExitStack()
    sb = lambda name, shape, dt: es.enter_context(nc.sbuf_tensor(name, list(shape), dt))
    xs = sb("xs", [128, 8, DM], F32)
    ring = [sb(f"ring{i}", [128, 4096], BF16) for i in range(NS)]
    hT = sb("hT", [128, 16, TOK], BF16)
    am = sb("am", [128, 8192], BF16)
    mixT = am
    actT = am[:].rearrange("p (a b) -> p a b", b=TOK)
    xt = [am[:, i * 4096:(i + 1) * 4096].bitcast(F32) for i in range(2)]
    xn = [sb("xn0", [128, DM], BF16)] * 2
    ph = sb("ph", [128, 11264], BF16)
    cmat = sb("cmat_s", [128, 672], BF16)
    gT = sb("gT_s", [128, 96], F32)
    stat = sb("stat", [128, 64], F32)
    misc = sb("misc", [128, 12 * 1024], BF16)
    ptS = [sb(f"pt{i}", [128, 512], BF16) for i in range(2)]
    pmS = [sb(f"pm{i}", [128, 512], BF16) for i in range(2)]
    memKT = sb("memKT", [128, 4, 256], BF16)
    memV = sb("memV", [128, 2, 512], BF16)
    sgt = ptS
    psall = es.enter_context(nc.psum_tensor("psall", [128, 8, 512], F32))

    ident = cmat[:, 0:128]
    M_own = cmat[:, 128:256]
    M_prev = cmat[:, 256:384]
    M_B3 = cmat[:, 384:512]
    ones = cmat[:, 512:640]
    Rsw = cmat[:, 640:672]

    def bank(b):
        return psall[:, b, :]

    def bank_bf(b):
        return psall[:, b, :].bitcast(BF16)

    bfree = [None] * 8

    def A(fn, waits=None):
        P.wait("act", waits)
        return P.inc(P.op("act", fn), "sA")

    def V(fn, waits=None):
        P.wait("dve", waits)
        return P.inc(P.op("dve", fn), "sV")

    def T(fn, waits=None, sig=False):
        P.wait("pe", waits)
        e = P.op("pe", fn)
        return P.inc(e, "sT") if sig else None

    def SPD(fn, sem, waits=None):
        P.wait("sp", waits)
        return P.inc(P.op("sp", fn), sem, 16)

    def misc_take(off_bytes, nbytes, dt, parts=128):
        a = misc[0:parts, off_bytes // 2:(off_bytes + nbytes) // 2]
        return a if dt == BF16 else a.bitcast(dt)

    class Ring:
        def __init__(self):
            self.free = [None] * NS
            self.n = 0

        def load(self, src3, nk, ncol):
            s = self.n % NS
            self.n += 1
            P.wait("pool", self.free[s])
            view = ring[s][:, 0:nk * ncol].rearrange("p (k c) -> p k c", c=ncol)
            e = P.op("pool", lambda g, view=view, src3=src3: g.dma_start(out=view, in_=src3))
            ev = P.inc(e, f"wld{s}", 16)
            return s, view, ev

        def release(self, s, ev):
            self.free[s] = ev

    R = Ring()

    def wpiece(w2d, row0, nk, col0, ncol):
        src = w2d[row0:row0 + nk * 128, col0:col0 + ncol].rearrange("(k p) c -> p k c", p=128)
        return R.load(src, nk, ncol)

    e = P.op("pool", lambda g: g.dma_start(out=cmat[:], in_=cmat_d))
    ev_c = P.inc(e, "cld", 16)
    ev_g = SPD(lambda q: q.dma_start(out=gT[:], in_=gT_d), "gld1")
    for en in ("pe", "dve", "act"):
        P.wait(en, ev_c)
    P.wait("dve", ev_g)

    eps_t = sb("eps_t", [128, 2], F32)
    eps_ap = eps_t[:, 0:1]
    lneps_ap = eps_t[:, 1:2]
    V(lambda v: v.memset(eps_t[:, 0:1], 1e-6))
    ev_eps = V(lambda v: v.memset(eps_t[:, 1:2], 1e-5))
    P.wait("act", ev_eps)

    stat_i = [0]

    def stat_col():
        i = stat_i[0] % 64
        stat_i[0] += 1
        return stat[:, i:i + 1]

    tp_i = [0]
    xn_free = [None]
    pend_rot = [None]

    def flush_rot():
        if pend_rot[0] is not None:
            f = pend_rot[0]
            pend_rot[0] = None
            f()
    xnb = xn[0]

    def norm_A(src, src_ev, gcol0, dst_of, dst_wait=None):
        ss = stat_col()
        sq = stat_col()
        rs = stat_col()
        ev = A(lambda a: a.activation(out=xnb[:], in_=src, func=AF.Square, accum_out=ss), [src_ev, xn_free[0]])
        ev = A(lambda a: a.activation(out=sq, in_=ss, func=AF.Sqrt, scale=1.0 / DM, bias=eps_ap), [ev])
        ev = V(lambda v: v.reciprocal(out=rs, in_=sq), [ev])
        ev_xn = A(lambda a: a.activation(out=xnb[:], in_=src, func=AF.Copy, scale=rs), [ev])
        return (ev_xn, gcol0, dst_of, dst_wait)

    def norm_B(st):
        ev_xn, gcol0, dst_of, dst_wait = st
        evd = None
        last_pe = None
        for h in range(2):
            b = 6 + (tp_i[0] % 2)
            tp_i[0] += 1
            P.wait("pe", [bfree[b], ev_xn])
            for j in range(8):
                kc = h * 8 + j
                last_pe = T(lambda t, b=b, j=j, kc=kc: t.transpose(
                    bank_bf(b)[:, j * 128:(j + 1) * 128], xnb[:, kc * 128:(kc + 1) * 128], ident), sig=(j == 7))
            flush_rot()
            gb = gT[:, gcol0 + h * 8: gcol0 + h * 8 + 8].unsqueeze(2).to_broadcast([128, 8, 128])
            dst = dst_of(h)
            evd = V(lambda v, b=b, dst=dst, gb=gb: v.tensor_tensor(
                out=dst, in0=bank_bf(b).rearrange("p (a c) -> p a c", c=128), in1=gb, op=ALU.mult),
                [last_pe, dst_wait])
            bfree[b] = evd
        xn_free[0] = last_pe
        return evd, ev_xn

    def norm_tile(src, src_ev, gcol0, dst_of, dst_wait=None):
        return norm_B(norm_A(src, src_ev, gcol0, dst_of, dst_wait))

    xt_free = [None, None]
    xt_n = [0]

    def load_xtile(src_rows):
        i = xt_n[0] % 2
        xt_n[0] += 1
        ev = SPD(lambda q, i=i, src_rows=src_rows: q.dma_start(out=xt[i], in_=src_rows), f"xld{i}", [xt_free[i]])
        return i, ev

    pj = [0]

    def proj_fm(wview, c0, rhs_of, nk, n, evac, waits=None, banks=(0, 1), oshape=None):
        b = banks[pj[0] % len(banks)]
        pj[0] += 1
        P.wait("pe", [bfree[b], waits])
        out = bank(b)[:, 0:n]
        if oshape is not None:
            out = oshape(out)
        ev = None
        for k in range(nk):
            ev = T(lambda t, k=k: t.matmul(out, wview[:, k, c0:c0 + 128], rhs_of(k),
                                            start=(k == 0), stop=(k == nk - 1)), sig=(k == nk - 1))
        flush_rot()
        r = evac(b, ev)
        if callable(r):
            pend_rot[0] = r
        else:
            bfree[b] = r
        return ev

    def proj_tm(lhs_of, wview, c0, n, nk, evac, waits=None, banks=(4, 5)):
        b = banks[pj[0] % len(banks)]
        pj[0] += 1
        P.wait("pe", [bfree[b], waits])
        ev = None
        for k in range(nk):
            ev = T(lambda t, b=b, k=k: t.matmul(bank(b)[:, 0:n], lhs_of(k), wview[:, k, c0:c0 + n],
                                                 start=(k == 0), stop=(k == nk - 1)), sig=(k == nk - 1))
        flush_rot()
        bfree[b] = evac(b, ev)
        return ev

    def mem_kv(layer, memT):
        evd = None
        for mt in range(2):
            i, evl = load_xtile(memd[mt * 128:(mt + 1) * 128, :])
            evd, evr = norm_tile(xt[i], evl, 64 + 16 * layer,
                                 lambda h, mt=mt: memT[:, h * 8:(h + 1) * 8, mt * 128:(mt + 1) * 128])
            xt_free[i] = evr
        last = None
        for pc in range(4):
            s, wv, evw = wpiece(wkv[layer], 0, 16, pc * 256, 256)
            if pc < 2:
                for c in range(2):
                    m = pc * 2 + c
                    last = proj_fm(wv, c * 128, lambda k: memT[:, k, :], 16, 256,
                                   lambda b, ev, m=m: A(lambda a: a.copy(out=memKT[:, m, :], in_=bank(b)[:, 0:256]), [ev]),
                                   waits=[evw, evd])
            else:
                for mt in range(2):
                    cc = (pc - 2) * 256
                    last = proj_tm(lambda k, mt=mt: memT[:, k, mt * 128:(mt + 1) * 128], wv, 0, 256, 16,
                                   lambda b, ev, mt=mt, cc=cc: A(lambda a: a.copy(out=memV[:, mt, cc:cc + 256], in_=bank(b)[:, 0:256]), [ev]),
                                   waits=[evw, evd])
            R.release(s, last)
        return last

    pt_free = [None, None]
    pm_free = [None, None]

    pv_i = [0]

    def mem_attn_multi(groups, rden):
        units = []
        for gi_, (m, q_ap, dst, q_ev) in enumerate(groups):
            accb, denb = (4, 6) if gi_ % 2 == 0 else (5, 7)
            for mt in range(2):
                units.append((m, q_ap, dst, q_ev, accb, denb, mt))

        def stage1(u):
            m, q_ap, dst, q_ev, accb, denb, mt = u
            i = pv_i[0] % 2
            pv_i[0] += 1
            sbk = 2 + i
            P.wait("pe", [bfree[sbk], q_ev])
            evs = T(lambda t: t.matmul(bank(sbk), memKT[:, m, mt * 128:(mt + 1) * 128], q_ap, start=True, stop=True), sig=True)
            eve = A(lambda a: a.activation(out=ptS[i][:], in_=bank(sbk), func=AF.Exp, scale=128 ** -0.5), [evs, pt_free[i]])
            bfree[sbk] = eve

            def stage2():
                if mt == 0:
                    P.wait("pe", [bfree[accb], bfree[denb]])
                T(lambda t: t.matmul(bank(accb), memV[:, mt, m * 128:(m + 1) * 128], ptS[i][:],
                                     start=(mt == 0), stop=(mt == 1)), waits=[eve])
                evp = T(lambda t: t.matmul(bank(denb), ones, ptS[i][:], start=(mt == 0), stop=(mt == 1)), sig=True)
                pt_free[i] = evp
                if mt == 1:
                    ev1 = V(lambda v: v.reciprocal(out=rden[:, 0:512], in_=bank(denb)), [evp])
                    ev2 = V(lambda v: v.tensor_tensor(out=dst, in0=bank(accb), in1=rden[:, 0:512], op=ALU.mult), [ev1])
                    bfree[accb] = ev2
                    bfree[denb] = ev1
                    return ev2
                return evp
            return stage2

        pend = None
        last = None
        for u in units:
            s2 = stage1(u)
            if pend is not None:
                last = pend()
            pend = s2
        last = pend()
        return last

    hT_free = [None]

    def ffn(layer, x_ready, tile_cb=None, evh_pre=None, pre_tiles=(), pre_evh=None):
        evh = evh_pre
        for t in range(8 if evh_pre is None else 0):
            if t in pre_tiles:
                continue
            evh, _ = norm_tile(xs[:, t, :], x_ready[t], 32 + 16 * layer,
                               lambda h, t=t: hT[:, h * 8:(h + 1) * 8, t * 128:(t + 1) * 128], dst_wait=hT_free[0])
        if pre_evh:
            evh = [evh, pre_evh]
        groups = [(0, 8), (8, 8), (16, 8), (24, 8), (32, 8), (40, 4)]
        xev = list(x_ready)
        act_free = x_ready
        gu_i = 0
        dn_i = 0
        sg_free = [None, None]
        evpu = None
        for (f0, nf) in groups:
            ev_act_last = None
            for pr in range(nf // 2):
                col = (f0 + 2 * pr) * 128
                sg_, wgv, evg = wpiece(wg[layer], 0, 16, col, 256)
                su_, wuv, evu = wpiece(wu[layer], 0, 16, col, 256)
                for c in range(2):
                    fi = 2 * pr + c
                    for th in range(2):
                        bg = 0 + 2 * (gu_i % 2)
                        bu = 1 + 2 * (gu_i % 2)
                        si = gu_i % 2
                        gu_i += 1
                        P.wait("pe", [bfree[bg], bfree[bu], evg, evu, evh])
                        evpg = None
                        for k in range(16):
                            evpg = T(lambda t, bg=bg, k=k, c=c, th=th, wgv=wgv: t.matmul(
                                bank(bg), wgv[:, k, c * 128:(c + 1) * 128], hT[:, k, th * 512:(th + 1) * 512],
                                start=(k == 0), stop=(k == 15)), sig=(k == 15))
                        for k in range(16):
                            evpu = T(lambda t, bu=bu, k=k, c=c, th=th, wuv=wuv: t.matmul(
                                bank(bu), wuv[:, k, c * 128:(c + 1) * 128], hT[:, k, th * 512:(th + 1) * 512],
                                start=(k == 0), stop=(k == 15)), sig=(k == 15))
                        evs = A(lambda a, bg=bg, si=si: a.activation(out=sgt[si][:], in_=bank(bg), func=AF.Silu),
                                [evpg, sg_free[si]])
                        bfree[bg] = evs
                        dst = actT[:, fi, th * 512:(th + 1) * 512]
                        evm = V(lambda v, bu=bu, si=si, dst=dst: v.tensor_tensor(
                            out=dst, in0=bank(bu), in1=sgt[si][:], op=ALU.mult), [evs, evpu, act_free])
                        bfree[bu] = evm
                        sg_free[si] = evm
                        ev_act_last = evm
                R.release(sg_, evpu)
                R.release(su_, evpu)
            lastdn = None
            if tile_cb is not None and f0 == 40:
                pcs = [wpiece(wd[layer], f0 * 128, nf, jp * 1024, 1024) for jp in range(2)]
                lastpe = None
                for t in range(8):
                    for j in range(4):
                        sd_, wdv, evd = pcs[j // 2]
                        jh = j % 2
                        b = 4 + (dn_i % 2)
                        dn_i += 1
                        P.wait("pe", [bfree[b], evd, ev_act_last])
                        for f in range(nf):
                            lastpe = T(lambda tt, b=b, f=f, t=t, wdv=wdv, jh=jh: tt.matmul(
                                bank(b), actT[:, f, t * 128:(t + 1) * 128], wdv[:, f, jh * 512:(jh + 1) * 512],
                                start=(f == 0), stop=(f == nf - 1)), sig=(f == nf - 1))
                        xsl = xs[:, t, j * 512:(j + 1) * 512]
                        eva = V(lambda v, b=b, xsl=xsl: v.tensor_tensor(out=xsl, in0=bank(b), in1=xsl, op=ALU.add),
                                [lastpe, xev[t]])
                        bfree[b] = eva
                        xev[t] = eva
                        lastdn = eva
                    tile_cb(t, xev[t], evpu)
                for jp in range(2):
                    R.release(pcs[jp][0], lastpe)
                act_free = lastdn
                continue
            for j in range(4):
                sd_, wdv, evd = wpiece(wd[layer], f0 * 128, nf, j * 512, 512)
                lastpe = None
                for t in range(8):
                    b = 4 + (dn_i % 2)
                    dn_i += 1
                    P.wait("pe", [bfree[b], evd, ev_act_last])
                    for f in range(nf):
                        lastpe = T(lambda tt, b=b, f=f, t=t, wdv=wdv: tt.matmul(
                            bank(b), actT[:, f, t * 128:(t + 1) * 128], wdv[:, f, :],
                            start=(f == 0), stop=(f == nf - 1)), sig=(f == nf - 1))
                    xsl = xs[:, t, j * 512:(j + 1) * 512]
                    eva = V(lambda v, b=b, xsl=xsl: v.tensor_tensor(out=xsl, in0=bank(b), in1=xsl, op=ALU.add),
                            [lastpe, xev[t]])
                    bfree[b] = eva
                    xev[t] = eva
                    lastdn = eva
                R.release(sd_, lastpe)
            act_free = lastdn
        hT_free[0] = evpu
        return xev

    def out_proj(wout, nk, catT_of, toks, x_evs, cat_ev):
        xev = dict((t, x_evs[t]) for t in toks)
        for j in range(8):
            s_, wv, evw = wpiece(wout, 0, nk, j * 256, 256)
            lastpe = None
            for t in toks:
                def evac(b, ev, t=t, j=j):
                    dst = xs[:, t, j * 256:(j + 1) * 256]
                    e2 = V(lambda v: v.tensor_tensor(out=dst, in0=bank(b)[:, 0:256], in1=dst, op=ALU.add), [ev, xev[t]])
                    xev[t] = e2
                    return e2
                lastpe = proj_tm(lambda k, t=t: catT_of(k, t), wv, 0, 256, nk, evac, waits=[evw, cat_ev],
                                 banks=(0, 1, 2, 3))
            R.release(s_, lastpe)
        return xev

    def out_proj_t(wout, nk, catT_of, toks, x_evs, cat_ev, tile_cb):
        xev = dict((t, x_evs[t]) for t in toks)
        pcs = [wpiece(wout, 0, nk, j * 512, 512) for j in range(4)]
        lastpe = None
        for t in toks:
            for j in range(4):
                s_, wv, evw = pcs[j]

                def evac(b, ev, t=t, j=j):
                    dst = xs[:, t, j * 512:(j + 1) * 512]
                    e2 = V(lambda v: v.tensor_tensor(out=dst, in0=bank(b), in1=dst, op=ALU.add), [ev, xev[t]])
                    xev[t] = e2
                    return e2
                lastpe = proj_tm(lambda k, t=t: catT_of(k, t), wv, 0, 512, nk, evac, waits=[evw, cat_ev], banks=(0, 1, 2, 3))
            tile_cb(t, xev[t])
        for j in range(4):
            R.release(pcs[j][0], lastpe)
        return xev

    x_ready = [None] * 8
    if 0 in layers:
        xsb = xs[:].rearrange("p a b -> p (a b)").bitcast(BF16)
        KTh = xsb[:, 0:10752].rearrange("p (h t) -> p h t", t=HALO)
        Vh = xsb[:, 10752:21504].rearrange("p (t c) -> p t c", c=512)
        QT = xsb[:, 21504:24576].rearrange("p (g t) -> p g t", t=TOK)
        KTo = xsb[:, 24576:27648].rearrange("p (g t) -> p g t", t=TOK)
        Vo = xsb[:, 27648:29696].rearrange("p (g j c) -> p g j c", j=8, c=128)
        Vo3 = xsb[:, 29696:31744].rearrange("p (r c) -> p r c", c=128)
        qmT = xsb[:, 31744:32768]
        qnb = [ph[:, 7168:7680], ph[:, 7680:8192]]
        attnT = am[:].rearrange("p (k t) -> p k t", t=TOK)
        Co = misc_take(0, 4096, F32, 32)
        So = misc_take(4096, 4096, F32, 32)
        Ch = misc_take(8192, 2048, F32, 32)
        Sh = misc_take(10240, 2048, F32, 32)
        ChB = [Ch, ph[0:32, 9216:10240].bitcast(F32)]
        ShB = [Sh, ph[0:32, 10240:11264].bitcast(F32)]
        tab_last_use = [None, None]
        t1 = misc_take(12288, 2048, F32, 32)
        t2 = misc_take(14336, 2048, F32, 32)
        onesv = misc_take(16384, 5376, BF16).rearrange("p (t c) -> p t c", c=128)
        angb = ph[0:32, 0:1024].bitcast(F32)
        kb_i = ph[0:32, 1024:2048].bitcast(I32)
        kb_f = ph[0:32, 2048:3072].bitcast(F32)
        cmpb = ph[0:32, 3072:4096].bitcast(F32)
        posb = ph[0:32, 4096:5120].bitcast(I32)
        rden = ph[:, 5120:7168].bitcast(F32)
        invf_s = sb("invf_s", [32, 2], F32)
        valid_s = sb("valid_s", [128, 21], F32)
        memT0 = hT[:].rearrange("p a b -> p (a b)")[:, 0:4096].rearrange("p (k t) -> p k t", t=256)

        ev_if = SPD(lambda q: q.dma_start(out=invf_s[:], in_=invf), "gld2")
        ev_vl = SPD(lambda q: q.dma_start(out=valid_s[:], in_=valid), "gld3")

        mem_last = mem_kv(0, memT0)
        ev_ov = V(lambda v: v.tensor_copy(out=onesv, in_=valid_s[:].unsqueeze(2).to_broadcast([128, 21, 128])), [ev_vl])

        TWO_PI = float(2.0 * np.pi)
        C1 = 6.28125
        C2 = float(2.0 * np.pi - 6.28125)
        tab_chain = [None]

        def rot_tables_p1(pcol0, n, Cdst, Sdst, waits):
            ang = angb[:, 0:n]
            ki = kb_i[:, 0:n]
            kf = kb_f[:, 0:n]
            cm = cmpb[:, 0:n]
            evp = SPD(lambda q: q.dma_start(out=posb[:, 0:n], in_=posr[:, pcol0:pcol0 + n]), "pld", [tab_chain[0]])
            ev = V(lambda v: v.tensor_scalar(out=ang, in0=posb[:, 0:n], scalar1=invf_s[:, 0:1], scalar2=None, op0=ALU.mult),
                   [waits, evp, ev_if])
            tab_chain[0] = ev
            for which, dst in ((0, Sdst), (1, Cdst)):
                src = ang
                if which == 1:
                    ev = V(lambda v: v.tensor_scalar(out=cm, in0=ang, scalar1=float(np.pi / 2), scalar2=None, op0=ALU.add), [ev])
                    src = cm
                ev = V(lambda v, src=src: v.tensor_scalar(out=ki, in0=src, scalar1=float(1.0 / TWO_PI), scalar2=None, op0=ALU.mult), [ev])
                ev = V(lambda v: v.tensor_copy(out=kf, in_=ki), [ev])
                ev = V(lambda v, src=src, dst=dst: v.scalar_tensor_tensor(out=dst, in0=kf, scalar=-C1, in1=src, op0=ALU.mult, op1=ALU.add), [ev])
                ev = V(lambda v, dst=dst: v.scalar_tensor_tensor(out=dst, in0=kf, scalar=-C2, in1=dst, op0=ALU.mult, op1=ALU.add), [ev])
                ev = V(lambda v, dst=dst: v.tensor_scalar(out=dst, in0=dst, scalar1=3.14159, scalar2=-3.14159, op0=ALU.min, op1=ALU.max), [ev])
            return (ev, Cdst, Sdst)

        def rot_tables_p2(st):
            ev, Cdst, Sdst = st
            e1 = A(lambda a: a.activation(out=Sdst, in_=Sdst, func=AF.Sin), [ev])
            e2 = A(lambda a: a.activation(out=Cdst, in_=Cdst, func=AF.Sin), [e1])
            e3 = V(lambda v: v.tensor_scalar(out=Sdst, in0=Sdst, scalar1=invf_s[:, 1:2], scalar2=None, op0=ALU.mult), [e1])
            P.wait("dve", e2)
            return [e2, e3]

        def rot_tables(pcol0, n, Cdst, Sdst, waits):
            return rot_tables_p2(rot_tables_p1(pcol0, n, Cdst, Sdst, waits))

        rot_i = [0]
        rot_free = [None]

        def rotary_evac(b, ev_pe, dst, Ct, St, tab_ev, n, shp=None):
            f = shp if shp is not None else (lambda a: a)
            ev_c = A(lambda a: a.copy(out=dst, in_=bank(b)[:, 0:n]), [ev_pe])

            def part2():
                rb = 2 + (rot_i[0] % 2)
                rot_i[0] += 1
                P.wait("pe", bfree[rb])
                ev_r = T(lambda t: t.matmul(bank(rb)[0:32, 0:n], Rsw, dst, start=True, stop=True), waits=[ev_c], sig=True)
                e1 = V(lambda v: v.tensor_tensor(out=f(t1[:, 0:n]), in0=f(bank(rb)[0:32, 0:n]), in1=St, op=ALU.mult),
                       [ev_r, tab_ev, rot_free[0]])
                bfree[rb] = e1
                e2 = V(lambda v: v.tensor_tensor(out=f(t2[:, 0:n]), in0=f(bank(b)[0:32, 0:n]), in1=Ct, op=ALU.mult), [ev_c])
                e3 = V(lambda v: v.tensor_tensor(out=dst[0:32], in0=t1[:, 0:n], in1=t2[:, 0:n], op=ALU.add), [e1, e2])
                rot_free[0] = e3
                bfree[b] = e3
            return part2

        rot_tables(HALO, 512, Co[:, 0:512], So[:, 0:512], None)
        ev_tabo = rot_tables(HALO + 512, 512, Co[:, 512:1024], So[:, 512:1024], None)

        hTf = hT[:].rearrange("p a b -> p (a b)")
        hTb = [hTf[:, i * 8192:(i + 1) * 8192].rearrange("p (k t) -> p k t", t=512) for i in range(2)]
        hb_free = [mem_last, mem_last]
        batch_evd = {}
        batch_tab = {}
        held = {}

        def halo_norm_A(bi, tt):
            tile0, ntile, gi = HALO_BATCHES[bi]
            hb = hTb[bi % 2]
            i, evl = load_xtile(xh[(tile0 + tt) * 128:(tile0 + tt + 1) * 128, :])
            st = norm_A(xt[i], evl, 0,
                        lambda h, tt=tt, hb=hb: hb[:, h * 8:(h + 1) * 8, tt * 128:(tt + 1) * 128],
                        dst_wait=hb_free[bi % 2])
            xt_free[i] = st[0]
            return (bi, st)

        def halo_norm_B(pst):
            bi, st = pst
            evd, evr = norm_B(st)
            batch_evd[bi] = evd

        def halo_norm(bi, tt):
            halo_norm_B(halo_norm_A(bi, tt))

        tab_st = {}

        def halo_tables_p1(bi):
            tile0, ntile, gi = HALO_BATCHES[bi]
            n = ntile * 128
            tab_st[bi] = rot_tables_p1(tile0 * 128, n, ChB[bi % 2][:, 0:n], ShB[bi % 2][:, 0:n], tab_last_use[bi % 2])

        def halo_tables_p2(bi):
            batch_tab[bi] = rot_tables_p2(tab_st[bi])

        def halo_tables(bi):
            halo_tables_p1(bi)
            halo_tables_p2(bi)

        def halo_piece(bi, pc):
            tile0, ntile, gi = HALO_BATCHES[bi]
            hb = hTb[bi % 2]
            n = ntile * 128
            evd = batch_evd[bi]
            kcol = 5120 + (2 - gi) * 1024
            if (gi, pc) not in held:
                held[(gi, pc)] = wpiece(w0, 0, 16, kcol + pc * 256, 256)
            s_, wv, evw = held[(gi, pc)]
            last_of_group = (bi + 1 >= len(HALO_BATCHES)) or (HALO_BATCHES[bi + 1][2] != gi)
            lastpe = None
            if pc < 2:
                ev_tab = batch_tab[bi]
                for c in range(2):
                    hd = pc * 2 + c
                    dst = KTh[:, hd, tile0 * 128: tile0 * 128 + n]
                    lastpe = proj_fm(wv, c * 128, lambda k, hb=hb, n=n: hb[:, k, 0:n], 16, n,
                                     lambda b, ev, dst=dst, n=n, ev_tab=ev_tab, bi=bi: rotary_evac(b, ev, dst, ChB[bi % 2][:, 0:n], ShB[bi % 2][:, 0:n], ev_tab, n),
                                     waits=[evw, evd])
            else:
                cc = (pc - 2) * 256
                for tt in range(ntile):
                    dst = Vh[:, tile0 + tt, cc:cc + 256]
                    lastpe = proj_tm(lambda k, hb=hb, tt=tt: hb[:, k, tt * 128:(tt + 1) * 128], wv, 0, 256, 16,
                                     lambda b, ev, dst=dst: A(lambda a: a.copy(out=dst, in_=bank(b)[:, 0:256]), [ev]),
                                     waits=[evw, evd])
            if pc == 1:
                flush_rot()
                tab_last_use[bi % 2] = rot_free[0]
            if last_of_group:
                R.release(s_, lastpe)
            if pc == 3:
                hb_free[bi % 2] = lastpe

        for tt in range(HALO_BATCHES[0][1]):
            halo_norm(0, tt)
        halo_tables(0)
        NB_H = len(HALO_BATCHES)
        for bi in range(NB_H):
            nxt = HALO_BATCHES[bi + 1][1] if bi + 1 < NB_H else 0
            for pc in range(4):
                pst = halo_norm_A(bi + 1, pc) if pc < nxt else None
                halo_piece(bi, pc)
                if pst is not None:
                    halo_norm_B(pst)
                if pc == 1 and bi + 1 < NB_H:
                    halo_tables_p1(bi + 1)
                if pc == 3 and bi + 1 < NB_H:
                    halo_tables_p2(bi + 1)

        ev_hT = None
        for t in range(8):
            i, evl = load_xtile(xo[t * 128:(t + 1) * 128, :])
            ev_hT, evr = norm_tile(xt[i], evl, 0,
                                   lambda h, t=t: hT[:, h * 8:(h + 1) * 8, t * 128:(t + 1) * 128],
                                   dst_wait=[hb_free[0], hb_free[1]])
            xt_free[i] = evr
        xt_last = [xt_free[0], xt_free[1]]

        def vtile(ap2, gi, j):
            if gi == 0:
                return ap2[:, j * 128:(j + 1) * 128]
            if gi == 1:
                r, bb = j // 2, j % 2
                return ap2.rearrange("p (i r) -> p r i", r=4)[:, r, bb * 128:(bb + 1) * 128]
            return ap2.rearrange("p (i r) -> p r i", r=16)[:, j, :]

        def gtile(ap2, gi, j):
            return vtile(ap2, gi, j)

        def nat_view(a, gi):
            if gi == 0:
                return a
            return a.rearrange("p (i r) -> p i r", r=(4 if gi == 1 else 16))

        def perm_view(row, gi, th):
            if gi == 0:
                return row[:, th * 512:(th + 1) * 512]
            if gi == 1:
                return row.rearrange("p (r i) -> p i r", r=4)[:, 128 * th:128 * (th + 1), :]
            return row.rearrange("p (r i) -> p i r", r=16)[:, 32 * th:32 * (th + 1), :]

        qn_i = [0]
        qn_free = [None, None]
        vt_free = [None]
        VTs1 = ph[:, 8192:9216]

        def rotary_evac_perm(b, ev_pe, dstrow, gi, th):
            qi = qn_i[0] % 2
            qn_i[0] += 1
            qn = qnb[qi]
            Ct = Co[:, th * 512:(th + 1) * 512]
            St = So[:, th * 512:(th + 1) * 512]
            ev_c = A(lambda a: a.copy(out=qn, in_=bank(b)), [ev_pe, qn_free[qi]])
            A(lambda a: a.copy(out=perm_view(dstrow[32:64], gi, th), in_=nat_view(bank(b)[32:64, :], gi)), [ev_pe])
            ev_c2 = A(lambda a: a.copy(out=perm_view(dstrow[64:128], gi, th), in_=nat_view(bank(b)[64:128, :], gi)), [ev_pe])
            def part2():
                rb = 2 + (rot_i[0] % 2)
                rot_i[0] += 1
                P.wait("pe", bfree[rb])
                ev_r = T(lambda t: t.matmul(bank(rb)[0:32, :], Rsw, qn, start=True, stop=True), waits=[ev_c], sig=True)
                qn_free[qi] = ev_r
                e1 = V(lambda v: v.tensor_tensor(out=t1, in0=bank(rb)[0:32, :], in1=St, op=ALU.mult),
                       [ev_r, ev_tabo, rot_free[0]])
                bfree[rb] = e1
                e2 = V(lambda v: v.tensor_tensor(out=t2, in0=bank(b)[0:32, :], in1=Ct, op=ALU.mult), [ev_c])
                e3 = V(lambda v: v.tensor_tensor(out=perm_view(dstrow[0:32], gi, th), in0=nat_view(t1, gi), in1=nat_view(t2, gi),
                                                 op=ALU.add), [e1, e2, ev_c2])
                rot_free[0] = e3
                bfree[b] = e3
            return part2

        accA = psall[:, 4:6, :].rearrange("p a b -> p (a b)")
        denA = psall[:, 6:8, :].rearrange("p a b -> p (a b)")
        att_done = None

        for s in range(4):
            base = s * 1280
            cur = [None, None, None]
            lastpe_piece = [None]

            def get_piece(pcI, cur=cur, base=base, lastpe_piece=lastpe_piece):
                if cur[0] != pcI:
                    if cur[0] is not None:
                        R.release(cur[1][0], lastpe_piece[0])
                    cur[0] = pcI
                    cur[1] = wpiece(w0, 0, 16, base + pcI * 256, 256)
                return cur[1]

            PB = (0, 1, 4, 5, 6, 7)
            for c in range(7):
                s_, wv, evw = get_piece(c // 2)
                cI = c % 2
                for th in range(2):
                    if c < 6:
                        gi = c % 3
                        dst = (QT if c < 3 else KTo)[:, gi, th * 512:(th + 1) * 512]
                        ev = proj_fm(wv, cI * 128, lambda k, th=th: hT[:, k, th * 512:(th + 1) * 512], 16, 512,
                                     lambda b, ev, dst=dst, th=th: rotary_evac(b, ev, dst, Co[:, th * 512:(th + 1) * 512],
                                                                               So[:, th * 512:(th + 1) * 512], ev_tabo, 512),
                                     waits=[evw, ev_hT, att_done], banks=PB)
                    else:
                        dst = qmT[:, th * 512:(th + 1) * 512]
                        ev = proj_fm(wv, cI * 128, lambda k, th=th: hT[:, k, th * 512:(th + 1) * 512], 16, 512,
                                     lambda b, ev, dst=dst: A(lambda a: a.copy(out=dst, in_=bank(b)), [ev]),
                                     waits=[evw, ev_hT, att_done], banks=PB)
                    lastpe_piece[0] = ev
            for gi in range(3):
                c = 7 + gi
                s_, wv, evw = get_piece(c // 2)
                cI = c % 2
                vrow = VTs1
                evv = []
                for th in range(2):
                    def evac_v(b, ev, vrow=vrow, th=th):
                        return A(lambda a: a.copy(out=vrow[:, th * 512:(th + 1) * 512], in_=bank(b)), [ev, vt_free[0]])
                    ev = proj_fm(wv, cI * 128, lambda k, th=th: hT[:, k, th * 512:(th + 1) * 512], 16, 512, evac_v,
                                 waits=[evw, ev_hT, att_done], banks=PB)
                    lastpe_piece[0] = ev
                    evv.append(bfree[PB[(pj[0] - 1) % len(PB)]])
                ntile = 8 if gi < 2 else 16
                tw = 128 if gi < 2 else 64
                for jb in range(ntile // 8):
                    b = 2 + (pj[0] % 2)
                    pj[0] += 1
                    P.wait("pe", [bfree[b], evv])
                    ev = None
                    for jj in range(8):
                        j = jb * 8 + jj
                        ev = T(lambda t, b=b, jj=jj, j=j, vrow=vrow, tw=tw, gi=gi: t.transpose(
                            bank_bf(b)[0:tw, jj * 128:(jj + 1) * 128], vtile(vrow, gi, j), ident), sig=(jj == 7))
                    if gi < 2:
                        dst = Vo[:, gi, :, :]
                    else:
                        dst = Vo3[0:64, jb * 8:(jb + 1) * 8, :]
                    bfree[b] = A(lambda a, b=b, dst=dst, tw=tw: a.copy(
                        out=dst, in_=bank_bf(b)[0:tw, :].rearrange("p (j c) -> p j c", c=128)), [ev])
                    vt_free[0] = ev
            flush_rot()
            R.release(cur[1][0], lastpe_piece[0])
            proj_done = [rot_free[0]] + [bfree[i] for i in range(8)]

            ev_z1 = V(lambda v: v.memset(accA, 0.0), [bfree[4], bfree[5]])
            ev_z2 = V(lambda v: v.memset(denA, 0.0), [bfree[6], bfree[7]])
            P.wait("pe", [ev_z1, ev_z2, proj_done, ev_ov])

            def g_attend(items, mask_ops, nk=128):
                def stage1():
                    i = pv_i[0] % 2
                    pv_i[0] += 1
                    sbk = 2 + i
                    P.wait("pe", bfree[sbk])
                    c = 0
                    offs = []
                    evs = None
                    for n_it, (k_ap, q_ap, nq, pvs) in enumerate(items):
                        evs = T(lambda t, c=c, nq=nq, k_ap=k_ap, q_ap=q_ap: t.matmul(
                            bank(sbk)[0:nk, c:c + nq], k_ap, q_ap, start=True, stop=True), sig=(n_it == len(items) - 1))
                        offs.append(c)
                        c += nq
                    tot = c
                    eve = A(lambda a: a.activation(out=ptS[i][0:nk, 0:tot], in_=bank(sbk)[0:nk, 0:tot],
                                                   func=AF.Exp, scale=128 ** -0.5), [evs, pt_free[i]])
                    bfree[sbk] = eve
                    evm = None
                    for (c0, ncl, in1_ap, inner) in mask_ops:
                        o = pmS[i][0:nk, c0:c0 + ncl].rearrange("p (a b) -> p a b", b=inner)
                        a_in = ptS[i][0:nk, c0:c0 + ncl].rearrange("p (a b) -> p a b", b=inner)
                        evm = V(lambda v, o=o, a_in=a_in, in1_ap=in1_ap: v.tensor_tensor(out=o, in0=a_in, in1=in1_ap, op=ALU.mult),
                                [eve, pm_free[i]])
                    pt_free[i] = evm

                    def stage2():
                        evp = None
                        first = True
                        for (k_ap, q_ap, nq, pvs), off in zip(items, offs):
                            for (co, ncl, v_ap, o_ap, acc_ap, den_ap) in pvs:
                                rhs = pmS[i][0:nk, off + co:off + co + ncl]
                                T(lambda t, v_ap=v_ap, rhs=rhs, acc_ap=acc_ap: t.matmul(
                                    acc_ap, v_ap, rhs, start=False, stop=False, skip_group_check=True),
                                  waits=[evm] if first else None)
                                first = False
                                evp = T(lambda t, o_ap=o_ap, rhs=rhs, den_ap=den_ap: t.matmul(
                                    den_ap, o_ap, rhs, start=False, stop=False, skip_group_check=True), sig=True)
                        pm_free[i] = evp
                        return evp
                    return stage2
                return stage1

            batches = []
            M_op = cmat[:, 128:384]
            M_po = cmat[:, 256:512]

            def g1_item(kt):
                if kt < 0:
                    k_ap, v_ap, o_ap = KTh[:, s, 2560:2688], Vh[:, 20, s * 128:(s + 1) * 128], onesv[:, 20, :]
                else:
                    k_ap, v_ap, o_ap = KTo[:, 0, kt * 128:(kt + 1) * 128], Vo[:, 0, kt, :], ones
                qlo, qhi = max(kt, 0), min(kt + 1, 7)
                nq = (qhi - qlo + 1) * 128
                pvs = [(qi * 128, 128, v_ap, o_ap, accA[:, qt * 128:(qt + 1) * 128], denA[:, qt * 128:(qt + 1) * 128])
                       for qi, qt in enumerate(range(qlo, qhi + 1))]
                return (k_ap, QT[:, 0, qlo * 128:qlo * 128 + nq], nq, pvs)

            batches.append(g_attend([g1_item(-1), g1_item(0)],
                                    [(0, 128, M_prev.unsqueeze(1), 128), (128, 256, M_op.unsqueeze(1), 256)]))
            for kt in (1, 3, 5):
                batches.append(g_attend([g1_item(kt), g1_item(kt + 1)],
                                        [(0, 512, M_op.unsqueeze(1).to_broadcast([128, 2, 256]), 256)]))
            batches.append(g_attend([g1_item(7)], [(0, 128, M_own.unsqueeze(1), 128)]))
            for r in range(4):
                items = []
                for kb in range(-1, 2):
                    if kb < 0:
                        k_ap, v_ap, o_ap = KTh[:, s, 2048 + r * 128:2048 + (r + 1) * 128], Vh[:, 16 + r, s * 128:(s + 1) * 128], onesv[:, 16 + r, :]
                    else:
                        k_ap, v_ap, o_ap = vtile(KTo[:, 1, :], 1, r * 2 + kb), Vo[:, 1, r * 2 + kb, :], ones
                    qlo, qhi = max(kb, 0), min(kb + 1, 1)
                    nq = (qhi - qlo + 1) * 128
                    pvs = [(qi * 128, 128, v_ap, o_ap, gtile(accA, 1, r * 2 + qb), gtile(denA, 1, r * 2 + qb))
                           for qi, qb in enumerate(range(qlo, qhi + 1))]
                    items.append((k_ap, QT[:, 1, :].rearrange("p (i r) -> p r i", r=4)[:, r, qlo * 128:qlo * 128 + nq], nq, pvs))
                batches.append(g_attend(items, [(0, 512, M_po.unsqueeze(1).to_broadcast([128, 2, 256]), 256)]))
            for rb in range(2):
                items = []
                for r in range(rb * 8, rb * 8 + 8):
                    accc = accA.rearrange("p (i r) -> p r i", r=16)[:, r, :]
                    denc = denA.rearrange("p (i r) -> p r i", r=16)[:, r, :]
                    items.append((KTh[:, s, r * 128:(r + 1) * 128], vtile(QT[:, 2, :], 2, r), 64,
                                  [(0, 32, Vh[:, r, s * 128:(s + 1) * 128], onesv[:, r, :], accc[:, 0:32], denc[:, 0:32]),
                                   (32, 32, Vh[:, r, s * 128:(s + 1) * 128], onesv[:, r, :], accc[:, 32:64], denc[:, 32:64])]))
                batches.append(g_attend(items, [(0, 512, M_prev[:, 0:64].unsqueeze(1).to_broadcast([128, 8, 64]), 64)]))
            for rb in range(2):
                items = []
                for r in range(rb * 8, rb * 8 + 8):
                    accc = accA.rearrange("p (i r) -> p r i", r=16)[:, r, :]
                    denc = denA.rearrange("p (i r) -> p r i", r=16)[:, r, :]
                    items.append((vtile(KTo[:, 2, :], 2, r), vtile(QT[:, 2, :], 2, r), 64,
                                  [(0, 32, Vo3[0:64, r, :], ones[0:64, :], accc[:, 0:32], denc[:, 0:32]),
                                   (32, 32, Vo3[0:64, r, :], ones[0:64, :], accc[:, 32:64], denc[:, 32:64])]))
                batches.append(g_attend(items, [(0, 512, M_own[0:64, 0:64].unsqueeze(1).to_broadcast([64, 8, 64]), 64)], nk=64))
            pend = None
            last = None
            for bt in batches:
                s2 = bt()
                if pend is not None:
                    last = pend()
                pend = s2
            last = pend()
            ev1 = V(lambda v: v.reciprocal(out=rden, in_=denA), [last])
            ev2 = V(lambda v, s=s: v.tensor_tensor(out=attnT[:, s, :], in0=accA, in1=rden, op=ALU.mult), [ev1, xt_last])
            for b in (4, 5):
                bfree[b] = ev2
            for b in (6, 7):
                bfree[b] = ev1
            att_done = mem_attn_multi([(s, qmT[:, th * 512:(th + 1) * 512], attnT[:, 4 + s, th * 512:(th + 1) * 512], proj_done) for th in range(2)], rden)

        xl = []
        for t in range(8):
            xl.append(SPD(lambda q, t=t: q.dma_start(out=xs[:, t, :], in_=xo[t * 128:(t + 1) * 128, :]), f"xsl{t}", [att_done]))
        pendF = [None]
        evhF = [None]

        def cb_ffn0norm(t, ev):
            if pendF[0] is not None:
                evhF[0], _ = norm_B(pendF[0])
            pendF[0] = norm_A(xs[:, t, :], ev, 32,
                              lambda h, t=t: hT[:, h * 8:(h + 1) * 8, t * 128:(t + 1) * 128], dst_wait=att_done)
        if stop != "attn0":
            xev = out_proj_t(wo0, 8, lambda k, t: attnT[:, k, t * 128:(t + 1) * 128], list(range(8)), xl, att_done, cb_ffn0norm)
            evhF[0], _ = norm_B(pendF[0])
        else:
            xev = out_proj(wo0, 8, lambda k, t: attnT[:, k, t * 128:(t + 1) * 128], list(range(8)), xl, att_done)
        x_ready = [xev[t] for t in range(8)]
        hT_free[0] = att_done
        xt_free[0] = xt_free[1] = x_ready
        evh_cb = [None]
        if stop != "attn0":
            pendB = [None]

            def cb_l1norm(t, ev, hfree):
                if pendB[0] is not None:
                    evh_cb[0], _ = norm_B(pendB[0])
                pendB[0] = norm_A(xs[:, t, :], ev, 16,
                                  lambda h, t=t: hT[:, h * 8:(h + 1) * 8, t * 128:(t + 1) * 128], dst_wait=hfree)
            x_ready = ffn(0, x_ready, cb_l1norm if (1 in layers) else None, evh_pre=evhF[0])
            if pendB[0] is not None:
                evh_cb[0], _ = norm_B(pendB[0])
            xt_free[0] = xt_free[1] = x_ready
    else:
        evh_cb = [None]
        for t in range(8):
            x_ready[t] = SPD(lambda q, t=t: q.dma_start(out=xs[:, t, :], in_=xo[t * 128:(t + 1) * 128, :]), f"xsl{t}")

    gfin = ph[:, 0:4096].bitcast(F32)
    hTf32 = hT[:].rearrange("p a b -> p (a b)").bitcast(F32)
    ystage = [hTf32[:, i * 2048:(i + 1) * 2048] for i in range(4)]
    yst_free = [None] * 4
    ev_gf_h = [None]
    final_done = [False]
    st_evs = []

    def final_tile(t, ev, hfree):
        ss, sq, rs = stat_col(), stat_col(), stat_col()
        e = A(lambda a: a.activation(out=xnb[:], in_=xs[:, t, :], func=AF.Square, accum_out=ss), [ev, xn_free[0]])
        e = A(lambda a: a.activation(out=sq, in_=ss, func=AF.Sqrt, scale=1.0 / DM, bias=eps_ap), [e])
        e = V(lambda v: v.reciprocal(out=rs, in_=sq), [e])
        i = t % 4
        e = V(lambda v: v.scalar_tensor_tensor(out=ystage[i], in0=xs[:, t, :], scalar=rs, in1=gfin,
                                               op0=ALU.mult, op1=ALU.mult), [e, ev_gf_h[0], yst_free[i], hfree])
        evs = SPD(lambda q: q.dma_start(out=y[t * 128:(t + 1) * 128, :], in_=ystage[i]), f"yst{i}", [e])
        yst_free[i] = evs
        st_evs.append(evs)

    if 1 in layers and stop != "attn0":
        lng = misc_take(0, 6144, F32)
        lnb = misc_take(6144, 6144, F32)
        bsp = misc_take(12288, 6144, F32)
        wsT = misc_take(18432, 3072, BF16).rearrange("p (g t) -> p g t", t=128)
        rden1 = misc_take(21504, 2048, F32)
        vtm = ph[:, 0:6144].rearrange("p (t c) -> p t c", c=1536)
        qm1 = ph[:, 6144:8192].rearrange("p (m t) -> p m t", t=512)
        vgf = ph[:, 8192:11264].bitcast(F32)
        memT1 = ph[:, 0:4096].rearrange("p (k t) -> p k t", t=256)
        e1 = SPD(lambda q: q.dma_start(out=lng, in_=lng_d), "gld4", x_ready)
        e2 = SPD(lambda q: q.dma_start(out=lnb, in_=lnb_d), "gld5")
        e3 = SPD(lambda q: q.dma_start(out=bsp, in_=bsp_d), "gld6")
        e4 = SPD(lambda q: q.dma_start(out=vgf, in_=wst_d), "gld7")
        ev_ws = V(lambda v: v.tensor_tensor(out=wsT, in0=vgf.rearrange("p (g t) -> p g t", t=128),
                                            in1=M_own.unsqueeze(1).to_broadcast([128, 12, 128]), op=ALU.mult), [e4])
        vgb = ph[:, 8192:11264]
        TA = vgb[:, 0:1536]
        TB = vgb[:, 1536:3072]
        bsp_hl = misc[0:64, 6144:7680]
        V(lambda v: v.memset(TA[0:64], 0.0), [ev_ws])
        V(lambda v: v.tensor_copy(out=TA[0:1], in_=bsp[0:1]), [e3])
        eb1 = V(lambda v: v.tensor_copy(out=TB[32:33], in_=bsp[32:33]))
        eb2 = V(lambda v: v.tensor_tensor(out=TA[32:33], in0=bsp[32:33], in1=TB[32:33], op=ALU.subtract), [eb1])
        ev_hl = V(lambda v: v.tensor_copy(out=bsp_hl, in_=TA[0:64]), [eb2])
        ev_tabs = [e1, e2, ev_ws, ev_hl]
        evh = evh_cb[0]
        if evh is None:
            for t in range(8):
                evh, _ = norm_tile(xs[:, t, :], x_ready[t], 16,
                                   lambda h, t=t: hT[:, h * 8:(h + 1) * 8, t * 128:(t + 1) * 128], dst_wait=hT_free[0])
        gTm = am[:].rearrange("p (k t) -> p k t", t=512)
        half_done = [xt_free[0], xt_free[1]]
        pre_evs = []
        for hf in range(2):
            tk0 = hf * 512
            last = None
            for pc in range(2):
                s_, wv, evw = wpiece(w1, 0, 16, 3072 + pc * 256, 256)
                for c in range(2):
                    m = pc * 2 + c
                    last = proj_fm(wv, c * 128, lambda k, tk0=tk0: hT[:, k, tk0:tk0 + 512], 16, 512,
                                   lambda b, ev, m=m: A(lambda a: a.copy(out=qm1[:, m, :], in_=bank(b)), [ev]),
                                   waits=[evw, evh, half_done])
                R.release(s_, last)
            qdone = [bfree[0], bfree[1]]
            if hf == 0:
                mem_kv(1, memT1)
            for cg in range(6):
                pstF = None
                if hf == 1 and cg < 4 and stop != "attn1":
                    pstF = norm_A(xs[:, cg, :], x_ready[cg], 48,
                                  lambda h, cg=cg: hT[:, h * 8:(h + 1) * 8, cg * 128:(cg + 1) * 128], dst_wait=half_done)
                s_, wv, evw = wpiece(w1, 0, 16, 1536 + cg * 256, 256)
                last = None
                for tt in range(4):
                    def evac(b, ev, tt=tt, cg=cg):
                        return A(lambda a: a.activation(out=vtm[:, tt, cg * 256:(cg + 1) * 256], in_=bank(b)[:, 0:256],
                                                        func=AF.Gelu), [ev, half_done])
                    last = proj_tm(lambda k, tt=tt, tk0=tk0: hT[:, k, tk0 + tt * 128: tk0 + (tt + 1) * 128], wv, 0, 256, 16, evac,
                                   waits=[evw, evh])
                R.release(s_, last)
                if pstF is not None:
                    pre_evs.append(norm_B(pstF)[0])
            vdone = [bfree[4], bfree[5]]
            ev_ln_h = [None]

            def ln_tile(tt, vdone=vdone, ev_ln_h=ev_ln_h):
                sm, sq2, mu, var, rs = stat_col(), stat_col(), stat_col(), stat_col(), stat_col()
                ea = A(lambda a: a.activation(out=xnb[:, 0:1536], in_=vtm[:, tt, :], func=AF.Copy, accum_out=sm), [vdone, ev_ws, xn_free[0]])
                eb = A(lambda a: a.activation(out=xnb[:, 0:1536], in_=vtm[:, tt, :], func=AF.Square, accum_out=sq2), [ea])
                e = V(lambda v: v.tensor_scalar(out=mu, in0=sm, scalar1=1.0 / 1536, scalar2=None, op0=ALU.mult), [ea])
                e = V(lambda v: v.tensor_tensor(out=var, in0=mu, in1=mu, op=ALU.mult), [e])
                e = V(lambda v: v.scalar_tensor_tensor(out=var, in0=sq2, scalar=1.0 / 1536, in1=var, op0=ALU.mult, op1=ALU.subtract), [e, eb])
                e = A(lambda a: a.activation(out=var, in_=var, func=AF.Sqrt, bias=lneps_ap), [e])
                e = V(lambda v: v.reciprocal(out=rs, in_=var), [e])
                e = V(lambda v: v.tensor_scalar(out=vgf, in0=vtm[:, tt, :], scalar1=mu, scalar2=rs, op0=ALU.subtract, op1=ALU.mult), [e, eb, ev_ln_h[0]])
                e = V(lambda v: v.tensor_tensor(out=vgf, in0=vgf, in1=lng, op=ALU.mult), [e, ev_tabs])
                ev_ln_h[0] = V(lambda v: v.tensor_tensor(out=vtm[:, tt, :], in0=vgf, in1=lnb, op=ALU.add), [e])

            for pc in range(6):
                s_, wv, evw = wpiece(w1, 0, 16, pc * 256, 256)
                last = None
                for c in range(2):
                    g = pc * 2 + c
                    last = proj_fm(wv, c * 128, lambda k, tk0=tk0: hT[:, k, tk0:tk0 + 512], 16, 512,
                                   lambda b, ev, g=g: A(lambda a: a.activation(out=gTm[:, g, :], in_=bank(b), func=AF.Gelu), [ev, half_done]),
                                   waits=[evw, evh])
                R.release(s_, last)
                if pc < 4:
                    ln_tile(pc)
            ev_ln = ev_ln_h[0]
            udone = [bfree[0], bfree[1]]
            last_ma = mem_attn_multi([(m, qm1[:, m, :], gTm[:, 12 + m, :], [qdone, half_done]) for m in range(4)], rden1)
            ev_gate = None
            for g in range(12):
                b = 2 + (g % 2)
                P.wait("pe", [bfree[b], ev_ln, ev_ws, ev_hl])
                ev = None
                for tt in range(4):
                    ob = bank(b)[:, tt * 128:(tt + 1) * 128]
                    T(lambda t, ob=ob, tt=tt, g=g: t.matmul(ob, vtm[:, tt, g * 128:(g + 1) * 128], wsT[:, g, :],
                                                          start=True, stop=False))
                    ev = T(lambda t, ob=ob, g=g: t.matmul(ob, ones[0:64, :], bsp_hl[:, g * 128:(g + 1) * 128],
                                                         start=False, stop=True), sig=(tt == 3))
                ev_gate = V(lambda v, b=b, g=g: v.tensor_tensor(out=gTm[:, g, :], in0=bank(b), in1=gTm[:, g, :], op=ALU.mult),
                            [ev, udone])
                bfree[b] = ev_gate
            cat_ev = [ev_gate, last_ma]
            toks = [hf * 4 + i for i in range(4)]
            r = out_proj(wo1, 16, lambda k, t: gTm[:, k, (t % 4) * 128:(t % 4 + 1) * 128], toks, x_ready, cat_ev)
            for t in toks:
                x_ready[t] = r[t]
            half_done = [r[t] for t in toks]
            P.wait("pe", half_done)
        hT_free[0] = half_done
        xt_free[0] = xt_free[1] = x_ready
        if stop != "attn1":
            if final:
                ev_gf_h[0] = SPD(lambda q: q.dma_start(out=gfin, in_=gfin_d), "gld8", x_ready)
                x_ready = ffn(1, x_ready, final_tile, pre_tiles=(0, 1, 2, 3) if pre_evs else (), pre_evh=pre_evs)
                final_done[0] = True
            else:
                x_ready = ffn(1, x_ready, pre_tiles=(0, 1, 2, 3) if pre_evs else (), pre_evh=pre_evs)
            xt_free[0] = xt_free[1] = x_ready

    if final and not final_done[0]:
        ev_gf_h[0] = SPD(lambda q: q.dma_start(out=gfin, in_=gfin_d), "gld8", x_ready)
        for t in range(8):
            final_tile(t, x_ready, hT_free[0])
    elif not final:
        for t in range(8):
            evs = SPD(lambda q, t=t: q.dma_start(out=y[t * 128:(t + 1) * 128, :], in_=xs[:, t, :]), "yst0", [x_ready[t]])
            st_evs.append(evs)
    P.wait("sp", st_evs)
    P.build()
    es.close()
    return nc


_NC_CACHE = {}


def _get_nc(layers, final, stop=None):
    key = (tuple(layers), final, stop)
    if key not in _NC_CACHE:
        _NC_CACHE[key] = build(layers, final, stop)
    return _NC_CACHE[key]


def _halo_idx(T0):
    g3 = (T0 - 2048 + 16 * np.arange(128)[None, :] + np.arange(16)[:, None]).reshape(-1)
    g2 = (T0 - 512 + 4 * np.arange(128)[None, :] + np.arange(4)[:, None]).reshape(-1)
    g1 = T0 - 128 + np.arange(128)
    return np.concatenate([g3, g2, g1]).astype(np.int64)


def _consts():
    k = np.arange(128)[:, None]
    q = np.arange(128)[None, :]
    ident = (k == q)
    m_own = (k <= q)
    m_prev = (k >= q)
    m_b3 = ((k // 64) == (q // 64)) & ((k % 64) <= (q % 64))
    ones = np.ones((128, 128), bool)
    m32 = np.arange(32)[None, :]
    rsw = (k < 32) & (k == ((m32 + 16) % 32))
    cmat = np.concatenate([ident, m_own, m_prev, m_own, ones, rsw], axis=1).astype(np.float32)
    half = 16
    inv_freq = (np.float32(500000.0) ** (-(np.arange(half, dtype=np.float32)) / np.float32(half))).astype(np.float32)
    invf = np.zeros((32, 2), np.float32)
    invf[:, 0] = inv_freq[np.arange(32) % 16]
    invf[:, 1] = np.where(np.arange(32) < 16, -1.0, 1.0)
    return np.ascontiguousarray(cmat), invf


def _prep(inp, layers):
    f = lambda a: np.ascontiguousarray(np.asarray(a, dtype=np.float32))
    cmat, invf = _consts()
    gl = [inp["mix_norm"][0], inp["mix_norm"][1], inp["ffn_norm"][0], inp["ffn_norm"][1],
          inp["mem_norm"][0], inp["mem_norm"][1]]
    gT = np.concatenate([np.asarray(g, np.float32).reshape(16, 128).T for g in gl], axis=1)
    common = {
        "mem": f(inp["mem"][0]), "cmat": cmat, "gT": f(gT),
        "gfin": f(np.broadcast_to(np.asarray(inp["final_norm"], np.float32)[None, :], (128, DM))),
        "wkv": f(inp["w_mem_kv"]), "wg": f(inp["w_gate"]), "wu": f(inp["w_up"]), "wd": f(inp["w_down"]),
    }
    if 0 in layers:
        w = np.asarray(inp["attn_w_in"][0], np.float32)
        qc = lambda h: w[:, h * 128:(h + 1) * 128]
        kc = lambda h: w[:, 1536 + h * 128:1536 + (h + 1) * 128]
        vc = lambda h: w[:, 3072 + h * 128:3072 + (h + 1) * 128]
        mc = lambda m: w[:, 4608 + m * 128:4608 + (m + 1) * 128]
        cols = []
        for s in range(4):
            cols += [qc(s), qc(4 + s), qc(8 + s), kc(s), kc(4 + s), kc(8 + s), mc(s), vc(s), vc(4 + s), vc(8 + s)]
        for gi in (2, 1, 0):
            cols += [w[:, 1536 + gi * 512:1536 + (gi + 1) * 512], w[:, 3072 + gi * 512:3072 + (gi + 1) * 512]]
        common["w0"] = np.ascontiguousarray(np.concatenate(cols, axis=1))
        common["wo0"] = f(inp["attn_w_out"][0])
        common["invf"] = invf
    if 1 in layers:
        common["w1"] = f(inp["sgu_w_in"][0])
        common["wo1"] = f(inp["sgu_w_out"][0])
        common["lng"] = f(np.broadcast_to(np.asarray(inp["sgu_ln_g"][0], np.float32)[None, :], (128, 1536)))
        common["lnb"] = f(np.broadcast_to(np.asarray(inp["sgu_ln_b"][0], np.float32)[None, :], (128, 1536)))
        common["bsp"] = f(np.broadcast_to(np.asarray(inp["sgu_b_spatial"][0], np.float32).reshape(1, 1536), (128, 1536)))
        common["wst"] = f(np.asarray(inp["sgu_w_spatial"][0], np.float32).transpose(2, 0, 1).reshape(128, 1536))
    return common


def _run(inp, x2, layers, final, stop=None, ncores=NCORES):
    nc = _get_nc(layers, final, stop)
    common = _prep(inp, layers)
    pos = np.asarray(inp["positions"][0], np.int32)
    in_maps = []
    for c in range(ncores):
        T0 = c * TOK
        m = dict(common)
        m["xo"] = np.ascontiguousarray(x2[T0:T0 + TOK])
        if 0 in layers:
            idx = _halo_idx(T0)
            ok = idx >= 0
            ic = np.clip(idx, 0, None)
            xh = x2[ic].copy()
            xh[~ok] = 0.0
            m["xh"] = xh
            pp = np.concatenate([pos[ic], pos[T0:T0 + TOK]]).astype(np.int32)
            m["posr"] = np.ascontiguousarray(np.broadcast_to(pp[None, :], (32, HALO + TOK)))
            m["valid"] = np.ascontiguousarray(ok.astype(np.float32).reshape(21, 128).T)
        in_maps.append(m)
    res = run_bass_kernel_spmd(nc, in_maps, core_ids=list(range(ncores)))
    return np.concatenate([r["y"] for r in res.results], axis=0)


def kernel(**inp):
    x2 = np.ascontiguousarray(np.asarray(inp["x"], np.float32)[0])
    out = _run(inp, x2, (0, 1), True)
    return out.reshape(1, NCORES * TOK, DM).astype(np.float32)
```

```python
import numpy as np
from contextlib import ExitStack
import concourse.bass as bass
import concourse.mybir as mybir
from concourse.bass_utils import run_bass_kernel_spmd

F32 = mybir.dt.float32
BF16 = mybir.dt.bfloat16
I32 = mybir.dt.int32
AF = mybir.ActivationFunctionType
ALU = mybir.AluOpType

NCORES = 8
TOK = 1024
DM = 2048
FF = 5632
NS = 4
HALO = 2688
ENGS = ("pe", "act", "dve", "pool", "sp")
HALO_BATCHES = [(0, 4, 2), (4, 4, 2), (8, 4, 2), (12, 4, 2), (16, 4, 1), (20, 1, 0)]
HALO_COL0 = {2: 0, 1: 2048, 0: 2560}


class Prog:
    def __init__(self, nc):
        self.nc = nc
        self.q = {e: [] for e in ENGS}
        self.semcnt = {}
        self.waited = {e: {} for e in ENGS}

    def op(self, eng, fn):
        ent = {"fn": fn, "inc": None}
        self.q[eng].append(ent)
        return ent

    def inc(self, ent, sem, amt=1):
        assert ent["inc"] is None
        self.semcnt[sem] = self.semcnt.get(sem, 0) + amt
        ent["inc"] = (sem, amt)
        return (sem, self.semcnt[sem])

    def wait(self, eng, ev):
        if ev is None:
            return
        if isinstance(ev, list):
            for e in ev:
                self.wait(eng, e)
            return
        sem, val = ev
        w = self.waited[eng]
        if w.get(sem, 0) >= val:
            return
        w[sem] = val
        self.q[eng].append({"wait": (sem, val)})

    def build(self):
        nc = self.nc
        with ExitStack() as es:
            sems = {}
            for name in self.semcnt:
                sems[name] = es.enter_context(nc.semaphore(name))
            block = es.enter_context(nc.Block())

            def replay(engname):
                def body(eng):
                    for ent in self.q[engname]:
                        if "wait" in ent:
                            s, v = ent["wait"]
                            eng.wait_ge(sems[s], v)
                        else:
                            ins = ent["fn"](eng)
                            if ent["inc"] is not None:
                                s, a = ent["inc"]
                                ins.then_inc(sems[s], a)
                return body

            for name, meth in (("sp", block.sync), ("pe", block.tensor), ("act", block.scalar),
                               ("dve", block.vector), ("pool", block.gpsimd)):
                if self.q[name]:
                    meth(replay(name))


def build(layers=(0, 1), final=True, stop=None):
    nc = bass.Bass("TRN2", target_bir_lowering=False)
    P = Prog(nc)

    def din(name, shape, dt=F32):
        return nc.dram_tensor(name, list(shape), dt, kind="ExternalInput").ap()

    xo = din("xo", [TOK, DM])
    if 0 in layers:
        xh = din("xh", [HALO, DM])
        posr = din("posr", [32, HALO + TOK], I32)
        valid = din("valid", [128, 21])
        w0 = din("w0", [DM, 8192])
        wo0 = din("wo0", [1024, DM])
        invf = din("invf", [32, 2])
    if 1 in layers:
        w1 = din("w1", [DM, 3584])
        wo1 = din("wo1", [DM, DM])
        lng_d = din("lng", [128, 1536])
        lnb_d = din("lnb", [128, 1536])
        bsp_d = din("bsp", [128, 1536])
        wst_d = din("wst", [128, 1536])
    memd = din("mem", [256, DM])
    cmat_d = din("cmat", [128, 672])
    gT_d = din("gT", [128, 96])
    gfin_d = din("gfin", [128, DM])
    wkv = din("wkv", [2, DM, 1024])
    wg = din("wg", [2, DM, FF])
    wu = din("wu", [2, DM, FF])
    wd = din("wd", [2, FF, DM])
    y = nc.dram_tensor("y", [TOK, DM], F32, kind="ExternalOutput").ap()

    es = ExitStack()
    sb = lambda name, shape, dt: es.enter_context(nc.sbuf_tensor(name, list(shape), dt))
    xs = sb("xs", [128, 8, DM], F32)
    ring = [sb(f"ring{i}", [128, 4096], BF16) for i in range(NS)]
    hT = sb("hT", [128, 16, TOK], BF16)
    am = sb("am", [128, 8192], BF16)
    mixT = am
    actT = am[:].rearrange("p (a b) -> p a b", b=TOK)
    xt = [am[:, i * 4096:(i + 1) * 4096].bitcast(F32) for i in range(2)]
    xn = [sb("xn0", [128, DM], BF16)] * 2
    ph = sb("ph", [128, 11264], BF16)
    cmat = sb("cmat_s", [128, 672], BF16)
    gT = sb("gT_s", [128, 96], F32)
    stat = sb("stat", [128, 64], F32)
    misc = sb("misc", [128, 12 * 1024], BF16)
    ptS = [sb(f"pt{i}", [128, 512], BF16) for i in range(2)]
    pmS = [sb(f"pm{i}", [128, 512], BF16) for i in range(2)]
    memKT = sb("memKT", [128, 4, 256], BF16)
    memV = sb("memV", [128, 2, 512], BF16)
    sgt = ptS
    psall = es.enter_context(nc.psum_tensor("psall", [128, 8, 512], F32))

    ident = cmat[:, 0:128]
    M_own = cmat[:, 128:256]
    M_prev = cmat[:, 256:384]
    M_B3 = cmat[:, 384:512]
    ones = cmat[:, 512:640]
    Rsw = cmat[:, 640:672]

    def bank(b):
        return psall[:, b, :]

    def bank_bf(b):
        return psall[:, b, :].bitcast(BF16)

    bfree = [None] * 8

    def A(fn, waits=None):
        P.wait("act", waits)
        return P.inc(P.op("act", fn), "sA")

    def V(fn, waits=None):
        P.wait("dve", waits)
        return P.inc(P.op("dve", fn), "sV")

    def T(fn, waits=None, sig=False):
        P.wait("pe", waits)
        e = P.op("pe", fn)
        return P.inc(e, "sT") if sig else None

    def SPD(fn, sem, waits=None):
        P.wait("sp", waits)
        return P.inc(P.op("sp", fn), sem, 16)

    def misc_take(off_bytes, nbytes, dt, parts=128):
        a = misc[0:parts, off_bytes // 2:(off_bytes + nbytes) // 2]
        return a if dt == BF16 else a.bitcast(dt)

    class Ring:
        def __init__(self):
            self.free = [None] * NS
            self.n = 0

        def load(self, src3, nk, ncol):
            s = self.n % NS
            self.n += 1
            P.wait("pool", self.free[s])
            view = ring[s][:, 0:nk * ncol].rearrange("p (k c) -> p k c", c=ncol)
            e = P.op("pool", lambda g, view=view, src3=src3: g.dma_start(out=view, in_=src3))
            ev = P.inc(e, f"wld{s}", 16)
            return s, view, ev

        def release(self, s, ev):
            self.free[s] = ev

    R = Ring()

    def wpiece(w2d, row0, nk, col0, ncol):
        src = w2d[row0:row0 + nk * 128, col0:col0 + ncol].rearrange("(k p) c -> p k c", p=128)
        return R.load(src, nk, ncol)

    e = P.op("pool", lambda g: g.dma_start(out=cmat[:], in_=cmat_d))
    ev_c = P.inc(e, "cld", 16)
    ev_g = SPD(lambda q: q.dma_start(out=gT[:], in_=gT_d), "gld1")
    for en in ("pe", "dve", "act"):
        P.wait(en, ev_c)
    P.wait("dve", ev_g)

    eps_t = sb("eps_t", [128, 2], F32)
    eps_ap = eps_t[:, 0:1]
    lneps_ap = eps_t[:, 1:2]
    V(lambda v: v.memset(eps_t[:, 0:1], 1e-6))
    ev_eps = V(lambda v: v.memset(eps_t[:, 1:2], 1e-5))
    P.wait("act", ev_eps)

    stat_i = [0]

    def stat_col():
        i = stat_i[0] % 64
        stat_i[0] += 1
        return stat[:, i:i + 1]

    tp_i = [0]
    xn_free = [None]
    pend_rot = [None]

    def flush_rot():
        if pend_rot[0] is not None:
            f = pend_rot[0]
            pend_rot[0] = None
            f()
    xnb = xn[0]

    def norm_A(src, src_ev, gcol0, dst_of, dst_wait=None):
        ss = stat_col()
        sq = stat_col()
        rs = stat_col()
        ev = A(lambda a: a.activation(out=xnb[:], in_=src, func=AF.Square, accum_out=ss), [src_ev, xn_free[0]])
        ev = A(lambda a: a.activation(out=sq, in_=ss, func=AF.Sqrt, scale=1.0 / DM, bias=eps_ap), [ev])
        ev = V(lambda v: v.reciprocal(out=rs, in_=sq), [ev])
        ev_xn = A(lambda a: a.activation(out=xnb[:], in_=src, func=AF.Copy, scale=rs), [ev])
        return (ev_xn, gcol0, dst_of, dst_wait)

    def norm_B(st):
        ev_xn, gcol0, dst_of, dst_wait = st
        evd = None
        last_pe = None
        for h in range(2):
            b = 6 + (tp_i[0] % 2)
            tp_i[0] += 1
            P.wait("pe", [bfree[b], ev_xn])
            for j in range(8):
                kc = h * 8 + j
                last_pe = T(lambda t, b=b, j=j, kc=kc: t.transpose(
                    bank_bf(b)[:, j * 128:(j + 1) * 128], xnb[:, kc * 128:(kc + 1) * 128], ident), sig=(j == 7))
            flush_rot()
            gb = gT[:, gcol0 + h * 8: gcol0 + h * 8 + 8].unsqueeze(2).to_broadcast([128, 8, 128])
            dst = dst_of(h)
            evd = V(lambda v, b=b, dst=dst, gb=gb: v.tensor_tensor(
                out=dst, in0=bank_bf(b).rearrange("p (a c) -> p a c", c=128), in1=gb, op=ALU.mult),
                [last_pe, dst_wait])
            bfree[b] = evd
        xn_free[0] = last_pe
        return evd, ev_xn

    def norm_tile(src, src_ev, gcol0, dst_of, dst_wait=None):
        return norm_B(norm_A(src, src_ev, gcol0, dst_of, dst_wait))

    xt_free = [None, None]
    xt_n = [0]

    def load_xtile(src_rows):
        i = xt_n[0] % 2
        xt_n[0] += 1
        ev = SPD(lambda q, i=i, src_rows=src_rows: q.dma_start(out=xt[i], in_=src_rows), f"xld{i}", [xt_free[i]])
        return i, ev

    pj = [0]

    def proj_fm(wview, c0, rhs_of, nk, n, evac, waits=None, banks=(0, 1), oshape=None):
        b = banks[pj[0] % len(banks)]
        pj[0] += 1
        P.wait("pe", [bfree[b], waits])
        out = bank(b)[:, 0:n]
        if oshape is not None:
            out = oshape(out)
        ev = None
        for k in range(nk):
            ev = T(lambda t, k=k: t.matmul(out, wview[:, k, c0:c0 + 128], rhs_of(k),
                                            start=(k == 0), stop=(k == nk - 1)), sig=(k == nk - 1))
        flush_rot()
        r = evac(b, ev)
        if callable(r):
            pend_rot[0] = r
        else:
            bfree[b] = r
        return ev

    def proj_tm(lhs_of, wview, c0, n, nk, evac, waits=None, banks=(4, 5)):
        b = banks[pj[0] % len(banks)]
        pj[0] += 1
        P.wait("pe", [bfree[b], waits])
        ev = None
        for k in range(nk):
            ev = T(lambda t, b=b, k=k: t.matmul(bank(b)[:, 0:n], lhs_of(k), wview[:, k, c0:c0 + n],
                                                 start=(k == 0), stop=(k == nk - 1)), sig=(k == nk - 1))
        flush_rot()
        bfree[b] = evac(b, ev)
        return ev

    def mem_kv(layer, memT):
        evd = None
        for mt in range(2):
            i, evl = load_xtile(memd[mt * 128:(mt + 1) * 128, :])
            evd, evr = norm_tile(xt[i], evl, 64 + 16 * layer,
                                 lambda h, mt=mt: memT[:, h * 8:(h + 1) * 8, mt * 128:(mt + 1) * 128])
            xt_free[i] = evr
        last = None
        for pc in range(4):
            s, wv, evw = wpiece(wkv[layer], 0, 16, pc * 256, 256)
            if pc < 2:
                for c in range(2):
                    m = pc * 2 + c
                    last = proj_fm(wv, c * 128, lambda k: memT[:, k, :], 16, 256,
                                   lambda b, ev, m=m: A(lambda a: a.copy(out=memKT[:, m, :], in_=bank(b)[:, 0:256]), [ev]),
                                   waits=[evw, evd])
            else:
                for mt in range(2):
                    cc = (pc - 2) * 256
                    last = proj_tm(lambda k, mt=mt: memT[:, k, mt * 128:(mt + 1) * 128], wv, 0, 256, 16,
                                   lambda b, ev, mt=mt, cc=cc: A(lambda a: a.copy(out=memV[:, mt, cc:cc + 256], in_=bank(b)[:, 0:256]), [ev]),
                                   waits=[evw, evd])
            R.release(s, last)
        return last

    pt_free = [None, None]
    pm_free = [None, None]

    def recip_act(dst, src, waits):
        e = A(lambda a: a.activation(out=dst, in_=src, func=AF.Ln), waits)
        return A(lambda a: a.activation(out=dst, in_=dst, func=AF.Exp, scale=-1.0), [e])

    pv_i = [0]
    rd_free = [None]

    def mem_attn_multi(groups, rden, pairs=((4, 6), (5, 7))):
        units = []
        for gi_, (m, q_ap, dst, q_ev) in enumerate(groups):
            accb, denb = pairs[gi_ % len(pairs)]
            for mt in range(2):
                units.append((m, q_ap, dst, q_ev, accb, denb, mt))

        def stage1(u):
            m, q_ap, dst, q_ev, accb, denb, mt = u
            i = pv_i[0] % 2
            pv_i[0] += 1
            sbk = 2 + i
            P.wait("pe", [bfree[sbk], q_ev])
            evs = T(lambda t: t.matmul(bank(sbk), memKT[:, m, mt * 128:(mt + 1) * 128], q_ap, start=True, stop=True), sig=True)
            eve = A(lambda a: a.activation(out=ptS[i][:], in_=bank(sbk), func=AF.Exp, scale=128 ** -0.5), [evs, pt_free[i]])
            bfree[sbk] = eve

            def stage2():
                if mt == 0:
                    P.wait("pe", [bfree[accb], bfree[denb]])
                T(lambda t: t.matmul(bank(accb), memV[:, mt, m * 128:(m + 1) * 128], ptS[i][:],
                                     start=(mt == 0), stop=(mt == 1)), waits=[eve])
                evp = T(lambda t: t.matmul(bank(denb), ones, ptS[i][:], start=(mt == 0), stop=(mt == 1)), sig=True)
                pt_free[i] = evp
                if mt == 1:
                    ev1 = recip_act(rden[:, 0:512], bank(denb), [evp, rd_free[0]])
                    ev2 = V(lambda v: v.tensor_tensor(out=dst, in0=bank(accb), in1=rden[:, 0:512], op=ALU.mult), [ev1])
                    rd_free[0] = ev2
                    bfree[accb] = ev2
                    bfree[denb] = ev1
                    return ev2
                return evp
            return stage2

        pend = None
        last = None
        for u in units:
            s2 = stage1(u)
            if pend is not None:
                last = pend()
            pend = s2
        last = pend()
        return last

    hT_free = [None]

    def ffn(layer, x_ready, tile_cb=None, evh_pre=None, pre_tiles=(), pre_evh=None):
        evh = evh_pre
        for t in range(8 if evh_pre is None else 0):
            if t in pre_tiles:
                continue
            evh, _ = norm_tile(xs[:, t, :], x_ready[t], 32 + 16 * layer,
                               lambda h, t=t: hT[:, h * 8:(h + 1) * 8, t * 128:(t + 1) * 128], dst_wait=hT_free[0])
        if pre_evh:
            evh = [evh, pre_evh]
        groups = [(0, 8), (8, 8), (16, 8), (24, 8), (32, 8), (40, 4)]
        xev = list(x_ready)
        act_free = x_ready
        gu_i = 0
        dn_i = 0
        sg_free = [None, None]
        evpu = None
        for (f0, nf) in groups:
            ev_act_last = None
            for pr in range(nf // 2):
                col = (f0 + 2 * pr) * 128
                sg_, wgv, evg = wpiece(wg[layer], 0, 16, col, 256)
                su_, wuv, evu = wpiece(wu[layer], 0, 16, col, 256)
                for c in range(2):
                    fi = 2 * pr + c
                    for th in range(2):
                        bg = 0 + 2 * (gu_i % 2)
                        bu = 1 + 2 * (gu_i % 2)
                        si = gu_i % 2
                        gu_i += 1
                        P.wait("pe", [bfree[bg], bfree[bu], evg, evu, evh])
                        evpg = None
                        for k in range(16):
                            evpg = T(lambda t, bg=bg, k=k, c=c, th=th, wgv=wgv: t.matmul(
                                bank(bg), wgv[:, k, c * 128:(c + 1) * 128], hT[:, k, th * 512:(th + 1) * 512],
                                start=(k == 0), stop=(k == 15)), sig=(k == 15))
                        for k in range(16):
                            evpu = T(lambda t, bu=bu, k=k, c=c, th=th, wuv=wuv: t.matmul(
                                bank(bu), wuv[:, k, c * 128:(c + 1) * 128], hT[:, k, th * 512:(th + 1) * 512],
                                start=(k == 0), stop=(k == 15)), sig=(k == 15))
                        evs = A(lambda a, bg=bg, si=si: a.activation(out=sgt[si][:], in_=bank(bg), func=AF.Silu),
                                [evpg, sg_free[si]])
                        bfree[bg] = evs
                        dst = actT[:, fi, th * 512:(th + 1) * 512]
                        evm = V(lambda v, bu=bu, si=si, dst=dst: v.tensor_tensor(
                            out=dst, in0=bank(bu), in1=sgt[si][:], op=ALU.mult), [evs, evpu, act_free])
                        bfree[bu] = evm
                        sg_free[si] = evm
                        ev_act_last = evm
                R.release(sg_, evpu)
                R.release(su_, evpu)
            lastdn = None
            if tile_cb is not None and f0 == 40:
                pcs = [wpiece(wd[layer], f0 * 128, nf, jp * 1024, 1024) for jp in range(2)]
                lastpe = None
                for t in range(8):
                    for j in range(4):
                        sd_, wdv, evd = pcs[j // 2]
                        jh = j % 2
                        b = 4 + (dn_i % 2)
                        dn_i += 1
                        P.wait("pe", [bfree[b], evd, ev_act_last])
                        for f in range(nf):
                            lastpe = T(lambda tt, b=b, f=f, t=t, wdv=wdv, jh=jh: tt.matmul(
                                bank(b), actT[:, f, t * 128:(t + 1) * 128], wdv[:, f, jh * 512:(jh + 1) * 512],
                                start=(f == 0), stop=(f == nf - 1)), sig=(f == nf - 1))
                        xsl = xs[:, t, j * 512:(j + 1) * 512]
                        eva = V(lambda v, b=b, xsl=xsl: v.tensor_tensor(out=xsl, in0=bank(b), in1=xsl, op=ALU.add),
                                [lastpe, xev[t]])
                        bfree[b] = eva
                        xev[t] = eva
                        lastdn = eva
                    tile_cb(t, xev[t], evpu)
                for jp in range(2):
                    R.release(pcs[jp][0], lastpe)
                act_free = lastdn
                continue
            for j in range(4):
                sd_, wdv, evd = wpiece(wd[layer], f0 * 128, nf, j * 512, 512)
                lastpe = None
                for t in range(8):
                    b = 4 + (dn_i % 2)
                    dn_i += 1
                    P.wait("pe", [bfree[b], evd, ev_act_last])
                    for f in range(nf):
                        lastpe = T(lambda tt, b=b, f=f, t=t, wdv=wdv: tt.matmul(
                            bank(b), actT[:, f, t * 128:(t + 1) * 128], wdv[:, f, :],
                            start=(f == 0), stop=(f == nf - 1)), sig=(f == nf - 1))
                    xsl = xs[:, t, j * 512:(j + 1) * 512]
                    eva = V(lambda v, b=b, xsl=xsl: v.tensor_tensor(out=xsl, in0=bank(b), in1=xsl, op=ALU.add),
                            [lastpe, xev[t]])
                    bfree[b] = eva
                    xev[t] = eva
                    lastdn = eva
                R.release(sd_, lastpe)
            act_free = lastdn
        hT_free[0] = evpu
        return xev

    def out_proj(wout, nk, catT_of, toks, x_evs, cat_ev):
        xev = dict((t, x_evs[t]) for t in toks)
        for j in range(8):
            s_, wv, evw = wpiece(wout, 0, nk, j * 256, 256)
            lastpe = None
            for t in toks:
                def evac(b, ev, t=t, j=j):
                    dst = xs[:, t, j * 256:(j + 1) * 256]
                    e2 = V(lambda v: v.tensor_tensor(out=dst, in0=bank(b)[:, 0:256], in1=dst, op=ALU.add), [ev, xev[t]])
                    xev[t] = e2
                    return e2
                lastpe = proj_tm(lambda k, t=t: catT_of(k, t), wv, 0, 256, nk, evac, waits=[evw, cat_ev],
                                 banks=(0, 1, 2, 3))
            R.release(s_, lastpe)
        return xev

    def out_proj_t(wout, nk, catT_of, toks, x_evs, cat_ev, tile_cb):
        xev = dict((t, x_evs[t]) for t in toks)
        pcs = [wpiece(wout, 0, nk, j * 512, 512) for j in range(4)]
        lastpe = None
        for t in toks:
            for j in range(4):
                s_, wv, evw = pcs[j]

                def evac(b, ev, t=t, j=j):
                    dst = xs[:, t, j * 512:(j + 1) * 512]
                    e2 = V(lambda v: v.tensor_tensor(out=dst, in0=bank(b), in1=dst, op=ALU.add), [ev, xev[t]])
                    xev[t] = e2
                    return e2
                lastpe = proj_tm(lambda k, t=t: catT_of(k, t), wv, 0, 512, nk, evac, waits=[evw, cat_ev], banks=(0, 1, 2, 3))
            tile_cb(t, xev[t])
        for j in range(4):
            R.release(pcs[j][0], lastpe)
        return xev

    x_ready = [None] * 8
    if 0 in layers:
        xsb = xs[:].rearrange("p a b -> p (a b)").bitcast(BF16)
        KTh = xsb[:, 0:10752].rearrange("p (h t) -> p h t", t=HALO)
        Vh = xsb[:, 10752:21504].rearrange("p (t c) -> p t c", c=512)
        QT = xsb[:, 21504:24576].rearrange("p (g t) -> p g t", t=TOK)
        KTo = xsb[:, 24576:27648].rearrange("p (g t) -> p g t", t=TOK)
        Vo = xsb[:, 27648:29696].rearrange("p (g j c) -> p g j c", j=8, c=128)
        Vo3 = xsb[:, 29696:31744].rearrange("p (r c) -> p r c", c=128)
        qmT = xsb[:, 31744:32768]
        qnb = [ph[:, 7168:7680], ph[:, 7680:8192]]
        attnT = am[:].rearrange("p (k t) -> p k t", t=TOK)
        Co = misc_take(0, 4096, F32, 32)
        So = misc_take(4096, 4096, F32, 32)
        Ch = misc_take(8192, 2048, F32, 32)
        Sh = misc_take(10240, 2048, F32, 32)
        ChB = [Ch, ph[0:32, 9216:10240].bitcast(F32)]
        ShB = [Sh, ph[0:32, 10240:11264].bitcast(F32)]
        tab_last_use = [None, None]
        t1 = misc_take(12288, 2048, F32, 32)
        t2 = misc_take(14336, 2048, F32, 32)
        onesv = misc_take(16384, 5376, BF16).rearrange("p (t c) -> p t c", c=128)
        angb = ph[0:32, 0:1024].bitcast(F32)
        kb_i = ph[0:32, 1024:2048].bitcast(I32)
        kb_f = ph[0:32, 2048:3072].bitcast(F32)
        cmpb = ph[0:32, 3072:4096].bitcast(F32)
        posb = ph[0:32, 4096:5120].bitcast(I32)
        rden = ph[:, 5120:7168].bitcast(F32)
        invf_s = sb("invf_s", [32, 2], F32)
        valid_s = sb("valid_s", [128, 21], F32)
        memT0 = hT[:].rearrange("p a b -> p (a b)")[:, 0:4096].rearrange("p (k t) -> p k t", t=256)

        ev_if = SPD(lambda q: q.dma_start(out=invf_s[:], in_=invf), "gld2")
        ev_vl = SPD(lambda q: q.dma_start(out=valid_s[:], in_=valid), "gld3")

        mem_last = mem_kv(0, memT0)
        ev_ov = V(lambda v: v.tensor_copy(out=onesv, in_=valid_s[:].unsqueeze(2).to_broadcast([128, 21, 128])), [ev_vl])

        TWO_PI = float(2.0 * np.pi)
        C1 = 6.28125
        C2 = float(2.0 * np.pi - 6.28125)
        tab_chain = [None]

        def rot_tables_p1(pcol0, n, Cdst, Sdst, waits):
            ang = angb[:, 0:n]
            ki = kb_i[:, 0:n]
            kf = kb_f[:, 0:n]
            cm = cmpb[:, 0:n]
            evp = SPD(lambda q: q.dma_start(out=posb[:, 0:n], in_=posr[:, pcol0:pcol0 + n]), "pld", [tab_chain[0]])
            ev = V(lambda v: v.tensor_scalar(out=ang, in0=posb[:, 0:n], scalar1=invf_s[:, 0:1], scalar2=None, op0=ALU.mult),
                   [waits, evp, ev_if])
            tab_chain[0] = ev
            for which, dst in ((0, Sdst), (1, Cdst)):
                src = ang
                if which == 1:
                    ev = V(lambda v: v.tensor_scalar(out=cm, in0=ang, scalar1=float(np.pi / 2), scalar2=None, op0=ALU.add), [ev])
                    src = cm
                ev = V(lambda v, src=src: v.tensor_scalar(out=ki, in0=src, scalar1=float(1.0 / TWO_PI), scalar2=None, op0=ALU.mult), [ev])
                ev = V(lambda v: v.tensor_copy(out=kf, in_=ki), [ev])
                ev = V(lambda v, src=src, dst=dst: v.scalar_tensor_tensor(out=dst, in0=kf, scalar=-C1, in1=src, op0=ALU.mult, op1=ALU.add), [ev])
                ev = V(lambda v, dst=dst: v.scalar_tensor_tensor(out=dst, in0=kf, scalar=-C2, in1=dst, op0=ALU.mult, op1=ALU.add), [ev])
                ev = V(lambda v, dst=dst: v.tensor_scalar(out=dst, in0=dst, scalar1=3.14159, scalar2=-3.14159, op0=ALU.min, op1=ALU.max), [ev])
            return (ev, Cdst, Sdst)

        def rot_tables_p2(st):
            ev, Cdst, Sdst = st
            e1 = A(lambda a: a.activation(out=Sdst, in_=Sdst, func=AF.Sin), [ev])
            e2 = A(lambda a: a.activation(out=Cdst, in_=Cdst, func=AF.Sin), [e1])
            e3 = V(lambda v: v.tensor_scalar(out=Sdst, in0=Sdst, scalar1=invf_s[:, 1:2], scalar2=None, op0=ALU.mult), [e1])
            P.wait("dve", e2)
            return [e2, e3]

        def rot_tables(pcol0, n, Cdst, Sdst, waits):
            return rot_tables_p2(rot_tables_p1(pcol0, n, Cdst, Sdst, waits))

        rot_i = [0]
        rot_free = [None]

        def rotary_evac(b, ev_pe, dst, Ct, St, tab_ev, n, shp=None):
            f = shp if shp is not None else (lambda a: a)
            ev_c = A(lambda a: a.copy(out=dst, in_=bank(b)[:, 0:n]), [ev_pe])

            def part2():
                rb = 2 + (rot_i[0] % 2)
                rot_i[0] += 1
                P.wait("pe", bfree[rb])
                ev_r = T(lambda t: t.matmul(bank(rb)[0:32, 0:n], Rsw, dst, start=True, stop=True), waits=[ev_c], sig=True)
                e1 = V(lambda v: v.tensor_tensor(out=f(t1[:, 0:n]), in0=f(bank(rb)[0:32, 0:n]), in1=St, op=ALU.mult),
                       [ev_r, tab_ev, rot_free[0]])
                bfree[rb] = e1
                e2 = V(lambda v: v.tensor_tensor(out=f(t2[:, 0:n]), in0=f(bank(b)[0:32, 0:n]), in1=Ct, op=ALU.mult), [ev_c])
                e3 = V(lambda v: v.tensor_tensor(out=dst[0:32], in0=t1[:, 0:n], in1=t2[:, 0:n], op=ALU.add), [e1, e2])
                rot_free[0] = e3
                bfree[b] = e3
            return part2

        rot_tables(HALO, 512, Co[:, 0:512], So[:, 0:512], None)
        ev_tabo = rot_tables(HALO + 512, 512, Co[:, 512:1024], So[:, 512:1024], None)

        hTf = hT[:].rearrange("p a b -> p (a b)")
        hTb = [hTf[:, i * 8192:(i + 1) * 8192].rearrange("p (k t) -> p k t", t=512) for i in range(2)]
        hb_free = [mem_last, mem_last]
        batch_evd = {}
        batch_tab = {}
        held = {}

        def halo_norm_A(bi, tt):
            tile0, ntile, gi = HALO_BATCHES[bi]
            hb = hTb[bi % 2]
            i, evl = load_xtile(xh[(tile0 + tt) * 128:(tile0 + tt + 1) * 128, :])
            st = norm_A(xt[i], evl, 0,
                        lambda h, tt=tt, hb=hb: hb[:, h * 8:(h + 1) * 8, tt * 128:(tt + 1) * 128],
                        dst_wait=hb_free[bi % 2])
            xt_free[i] = st[0]
            return (bi, st)

        def halo_norm_B(pst):
            bi, st = pst
            evd, evr = norm_B(st)
            batch_evd[bi] = evd

        def halo_norm(bi, tt):
            halo_norm_B(halo_norm_A(bi, tt))

        tab_st = {}

        def halo_tables_p1(bi):
            tile0, ntile, gi = HALO_BATCHES[bi]
            n = ntile * 128
            tab_st[bi] = rot_tables_p1(tile0 * 128, n, ChB[bi % 2][:, 0:n], ShB[bi % 2][:, 0:n], tab_last_use[bi % 2])

        def halo_tables_p2(bi):
            batch_tab[bi] = rot_tables_p2(tab_st[bi])

        def halo_tables(bi):
            halo_tables_p1(bi)
            halo_tables_p2(bi)

        def halo_piece(bi, pc):
            tile0, ntile, gi = HALO_BATCHES[bi]
            hb = hTb[bi % 2]
            n = ntile * 128
            evd = batch_evd[bi]
            kcol = 5120 + (2 - gi) * 1024
            if (gi, pc) not in held:
                held[(gi, pc)] = wpiece(w0, 0, 16, kcol + pc * 256, 256)
            s_, wv, evw = held[(gi, pc)]
            last_of_group = (bi + 1 >= len(HALO_BATCHES)) or (HALO_BATCHES[bi + 1][2] != gi)
            lastpe = None
            if pc < 2:
                ev_tab = batch_tab[bi]
                for c in range(2):
                    hd = pc * 2 + c
                    dst = KTh[:, hd, tile0 * 128: tile0 * 128 + n]
                    lastpe = proj_fm(wv, c * 128, lambda k, hb=hb, n=n: hb[:, k, 0:n], 16, n,
                                     lambda b, ev, dst=dst, n=n, ev_tab=ev_tab, bi=bi: rotary_evac(b, ev, dst, ChB[bi % 2][:, 0:n], ShB[bi % 2][:, 0:n], ev_tab, n),
                                     waits=[evw, evd])
            else:
                cc = (pc - 2) * 256
                for tt in range(ntile):
                    dst = Vh[:, tile0 + tt, cc:cc + 256]
                    lastpe = proj_tm(lambda k, hb=hb, tt=tt: hb[:, k, tt * 128:(tt + 1) * 128], wv, 0, 256, 16,
                                     lambda b, ev, dst=dst: A(lambda a: a.copy(out=dst, in_=bank(b)[:, 0:256]), [ev]),
                                     waits=[evw, evd])
            if pc == 1:
                flush_rot()
                tab_last_use[bi % 2] = rot_free[0]
            if last_of_group:
                R.release(s_, lastpe)
            if pc == 3:
                hb_free[bi % 2] = lastpe

        for tt in range(HALO_BATCHES[0][1]):
            halo_norm(0, tt)
        halo_tables(0)
        NB_H = len(HALO_BATCHES)
        for bi in range(NB_H):
            nxt = HALO_BATCHES[bi + 1][1] if bi + 1 < NB_H else 0
            for pc in range(4):
                pst = halo_norm_A(bi + 1, pc) if pc < nxt else None
                halo_piece(bi, pc)
                if pst is not None:
                    halo_norm_B(pst)
                if pc == 1 and bi + 1 < NB_H:
                    halo_tables_p1(bi + 1)
                if pc == 3 and bi + 1 < NB_H:
                    halo_tables_p2(bi + 1)

        ev_hT = None
        for t in range(8):
            i, evl = load_xtile(xo[t * 128:(t + 1) * 128, :])
            ev_hT, evr = norm_tile(xt[i], evl, 0,
                                   lambda h, t=t: hT[:, h * 8:(h + 1) * 8, t * 128:(t + 1) * 128],
                                   dst_wait=[hb_free[0], hb_free[1]])
            xt_free[i] = evr
        xt_last = [xt_free[0], xt_free[1]]

        def vtile(ap2, gi, j):
            if gi == 0:
                return ap2[:, j * 128:(j + 1) * 128]
            if gi == 1:
                r, bb = j // 2, j % 2
                return ap2.rearrange("p (i r) -> p r i", r=4)[:, r, bb * 128:(bb + 1) * 128]
            return ap2.rearrange("p (i r) -> p r i", r=16)[:, j, :]

        def gtile(ap2, gi, j):
            return vtile(ap2, gi, j)

        def nat_view(a, gi):
            if gi == 0:
                return a
            return a.rearrange("p (i r) -> p i r", r=(4 if gi == 1 else 16))

        def perm_view(row, gi, th):
            if gi == 0:
                return row[:, th * 512:(th + 1) * 512]
            if gi == 1:
                return row.rearrange("p (r i) -> p i r", r=4)[:, 128 * th:128 * (th + 1), :]
            return row.rearrange("p (r i) -> p i r", r=16)[:, 32 * th:32 * (th + 1), :]

        qn_i = [0]
        qn_free = [None, None]
        vt_free = [None]
        VTs1 = ph[:, 8192:9216]

        def rotary_evac_perm(b, ev_pe, dstrow, gi, th):
            qi = qn_i[0] % 2
            qn_i[0] += 1
            qn = qnb[qi]
            Ct = Co[:, th * 512:(th + 1) * 512]
            St = So[:, th * 512:(th + 1) * 512]
            ev_c = A(lambda a: a.copy(out=qn, in_=bank(b)), [ev_pe, qn_free[qi]])
            A(lambda a: a.copy(out=perm_view(dstrow[32:64], gi, th), in_=nat_view(bank(b)[32:64, :], gi)), [ev_pe])
            ev_c2 = A(lambda a: a.copy(out=perm_view(dstrow[64:128], gi, th), in_=nat_view(bank(b)[64:128, :], gi)), [ev_pe])
            def part2():
                rb = 2 + (rot_i[0] % 2)
                rot_i[0] += 1
                P.wait("pe", bfree[rb])
                ev_r = T(lambda t: t.matmul(bank(rb)[0:32, :], Rsw, qn, start=True, stop=True), waits=[ev_c], sig=True)
                qn_free[qi] = ev_r
                e1 = V(lambda v: v.tensor_tensor(out=t1, in0=bank(rb)[0:32, :], in1=St, op=ALU.mult),
                       [ev_r, ev_tabo, rot_free[0]])
                bfree[rb] = e1
                e2 = V(lambda v: v.tensor_tensor(out=t2, in0=bank(b)[0:32, :], in1=Ct, op=ALU.mult), [ev_c])
                e3 = V(lambda v: v.tensor_tensor(out=perm_view(dstrow[0:32], gi, th), in0=nat_view(t1, gi), in1=nat_view(t2, gi),
                                                 op=ALU.add), [e1, e2, ev_c2])
                rot_free[0] = e3
                bfree[b] = e3
            return part2

        accA = psall[:, 4:6, :].rearrange("p a b -> p (a b)")
        denA = psall[:, 6:8, :].rearrange("p a b -> p (a b)")
        att_done = None
        rdA_free = [None]
        rdm = misc_take(21760, 2048, F32)

        for s in range(4):
            base = s * 1280
            cur = [None, None, None]
            lastpe_piece = [None]

            def get_piece(pcI, cur=cur, base=base, lastpe_piece=lastpe_piece):
                if cur[0] != pcI:
                    if cur[0] is not None:
                        R.release(cur[1][0], lastpe_piece[0])
                    cur[0] = pcI
                    cur[1] = wpiece(w0, 0, 16, base + pcI * 256, 256)
                return cur[1]

            PB = (0, 1, 4, 5, 6, 7)
            for c in range(7):
                s_, wv, evw = get_piece(c // 2)
                cI = c % 2
                for th in range(2):
                    if c < 6:
                        gi = c % 3
                        dst = (QT if c < 3 else KTo)[:, gi, th * 512:(th + 1) * 512]
                        ev = proj_fm(wv, cI * 128, lambda k, th=th: hT[:, k, th * 512:(th + 1) * 512], 16, 512,
                                     lambda b, ev, dst=dst, th=th: rotary_evac(b, ev, dst, Co[:, th * 512:(th + 1) * 512],
                                                                               So[:, th * 512:(th + 1) * 512], ev_tabo, 512),
                                     waits=[evw, ev_hT, att_done], banks=PB)
                    else:
                        dst = qmT[:, th * 512:(th + 1) * 512]
                        ev = proj_fm(wv, cI * 128, lambda k, th=th: hT[:, k, th * 512:(th + 1) * 512], 16, 512,
                                     lambda b, ev, dst=dst: A(lambda a: a.copy(out=dst, in_=bank(b)), [ev]),
                                     waits=[evw, ev_hT, att_done], banks=PB)
                    lastpe_piece[0] = ev
            for gi in range(3):
                c = 7 + gi
                s_, wv, evw = get_piece(c // 2)
                cI = c % 2
                vrow = VTs1
                evv = []
                for th in range(2):
                    def evac_v(b, ev, vrow=vrow, th=th):
                        return A(lambda a: a.copy(out=vrow[:, th * 512:(th + 1) * 512], in_=bank(b)), [ev, vt_free[0]])
                    ev = proj_fm(wv, cI * 128, lambda k, th=th: hT[:, k, th * 512:(th + 1) * 512], 16, 512, evac_v,
                                 waits=[evw, ev_hT, att_done], banks=PB)
                    lastpe_piece[0] = ev
                    evv.append(bfree[PB[(pj[0] - 1) % len(PB)]])
                ntile = 8 if gi < 2 else 16
                tw = 128 if gi < 2 else 64
                for jb in range(ntile // 8):
                    b = 2 + (pj[0] % 2)
                    pj[0] += 1
                    P.wait("pe", [bfree[b], evv])
                    ev = None
                    for jj in range(8):
                        j = jb * 8 + jj
                        ev = T(lambda t, b=b, jj=jj, j=j, vrow=vrow, tw=tw, gi=gi: t.transpose(
                            bank_bf(b)[0:tw, jj * 128:(jj + 1) * 128], vtile(vrow, gi, j), ident), sig=(jj == 7))
                    if gi < 2:
                        dst = Vo[:, gi, :, :]
                    else:
                        dst = Vo3[0:64, jb * 8:(jb + 1) * 8, :]
                    bfree[b] = A(lambda a, b=b, dst=dst, tw=tw: a.copy(
                        out=dst, in_=bank_bf(b)[0:tw, :].rearrange("p (j c) -> p j c", c=128)), [ev])
                    vt_free[0] = ev
            flush_rot()
            R.release(cur[1][0], lastpe_piece[0])
            proj_done = [rot_free[0]] + [bfree[i] for i in range(8)]

            ev_z1 = V(lambda v: v.memset(accA, 0.0), [bfree[4], bfree[5]])
            ev_z2 = V(lambda v: v.memset(denA, 0.0), [bfree[6], bfree[7]])
            P.wait("pe", [ev_z1, ev_z2, proj_done, ev_ov])

            def g_attend(items, mask_ops, nk=128):
                def stage1():
                    i = pv_i[0] % 2
                    pv_i[0] += 1
                    sbk = 2 + i
                    P.wait("pe", bfree[sbk])
                    c = 0
                    offs = []
                    evs = None
                    for n_it, (k_ap, q_ap, nq, pvs) in enumerate(items):
                        evs = T(lambda t, c=c, nq=nq, k_ap=k_ap, q_ap=q_ap: t.matmul(
                            bank(sbk)[0:nk, c:c + nq], k_ap, q_ap, start=True, stop=True), sig=(n_it == len(items) - 1))
                        offs.append(c)
                        c += nq
                    tot = c
                    eve = A(lambda a: a.activation(out=ptS[i][0:nk, 0:tot], in_=bank(sbk)[0:nk, 0:tot],
                                                   func=AF.Exp, scale=128 ** -0.5), [evs, pt_free[i]])
                    bfree[sbk] = eve
                    evm = None
                    for (c0, ncl, in1_ap, inner) in mask_ops:
                        o = pmS[i][0:nk, c0:c0 + ncl].rearrange("p (a b) -> p a b", b=inner)
                        a_in = ptS[i][0:nk, c0:c0 + ncl].rearrange("p (a b) -> p a b", b=inner)
                        evm = V(lambda v, o=o, a_in=a_in, in1_ap=in1_ap: v.tensor_tensor(out=o, in0=a_in, in1=in1_ap, op=ALU.mult),
                                [eve, pm_free[i]])
                    pt_free[i] = evm

                    def stage2():
                        evp = None
                        first = True
                        for (k_ap, q_ap, nq, pvs), off in zip(items, offs):
                            for (co, ncl, v_ap, o_ap, acc_ap, den_ap) in pvs:
                                rhs = pmS[i][0:nk, off + co:off + co + ncl]
                                T(lambda t, v_ap=v_ap, rhs=rhs, acc_ap=acc_ap: t.matmul(
                                    acc_ap, v_ap, rhs, start=False, stop=False, skip_group_check=True),
                                  waits=[evm] if first else None)
                                first = False
                                evp = T(lambda t, o_ap=o_ap, rhs=rhs, den_ap=den_ap: t.matmul(
                                    den_ap, o_ap, rhs, start=False, stop=False, skip_group_check=True), sig=True)
                        pm_free[i] = evp
                        return evp
                    return stage2
                return stage1

            batches = []
            M_op = cmat[:, 128:384]
            M_po = cmat[:, 256:512]

            def g1_item(kt):
                if kt < 0:
                    k_ap, v_ap, o_ap = KTh[:, s, 2560:2688], Vh[:, 20, s * 128:(s + 1) * 128], onesv[:, 20, :]
                else:
                    k_ap, v_ap, o_ap = KTo[:, 0, kt * 128:(kt + 1) * 128], Vo[:, 0, kt, :], ones
                qlo, qhi = max(kt, 0), min(kt + 1, 7)
                nq = (qhi - qlo + 1) * 128
                pvs = [(qi * 128, 128, v_ap, o_ap, accA[:, qt * 128:(qt + 1) * 128], denA[:, qt * 128:(qt + 1) * 128])
                       for qi, qt in enumerate(range(qlo, qhi + 1))]
                return (k_ap, QT[:, 0, qlo * 128:qlo * 128 + nq], nq, pvs)

            batches.append(g_attend([g1_item(-1), g1_item(0)],
                                    [(0, 128, M_prev.unsqueeze(1), 128), (128, 256, M_op.unsqueeze(1), 256)]))
            for kt in (1, 3, 5):
                batches.append(g_attend([g1_item(kt), g1_item(kt + 1)],
                                        [(0, 512, M_op.unsqueeze(1).to_broadcast([128, 2, 256]), 256)]))
            batches.append(g_attend([g1_item(7)], [(0, 128, M_own.unsqueeze(1), 128)]))
            for r in range(4):
                items = []
                for kb in range(-1, 2):
                    if kb < 0:
                        k_ap, v_ap, o_ap = KTh[:, s, 2048 + r * 128:2048 + (r + 1) * 128], Vh[:, 16 + r, s * 128:(s + 1) * 128], onesv[:, 16 + r, :]
                    else:
                        k_ap, v_ap, o_ap = vtile(KTo[:, 1, :], 1, r * 2 + kb), Vo[:, 1, r * 2 + kb, :], ones
                    qlo, qhi = max(kb, 0), min(kb + 1, 1)
                    nq = (qhi - qlo + 1) * 128
                    pvs = [(qi * 128, 128, v_ap, o_ap, gtile(accA, 1, r * 2 + qb), gtile(denA, 1, r * 2 + qb))
                           for qi, qb in enumerate(range(qlo, qhi + 1))]
                    items.append((k_ap, QT[:, 1, :].rearrange("p (i r) -> p r i", r=4)[:, r, qlo * 128:qlo * 128 + nq], nq, pvs))
                batches.append(g_attend(items, [(0, 512, M_po.unsqueeze(1).to_broadcast([128, 2, 256]), 256)]))
            for rb in range(2):
                items = []
                for r in range(rb * 8, rb * 8 + 8):
                    accc = accA.rearrange("p (i r) -> p r i", r=16)[:, r, :]
                    denc = denA.rearrange("p (i r) -> p r i", r=16)[:, r, :]
                    items.append((KTh[:, s, r * 128:(r + 1) * 128], vtile(QT[:, 2, :], 2, r), 64,
                                  [(0, 32, Vh[:, r, s * 128:(s + 1) * 128], onesv[:, r, :], accc[:, 0:32], denc[:, 0:32]),
                                   (32, 32, Vh[:, r, s * 128:(s + 1) * 128], onesv[:, r, :], accc[:, 32:64], denc[:, 32:64])]))
                batches.append(g_attend(items, [(0, 512, M_prev[:, 0:64].unsqueeze(1).to_broadcast([128, 8, 64]), 64)]))
            for rb in range(2):
                items = []
                for r in range(rb * 8, rb * 8 + 8):
                    accc = accA.rearrange("p (i r) -> p r i", r=16)[:, r, :]
                    denc = denA.rearrange("p (i r) -> p r i", r=16)[:, r, :]
                    items.append((vtile(KTo[:, 2, :], 2, r), vtile(QT[:, 2, :], 2, r), 64,
                                  [(0, 32, Vo3[0:64, r, :], ones[0:64, :], accc[:, 0:32], denc[:, 0:32]),
                                   (32, 32, Vo3[0:64, r, :], ones[0:64, :], accc[:, 32:64], denc[:, 32:64])]))
                batches.append(g_attend(items, [(0, 512, M_own[0:64, 0:64].unsqueeze(1).to_broadcast([64, 8, 64]), 64)], nk=64))
            pend = None
            last = None
            for bt in batches:
                s2 = bt()
                if pend is not None:
                    last = pend()
                pend = s2
            last = pend()
            ev1 = recip_act(rden, denA, [last, rdA_free[0]])
            ev2 = V(lambda v, s=s: v.tensor_tensor(out=attnT[:, s, :], in0=accA, in1=rden, op=ALU.mult), [ev1, xt_last])
            rdA_free[0] = ev2
            for b in (4, 5):
                bfree[b] = ev2
            for b in (6, 7):
                bfree[b] = ev1
            att_done = [ev2, mem_attn_multi([(s, qmT[:, th * 512:(th + 1) * 512], attnT[:, 4 + s, th * 512:(th + 1) * 512], [proj_done, xt_last]) for th in range(2)], rdm, pairs=((0, 1), (4, 6)))]

        xl = []
        for t in range(8):
            xl.append(SPD(lambda q, t=t: q.dma_start(out=xs[:, t, :], in_=xo[t * 128:(t + 1) * 128, :]), f"xsl{t}", [att_done]))
        pendF = [None]
        evhF = [None]

        def cb_ffn0norm(t, ev):
            if pendF[0] is not None:
                evhF[0], _ = norm_B(pendF[0])
            pendF[0] = norm_A(xs[:, t, :], ev, 32,
                              lambda h, t=t: hT[:, h * 8:(h + 1) * 8, t * 128:(t + 1) * 128], dst_wait=att_done)
        if stop != "attn0":
            xev = out_proj_t(wo0, 8, lambda k, t: attnT[:, k, t * 128:(t + 1) * 128], list(range(8)), xl, att_done, cb_ffn0norm)
            evhF[0], _ = norm_B(pendF[0])
        else:
            xev = out_proj(wo0, 8, lambda k, t: attnT[:, k, t * 128:(t + 1) * 128], list(range(8)), xl, att_done)
        x_ready = [xev[t] for t in range(8)]
        hT_free[0] = att_done
        xt_free[0] = xt_free[1] = x_ready
        evh_cb = [None]
        if stop != "attn0":
            pendB = [None]

            def cb_l1norm(t, ev, hfree):
                if pendB[0] is not None:
                    evh_cb[0], _ = norm_B(pendB[0])
                pendB[0] = norm_A(xs[:, t, :], ev, 16,
                                  lambda h, t=t: hT[:, h * 8:(h + 1) * 8, t * 128:(t + 1) * 128], dst_wait=hfree)
            x_ready = ffn(0, x_ready, cb_l1norm if (1 in layers) else None, evh_pre=evhF[0])
            if pendB[0] is not None:
                evh_cb[0], _ = norm_B(pendB[0])
            xt_free[0] = xt_free[1] = x_ready
    else:
        evh_cb = [None]
        for t in range(8):
            x_ready[t] = SPD(lambda q, t=t: q.dma_start(out=xs[:, t, :], in_=xo[t * 128:(t + 1) * 128, :]), f"xsl{t}")

    gfin = ph[:, 0:4096].bitcast(F32)
    hTf32 = hT[:].rearrange("p a b -> p (a b)").bitcast(F32)
    ystage = [hTf32[:, i * 2048:(i + 1) * 2048] for i in range(4)]
    yst_free = [None] * 4
    ev_gf_h = [None]
    final_done = [False]
    st_evs = []

    def final_tile(t, ev, hfree):
        ss, sq, rs = stat_col(), stat_col(), stat_col()
        e = A(lambda a: a.activation(out=xnb[:], in_=xs[:, t, :], func=AF.Square, accum_out=ss), [ev, xn_free[0]])
        e = A(lambda a: a.activation(out=sq, in_=ss, func=AF.Sqrt, scale=1.0 / DM, bias=eps_ap), [e])
        e = V(lambda v: v.reciprocal(out=rs, in_=sq), [e])
        i = t % 4
        e = V(lambda v: v.scalar_tensor_tensor(out=ystage[i], in0=xs[:, t, :], scalar=rs, in1=gfin,
                                               op0=ALU.mult, op1=ALU.mult), [e, ev_gf_h[0], yst_free[i], hfree])
        evs = SPD(lambda q: q.dma_start(out=y[t * 128:(t + 1) * 128, :], in_=ystage[i]), f"yst{i}", [e])
        yst_free[i] = evs
        st_evs.append(evs)

    if 1 in layers and stop != "attn0":
        lng = misc_take(0, 6144, F32)
        lnb = misc_take(6144, 6144, F32)
        bsp = misc_take(12288, 6144, F32)
        wsT = misc_take(18432, 3072, BF16).rearrange("p (g t) -> p g t", t=128)
        rden1 = misc_take(21504, 2048, F32)
        vtm = ph[:, 0:6144].rearrange("p (t c) -> p t c", c=1536)
        qm1 = ph[:, 6144:8192].rearrange("p (m t) -> p m t", t=512)
        vgf = ph[:, 8192:11264].bitcast(F32)
        memT1 = ph[:, 0:4096].rearrange("p (k t) -> p k t", t=256)
        e1 = SPD(lambda q: q.dma_start(out=lng, in_=lng_d), "gld4", x_ready)
        e2 = SPD(lambda q: q.dma_start(out=lnb, in_=lnb_d), "gld5")
        e3 = SPD(lambda q: q.dma_start(out=bsp, in_=bsp_d), "gld6")
        e4 = SPD(lambda q: q.dma_start(out=vgf, in_=wst_d), "gld7")
        ev_ws = V(lambda v: v.tensor_tensor(out=wsT, in0=vgf.rearrange("p (g t) -> p g t", t=128),
                                            in1=M_own.unsqueeze(1).to_broadcast([128, 12, 128]), op=ALU.mult), [e4])
        vgb = ph[:, 8192:11264]
        TA = vgb[:, 0:1536]
        TB = vgb[:, 1536:3072]
        bsp_hl = misc[0:64, 6144:7680]
        V(lambda v: v.memset(TA[0:64], 0.0), [ev_ws])
        V(lambda v: v.tensor_copy(out=TA[0:1], in_=bsp[0:1]), [e3])
        eb1 = V(lambda v: v.tensor_copy(out=TB[32:33], in_=bsp[32:33]))
        eb2 = V(lambda v: v.tensor_tensor(out=TA[32:33], in0=bsp[32:33], in1=TB[32:33], op=ALU.subtract), [eb1])
        ev_hl = V(lambda v: v.tensor_copy(out=bsp_hl, in_=TA[0:64]), [eb2])
        ev_tabs = [e1, e2, ev_ws, ev_hl]
        evh = evh_cb[0]
        if evh is None:
            for t in range(8):
                evh, _ = norm_tile(xs[:, t, :], x_ready[t], 16,
                                   lambda h, t=t: hT[:, h * 8:(h + 1) * 8, t * 128:(t + 1) * 128], dst_wait=hT_free[0])
        gTm = am[:].rearrange("p (k t) -> p k t", t=512)
        half_done = [xt_free[0], xt_free[1]]
        pre_evs = []
        for hf in range(2):
            tk0 = hf * 512
            last = None
            for pc in range(2):
                s_, wv, evw = wpiece(w1, 0, 16, 3072 + pc * 256, 256)
                for c in range(2):
                    m = pc * 2 + c
                    last = proj_fm(wv, c * 128, lambda k, tk0=tk0: hT[:, k, tk0:tk0 + 512], 16, 512,
                                   lambda b, ev, m=m: A(lambda a: a.copy(out=qm1[:, m, :], in_=bank(b)), [ev]),
                                   waits=[evw, evh, half_done])
                R.release(s_, last)
            qdone = [bfree[0], bfree[1]]
            if hf == 0:
                mem_kv(1, memT1)
            for cg in range(6):
                pstF = None
                if hf == 1 and cg < 4 and stop != "attn1":
                    pstF = norm_A(xs[:, cg, :], x_ready[cg], 48,
                                  lambda h, cg=cg: hT[:, h * 8:(h + 1) * 8, cg * 128:(cg + 1) * 128], dst_wait=half_done)
                s_, wv, evw = wpiece(w1, 0, 16, 1536 + cg * 256, 256)
                last = None
                for tt in range(4):
                    def evac(b, ev, tt=tt, cg=cg):
                        return A(lambda a: a.activation(out=vtm[:, tt, cg * 256:(cg + 1) * 256], in_=bank(b)[:, 0:256],
                                                        func=AF.Gelu), [ev, half_done])
                    last = proj_tm(lambda k, tt=tt, tk0=tk0: hT[:, k, tk0 + tt * 128: tk0 + (tt + 1) * 128], wv, 0, 256, 16, evac,
                                   waits=[evw, evh])
                R.release(s_, last)
                if pstF is not None:
                    pre_evs.append(norm_B(pstF)[0])
            vdone = [bfree[4], bfree[5]]
            ev_ln_h = [None]

            def ln_tile(tt, vdone=vdone, ev_ln_h=ev_ln_h):
                sm, sq2, mu, var, rs = stat_col(), stat_col(), stat_col(), stat_col(), stat_col()
                ea = A(lambda a: a.activation(out=xnb[:, 0:1536], in_=vtm[:, tt, :], func=AF.Copy, accum_out=sm), [vdone, ev_ws, xn_free[0]])
                eb = A(lambda a: a.activation(out=xnb[:, 0:1536], in_=vtm[:, tt, :], func=AF.Square, accum_out=sq2), [ea])
                e = V(lambda v: v.tensor_scalar(out=mu, in0=sm, scalar1=1.0 / 1536, scalar2=None, op0=ALU.mult), [ea])
                e = V(lambda v: v.tensor_tensor(out=var, in0=mu, in1=mu, op=ALU.mult), [e])
                e = V(lambda v: v.scalar_tensor_tensor(out=var, in0=sq2, scalar=1.0 / 1536, in1=var, op0=ALU.mult, op1=ALU.subtract), [e, eb])
                e = A(lambda a: a.activation(out=var, in_=var, func=AF.Sqrt, bias=lneps_ap), [e])
                e = V(lambda v: v.reciprocal(out=rs, in_=var), [e])
                e = V(lambda v: v.tensor_scalar(out=vgf, in0=vtm[:, tt, :], scalar1=mu, scalar2=rs, op0=ALU.subtract, op1=ALU.mult), [e, eb, ev_ln_h[0]])
                e = V(lambda v: v.tensor_tensor(out=vgf, in0=vgf, in1=lng, op=ALU.mult), [e, ev_tabs])
                ev_ln_h[0] = V(lambda v: v.tensor_tensor(out=vtm[:, tt, :], in0=vgf, in1=lnb, op=ALU.add), [e])

            for pc in range(6):
                s_, wv, evw = wpiece(w1, 0, 16, pc * 256, 256)
                last = None
                for c in range(2):
                    g = pc * 2 + c
                    last = proj_fm(wv, c * 128, lambda k, tk0=tk0: hT[:, k, tk0:tk0 + 512], 16, 512,
                                   lambda b, ev, g=g: A(lambda a: a.activation(out=gTm[:, g, :], in_=bank(b), func=AF.Gelu), [ev, half_done]),
                                   waits=[evw, evh])
                R.release(s_, last)
                if pc < 4:
                    ln_tile(pc)
            ev_ln = ev_ln_h[0]
            udone = [bfree[0], bfree[1]]
            last_ma = mem_attn_multi([(m, qm1[:, m, :], gTm[:, 12 + m, :], [qdone, half_done]) for m in range(4)], rden1)
            ev_gate = None
            for g in range(12):
                b = 2 + (g % 2)
                P.wait("pe", [bfree[b], ev_ln, ev_ws, ev_hl])
                ev = None
                for tt in range(4):
                    ob = bank(b)[:, tt * 128:(tt + 1) * 128]
                    T(lambda t, ob=ob, tt=tt, g=g: t.matmul(ob, vtm[:, tt, g * 128:(g + 1) * 128], wsT[:, g, :],
                                                          start=True, stop=False))
                    ev = T(lambda t, ob=ob, g=g: t.matmul(ob, ones[0:64, :], bsp_hl[:, g * 128:(g + 1) * 128],
                                                         start=False, stop=True), sig=(tt == 3))
                ev_gate = V(lambda v, b=b, g=g: v.tensor_tensor(out=gTm[:, g, :], in0=bank(b), in1=gTm[:, g, :], op=ALU.mult),
                            [ev, udone])
                bfree[b] = ev_gate
            cat_ev = [ev_gate, last_ma]
            toks = [hf * 4 + i for i in range(4)]
            r = out_proj(wo1, 16, lambda k, t: gTm[:, k, (t % 4) * 128:(t % 4 + 1) * 128], toks, x_ready, cat_ev)
            for t in toks:
                x_ready[t] = r[t]
            half_done = [r[t] for t in toks]
            P.wait("pe", half_done)
        hT_free[0] = half_done
        xt_free[0] = xt_free[1] = x_ready
        if stop != "attn1":
            if final:
                ev_gf_h[0] = SPD(lambda q: q.dma_start(out=gfin, in_=gfin_d), "gld8", x_ready)
                x_ready = ffn(1, x_ready, final_tile, pre_tiles=(0, 1, 2, 3) if pre_evs else (), pre_evh=pre_evs)
                final_done[0] = True
            else:
                x_ready = ffn(1, x_ready, pre_tiles=(0, 1, 2, 3) if pre_evs else (), pre_evh=pre_evs)
            xt_free[0] = xt_free[1] = x_ready

    if final and not final_done[0]:
        ev_gf_h[0] = SPD(lambda q: q.dma_start(out=gfin, in_=gfin_d), "gld8", x_ready)
        for t in range(8):
            final_tile(t, x_ready, hT_free[0])
    elif not final:
        for t in range(8):
            evs = SPD(lambda q, t=t: q.dma_start(out=y[t * 128:(t + 1) * 128, :], in_=xs[:, t, :]), "yst0", [x_ready[t]])
            st_evs.append(evs)
    P.wait("sp", st_evs)
    P.build()
    es.close()
    return nc


_NC_CACHE = {}


def _get_nc(layers, final, stop=None):
    key = (tuple(layers), final, stop)
    if key not in _NC_CACHE:
        _NC_CACHE[key] = build(layers, final, stop)
    return _NC_CACHE[key]


def _halo_idx(T0):
    g3 = (T0 - 2048 + 16 * np.arange(128)[None, :] + np.arange(16)[:, None]).reshape(-1)
    g2 = (T0 - 512 + 4 * np.arange(128)[None, :] + np.arange(4)[:, None]).reshape(-1)
    g1 = T0 - 128 + np.arange(128)
    return np.concatenate([g3, g2, g1]).astype(np.int64)


def _consts():
    k = np.arange(128)[:, None]
    q = np.arange(128)[None, :]
    ident = (k == q)
    m_own = (k <= q)
    m_prev = (k >= q)
    m_b3 = ((k // 64) == (q // 64)) & ((k % 64) <= (q % 64))
    ones = np.ones((128, 128), bool)
    m32 = np.arange(32)[None, :]
    rsw = (k < 32) & (k == ((m32 + 16) % 32))
    cmat = np.concatenate([ident, m_own, m_prev, m_own, ones, rsw], axis=1).astype(np.float32)
    half = 16
    inv_freq = (np.float32(500000.0) ** (-(np.arange(half, dtype=np.float32)) / np.float32(half))).astype(np.float32)
    invf = np.zeros((32, 2), np.float32)
    invf[:, 0] = inv_freq[np.arange(32) % 16]
    invf[:, 1] = np.where(np.arange(32) < 16, -1.0, 1.0)
    return np.ascontiguousarray(cmat), invf


def _prep(inp, layers):
    f = lambda a: np.ascontiguousarray(np.asarray(a, dtype=np.float32))
    cmat, invf = _consts()
    gl = [inp["mix_norm"][0], inp["mix_norm"][1], inp["ffn_norm"][0], inp["ffn_norm"][1],
          inp["mem_norm"][0], inp["mem_norm"][1]]
    gT = np.concatenate([np.asarray(g, np.float32).reshape(16, 128).T for g in gl], axis=1)
    common = {
        "mem": f(inp["mem"][0]), "cmat": cmat, "gT": f(gT),
        "gfin": f(np.broadcast_to(np.asarray(inp["final_norm"], np.float32)[None, :], (128, DM))),
        "wkv": f(inp["w_mem_kv"]), "wg": f(inp["w_gate"]), "wu": f(inp["w_up"]), "wd": f(inp["w_down"]),
    }
    if 0 in layers:
        w = np.asarray(inp["attn_w_in"][0], np.float32)
        qc = lambda h: w[:, h * 128:(h + 1) * 128]
        kc = lambda h: w[:, 1536 + h * 128:1536 + (h + 1) * 128]
        vc = lambda h: w[:, 3072 + h * 128:3072 + (h + 1) * 128]
        mc = lambda m: w[:, 4608 + m * 128:4608 + (m + 1) * 128]
        cols = []
        for s in range(4):
            cols += [qc(s), qc(4 + s), qc(8 + s), kc(s), kc(4 + s), kc(8 + s), mc(s), vc(s), vc(4 + s), vc(8 + s)]
        for gi in (2, 1, 0):
            cols += [w[:, 1536 + gi * 512:1536 + (gi + 1) * 512], w[:, 3072 + gi * 512:3072 + (gi + 1) * 512]]
        common["w0"] = np.ascontiguousarray(np.concatenate(cols, axis=1))
        common["wo0"] = f(inp["attn_w_out"][0])
        common["invf"] = invf
    if 1 in layers:
        common["w1"] = f(inp["sgu_w_in"][0])
        common["wo1"] = f(inp["sgu_w_out"][0])
        common["lng"] = f(np.broadcast_to(np.asarray(inp["sgu_ln_g"][0], np.float32)[None, :], (128, 1536)))
        common["lnb"] = f(np.broadcast_to(np.asarray(inp["sgu_ln_b"][0], np.float32)[None, :], (128, 1536)))
        common["bsp"] = f(np.broadcast_to(np.asarray(inp["sgu_b_spatial"][0], np.float32).reshape(1, 1536), (128, 1536)))
        common["wst"] = f(np.asarray(inp["sgu_w_spatial"][0], np.float32).transpose(2, 0, 1).reshape(128, 1536))
    return common


def _run(inp, x2, layers, final, stop=None, ncores=NCORES):
    nc = _get_nc(layers, final, stop)
    common = _prep(inp, layers)
    pos = np.asarray(inp["positions"][0], np.int32)
    in_maps = []
    for c in range(ncores):
        T0 = c * TOK
        m = dict(common)
        m["xo"] = np.ascontiguousarray(x2[T0:T0 + TOK])
        if 0 in layers:
            idx = _halo_idx(T0)
            ok = idx >= 0
            ic = np.clip(idx, 0, None)
            xh = x2[ic].copy()
            xh[~ok] = 0.0
            m["xh"] = xh
            pp = np.concatenate([pos[ic], pos[T0:T0 + TOK]]).astype(np.int32)
            m["posr"] = np.ascontiguousarray(np.broadcast_to(pp[None, :], (32, HALO + TOK)))
            m["valid"] = np.ascontiguousarray(ok.astype(np.float32).reshape(21, 128).T)
        in_maps.append(m)
    res = run_bass_kernel_spmd(nc, in_maps, core_ids=list(range(ncores)))
    return np.concatenate([r["y"] for r in res.results], axis=0)


def kernel(**inp):
    x2 = np.ascontiguousarray(np.asarray(inp["x"], np.float32)[0])
    out = _run(inp, x2, (0, 1), True)
    return out.reshape(1, NCORES * TOK, DM).astype(np.float32)
```

```python
import numpy as np
from contextlib import ExitStack
import concourse.bass as bass
import concourse.mybir as mybir
from concourse.bass_utils import run_bass_kernel_spmd

F32 = mybir.dt.float32
BF16 = mybir.dt.bfloat16
I32 = mybir.dt.int32
AF = mybir.ActivationFunctionType
ALU = mybir.AluOpType

NCORES = 8
TOK = 1024
DM = 2048
FF = 5632
NS = 4
HALO = 2688
ENGS = ("pe", "act", "dve", "pool", "sp")
HALO_BATCHES = [(2 * i, 2, 2) for i in range(8)] + [(16, 2, 1), (18, 2, 1), (20, 1, 0)]
HALO_COL0 = {2: 0, 1: 2048, 0: 2560}


class Prog:
    def __init__(self, nc):
        self.nc = nc
        self.q = {e: [] for e in ENGS}
        self.semcnt = {}
        self.waited = {e: {} for e in ENGS}

    def op(self, eng, fn):
        ent = {"fn": fn, "inc": None}
        self.q[eng].append(ent)
        return ent

    def inc(self, ent, sem, amt=1):
        assert ent["inc"] is None
        self.semcnt[sem] = self.semcnt.get(sem, 0) + amt
        ent["inc"] = (sem, amt)
        return (sem, self.semcnt[sem])

    def wait(self, eng, ev):
        if ev is None:
            return
        if isinstance(ev, list):
            for e in ev:
                self.wait(eng, e)
            return
        sem, val = ev
        w = self.waited[eng]
        if w.get(sem, 0) >= val:
            return
        w[sem] = val
        self.q[eng].append({"wait": (sem, val)})

    def build(self):
        nc = self.nc
        with ExitStack() as es:
            sems = {}
            for name in self.semcnt:
                sems[name] = es.enter_context(nc.semaphore(name))
            block = es.enter_context(nc.Block())

            def replay(engname):
                def body(eng):
                    for ent in self.q[engname]:
                        if "wait" in ent:
                            s, v = ent["wait"]
                            eng.wait_ge(sems[s], v)
                        else:
                            ins = ent["fn"](eng)
                            if ent["inc"] is not None:
                                s, a = ent["inc"]
                                ins.then_inc(sems[s], a)
                return body

            for name, meth in (("sp", block.sync), ("pe", block.tensor), ("act", block.scalar),
                               ("dve", block.vector), ("pool", block.gpsimd)):
                if self.q[name]:
                    meth(replay(name))


def build(layers=(0, 1), final=True, stop=None):
    nc = bass.Bass("TRN2", target_bir_lowering=False)
    P = Prog(nc)

    def din(name, shape, dt=F32):
        return nc.dram_tensor(name, list(shape), dt, kind="ExternalInput").ap()

    xo = din("xo", [TOK, DM])
    if 0 in layers:
        xh = din("xh", [HALO, DM])
        posr = din("posr", [32, HALO + TOK], I32)
        valid = din("valid", [128, 21])
        w0 = din("w0", [DM, 8192])
        wo0 = din("wo0", [1024, DM])
        invf = din("invf", [32, 2])
    if 1 in layers:
        w1 = din("w1", [DM, 3584])
        wo1 = din("wo1", [DM, DM])
        lng_d = din("lng", [128, 1536])
        lnb_d = din("lnb", [128, 1536])
        bsp_d = din("bsp", [128, 1536])
        wst_d = din("wst", [128, 1536])
    memd = din("mem", [256, DM])
    cmat_d = din("cmat", [128, 672])
    gT_d = din("gT", [128, 96])
    gfin_d = din("gfin", [128, DM])
    wkv = din("wkv", [2, DM, 1024])
    wg = din("wg", [2, DM, FF])
    wu = din("wu", [2, DM, FF])
    wd = din("wd", [2, FF, DM])
    y = nc.dram_tensor("y", [TOK, DM], F32, kind="ExternalOutput").ap()

    es = ExitStack()
    sb = lambda name, shape, dt: es.enter_context(nc.sbuf_tensor(name, list(shape), dt))
    xs = sb("xs", [128, 8, DM], F32)
    ring = [sb(f"ring{i}", [128, 4096], BF16) for i in range(NS)]
    hT = sb("hT", [128, 16, TOK], BF16)
    am = sb("am", [128, 8192], BF16)
    mixT = am
    actT = am[:].rearrange("p (a b) -> p a b", b=TOK)
    xt_am = [am[:, i * 4096:(i + 1) * 4096].bitcast(F32) for i in range(2)]
    xt = list(xt_am)
    xn = [sb("xn0", [128, DM], BF16)] * 2
    ph = sb("ph", [128, 11264], BF16)
    cmat = sb("cmat_s", [128, 672], BF16)
    gT = sb("gT_s", [128, 96], F32)
    stat = sb("stat", [128, 64], F32)
    misc = sb("misc", [128, 12 * 1024], BF16)
    ptS = [sb(f"pt{i}", [128, 512], BF16) for i in range(2)]
    pmS = [sb(f"pm{i}", [128, 512], BF16) for i in range(2)]
    memKT = sb("memKT", [128, 4, 256], BF16)
    memV = sb("memV", [128, 2, 512], BF16)
    sgt = ptS
    psall = es.enter_context(nc.psum_tensor("psall", [128, 8, 512], F32))

    ident = cmat[:, 0:128]
    M_own = cmat[:, 128:256]
    M_prev = cmat[:, 256:384]
    M_B3 = cmat[:, 384:512]
    ones = cmat[:, 512:640]
    Rsw = cmat[:, 640:672]

    def bank(b):
        return psall[:, b, :]

    def bank_bf(b):
        return psall[:, b, :].bitcast(BF16)

    bfree = [None] * 8

    def A(fn, waits=None):
        P.wait("act", waits)
        return P.inc(P.op("act", fn), "sA")

    def V(fn, waits=None):
        P.wait("dve", waits)
        return P.inc(P.op("dve", fn), "sV")

    def T(fn, waits=None, sig=False):
        P.wait("pe", waits)
        e = P.op("pe", fn)
        return P.inc(e, "sT") if sig else None

    def SPD(fn, sem, waits=None):
        P.wait("sp", waits)
        return P.inc(P.op("sp", fn), sem, 16)

    def misc_take(off_bytes, nbytes, dt, parts=128):
        a = misc[0:parts, off_bytes // 2:(off_bytes + nbytes) // 2]
        return a if dt == BF16 else a.bitcast(dt)

    class Ring:
        def __init__(self):
            self.free = [None] * NS
            self.n = 0

        def load(self, src3, nk, ncol):
            s = self.n % NS
            self.n += 1
            P.wait("pool", self.free[s])
            view = ring[s][:, 0:nk * ncol].rearrange("p (k c) -> p k c", c=ncol)
            e = P.op("pool", lambda g, view=view, src3=src3: g.dma_start(out=view, in_=src3))
            ev = P.inc(e, f"wld{s}", 16)
            return s, view, ev

        def release(self, s, ev):
            self.free[s] = ev

    R = Ring()

    def wpiece(w2d, row0, nk, col0, ncol):
        src = w2d[row0:row0 + nk * 128, col0:col0 + ncol].rearrange("(k p) c -> p k c", p=128)
        return R.load(src, nk, ncol)

    e = P.op("pool", lambda g: g.dma_start(out=cmat[:], in_=cmat_d))
    ev_c = P.inc(e, "cld", 16)
    ev_g = SPD(lambda q: q.dma_start(out=gT[:], in_=gT_d), "gld1")
    for en in ("pe", "dve", "act"):
        P.wait(en, ev_c)
    P.wait("dve", ev_g)

    eps_t = sb("eps_t", [128, 2], F32)
    eps_ap = eps_t[:, 0:1]
    lneps_ap = eps_t[:, 1:2]
    V(lambda v: v.memset(eps_t[:, 0:1], 1e-6))
    ev_eps = V(lambda v: v.memset(eps_t[:, 1:2], 1e-5))
    P.wait("act", ev_eps)

    stat_i = [0]

    def stat_col():
        i = stat_i[0] % 64
        stat_i[0] += 1
        return stat[:, i:i + 1]

    tp_i = [0]
    xn_free = [None]
    pend_rot = [None]

    def flush_rot():
        if pend_rot[0] is not None:
            f = pend_rot[0]
            pend_rot[0] = None
            f()
    xnb = xn[0]

    def norm_A(src, src_ev, gcol0, dst_of, dst_wait=None):
        ss = stat_col()
        sq = stat_col()
        rs = stat_col()
        ev = A(lambda a: a.activation(out=xnb[:], in_=src, func=AF.Square, accum_out=ss), [src_ev, xn_free[0]])
        ev = A(lambda a: a.activation(out=sq, in_=ss, func=AF.Sqrt, scale=1.0 / DM, bias=eps_ap), [ev])
        ev = V(lambda v: v.reciprocal(out=rs, in_=sq), [ev])
        ev_xn = A(lambda a: a.activation(out=xnb[:], in_=src, func=AF.Copy, scale=rs), [ev])
        return (ev_xn, gcol0, dst_of, dst_wait)

    def norm_B(st):
        ev_xn, gcol0, dst_of, dst_wait = st
        evd = None
        last_pe = None
        for h in range(2):
            b = 6 + (tp_i[0] % 2)
            tp_i[0] += 1
            P.wait("pe", [bfree[b], ev_xn])
            for j in range(8):
                kc = h * 8 + j
                last_pe = T(lambda t, b=b, j=j, kc=kc: t.transpose(
                    bank_bf(b)[:, j * 128:(j + 1) * 128], xnb[:, kc * 128:(kc + 1) * 128], ident), sig=(j == 7))
            flush_rot()
            gb = gT[:, gcol0 + h * 8: gcol0 + h * 8 + 8].unsqueeze(2).to_broadcast([128, 8, 128])
            dst = dst_of(h)
            evd = V(lambda v, b=b, dst=dst, gb=gb: v.tensor_tensor(
                out=dst, in0=bank_bf(b).rearrange("p (a c) -> p a c", c=128), in1=gb, op=ALU.mult),
                [last_pe, dst_wait])
            bfree[b] = evd
        xn_free[0] = last_pe
        return evd, ev_xn

    def norm_tile(src, src_ev, gcol0, dst_of, dst_wait=None):
        return norm_B(norm_A(src, src_ev, gcol0, dst_of, dst_wait))

    xt_free = [None, None]
    xt_n = [0]

    def load_xtile(src_rows):
        i = xt_n[0] % 2
        xt_n[0] += 1
        dst = xt[i]
        ev = SPD(lambda q, dst=dst, src_rows=src_rows: q.dma_start(out=dst, in_=src_rows), f"xld{i}", [xt_free[i]])
        return i, ev

    pj = [0]

    def proj_fm(wview, c0, rhs_of, nk, n, evac, waits=None, banks=(0, 1), oshape=None):
        b = banks[pj[0] % len(banks)]
        pj[0] += 1
        P.wait("pe", [bfree[b], waits])
        out = bank(b)[:, 0:n]
        if oshape is not None:
            out = oshape(out)
        ev = None
        for k in range(nk):
            ev = T(lambda t, k=k: t.matmul(out, wview[:, k, c0:c0 + 128], rhs_of(k),
                                            start=(k == 0), stop=(k == nk - 1)), sig=(k == nk - 1))
        flush_rot()
        r = evac(b, ev)
        if callable(r):
            pend_rot[0] = r
        else:
            bfree[b] = r
        return ev

    def proj_tm(lhs_of, wview, c0, n, nk, evac, waits=None, banks=(4, 5)):
        b = banks[pj[0] % len(banks)]
        pj[0] += 1
        P.wait("pe", [bfree[b], waits])
        ev = None
        for k in range(nk):
            ev = T(lambda t, b=b, k=k: t.matmul(bank(b)[:, 0:n], lhs_of(k), wview[:, k, c0:c0 + n],
                                                 start=(k == 0), stop=(k == nk - 1)), sig=(k == nk - 1))
        flush_rot()
        bfree[b] = evac(b, ev)
        return ev

    def mem_kv(layer, memT):
        evd = None
        for mt in range(2):
            i, evl = load_xtile(memd[mt * 128:(mt + 1) * 128, :])
            evd, evr = norm_tile(xt[i], evl, 64 + 16 * layer,
                                 lambda h, mt=mt: memT[:, h * 8:(h + 1) * 8, mt * 128:(mt + 1) * 128])
            xt_free[i] = evr
        last = None
        for pc in range(4):
            s, wv, evw = wpiece(wkv[layer], 0, 16, pc * 256, 256)
            if pc < 2:
                for c in range(2):
                    m = pc * 2 + c
                    last = proj_fm(wv, c * 128, lambda k: memT[:, k, :], 16, 256,
                                   lambda b, ev, m=m: A(lambda a: a.copy(out=memKT[:, m, :], in_=bank(b)[:, 0:256]), [ev]),
                                   waits=[evw, evd])
            else:
                for mt in range(2):
                    cc = (pc - 2) * 256
                    last = proj_tm(lambda k, mt=mt: memT[:, k, mt * 128:(mt + 1) * 128], wv, 0, 256, 16,
                                   lambda b, ev, mt=mt, cc=cc: A(lambda a: a.copy(out=memV[:, mt, cc:cc + 256], in_=bank(b)[:, 0:256]), [ev]),
                                   waits=[evw, evd])
            R.release(s, last)
        return last

    pt_free = [None, None]
    pm_free = [None, None]

    def recip_act(dst, src, waits):
        e = A(lambda a: a.activation(out=dst, in_=src, func=AF.Ln), waits)
        return A(lambda a: a.activation(out=dst, in_=dst, func=AF.Exp, scale=-1.0), [e])

    pv_i = [0]
    rd_free = [None]

    def mem_attn_multi(groups, rden, pairs=((4, 6), (5, 7))):
        units = []
        for gi_, (m, q_ap, dst, q_ev) in enumerate(groups):
            accb, denb = pairs[gi_ % len(pairs)]
            for mt in range(2):
                units.append((m, q_ap, dst, q_ev, accb, denb, mt))

        def stage1(u):
            m, q_ap, dst, q_ev, accb, denb, mt = u
            i = pv_i[0] % 2
            pv_i[0] += 1
            sbk = 2 + i
            P.wait("pe", [bfree[sbk], q_ev])
            evs = T(lambda t: t.matmul(bank(sbk), memKT[:, m, mt * 128:(mt + 1) * 128], q_ap, start=True, stop=True), sig=True)
            eve = A(lambda a: a.activation(out=ptS[i][:], in_=bank(sbk), func=AF.Exp, scale=128 ** -0.5), [evs, pt_free[i]])
            bfree[sbk] = eve

            def stage2():
                if mt == 0:
                    P.wait("pe", [bfree[accb], bfree[denb]])
                T(lambda t: t.matmul(bank(accb), memV[:, mt, m * 128:(m + 1) * 128], ptS[i][:],
                                     start=(mt == 0), stop=(mt == 1)), waits=[eve])
                evp = T(lambda t: t.matmul(bank(denb), ones, ptS[i][:], start=(mt == 0), stop=(mt == 1)), sig=True)
                pt_free[i] = evp
                if mt == 1:
                    ev1 = recip_act(rden[:, 0:512], bank(denb), [evp, rd_free[0]])
                    ev2 = V(lambda v: v.tensor_tensor(out=dst, in0=bank(accb), in1=rden[:, 0:512], op=ALU.mult), [ev1])
                    rd_free[0] = ev2
                    bfree[accb] = ev2
                    bfree[denb] = ev1
                    return ev2
                return evp
            return stage2

        pend = None
        last = None
        for u in units:
            s2 = stage1(u)
            if pend is not None:
                last = pend()
            pend = s2
        last = pend()
        return last

    hT_free = [None]

    def ffn(layer, x_ready, tile_cb=None, evh_pre=None, pre_tiles=(), pre_evh=None):
        evh = evh_pre
        for t in range(8 if evh_pre is None else 0):
            if t in pre_tiles:
                continue
            evh, _ = norm_tile(xs[:, t, :], x_ready[t], 32 + 16 * layer,
                               lambda h, t=t: hT[:, h * 8:(h + 1) * 8, t * 128:(t + 1) * 128], dst_wait=hT_free[0])
        if pre_evh:
            evh = [evh, pre_evh]
        groups = [(0, 8), (8, 8), (16, 8), (24, 8), (32, 8), (40, 4)]
        xev = list(x_ready)
        act_free = x_ready
        gu_i = 0
        dn_i = 0
        sg_free = [None, None]
        evpu = None
        for (f0, nf) in groups:
            ev_act_last = None
            for pr in range(nf // 2):
                col = (f0 + 2 * pr) * 128
                sg_, wgv, evg = wpiece(wg[layer], 0, 16, col, 256)
                su_, wuv, evu = wpiece(wu[layer], 0, 16, col, 256)
                for c in range(2):
                    fi = 2 * pr + c
                    for th in range(2):
                        bg = 0 + 2 * (gu_i % 2)
                        bu = 1 + 2 * (gu_i % 2)
                        si = gu_i % 2
                        gu_i += 1
                        P.wait("pe", [bfree[bg], bfree[bu], evg, evu, evh])
                        evpg = None
                        for k in range(16):
                            evpg = T(lambda t, bg=bg, k=k, c=c, th=th, wgv=wgv: t.matmul(
                                bank(bg), wgv[:, k, c * 128:(c + 1) * 128], hT[:, k, th * 512:(th + 1) * 512],
                                start=(k == 0), stop=(k == 15)), sig=(k == 15))
                        for k in range(16):
                            evpu = T(lambda t, bu=bu, k=k, c=c, th=th, wuv=wuv: t.matmul(
                                bank(bu), wuv[:, k, c * 128:(c + 1) * 128], hT[:, k, th * 512:(th + 1) * 512],
                                start=(k == 0), stop=(k == 15)), sig=(k == 15))
                        evs = A(lambda a, bg=bg, si=si: a.activation(out=sgt[si][:], in_=bank(bg), func=AF.Silu),
                                [evpg, sg_free[si]])
                        bfree[bg] = evs
                        dst = actT[:, fi, th * 512:(th + 1) * 512]
                        evm = V(lambda v, bu=bu, si=si, dst=dst: v.tensor_tensor(
                            out=dst, in0=bank(bu), in1=sgt[si][:], op=ALU.mult), [evs, evpu, act_free])
                        bfree[bu] = evm
                        sg_free[si] = evm
                        ev_act_last = evm
                R.release(sg_, evpu)
                R.release(su_, evpu)
            lastdn = None
            if tile_cb is not None and f0 == 40:
                pcs = [wpiece(wd[layer], f0 * 128, nf, jp * 1024, 1024) for jp in range(2)]
                lastpe = None
                for t in range(8):
                    for j in range(4):
                        sd_, wdv, evd = pcs[j // 2]
                        jh = j % 2
                        b = 4 + (dn_i % 2)
                        dn_i += 1
                        P.wait("pe", [bfree[b], evd, ev_act_last])
                        for f in range(nf):
                            lastpe = T(lambda tt, b=b, f=f, t=t, wdv=wdv, jh=jh: tt.matmul(
                                bank(b), actT[:, f, t * 128:(t + 1) * 128], wdv[:, f, jh * 512:(jh + 1) * 512],
                                start=(f == 0), stop=(f == nf - 1)), sig=(f == nf - 1))
                        xsl = xs[:, t, j * 512:(j + 1) * 512]
                        eva = V(lambda v, b=b, xsl=xsl: v.tensor_tensor(out=xsl, in0=bank(b), in1=xsl, op=ALU.add),
                                [lastpe, xev[t]])
                        bfree[b] = eva
                        xev[t] = eva
                        lastdn = eva
                    tile_cb(t, xev[t], evpu)
                for jp in range(2):
                    R.release(pcs[jp][0], lastpe)
                act_free = lastdn
                continue
            for j in range(4):
                sd_, wdv, evd = wpiece(wd[layer], f0 * 128, nf, j * 512, 512)
                lastpe = None
                for t in range(8):
                    b = 4 + (dn_i % 2)
                    dn_i += 1
                    P.wait("pe", [bfree[b], evd, ev_act_last])
                    for f in range(nf):
                        lastpe = T(lambda tt, b=b, f=f, t=t, wdv=wdv: tt.matmul(
                            bank(b), actT[:, f, t * 128:(t + 1) * 128], wdv[:, f, :],
                            start=(f == 0), stop=(f == nf - 1)), sig=(f == nf - 1))
                    xsl = xs[:, t, j * 512:(j + 1) * 512]
                    eva = V(lambda v, b=b, xsl=xsl: v.tensor_tensor(out=xsl, in0=bank(b), in1=xsl, op=ALU.add),
                            [lastpe, xev[t]])
                    bfree[b] = eva
                    xev[t] = eva
                    lastdn = eva
                R.release(sd_, lastpe)
            act_free = lastdn
        hT_free[0] = evpu
        return xev

    def out_proj(wout, nk, catT_of, toks, x_evs, cat_ev):
        xev = dict((t, x_evs[t]) for t in toks)
        for j in range(8):
            s_, wv, evw = wpiece(wout, 0, nk, j * 256, 256)
            lastpe = None
            for t in toks:
                def evac(b, ev, t=t, j=j):
                    dst = xs[:, t, j * 256:(j + 1) * 256]
                    e2 = V(lambda v: v.tensor_tensor(out=dst, in0=bank(b)[:, 0:256], in1=dst, op=ALU.add), [ev, xev[t]])
                    xev[t] = e2
                    return e2
                lastpe = proj_tm(lambda k, t=t: catT_of(k, t), wv, 0, 256, nk, evac, waits=[evw, cat_ev],
                                 banks=(0, 1, 2, 3))
            R.release(s_, lastpe)
        return xev

    def out_proj_t(wout, nk, catT_of, toks, x_evs, cat_ev, tile_cb):
        xev = dict((t, x_evs[t]) for t in toks)
        pcs = [wpiece(wout, 0, nk, j * 512, 512) for j in range(4)]
        lastpe = None
        for t in toks:
            for j in range(4):
                s_, wv, evw = pcs[j]

                def evac(b, ev, t=t, j=j):
                    dst = xs[:, t, j * 512:(j + 1) * 512]
                    e2 = V(lambda v: v.tensor_tensor(out=dst, in0=bank(b), in1=dst, op=ALU.add), [ev, xev[t]])
                    xev[t] = e2
                    return e2
                lastpe = proj_tm(lambda k, t=t: catT_of(k, t), wv, 0, 512, nk, evac, waits=[evw, cat_ev], banks=(0, 1, 2, 3))
            tile_cb(t, xev[t])
        for j in range(4):
            R.release(pcs[j][0], lastpe)
        return xev

    x_ready = [None] * 8
    if 0 in layers:
        xsb = xs[:].rearrange("p a b -> p (a b)").bitcast(BF16)
        KTh = xsb[:, 0:10752].rearrange("p (h t) -> p h t", t=HALO)
        Vh = xsb[:, 10752:21504].rearrange("p (t c) -> p t c", c=512)
        QT = xsb[:, 21504:24576].rearrange("p (g t) -> p g t", t=TOK)
        KTo = xsb[:, 24576:27648].rearrange("p (g t) -> p g t", t=TOK)
        Vo = xsb[:, 27648:29696].rearrange("p (g j c) -> p g j c", j=8, c=128)
        Vo3 = xsb[:, 29696:31744].rearrange("p (r c) -> p r c", c=128)
        qmT = xsb[:, 31744:32768]
        xt[0] = xsb[:, 21504:25600].bitcast(F32)
        xt[1] = xsb[:, 25600:29696].bitcast(F32)
        qnb = [ph[:, 7168:7680], ph[:, 7680:8192]]
        attnT = am[:].rearrange("p (k t) -> p k t", t=TOK)
        Co = misc_take(0, 4096, F32, 32)
        So = misc_take(4096, 4096, F32, 32)
        Ch = misc_take(8192, 2048, F32, 32)
        Sh = misc_take(10240, 2048, F32, 32)
        ChB = [Ch, ph[0:32, 9216:10240].bitcast(F32)]
        ShB = [Sh, ph[0:32, 10240:11264].bitcast(F32)]
        tab_last_use = [None, None]
        t1 = misc_take(12288, 2048, F32, 32)
        t2 = misc_take(14336, 2048, F32, 32)
        onesv = misc_take(16384, 5376, BF16).rearrange("p (t c) -> p t c", c=128)
        angb = ph[0:32, 0:1024].bitcast(F32)
        kb_i = ph[0:32, 1024:2048].bitcast(I32)
        kb_f = ph[0:32, 2048:3072].bitcast(F32)
        cmpb = ph[0:32, 3072:4096].bitcast(F32)
        posb = ph[0:32, 4096:5120].bitcast(I32)
        rden = ph[:, 5120:7168].bitcast(F32)
        invf_s = sb("invf_s", [32, 2], F32)
        valid_s = sb("valid_s", [128, 21], F32)
        memT0 = hT[:].rearrange("p a b -> p (a b)")[:, 0:4096].rearrange("p (k t) -> p k t", t=256)

        ev_if = SPD(lambda q: q.dma_start(out=invf_s[:], in_=invf), "gld2")
        ev_vl = SPD(lambda q: q.dma_start(out=valid_s[:], in_=valid), "gld3")

        mem_last = mem_kv(0, memT0)
        ev_ov = V(lambda v: v.tensor_copy(out=onesv, in_=valid_s[:].unsqueeze(2).to_broadcast([128, 21, 128])), [ev_vl])

        TWO_PI = float(2.0 * np.pi)
        C1 = 6.28125
        C2 = float(2.0 * np.pi - 6.28125)
        tab_chain = [None]

        def rot_tables_p1(pcol0, n, Cdst, Sdst, waits):
            ang = angb[:, 0:n]
            ki = kb_i[:, 0:n]
            kf = kb_f[:, 0:n]
            cm = cmpb[:, 0:n]
            evp = SPD(lambda q: q.dma_start(out=posb[:, 0:n], in_=posr[:, pcol0:pcol0 + n]), "pld", [tab_chain[0]])
            ev = V(lambda v: v.tensor_scalar(out=ang, in0=posb[:, 0:n], scalar1=invf_s[:, 0:1], scalar2=None, op0=ALU.mult),
                   [waits, evp, ev_if])
            tab_chain[0] = ev
            for which, dst in ((0, Sdst), (1, Cdst)):
                src = ang
                if which == 1:
                    ev = V(lambda v: v.tensor_scalar(out=cm, in0=ang, scalar1=float(np.pi / 2), scalar2=None, op0=ALU.add), [ev])
                    src = cm
                ev = V(lambda v, src=src: v.tensor_scalar(out=ki, in0=src, scalar1=float(1.0 / TWO_PI), scalar2=None, op0=ALU.mult), [ev])
                ev = V(lambda v: v.tensor_copy(out=kf, in_=ki), [ev])
                ev = V(lambda v, src=src, dst=dst: v.scalar_tensor_tensor(out=dst, in0=kf, scalar=-C1, in1=src, op0=ALU.mult, op1=ALU.add), [ev])
                ev = V(lambda v, dst=dst: v.scalar_tensor_tensor(out=dst, in0=kf, scalar=-C2, in1=dst, op0=ALU.mult, op1=ALU.add), [ev])
                ev = V(lambda v, dst=dst: v.tensor_scalar(out=dst, in0=dst, scalar1=3.14159, scalar2=-3.14159, op0=ALU.min, op1=ALU.max), [ev])
            return (ev, Cdst, Sdst)

        def rot_tables_p2(st):
            ev, Cdst, Sdst = st
            e1 = A(lambda a: a.activation(out=Sdst, in_=Sdst, func=AF.Sin), [ev])
            e2 = A(lambda a: a.activation(out=Cdst, in_=Cdst, func=AF.Sin), [e1])
            e3 = V(lambda v: v.tensor_scalar(out=Sdst, in0=Sdst, scalar1=invf_s[:, 1:2], scalar2=None, op0=ALU.mult), [e1])
            P.wait("dve", e2)
            return [e2, e3]

        def rot_tables(pcol0, n, Cdst, Sdst, waits):
            return rot_tables_p2(rot_tables_p1(pcol0, n, Cdst, Sdst, waits))

        rot_i = [0]
        rot_free = [None]

        def rotary_evac(b, ev_pe, dst, Ct, St, tab_ev, n, shp=None):
            f = shp if shp is not None else (lambda a: a)
            ev_c = A(lambda a: a.copy(out=dst, in_=bank(b)[:, 0:n]), [ev_pe])

            def part2():
                rb = 2 + (rot_i[0] % 2)
                rot_i[0] += 1
                P.wait("pe", bfree[rb])
                ev_r = T(lambda t: t.matmul(bank(rb)[0:32, 0:n], Rsw, dst, start=True, stop=True), waits=[ev_c], sig=True)
                e1 = V(lambda v: v.tensor_tensor(out=f(t1[:, 0:n]), in0=f(bank(rb)[0:32, 0:n]), in1=St, op=ALU.mult),
                       [ev_r, tab_ev, rot_free[0]])
                bfree[rb] = e1
                e2 = V(lambda v: v.tensor_tensor(out=f(t2[:, 0:n]), in0=f(bank(b)[0:32, 0:n]), in1=Ct, op=ALU.mult), [ev_c])
                e3 = V(lambda v: v.tensor_tensor(out=dst[0:32], in0=t1[:, 0:n], in1=t2[:, 0:n], op=ALU.add), [e1, e2])
                rot_free[0] = e3
                bfree[b] = e3
            return part2

        rot_tables(HALO, 512, Co[:, 0:512], So[:, 0:512], None)
        ev_tabo = rot_tables(HALO + 512, 512, Co[:, 512:1024], So[:, 512:1024], None)

        hTb = [am[:, i * 4096:(i + 1) * 4096].rearrange("p (k t) -> p k t", t=256) for i in range(2)]
        hb_free = [mem_last, mem_last]
        batch_evd = {}
        batch_tab = {}
        held = {}

        def halo_norm_A(bi, tt):
            tile0, ntile, gi = HALO_BATCHES[bi]
            hb = hTb[bi % 2]
            i, evl = load_xtile(xh[(tile0 + tt) * 128:(tile0 + tt + 1) * 128, :])
            st = norm_A(xt[i], evl, 0,
                        lambda h, tt=tt, hb=hb: hb[:, h * 8:(h + 1) * 8, tt * 128:(tt + 1) * 128],
                        dst_wait=hb_free[bi % 2])
            xt_free[i] = st[0]
            return (bi, st)

        def halo_norm_B(pst):
            bi, st = pst
            evd, evr = norm_B(st)
            batch_evd[bi] = evd

        def halo_norm(bi, tt):
            halo_norm_B(halo_norm_A(bi, tt))

        tab_st = {}

        def halo_tables_p1(bi):
            tile0, ntile, gi = HALO_BATCHES[bi]
            n = ntile * 128
            tab_st[bi] = rot_tables_p1(tile0 * 128, n, ChB[bi % 2][:, 0:n], ShB[bi % 2][:, 0:n], tab_last_use[bi % 2])

        def halo_tables_p2(bi):
            batch_tab[bi] = rot_tables_p2(tab_st[bi])

        def halo_tables(bi):
            halo_tables_p1(bi)
            halo_tables_p2(bi)

        def halo_piece(bi, pc):
            tile0, ntile, gi = HALO_BATCHES[bi]
            hb = hTb[bi % 2]
            n = ntile * 128
            evd = batch_evd[bi]
            kcol = 5120 + (2 - gi) * 1024
            if (gi, pc) not in held:
                held[(gi, pc)] = wpiece(w0, 0, 16, kcol + pc * 256, 256)
            s_, wv, evw = held[(gi, pc)]
            last_of_group = (bi + 1 >= len(HALO_BATCHES)) or (HALO_BATCHES[bi + 1][2] != gi)
            lastpe = None
            if pc < 2:
                ev_tab = batch_tab[bi]
                for c in range(2):
                    hd = pc * 2 + c
                    dst = KTh[:, hd, tile0 * 128: tile0 * 128 + n]
                    lastpe = proj_fm(wv, c * 128, lambda k, hb=hb, n=n: hb[:, k, 0:n], 16, n,
                                     lambda b, ev, dst=dst, n=n, ev_tab=ev_tab, bi=bi: rotary_evac(b, ev, dst, ChB[bi % 2][:, 0:n], ShB[bi % 2][:, 0:n], ev_tab, n),
                                     waits=[evw, evd])
            else:
                cc = (pc - 2) * 256
                for tt in range(ntile):
                    dst = Vh[:, tile0 + tt, cc:cc + 256]
                    lastpe = proj_tm(lambda k, hb=hb, tt=tt: hb[:, k, tt * 128:(tt + 1) * 128], wv, 0, 256, 16,
                                     lambda b, ev, dst=dst: A(lambda a: a.copy(out=dst, in_=bank(b)[:, 0:256]), [ev]),
                                     waits=[evw, evd])
            if pc == 1:
                flush_rot()
                tab_last_use[bi % 2] = rot_free[0]
            if last_of_group:
                R.release(s_, lastpe)
            if pc == 3:
                hb_free[bi % 2] = lastpe

        for tt in range(HALO_BATCHES[0][1]):
            halo_norm(0, tt)
        halo_tables(0)
        NB_H = len(HALO_BATCHES)
        ev_hT_h = [None]

        def own_A(t):
            i, evl = load_xtile(xo[t * 128:(t + 1) * 128, :])
            st = norm_A(xt[i], evl, 0, lambda h, t=t: hT[:, h * 8:(h + 1) * 8, t * 128:(t + 1) * 128], dst_wait=mem_last)
            xt_free[i] = st[0]
            return st

        own_next = [0]
        for bi in range(NB_H):
            nxt = HALO_BATCHES[bi + 1][1] if bi + 1 < NB_H else 0
            for pc in range(4):
                pst = None
                ost = None
                if pc < nxt:
                    pst = halo_norm_A(bi + 1, pc)
                elif bi >= NB_H - 4 and own_next[0] < 8:
                    ost = own_A(own_next[0])
                    own_next[0] += 1
                halo_piece(bi, pc)
                if pst is not None:
                    halo_norm_B(pst)
                if ost is not None:
                    ev_hT_h[0], _ = norm_B(ost)
                if pc == 1 and bi + 1 < NB_H:
                    halo_tables_p1(bi + 1)
                if pc == 3 and bi + 1 < NB_H:
                    halo_tables_p2(bi + 1)
        while own_next[0] < 8:
            ev_hT_h[0], _ = norm_B(own_A(own_next[0]))
            own_next[0] += 1
        ev_hT = ev_hT_h[0]
        xt_last = [xt_free[0], xt_free[1]]

        def vtile(ap2, gi, j):
            if gi == 0:
                return ap2[:, j * 128:(j + 1) * 128]
            if gi == 1:
                r, bb = j // 2, j % 2
                return ap2.rearrange("p (i r) -> p r i", r=4)[:, r, bb * 128:(bb + 1) * 128]
            return ap2.rearrange("p (i r) -> p r i", r=16)[:, j, :]

        def gtile(ap2, gi, j):
            return vtile(ap2, gi, j)

        def nat_view(a, gi):
            if gi == 0:
                return a
            return a.rearrange("p (i r) -> p i r", r=(4 if gi == 1 else 16))

        def perm_view(row, gi, th):
            if gi == 0:
                return row[:, th * 512:(th + 1) * 512]
            if gi == 1:
                return row.rearrange("p (r i) -> p i r", r=4)[:, 128 * th:128 * (th + 1), :]
            return row.rearrange("p (r i) -> p i r", r=16)[:, 32 * th:32 * (th + 1), :]

        qn_i = [0]
        qn_free = [None, None]
        vt_free = [None]
        VTs1 = ph[:, 8192:9216]

        def rotary_evac_perm(b, ev_pe, dstrow, gi, th):
            qi = qn_i[0] % 2
            qn_i[0] += 1
            qn = qnb[qi]
            Ct = Co[:, th * 512:(th + 1) * 512]
            St = So[:, th * 512:(th + 1) * 512]
            ev_c = A(lambda a: a.copy(out=qn, in_=bank(b)), [ev_pe, qn_free[qi]])
            A(lambda a: a.copy(out=perm_view(dstrow[32:64], gi, th), in_=nat_view(bank(b)[32:64, :], gi)), [ev_pe])
            ev_c2 = A(lambda a: a.copy(out=perm_view(dstrow[64:128], gi, th), in_=nat_view(bank(b)[64:128, :], gi)), [ev_pe])
            def part2():
                rb = 2 + (rot_i[0] % 2)
                rot_i[0] += 1
                P.wait("pe", bfree[rb])
                ev_r = T(lambda t: t.matmul(bank(rb)[0:32, :], Rsw, qn, start=True, stop=True), waits=[ev_c], sig=True)
                qn_free[qi] = ev_r
                e1 = V(lambda v: v.tensor_tensor(out=t1, in0=bank(rb)[0:32, :], in1=St, op=ALU.mult),
                       [ev_r, ev_tabo, rot_free[0]])
                bfree[rb] = e1
                e2 = V(lambda v: v.tensor_tensor(out=t2, in0=bank(b)[0:32, :], in1=Ct, op=ALU.mult), [ev_c])
                e3 = V(lambda v: v.tensor_tensor(out=perm_view(dstrow[0:32], gi, th), in0=nat_view(t1, gi), in1=nat_view(t2, gi),
                                                 op=ALU.add), [e1, e2, ev_c2])
                rot_free[0] = e3
                bfree[b] = e3
            return part2

        accA = psall[:, 4:6, :].rearrange("p a b -> p (a b)")
        denA = psall[:, 6:8, :].rearrange("p a b -> p (a b)")
        att_done = None
        rdA_free = [None]
        rdm = misc_take(21760, 2048, F32)

        for s in range(4):
            base = s * 1280
            cur = [None, None, None]
            lastpe_piece = [None]

            def get_piece(pcI, cur=cur, base=base, lastpe_piece=lastpe_piece):
                if cur[0] != pcI:
                    if cur[0] is not None:
                        R.release(cur[1][0], lastpe_piece[0])
                    cur[0] = pcI
                    cur[1] = wpiece(w0, 0, 16, base + pcI * 256, 256)
                return cur[1]

            PB = (0, 1, 4, 5, 6, 7)
            for c in range(7):
                s_, wv, evw = get_piece(c // 2)
                cI = c % 2
                for th in range(2):
                    if c < 6:
                        gi = c % 3
                        dst = (QT if c < 3 else KTo)[:, gi, th * 512:(th + 1) * 512]
                        ev = proj_fm(wv, cI * 128, lambda k, th=th: hT[:, k, th * 512:(th + 1) * 512], 16, 512,
                                     lambda b, ev, dst=dst, th=th: rotary_evac(b, ev, dst, Co[:, th * 512:(th + 1) * 512],
                                                                               So[:, th * 512:(th + 1) * 512], ev_tabo, 512),
                                     waits=[evw, ev_hT, att_done], banks=PB)
                    else:
                        dst = qmT[:, th * 512:(th + 1) * 512]
                        ev = proj_fm(wv, cI * 128, lambda k, th=th: hT[:, k, th * 512:(th + 1) * 512], 16, 512,
                                     lambda b, ev, dst=dst: A(lambda a: a.copy(out=dst, in_=bank(b)), [ev]),
                                     waits=[evw, ev_hT, att_done], banks=PB)
                    lastpe_piece[0] = ev
            for gi in range(3):
                c = 7 + gi
                s_, wv, evw = get_piece(c // 2)
                cI = c % 2
                vrow = VTs1
                evv = []
                for th in range(2):
                    def evac_v(b, ev, vrow=vrow, th=th):
                        return A(lambda a: a.copy(out=vrow[:, th * 512:(th + 1) * 512], in_=bank(b)), [ev, vt_free[0]])
                    ev = proj_fm(wv, cI * 128, lambda k, th=th: hT[:, k, th * 512:(th + 1) * 512], 16, 512, evac_v,
                                 waits=[evw, ev_hT, att_done], banks=PB)
                    lastpe_piece[0] = ev
                    evv.append(bfree[PB[(pj[0] - 1) % len(PB)]])
                ntile = 8 if gi < 2 else 16
                tw = 128 if gi < 2 else 64
                for jb in range(ntile // 8):
                    b = 2 + (pj[0] % 2)
                    pj[0] += 1
                    P.wait("pe", [bfree[b], evv])
                    ev = None
                    for jj in range(8):
                        j = jb * 8 + jj
                        ev = T(lambda t, b=b, jj=jj, j=j, vrow=vrow, tw=tw, gi=gi: t.transpose(
                            bank_bf(b)[0:tw, jj * 128:(jj + 1) * 128], vtile(vrow, gi, j), ident), sig=(jj == 7))
                    if gi < 2:
                        dst = Vo[:, gi, :, :]
                    else:
                        dst = Vo3[0:64, jb * 8:(jb + 1) * 8, :]
                    bfree[b] = A(lambda a, b=b, dst=dst, tw=tw: a.copy(
                        out=dst, in_=bank_bf(b)[0:tw, :].rearrange("p (j c) -> p j c", c=128)), [ev])
                    vt_free[0] = ev
            flush_rot()
            R.release(cur[1][0], lastpe_piece[0])
            proj_done = [rot_free[0]] + [bfree[i] for i in range(8)]

            ev_z1 = V(lambda v: v.memset(accA, 0.0), [bfree[4], bfree[5]])
            ev_z2 = V(lambda v: v.memset(denA, 0.0), [bfree[6], bfree[7]])
            P.wait("pe", [ev_z1, ev_z2, proj_done, ev_ov])

            def g_attend(items, mask_ops, nk=128):
                def stage1():
                    i = pv_i[0] % 2
                    pv_i[0] += 1
                    sbk = 2 + i
                    P.wait("pe", bfree[sbk])
                    c = 0
                    offs = []
                    evs = None
                    for n_it, (k_ap, q_ap, nq, pvs) in enumerate(items):
                        evs = T(lambda t, c=c, nq=nq, k_ap=k_ap, q_ap=q_ap: t.matmul(
                            bank(sbk)[0:nk, c:c + nq], k_ap, q_ap, start=True, stop=True), sig=(n_it == len(items) - 1))
                        offs.append(c)
                        c += nq
                    tot = c
                    eve = A(lambda a: a.activation(out=ptS[i][0:nk, 0:tot], in_=bank(sbk)[0:nk, 0:tot],
                                                   func=AF.Exp, scale=128 ** -0.5), [evs, pt_free[i]])
                    bfree[sbk] = eve
                    evm = None
                    for (c0, ncl, in1_ap, inner) in mask_ops:
                        o = pmS[i][0:nk, c0:c0 + ncl].rearrange("p (a b) -> p a b", b=inner)
                        a_in = ptS[i][0:nk, c0:c0 + ncl].rearrange("p (a b) -> p a b", b=inner)
                        evm = V(lambda v, o=o, a_in=a_in, in1_ap=in1_ap: v.tensor_tensor(out=o, in0=a_in, in1=in1_ap, op=ALU.mult),
                                [eve, pm_free[i]])
                    pt_free[i] = evm

                    def stage2():
                        evp = None
                        first = True
                        for (k_ap, q_ap, nq, pvs), off in zip(items, offs):
                            for (co, ncl, v_ap, o_ap, acc_ap, den_ap) in pvs:
                                rhs = pmS[i][0:nk, off + co:off + co + ncl]
                                T(lambda t, v_ap=v_ap, rhs=rhs, acc_ap=acc_ap: t.matmul(
                                    acc_ap, v_ap, rhs, start=False, stop=False, skip_group_check=True),
                                  waits=[evm] if first else None)
                                first = False
                                evp = T(lambda t, o_ap=o_ap, rhs=rhs, den_ap=den_ap: t.matmul(
                                    den_ap, o_ap, rhs, start=False, stop=False, skip_group_check=True), sig=True)
                        pm_free[i] = evp
                        return evp
                    return stage2
                return stage1

            batches = []
            M_op = cmat[:, 128:384]
            M_po = cmat[:, 256:512]

            def g1_item(kt):
                if kt < 0:
                    k_ap, v_ap, o_ap = KTh[:, s, 2560:2688], Vh[:, 20, s * 128:(s + 1) * 128], onesv[:, 20, :]
                else:
                    k_ap, v_ap, o_ap = KTo[:, 0, kt * 128:(kt + 1) * 128], Vo[:, 0, kt, :], ones
                qlo, qhi = max(kt, 0), min(kt + 1, 7)
                nq = (qhi - qlo + 1) * 128
                pvs = [(qi * 128, 128, v_ap, o_ap, accA[:, qt * 128:(qt + 1) * 128], denA[:, qt * 128:(qt + 1) * 128])
                       for qi, qt in enumerate(range(qlo, qhi + 1))]
                return (k_ap, QT[:, 0, qlo * 128:qlo * 128 + nq], nq, pvs)

            batches.append(g_attend([g1_item(-1), g1_item(0)],
                                    [(0, 128, M_prev.unsqueeze(1), 128), (128, 256, M_op.unsqueeze(1), 256)]))
            for kt in (1, 3, 5):
                batches.append(g_attend([g1_item(kt), g1_item(kt + 1)],
                                        [(0, 512, M_op.unsqueeze(1).to_broadcast([128, 2, 256]), 256)]))
            batches.append(g_attend([g1_item(7)], [(0, 128, M_own.unsqueeze(1), 128)]))
            for r in range(4):
                items = []
                for kb in range(-1, 2):
                    if kb < 0:
                        k_ap, v_ap, o_ap = KTh[:, s, 2048 + r * 128:2048 + (r + 1) * 128], Vh[:, 16 + r, s * 128:(s + 1) * 128], onesv[:, 16 + r, :]
                    else:
                        k_ap, v_ap, o_ap = vtile(KTo[:, 1, :], 1, r * 2 + kb), Vo[:, 1, r * 2 + kb, :], ones
                    qlo, qhi = max(kb, 0), min(kb + 1, 1)
                    nq = (qhi - qlo + 1) * 128
                    pvs = [(qi * 128, 128, v_ap, o_ap, gtile(accA, 1, r * 2 + qb), gtile(denA, 1, r * 2 + qb))
                           for qi, qb in enumerate(range(qlo, qhi + 1))]
                    items.append((k_ap, QT[:, 1, :].rearrange("p (i r) -> p r i", r=4)[:, r, qlo * 128:qlo * 128 + nq], nq, pvs))
                batches.append(g_attend(items, [(0, 512, M_po.unsqueeze(1).to_broadcast([128, 2, 256]), 256)]))
            for rb in range(2):
                items = []
                for r in range(rb * 8, rb * 8 + 8):
                    accc = accA.rearrange("p (i r) -> p r i", r=16)[:, r, :]
                    denc = denA.rearrange("p (i r) -> p r i", r=16)[:, r, :]
                    items.append((KTh[:, s, r * 128:(r + 1) * 128], vtile(QT[:, 2, :], 2, r), 64,
                                  [(0, 32, Vh[:, r, s * 128:(s + 1) * 128], onesv[:, r, :], accc[:, 0:32], denc[:, 0:32]),
                                   (32, 32, Vh[:, r, s * 128:(s + 1) * 128], onesv[:, r, :], accc[:, 32:64], denc[:, 32:64])]))
                batches.append(g_attend(items, [(0, 512, M_prev[:, 0:64].unsqueeze(1).to_broadcast([128, 8, 64]), 64)]))
            for rb in range(2):
                items = []
                for r in range(rb * 8, rb * 8 + 8):
                    accc = accA.rearrange("p (i r) -> p r i", r=16)[:, r, :]
                    denc = denA.rearrange("p (i r) -> p r i", r=16)[:, r, :]
                    items.append((vtile(KTo[:, 2, :], 2, r), vtile(QT[:, 2, :], 2, r), 64,
                                  [(0, 32, Vo3[0:64, r, :], ones[0:64, :], accc[:, 0:32], denc[:, 0:32]),
                                   (32, 32, Vo3[0:64, r, :], ones[0:64, :], accc[:, 32:64], denc[:, 32:64])]))
                batches.append(g_attend(items, [(0, 512, M_own[0:64, 0:64].unsqueeze(1).to_broadcast([64, 8, 64]), 64)], nk=64))
            pend = None
            last = None
            for bt in batches:
                s2 = bt()
                if pend is not None:
                    last = pend()
                pend = s2
            last = pend()
            ev1 = recip_act(rden, denA, [last, rdA_free[0]])
            ev2 = V(lambda v, s=s: v.tensor_tensor(out=attnT[:, s, :], in0=accA, in1=rden, op=ALU.mult), [ev1, xt_last])
            rdA_free[0] = ev2
            for b in (4, 5):
                bfree[b] = ev2
            for b in (6, 7):
                bfree[b] = ev1
            att_done = [ev2, mem_attn_multi([(s, qmT[:, th * 512:(th + 1) * 512], attnT[:, 4 + s, th * 512:(th + 1) * 512], [proj_done, xt_last]) for th in range(2)], rdm, pairs=((0, 1), (4, 6)))]

        xl = []
        for t in range(8):
            xl.append(SPD(lambda q, t=t: q.dma_start(out=xs[:, t, :], in_=xo[t * 128:(t + 1) * 128, :]), f"xsl{t}", [att_done]))
        pendF = [None]
        evhF = [None]

        def cb_ffn0norm(t, ev):
            if pendF[0] is not None:
                evhF[0], _ = norm_B(pendF[0])
            pendF[0] = norm_A(xs[:, t, :], ev, 32,
                              lambda h, t=t: hT[:, h * 8:(h + 1) * 8, t * 128:(t + 1) * 128], dst_wait=att_done)
        if stop != "attn0":
            xev = out_proj_t(wo0, 8, lambda k, t: attnT[:, k, t * 128:(t + 1) * 128], list(range(8)), xl, att_done, cb_ffn0norm)
            evhF[0], _ = norm_B(pendF[0])
        else:
            xev = out_proj(wo0, 8, lambda k, t: attnT[:, k, t * 128:(t + 1) * 128], list(range(8)), xl, att_done)
        x_ready = [xev[t] for t in range(8)]
        hT_free[0] = att_done
        xt[0], xt[1] = xt_am[0], xt_am[1]
        xt_free[0] = xt_free[1] = x_ready
        evh_cb = [None]
        if stop != "attn0":
            pendB = [None]

            def cb_l1norm(t, ev, hfree):
                if pendB[0] is not None:
                    evh_cb[0], _ = norm_B(pendB[0])
                pendB[0] = norm_A(xs[:, t, :], ev, 16,
                                  lambda h, t=t: hT[:, h * 8:(h + 1) * 8, t * 128:(t + 1) * 128], dst_wait=hfree)
            x_ready = ffn(0, x_ready, cb_l1norm if (1 in layers) else None, evh_pre=evhF[0])
            if pendB[0] is not None:
                evh_cb[0], _ = norm_B(pendB[0])
            xt_free[0] = xt_free[1] = x_ready
    else:
        evh_cb = [None]
        for t in range(8):
            x_ready[t] = SPD(lambda q, t=t: q.dma_start(out=xs[:, t, :], in_=xo[t * 128:(t + 1) * 128, :]), f"xsl{t}")

    gfin = ph[:, 0:4096].bitcast(F32)
    hTf32 = hT[:].rearrange("p a b -> p (a b)").bitcast(F32)
    ystage = [hTf32[:, i * 2048:(i + 1) * 2048] for i in range(4)]
    yst_free = [None] * 4
    ev_gf_h = [None]
    final_done = [False]
    st_evs = []

    def final_tile(t, ev, hfree):
        ss, sq, rs = stat_col(), stat_col(), stat_col()
        e = A(lambda a: a.activation(out=xnb[:], in_=xs[:, t, :], func=AF.Square, accum_out=ss), [ev, xn_free[0]])
        e = A(lambda a: a.activation(out=sq, in_=ss, func=AF.Sqrt, scale=1.0 / DM, bias=eps_ap), [e])
        e = V(lambda v: v.reciprocal(out=rs, in_=sq), [e])
        i = t % 4
        e = V(lambda v: v.scalar_tensor_tensor(out=ystage[i], in0=xs[:, t, :], scalar=rs, in1=gfin,
                                               op0=ALU.mult, op1=ALU.mult), [e, ev_gf_h[0], yst_free[i], hfree])
        evs = SPD(lambda q: q.dma_start(out=y[t * 128:(t + 1) * 128, :], in_=ystage[i]), f"yst{i}", [e])
        yst_free[i] = evs
        st_evs.append(evs)

    if 1 in layers and stop != "attn0":
        lng = misc_take(0, 6144, F32)
        lnb = misc_take(6144, 6144, F32)
        bsp = misc_take(12288, 6144, F32)
        wsT = misc_take(18432, 3072, BF16).rearrange("p (g t) -> p g t", t=128)
        rden1 = misc_take(21504, 2048, F32)
        vtm = ph[:, 0:6144].rearrange("p (t c) -> p t c", c=1536)
        qm1 = ph[:, 6144:8192].rearrange("p (m t) -> p m t", t=512)
        vgf = ph[:, 8192:11264].bitcast(F32)
        memT1 = ph[:, 0:4096].rearrange("p (k t) -> p k t", t=256)
        e1 = SPD(lambda q: q.dma_start(out=lng, in_=lng_d), "gld4", x_ready)
        e2 = SPD(lambda q: q.dma_start(out=lnb, in_=lnb_d), "gld5")
        e3 = SPD(lambda q: q.dma_start(out=bsp, in_=bsp_d), "gld6")
        e4 = SPD(lambda q: q.dma_start(out=vgf, in_=wst_d), "gld7")
        ev_ws = V(lambda v: v.tensor_tensor(out=wsT, in0=vgf.rearrange("p (g t) -> p g t", t=128),
                                            in1=M_own.unsqueeze(1).to_broadcast([128, 12, 128]), op=ALU.mult), [e4])
        vgb = ph[:, 8192:11264]
        TA = vgb[:, 0:1536]
        TB = vgb[:, 1536:3072]
        bsp_hl = misc[0:64, 6144:7680]
        V(lambda v: v.memset(TA[0:64], 0.0), [ev_ws])
        V(lambda v: v.tensor_copy(out=TA[0:1], in_=bsp[0:1]), [e3])
        eb1 = V(lambda v: v.tensor_copy(out=TB[32:33], in_=bsp[32:33]))
        eb2 = V(lambda v: v.tensor_tensor(out=TA[32:33], in0=bsp[32:33], in1=TB[32:33], op=ALU.subtract), [eb1])
        ev_hl = V(lambda v: v.tensor_copy(out=bsp_hl, in_=TA[0:64]), [eb2])
        ev_tabs = [e1, e2, ev_ws, ev_hl]
        evh = evh_cb[0]
        if evh is None:
            for t in range(8):
                evh, _ = norm_tile(xs[:, t, :], x_ready[t], 16,
                                   lambda h, t=t: hT[:, h * 8:(h + 1) * 8, t * 128:(t + 1) * 128], dst_wait=hT_free[0])
        gTm = am[:].rearrange("p (k t) -> p k t", t=512)
        half_done = [xt_free[0], xt_free[1]]
        pre_evs = []
        for hf in range(2):
            tk0 = hf * 512
            last = None
            for pc in range(2):
                s_, wv, evw = wpiece(w1, 0, 16, 3072 + pc * 256, 256)
                for c in range(2):
                    m = pc * 2 + c
                    last = proj_fm(wv, c * 128, lambda k, tk0=tk0: hT[:, k, tk0:tk0 + 512], 16, 512,
                                   lambda b, ev, m=m: A(lambda a: a.copy(out=qm1[:, m, :], in_=bank(b)), [ev]),
                                   waits=[evw, evh, half_done])
                R.release(s_, last)
            qdone = [bfree[0], bfree[1]]
            if hf == 0:
                mem_kv(1, memT1)
            for cg in range(6):
                pstF = None
                if hf == 1 and cg < 4 and stop != "attn1":
                    pstF = norm_A(xs[:, cg, :], x_ready[cg], 48,
                                  lambda h, cg=cg: hT[:, h * 8:(h + 1) * 8, cg * 128:(cg + 1) * 128], dst_wait=half_done)
                s_, wv, evw = wpiece(w1, 0, 16, 1536 + cg * 256, 256)
                last = None
                for tt in range(4):
                    def evac(b, ev, tt=tt, cg=cg):
                        return A(lambda a: a.activation(out=vtm[:, tt, cg * 256:(cg + 1) * 256], in_=bank(b)[:, 0:256],
                                                        func=AF.Gelu), [ev, half_done])
                    last = proj_tm(lambda k, tt=tt, tk0=tk0: hT[:, k, tk0 + tt * 128: tk0 + (tt + 1) * 128], wv, 0, 256, 16, evac,
                                   waits=[evw, evh])
                R.release(s_, last)
                if pstF is not None:
                    pre_evs.append(norm_B(pstF)[0])
            vdone = [bfree[4], bfree[5]]
            ev_ln_h = [None]

            def ln_tile(tt, vdone=vdone, ev_ln_h=ev_ln_h):
                sm, sq2, mu, var, rs = stat_col(), stat_col(), stat_col(), stat_col(), stat_col()
                ea = A(lambda a: a.activation(out=xnb[:, 0:1536], in_=vtm[:, tt, :], func=AF.Copy, accum_out=sm), [vdone, ev_ws, xn_free[0]])
                eb = A(lambda a: a.activation(out=xnb[:, 0:1536], in_=vtm[:, tt, :], func=AF.Square, accum_out=sq2), [ea])
                e = V(lambda v: v.tensor_scalar(out=mu, in0=sm, scalar1=1.0 / 1536, scalar2=None, op0=ALU.mult), [ea])
                e = V(lambda v: v.tensor_tensor(out=var, in0=mu, in1=mu, op=ALU.mult), [e])
                e = V(lambda v: v.scalar_tensor_tensor(out=var, in0=sq2, scalar=1.0 / 1536, in1=var, op0=ALU.mult, op1=ALU.subtract), [e, eb])
                e = A(lambda a: a.activation(out=var, in_=var, func=AF.Sqrt, bias=lneps_ap), [e])
                e = V(lambda v: v.reciprocal(out=rs, in_=var), [e])
                e = V(lambda v: v.tensor_scalar(out=vgf, in0=vtm[:, tt, :], scalar1=mu, scalar2=rs, op0=ALU.subtract, op1=ALU.mult), [e, eb, ev_ln_h[0]])
                e = V(lambda v: v.tensor_tensor(out=vgf, in0=vgf, in1=lng, op=ALU.mult), [e, ev_tabs])
                ev_ln_h[0] = V(lambda v: v.tensor_tensor(out=vtm[:, tt, :], in0=vgf, in1=lnb, op=ALU.add), [e])

            for pc in range(6):
                s_, wv, evw = wpiece(w1, 0, 16, pc * 256, 256)
                last = None
                for c in range(2):
                    g = pc * 2 + c
                    last = proj_fm(wv, c * 128, lambda k, tk0=tk0: hT[:, k, tk0:tk0 + 512], 16, 512,
                                   lambda b, ev, g=g: A(lambda a: a.activation(out=gTm[:, g, :], in_=bank(b), func=AF.Gelu), [ev, half_done]),
                                   waits=[evw, evh])
                R.release(s_, last)
                if pc < 4:
                    ln_tile(pc)
            ev_ln = ev_ln_h[0]
            udone = [bfree[0], bfree[1]]
            last_ma = mem_attn_multi([(m, qm1[:, m, :], gTm[:, 12 + m, :], [qdone, half_done]) for m in range(4)], rden1)
            ev_gate = None
            for g in range(12):
                b = 2 + (g % 2)
                P.wait("pe", [bfree[b], ev_ln, ev_ws, ev_hl])
                ev = None
                for tt in range(4):
                    ob = bank(b)[:, tt * 128:(tt + 1) * 128]
                    T(lambda t, ob=ob, tt=tt, g=g: t.matmul(ob, vtm[:, tt, g * 128:(g + 1) * 128], wsT[:, g, :],
                                                          start=True, stop=False))
                    ev = T(lambda t, ob=ob, g=g: t.matmul(ob, ones[0:64, :], bsp_hl[:, g * 128:(g + 1) * 128],
                                                         start=False, stop=True), sig=(tt == 3))
                ev_gate = V(lambda v, b=b, g=g: v.tensor_tensor(out=gTm[:, g, :], in0=bank(b), in1=gTm[:, g, :], op=ALU.mult),
                            [ev, udone])
                bfree[b] = ev_gate
            cat_ev = [ev_gate, last_ma]
            toks = [hf * 4 + i for i in range(4)]
            r = out_proj(wo1, 16, lambda k, t: gTm[:, k, (t % 4) * 128:(t % 4 + 1) * 128], toks, x_ready, cat_ev)
            for t in toks:
                x_ready[t] = r[t]
            half_done = [r[t] for t in toks]
            P.wait("pe", half_done)
        hT_free[0] = half_done
        xt_free[0] = xt_free[1] = x_ready
        if stop != "attn1":
            if final:
                ev_gf_h[0] = SPD(lambda q: q.dma_start(out=gfin, in_=gfin_d), "gld8", x_ready)
                x_ready = ffn(1, x_ready, final_tile, pre_tiles=(0, 1, 2, 3) if pre_evs else (), pre_evh=pre_evs)
                final_done[0] = True
            else:
                x_ready = ffn(1, x_ready, pre_tiles=(0, 1, 2, 3) if pre_evs else (), pre_evh=pre_evs)
            xt_free[0] = xt_free[1] = x_ready

    if final and not final_done[0]:
        ev_gf_h[0] = SPD(lambda q: q.dma_start(out=gfin, in_=gfin_d), "gld8", x_ready)
        for t in range(8):
            final_tile(t, x_ready, hT_free[0])
    elif not final:
        for t in range(8):
            evs = SPD(lambda q, t=t: q.dma_start(out=y[t * 128:(t + 1) * 128, :], in_=xs[:, t, :]), "yst0", [x_ready[t]])
            st_evs.append(evs)
    P.wait("sp", st_evs)
    P.build()
    es.close()
    return nc


_NC_CACHE = {}


def _get_nc(layers, final, stop=None):
    key = (tuple(layers), final, stop)
    if key not in _NC_CACHE:
        _NC_CACHE[key] = build(layers, final, stop)
    return _NC_CACHE[key]


def _halo_idx(T0):
    g3 = (T0 - 2048 + 16 * np.arange(128)[None, :] + np.arange(16)[:, None]).reshape(-1)
    g2 = (T0 - 512 + 4 * np.arange(128)[None, :] + np.arange(4)[:, None]).reshape(-1)
    g1 = T0 - 128 + np.arange(128)
    return np.concatenate([g3, g2, g1]).astype(np.int64)


def _consts():
    k = np.arange(128)[:, None]
    q = np.arange(128)[None, :]
    ident = (k == q)
    m_own = (k <= q)
    m_prev = (k >= q)
    m_b3 = ((k // 64) == (q // 64)) & ((k % 64) <= (q % 64))
    ones = np.ones((128, 128), bool)
    m32 = np.arange(32)[None, :]
    rsw = (k < 32) & (k == ((m32 + 16) % 32))
    cmat = np.concatenate([ident, m_own, m_prev, m_own, ones, rsw], axis=1).astype(np.float32)
    half = 16
    inv_freq = (np.float32(500000.0) ** (-(np.arange(half, dtype=np.float32)) / np.float32(half))).astype(np.float32)
    invf = np.zeros((32, 2), np.float32)
    invf[:, 0] = inv_freq[np.arange(32) % 16]
    invf[:, 1] = np.where(np.arange(32) < 16, -1.0, 1.0)
    return np.ascontiguousarray(cmat), invf


def _prep(inp, layers):
    f = lambda a: np.ascontiguousarray(np.asarray(a, dtype=np.float32))
    cmat, invf = _consts()
    gl = [inp["mix_norm"][0], inp["mix_norm"][1], inp["ffn_norm"][0], inp["ffn_norm"][1],
          inp["mem_norm"][0], inp["mem_norm"][1]]
    gT = np.concatenate([np.asarray(g, np.float32).reshape(16, 128).T for g in gl], axis=1)
    common = {
        "mem": f(inp["mem"][0]), "cmat": cmat, "gT": f(gT),
        "gfin": f(np.broadcast_to(np.asarray(inp["final_norm"], np.float32)[None, :], (128, DM))),
        "wkv": f(inp["w_mem_kv"]), "wg": f(inp["w_gate"]), "wu": f(inp["w_up"]), "wd": f(inp["w_down"]),
    }
    if 0 in layers:
        w = np.asarray(inp["attn_w_in"][0], np.float32)
        qc = lambda h: w[:, h * 128:(h + 1) * 128]
        kc = lambda h: w[:, 1536 + h * 128:1536 + (h + 1) * 128]
        vc = lambda h: w[:, 3072 + h * 128:3072 + (h + 1) * 128]
        mc = lambda m: w[:, 4608 + m * 128:4608 + (m + 1) * 128]
        cols = []
        for s in range(4):
            cols += [qc(s), qc(4 + s), qc(8 + s), kc(s), kc(4 + s), kc(8 + s), mc(s), vc(s), vc(4 + s), vc(8 + s)]
        for gi in (2, 1, 0):
            cols += [w[:, 1536 + gi * 512:1536 + (gi + 1) * 512], w[:, 3072 + gi * 512:3072 + (gi + 1) * 512]]
        common["w0"] = np.ascontiguousarray(np.concatenate(cols, axis=1))
        common["wo0"] = f(inp["attn_w_out"][0])
        common["invf"] = invf
    if 1 in layers:
        common["w1"] = f(inp["sgu_w_in"][0])
        common["wo1"] = f(inp["sgu_w_out"][0])
        common["lng"] = f(np.broadcast_to(np.asarray(inp["sgu_ln_g"][0], np.float32)[None, :], (128, 1536)))
        common["lnb"] = f(np.broadcast_to(np.asarray(inp["sgu_ln_b"][0], np.float32)[None, :], (128, 1536)))
        common["bsp"] = f(np.broadcast_to(np.asarray(inp["sgu_b_spatial"][0], np.float32).reshape(1, 1536), (128, 1536)))
        common["wst"] = f(np.asarray(inp["sgu_w_spatial"][0], np.float32).transpose(2, 0, 1).reshape(128, 1536))
    return common


def _run(inp, x2, layers, final, stop=None, ncores=NCORES):
    nc = _get_nc(layers, final, stop)
    common = _prep(inp, layers)
    pos = np.asarray(inp["positions"][0], np.int32)
    in_maps = []
    for c in range(ncores):
        T0 = c * TOK
        m = dict(common)
        m["xo"] = np.ascontiguousarray(x2[T0:T0 + TOK])
        if 0 in layers:
            idx = _halo_idx(T0)
            ok = idx >= 0
            ic = np.clip(idx, 0, None)
            xh = x2[ic].copy()
            xh[~ok] = 0.0
            m["xh"] = xh
            pp = np.concatenate([pos[ic], pos[T0:T0 + TOK]]).astype(np.int32)
            m["posr"] = np.ascontiguousarray(np.broadcast_to(pp[None, :], (32, HALO + TOK)))
            m["valid"] = np.ascontiguousarray(ok.astype(np.float32).reshape(21, 128).T)
        in_maps.append(m)
    res = run_bass_kernel_spmd(nc, in_maps, core_ids=list(range(ncores)))
    return np.concatenate([r["y"] for r in res.results], axis=0)


def kernel(**inp):
    x2 = np.ascontiguousarray(np.asarray(inp["x"], np.float32)[0])
    out = _run(inp, x2, (0, 1), True)
    return out.reshape(1, NCORES * TOK, DM).astype(np.float32)
```

```python
import numpy as np
from contextlib import ExitStack
import concourse.bass as bass
import concourse.mybir as mybir
from concourse.bass_utils import run_bass_kernel_spmd

F32 = mybir.dt.float32
BF16 = mybir.dt.bfloat16
I32 = mybir.dt.int32
AF = mybir.ActivationFunctionType
ALU = mybir.AluOpType

NCORES = 8
TOK = 1024
DM = 2048
FF = 5632
NS = 4
HALO = 2688
ENGS = ("pe", "act", "dve", "pool", "sp")
HALO_BATCHES = [(2 * i, 2, 2) for i in range(8)] + [(16, 2, 1), (18, 2, 1), (20, 1, 0)]
HALO_COL0 = {2: 0, 1: 2048, 0: 2560}


class Prog:
    def __init__(self, nc):
        self.nc = nc
        self.q = {e: [] for e in ENGS}
        self.semcnt = {}
        self.waited = {e: {} for e in ENGS}

    def op(self, eng, fn):
        ent = {"fn": fn, "inc": None}
        self.q[eng].append(ent)
        return ent

    def inc(self, ent, sem, amt=1):
        assert ent["inc"] is None
        self.semcnt[sem] = self.semcnt.get(sem, 0) + amt
        ent["inc"] = (sem, amt)
        return (sem, self.semcnt[sem])

    def wait(self, eng, ev):
        if ev is None:
            return
        if isinstance(ev, list):
            for e in ev:
                self.wait(eng, e)
            return
        sem, val = ev
        w = self.waited[eng]
        if w.get(sem, 0) >= val:
            return
        w[sem] = val
        self.q[eng].append({"wait": (sem, val)})

    def build(self):
        nc = self.nc
        with ExitStack() as es:
            sems = {}
            for name in self.semcnt:
                sems[name] = es.enter_context(nc.semaphore(name))
            block = es.enter_context(nc.Block())

            def replay(engname):
                def body(eng):
                    for ent in self.q[engname]:
                        if "wait" in ent:
                            s, v = ent["wait"]
                            eng.wait_ge(sems[s], v)
                        else:
                            ins = ent["fn"](eng)
                            if ent["inc"] is not None:
                                s, a = ent["inc"]
                                ins.then_inc(sems[s], a)
                return body

            for name, meth in (("sp", block.sync), ("pe", block.tensor), ("act", block.scalar),
                               ("dve", block.vector), ("pool", block.gpsimd)):
                if self.q[name]:
                    meth(replay(name))


def build(layers=(0, 1), final=True, stop=None):
    nc = bass.Bass("TRN2", target_bir_lowering=False)
    P = Prog(nc)

    def din(name, shape, dt=F32):
        return nc.dram_tensor(name, list(shape), dt, kind="ExternalInput").ap()

    xo = din("xo", [TOK, DM])
    if 0 in layers:
        xh = din("xh", [HALO, DM])
        posr = din("posr", [32, HALO + TOK], I32)
        valid = din("valid", [128, 21])
        w0 = din("w0", [DM, 8192])
        wo0 = din("wo0", [1024, DM])
        invf = din("invf", [32, 2])
    if 1 in layers:
        w1 = din("w1", [DM, 3584])
        wo1 = din("wo1", [DM, DM])
        lng_d = din("lng", [128, 1536])
        lnb_d = din("lnb", [128, 1536])
        bsp_d = din("bsp", [128, 1536])
        wst_d = din("wst", [128, 1536])
    memd = din("mem", [256, DM])
    cmat_d = din("cmat", [128, 672])
    gT_d = din("gT", [128, 96])
    gfin_d = din("gfin", [128, DM])
    wkv = din("wkv", [2, DM, 1024])
    wg = din("wg", [2, DM, FF])
    wu = din("wu", [2, DM, FF])
    wd = din("wd", [2, FF, DM])
    y = nc.dram_tensor("y", [TOK, DM], F32, kind="ExternalOutput").ap()

    es = ExitStack()
    sb = lambda name, shape, dt: es.enter_context(nc.sbuf_tensor(name, list(shape), dt))
    xs = sb("xs", [128, 8, DM], F32)
    ring = [sb(f"ring{i}", [128, 4096], BF16) for i in range(NS)]
    hT = sb("hT", [128, 16, TOK], BF16)
    am = sb("am", [128, 8192], BF16)
    mixT = am
    actT = am[:].rearrange("p (a b) -> p a b", b=TOK)
    xt_am = [am[:, i * 4096:(i + 1) * 4096].bitcast(F32) for i in range(2)]
    xt = list(xt_am)
    xn = [sb("xn0", [128, DM], BF16)] * 2
    ph = sb("ph", [128, 11264], BF16)
    cmat = sb("cmat_s", [128, 672], BF16)
    gT = sb("gT_s", [128, 96], F32)
    stat = sb("stat", [128, 64], F32)
    misc = sb("misc", [128, 12 * 1024], BF16)
    ptS = [sb(f"pt{i}", [128, 512], BF16) for i in range(2)]
    pmS = [sb(f"pm{i}", [128, 512], BF16) for i in range(2)]
    memKT = sb("memKT", [128, 4, 256], BF16)
    memV = sb("memV", [128, 2, 512], BF16)
    sgt = ptS
    psall = es.enter_context(nc.psum_tensor("psall", [128, 8, 512], F32))

    ident = cmat[:, 0:128]
    M_own = cmat[:, 128:256]
    M_prev = cmat[:, 256:384]
    M_B3 = cmat[:, 384:512]
    ones = cmat[:, 512:640]
    Rsw = cmat[:, 640:672]

    def bank(b):
        return psall[:, b, :]

    def bank_bf(b):
        return psall[:, b, :].bitcast(BF16)

    bfree = [None] * 8

    def A(fn, waits=None):
        P.wait("act", waits)
        return P.inc(P.op("act", fn), "sA")

    def V(fn, waits=None):
        P.wait("dve", waits)
        return P.inc(P.op("dve", fn), "sV")

    def T(fn, waits=None, sig=False):
        P.wait("pe", waits)
        e = P.op("pe", fn)
        return P.inc(e, "sT") if sig else None

    def SPD(fn, sem, waits=None):
        P.wait("sp", waits)
        return P.inc(P.op("sp", fn), sem, 16)

    def misc_take(off_bytes, nbytes, dt, parts=128):
        a = misc[0:parts, off_bytes // 2:(off_bytes + nbytes) // 2]
        return a if dt == BF16 else a.bitcast(dt)

    class Ring:
        def __init__(self):
            self.free = [None] * NS
            self.n = 0

        def load(self, src3, nk, ncol):
            s = self.n % NS
            self.n += 1
            P.wait("pool", self.free[s])
            view = ring[s][:, 0:nk * ncol].rearrange("p (k c) -> p k c", c=ncol)
            e = P.op("pool", lambda g, view=view, src3=src3: g.dma_start(out=view, in_=src3))
            ev = P.inc(e, f"wld{s}", 16)
            return s, view, ev

        def release(self, s, ev):
            self.free[s] = ev

    R = Ring()

    def wpiece(w2d, row0, nk, col0, ncol):
        src = w2d[row0:row0 + nk * 128, col0:col0 + ncol].rearrange("(k p) c -> p k c", p=128)
        return R.load(src, nk, ncol)

    e = P.op("pool", lambda g: g.dma_start(out=cmat[:], in_=cmat_d))
    ev_c = P.inc(e, "cld", 16)
    ev_g = SPD(lambda q: q.dma_start(out=gT[:], in_=gT_d), "gld1")
    for en in ("pe", "dve", "act"):
        P.wait(en, ev_c)
    P.wait("dve", ev_g)

    eps_t = sb("eps_t", [128, 2], F32)
    eps_ap = eps_t[:, 0:1]
    lneps_ap = eps_t[:, 1:2]
    V(lambda v: v.memset(eps_t[:, 0:1], 1e-6))
    ev_eps = V(lambda v: v.memset(eps_t[:, 1:2], 1e-5))
    P.wait("act", ev_eps)

    stat_i = [0]

    def stat_col():
        i = stat_i[0] % 64
        stat_i[0] += 1
        return stat[:, i:i + 1]

    tp_i = [0]
    xn_free = [None]
    pend_rot = [None]

    def flush_rot():
        if pend_rot[0] is not None:
            f = pend_rot[0]
            pend_rot[0] = None
            f()
    xnb = xn[0]

    def norm_A(src, src_ev, gcol0, dst_of, dst_wait=None):
        ss = stat_col()
        sq = stat_col()
        rs = stat_col()
        ev = A(lambda a: a.activation(out=xnb[:], in_=src, func=AF.Square, accum_out=ss), [src_ev, xn_free[0]])
        ev = A(lambda a: a.activation(out=sq, in_=ss, func=AF.Sqrt, scale=1.0 / DM, bias=eps_ap), [ev])
        ev = V(lambda v: v.reciprocal(out=rs, in_=sq), [ev])
        ev_xn = A(lambda a: a.activation(out=xnb[:], in_=src, func=AF.Copy, scale=rs), [ev])
        return (ev_xn, gcol0, dst_of, dst_wait)

    def norm_B(st):
        ev_xn, gcol0, dst_of, dst_wait = st
        evd = None
        last_pe = None
        for h in range(2):
            b = 6 + (tp_i[0] % 2)
            tp_i[0] += 1
            P.wait("pe", [bfree[b], ev_xn])
            for j in range(8):
                kc = h * 8 + j
                last_pe = T(lambda t, b=b, j=j, kc=kc: t.transpose(
                    bank_bf(b)[:, j * 128:(j + 1) * 128], xnb[:, kc * 128:(kc + 1) * 128], ident), sig=(j == 7))
            flush_rot()
            gb = gT[:, gcol0 + h * 8: gcol0 + h * 8 + 8].unsqueeze(2).to_broadcast([128, 8, 128])
            dst = dst_of(h)
            evd = V(lambda v, b=b, dst=dst, gb=gb: v.tensor_tensor(
                out=dst, in0=bank_bf(b).rearrange("p (a c) -> p a c", c=128), in1=gb, op=ALU.mult),
                [last_pe, dst_wait])
            bfree[b] = evd
        xn_free[0] = last_pe
        return evd, ev_xn

    def norm_tile(src, src_ev, gcol0, dst_of, dst_wait=None):
        return norm_B(norm_A(src, src_ev, gcol0, dst_of, dst_wait))

    xt_free = [None, None]
    xt_n = [0]

    def load_xtile(src_rows):
        i = xt_n[0] % 2
        xt_n[0] += 1
        dst = xt[i]
        ev = SPD(lambda q, dst=dst, src_rows=src_rows: q.dma_start(out=dst, in_=src_rows), f"xld{i}", [xt_free[i]])
        return i, ev

    pj = [0]

    def proj_fm(wview, c0, rhs_of, nk, n, evac, waits=None, banks=(0, 1), oshape=None):
        b = banks[pj[0] % len(banks)]
        pj[0] += 1
        P.wait("pe", [bfree[b], waits])
        out = bank(b)[:, 0:n]
        if oshape is not None:
            out = oshape(out)
        ev = None
        for k in range(nk):
            ev = T(lambda t, k=k: t.matmul(out, wview[:, k, c0:c0 + 128], rhs_of(k),
                                            start=(k == 0), stop=(k == nk - 1)), sig=(k == nk - 1))
        flush_rot()
        r = evac(b, ev)
        if callable(r):
            pend_rot[0] = r
        else:
            bfree[b] = r
        return ev

    def proj_tm(lhs_of, wview, c0, n, nk, evac, waits=None, banks=(4, 5)):
        b = banks[pj[0] % len(banks)]
        pj[0] += 1
        P.wait("pe", [bfree[b], waits])
        ev = None
        for k in range(nk):
            ev = T(lambda t, b=b, k=k: t.matmul(bank(b)[:, 0:n], lhs_of(k), wview[:, k, c0:c0 + n],
                                                 start=(k == 0), stop=(k == nk - 1)), sig=(k == nk - 1))
        flush_rot()
        bfree[b] = evac(b, ev)
        return ev

    def mem_kv(layer, memT):
        evd = None
        for mt in range(2):
            i, evl = load_xtile(memd[mt * 128:(mt + 1) * 128, :])
            evd, evr = norm_tile(xt[i], evl, 64 + 16 * layer,
                                 lambda h, mt=mt: memT[:, h * 8:(h + 1) * 8, mt * 128:(mt + 1) * 128])
            xt_free[i] = evr
        last = None
        for pc in range(4):
            s, wv, evw = wpiece(wkv[layer], 0, 16, pc * 256, 256)
            if pc < 2:
                for c in range(2):
                    m = pc * 2 + c
                    last = proj_fm(wv, c * 128, lambda k: memT[:, k, :], 16, 256,
                                   lambda b, ev, m=m: A(lambda a: a.copy(out=memKT[:, m, :], in_=bank(b)[:, 0:256]), [ev]),
                                   waits=[evw, evd])
            else:
                for mt in range(2):
                    cc = (pc - 2) * 256
                    last = proj_tm(lambda k, mt=mt: memT[:, k, mt * 128:(mt + 1) * 128], wv, 0, 256, 16,
                                   lambda b, ev, mt=mt, cc=cc: A(lambda a: a.copy(out=memV[:, mt, cc:cc + 256], in_=bank(b)[:, 0:256]), [ev]),
                                   waits=[evw, evd])
            R.release(s, last)
        return last

    pt_free = [None, None]
    pm_free = [None, None]

    def recip_act(dst, src, waits):
        e = A(lambda a: a.activation(out=dst, in_=src, func=AF.Ln), waits)
        return A(lambda a: a.activation(out=dst, in_=dst, func=AF.Exp, scale=-1.0), [e])

    pv_i = [0]
    rd_free = [None]

    def mem_attn_multi(groups, rden, pairs=((4, 6), (5, 7))):
        units = []
        for gi_, (m, q_ap, dst, q_ev) in enumerate(groups):
            accb, denb = pairs[gi_ % len(pairs)]
            for mt in range(2):
                units.append((m, q_ap, dst, q_ev, accb, denb, mt))

        def stage1(u):
            m, q_ap, dst, q_ev, accb, denb, mt = u
            i = pv_i[0] % 2
            pv_i[0] += 1
            sbk = 2 + i
            P.wait("pe", [bfree[sbk], q_ev])
            evs = T(lambda t: t.matmul(bank(sbk), memKT[:, m, mt * 128:(mt + 1) * 128], q_ap, start=True, stop=True), sig=True)
            eve = A(lambda a: a.activation(out=ptS[i][:], in_=bank(sbk), func=AF.Exp, scale=128 ** -0.5), [evs, pt_free[i]])
            bfree[sbk] = eve

            def stage2():
                if mt == 0:
                    P.wait("pe", [bfree[accb], bfree[denb]])
                T(lambda t: t.matmul(bank(accb), memV[:, mt, m * 128:(m + 1) * 128], ptS[i][:],
                                     start=(mt == 0), stop=(mt == 1)), waits=[eve])
                evp = T(lambda t: t.matmul(bank(denb), ones, ptS[i][:], start=(mt == 0), stop=(mt == 1)), sig=True)
                pt_free[i] = evp
                if mt == 1:
                    ev1 = recip_act(rden[:, 0:512], bank(denb), [evp, rd_free[0]])
                    ev2 = V(lambda v: v.tensor_tensor(out=dst, in0=bank(accb), in1=rden[:, 0:512], op=ALU.mult), [ev1])
                    rd_free[0] = ev2
                    bfree[accb] = ev2
                    bfree[denb] = ev1
                    return ev2
                return evp
            return stage2

        pend = None
        last = None
        for u in units:
            s2 = stage1(u)
            if pend is not None:
                last = pend()
            pend = s2
        last = pend()
        return last

    hT_free = [None]

    def ffn(layer, x_ready, tile_cb=None, evh_pre=None, pre_tiles=(), pre_evh=None):
        evh = evh_pre
        for t in range(8 if evh_pre is None else 0):
            if t in pre_tiles:
                continue
            evh, _ = norm_tile(xs[:, t, :], x_ready[t], 32 + 16 * layer,
                               lambda h, t=t: hT[:, h * 8:(h + 1) * 8, t * 128:(t + 1) * 128], dst_wait=hT_free[0])
        if pre_evh:
            evh = [evh, pre_evh]
        groups = [(0, 8), (8, 8), (16, 8), (24, 8), (32, 8), (40, 4)]
        xev = list(x_ready)
        act_free = x_ready
        gu_i = 0
        dn_i = 0
        sg_free = [None, None]
        evpu = None
        for (f0, nf) in groups:
            ev_act_last = None
            for pr in range(nf // 2):
                col = (f0 + 2 * pr) * 128
                sg_, wgv, evg = wpiece(wg[layer], 0, 16, col, 256)
                su_, wuv, evu = wpiece(wu[layer], 0, 16, col, 256)
                for c in range(2):
                    fi = 2 * pr + c
                    for th in range(2):
                        bg = 0 + 2 * (gu_i % 2)
                        bu = 1 + 2 * (gu_i % 2)
                        si = gu_i % 2
                        gu_i += 1
                        P.wait("pe", [bfree[bg], bfree[bu], evg, evu, evh])
                        evpg = None
                        for k in range(16):
                            evpg = T(lambda t, bg=bg, k=k, c=c, th=th, wgv=wgv: t.matmul(
                                bank(bg), wgv[:, k, c * 128:(c + 1) * 128], hT[:, k, th * 512:(th + 1) * 512],
                                start=(k == 0), stop=(k == 15)), sig=(k == 15))
                        for k in range(16):
                            evpu = T(lambda t, bu=bu, k=k, c=c, th=th, wuv=wuv: t.matmul(
                                bank(bu), wuv[:, k, c * 128:(c + 1) * 128], hT[:, k, th * 512:(th + 1) * 512],
                                start=(k == 0), stop=(k == 15)), sig=(k == 15))
                        evs = A(lambda a, bg=bg, si=si: a.activation(out=sgt[si][:], in_=bank(bg), func=AF.Silu),
                                [evpg, sg_free[si]])
                        bfree[bg] = evs
                        dst = actT[:, fi, th * 512:(th + 1) * 512]
                        evm = V(lambda v, bu=bu, si=si, dst=dst: v.tensor_tensor(
                            out=dst, in0=bank(bu), in1=sgt[si][:], op=ALU.mult), [evs, evpu, act_free])
                        bfree[bu] = evm
                        sg_free[si] = evm
                        ev_act_last = evm
                R.release(sg_, evpu)
                R.release(su_, evpu)
            lastdn = None
            if tile_cb is not None and f0 == 40:
                pcs = [wpiece(wd[layer], f0 * 128, nf, jp * 1024, 1024) for jp in range(2)]
                lastpe = None
                for t in range(8):
                    for j in range(4):
                        sd_, wdv, evd = pcs[j // 2]
                        jh = j % 2
                        b = 4 + (dn_i % 2)
                        dn_i += 1
                        P.wait("pe", [bfree[b], evd, ev_act_last])
                        for f in range(nf):
                            lastpe = T(lambda tt, b=b, f=f, t=t, wdv=wdv, jh=jh: tt.matmul(
                                bank(b), actT[:, f, t * 128:(t + 1) * 128], wdv[:, f, jh * 512:(jh + 1) * 512],
                                start=(f == 0), stop=(f == nf - 1)), sig=(f == nf - 1))
                        xsl = xs[:, t, j * 512:(j + 1) * 512]
                        eva = V(lambda v, b=b, xsl=xsl: v.tensor_tensor(out=xsl, in0=bank(b), in1=xsl, op=ALU.add),
                                [lastpe, xev[t]])
                        bfree[b] = eva
                        xev[t] = eva
                        lastdn = eva
                    tile_cb(t, xev[t], evpu)
                for jp in range(2):
                    R.release(pcs[jp][0], lastpe)
                act_free = lastdn
                continue
            for j in range(4):
                sd_, wdv, evd = wpiece(wd[layer], f0 * 128, nf, j * 512, 512)
                lastpe = None
                for t in range(8):
                    b = 4 + (dn_i % 2)
                    dn_i += 1
                    P.wait("pe", [bfree[b], evd, ev_act_last])
                    for f in range(nf):
                        lastpe = T(lambda tt, b=b, f=f, t=t, wdv=wdv: tt.matmul(
                            bank(b), actT[:, f, t * 128:(t + 1) * 128], wdv[:, f, :],
                            start=(f == 0), stop=(f == nf - 1)), sig=(f == nf - 1))
                    xsl = xs[:, t, j * 512:(j + 1) * 512]
                    eva = V(lambda v, b=b, xsl=xsl: v.tensor_tensor(out=xsl, in0=bank(b), in1=xsl, op=ALU.add),
                            [lastpe, xev[t]])
                    bfree[b] = eva
                    xev[t] = eva
                    lastdn = eva
                R.release(sd_, lastpe)
            act_free = lastdn
        hT_free[0] = evpu
        return xev

    def out_proj(wout, nk, catT_of, toks, x_evs, cat_ev):
        xev = dict((t, x_evs[t]) for t in toks)
        for j in range(8):
            s_, wv, evw = wpiece(wout, 0, nk, j * 256, 256)
            lastpe = None
            for t in toks:
                def evac(b, ev, t=t, j=j):
                    dst = xs[:, t, j * 256:(j + 1) * 256]
                    e2 = V(lambda v: v.tensor_tensor(out=dst, in0=bank(b)[:, 0:256], in1=dst, op=ALU.add), [ev, xev[t]])
                    xev[t] = e2
                    return e2
                lastpe = proj_tm(lambda k, t=t: catT_of(k, t), wv, 0, 256, nk, evac, waits=[evw, cat_ev],
                                 banks=(0, 1, 2, 3))
            R.release(s_, lastpe)
        return xev

    def out_proj_t(wout, nk, catT_of, toks, x_evs, cat_ev, tile_cb):
        xev = dict((t, x_evs[t]) for t in toks)
        pcs = [wpiece(wout, 0, nk, j * 512, 512) for j in range(4)]
        lastpe = None
        for t in toks:
            for j in range(4):
                s_, wv, evw = pcs[j]

                def evac(b, ev, t=t, j=j):
                    dst = xs[:, t, j * 512:(j + 1) * 512]
                    e2 = V(lambda v: v.tensor_tensor(out=dst, in0=bank(b), in1=dst, op=ALU.add), [ev, xev[t]])
                    xev[t] = e2
                    return e2
                lastpe = proj_tm(lambda k, t=t: catT_of(k, t), wv, 0, 512, nk, evac, waits=[evw, cat_ev], banks=(0, 1, 2, 3))
            tile_cb(t, xev[t])
        for j in range(4):
            R.release(pcs[j][0], lastpe)
        return xev

    x_ready = [None] * 8
    if 0 in layers:
        xsb = xs[:].rearrange("p a b -> p (a b)").bitcast(BF16)
        KTh = xsb[:, 0:10752].rearrange("p (h t) -> p h t", t=HALO)
        Vh = xsb[:, 10752:21504].rearrange("p (t c) -> p t c", c=512)
        QT = xsb[:, 21504:24576].rearrange("p (g t) -> p g t", t=TOK)
        KTo = xsb[:, 24576:27648].rearrange("p (g t) -> p g t", t=TOK)
        Vo = xsb[:, 27648:29696].rearrange("p (g j c) -> p g j c", j=8, c=128)
        Vo3 = xsb[:, 29696:31744].rearrange("p (r c) -> p r c", c=128)
        qmT = xsb[:, 31744:32768]
        xt[0] = xsb[:, 21504:25600].bitcast(F32)
        xt[1] = xsb[:, 25600:29696].bitcast(F32)
        qnb = [ph[:, 7168:7680], ph[:, 7680:8192]]
        attnT = am[:].rearrange("p (k t) -> p k t", t=TOK)
        Co = misc_take(0, 4096, F32, 32)
        So = misc_take(4096, 4096, F32, 32)
        Ch = misc_take(8192, 2048, F32, 32)
        Sh = misc_take(10240, 2048, F32, 32)
        ChB = [Ch, ph[0:32, 9216:10240].bitcast(F32)]
        ShB = [Sh, ph[0:32, 10240:11264].bitcast(F32)]
        tab_last_use = [None, None]
        t1 = misc_take(12288, 2048, F32, 32)
        t2 = misc_take(14336, 2048, F32, 32)
        onesv = misc_take(16384, 5376, BF16).rearrange("p (t c) -> p t c", c=128)
        angb = ph[0:32, 0:1024].bitcast(F32)
        kb_i = ph[0:32, 1024:2048].bitcast(I32)
        kb_f = ph[0:32, 2048:3072].bitcast(F32)
        cmpb = ph[0:32, 3072:4096].bitcast(F32)
        posb = ph[0:32, 4096:5120].bitcast(I32)
        rden = ph[:, 5120:7168].bitcast(F32)
        invf_s = sb("invf_s", [32, 2], F32)
        valid_s = sb("valid_s", [128, 21], F32)
        memT0 = hT[:].rearrange("p a b -> p (a b)")[:, 0:4096].rearrange("p (k t) -> p k t", t=256)

        ev_if = SPD(lambda q: q.dma_start(out=invf_s[:], in_=invf), "gld2")
        ev_vl = SPD(lambda q: q.dma_start(out=valid_s[:], in_=valid), "gld3")

        mem_last = mem_kv(0, memT0)
        ev_ov = V(lambda v: v.tensor_copy(out=onesv, in_=valid_s[:].unsqueeze(2).to_broadcast([128, 21, 128])), [ev_vl])

        TWO_PI = float(2.0 * np.pi)
        C1 = 6.28125
        C2 = float(2.0 * np.pi - 6.28125)
        tab_chain = [None]

        def rot_tables_p1(pcol0, n, Cdst, Sdst, waits):
            ang = angb[:, 0:n]
            ki = kb_i[:, 0:n]
            kf = kb_f[:, 0:n]
            cm = cmpb[:, 0:n]
            evp = SPD(lambda q: q.dma_start(out=posb[:, 0:n], in_=posr[:, pcol0:pcol0 + n]), "pld", [tab_chain[0]])
            ev = V(lambda v: v.tensor_scalar(out=ang, in0=posb[:, 0:n], scalar1=invf_s[:, 0:1], scalar2=None, op0=ALU.mult),
                   [waits, evp, ev_if])
            tab_chain[0] = ev
            for which, dst in ((0, Sdst), (1, Cdst)):
                src = ang
                if which == 1:
                    ev = V(lambda v: v.tensor_scalar(out=cm, in0=ang, scalar1=float(np.pi / 2), scalar2=None, op0=ALU.add), [ev])
                    src = cm
                ev = V(lambda v, src=src: v.tensor_scalar(out=ki, in0=src, scalar1=float(1.0 / TWO_PI), scalar2=None, op0=ALU.mult), [ev])
                ev = V(lambda v: v.tensor_copy(out=kf, in_=ki), [ev])
                ev = V(lambda v, src=src, dst=dst: v.scalar_tensor_tensor(out=dst, in0=kf, scalar=-C1, in1=src, op0=ALU.mult, op1=ALU.add), [ev])
                ev = V(lambda v, dst=dst: v.scalar_tensor_tensor(out=dst, in0=kf, scalar=-C2, in1=dst, op0=ALU.mult, op1=ALU.add), [ev])
                ev = V(lambda v, dst=dst: v.tensor_scalar(out=dst, in0=dst, scalar1=3.14159, scalar2=-3.14159, op0=ALU.min, op1=ALU.max), [ev])
            return (ev, Cdst, Sdst)

        def rot_tables_p2(st):
            ev, Cdst, Sdst = st
            e1 = A(lambda a: a.activation(out=Sdst, in_=Sdst, func=AF.Sin), [ev])
            e2 = A(lambda a: a.activation(out=Cdst, in_=Cdst, func=AF.Sin), [e1])
            e3 = V(lambda v: v.tensor_scalar(out=Sdst, in0=Sdst, scalar1=invf_s[:, 1:2], scalar2=None, op0=ALU.mult), [e1])
            P.wait("dve", e2)
            return [e2, e3]

        def rot_tables(pcol0, n, Cdst, Sdst, waits):
            return rot_tables_p2(rot_tables_p1(pcol0, n, Cdst, Sdst, waits))

        rot_i = [0]
        rot_free = [None]

        def rotary_evac(b, ev_pe, dst, Ct, St, tab_ev, n, shp=None):
            f = shp if shp is not None else (lambda a: a)
            ev_c = A(lambda a: a.copy(out=dst, in_=bank(b)[:, 0:n]), [ev_pe])

            def part2():
                rb = 2 + (rot_i[0] % 2)
                rot_i[0] += 1
                P.wait("pe", bfree[rb])
                ev_r = T(lambda t: t.matmul(bank(rb)[0:32, 0:n], Rsw, dst, start=True, stop=True), waits=[ev_c], sig=True)
                e1 = V(lambda v: v.tensor_tensor(out=f(t1[:, 0:n]), in0=f(bank(rb)[0:32, 0:n]), in1=St, op=ALU.mult),
                       [ev_r, tab_ev, rot_free[0]])
                bfree[rb] = e1
                e2 = V(lambda v: v.tensor_tensor(out=f(t2[:, 0:n]), in0=f(bank(b)[0:32, 0:n]), in1=Ct, op=ALU.mult), [ev_c])
                e3 = V(lambda v: v.tensor_tensor(out=dst[0:32], in0=t1[:, 0:n], in1=t2[:, 0:n], op=ALU.add), [e1, e2])
                rot_free[0] = e3
                bfree[b] = e3
            return part2

        ev_tabo_h = [None]

        hTb = [am[:, i * 4096:(i + 1) * 4096].rearrange("p (k t) -> p k t", t=256) for i in range(2)]
        hb_free = [mem_last, mem_last]
        batch_evd = {}
        batch_tab = {}
        held = {}

        def halo_norm_A(bi, tt):
            tile0, ntile, gi = HALO_BATCHES[bi]
            hb = hTb[bi % 2]
            i, evl = load_xtile(xh[(tile0 + tt) * 128:(tile0 + tt + 1) * 128, :])
            st = norm_A(xt[i], evl, 0,
                        lambda h, tt=tt, hb=hb: hb[:, h * 8:(h + 1) * 8, tt * 128:(tt + 1) * 128],
                        dst_wait=hb_free[bi % 2])
            xt_free[i] = st[0]
            return (bi, st)

        def halo_norm_B(pst):
            bi, st = pst
            evd, evr = norm_B(st)
            batch_evd[bi] = evd

        def halo_norm(bi, tt):
            halo_norm_B(halo_norm_A(bi, tt))

        tab_st = {}

        def halo_tables_p1(bi):
            tile0, ntile, gi = HALO_BATCHES[bi]
            n = ntile * 128
            tab_st[bi] = rot_tables_p1(tile0 * 128, n, ChB[bi % 2][:, 0:n], ShB[bi % 2][:, 0:n], tab_last_use[bi % 2])

        def halo_tables_p2(bi):
            batch_tab[bi] = rot_tables_p2(tab_st[bi])

        def halo_tables(bi):
            halo_tables_p1(bi)
            halo_tables_p2(bi)

        def halo_piece(bi, pc):
            tile0, ntile, gi = HALO_BATCHES[bi]
            hb = hTb[bi % 2]
            n = ntile * 128
            evd = batch_evd[bi]
            kcol = 5120 + (2 - gi) * 1024
            if (gi, pc) not in held:
                held[(gi, pc)] = wpiece(w0, 0, 16, kcol + pc * 256, 256)
            s_, wv, evw = held[(gi, pc)]
            last_of_group = (bi + 1 >= len(HALO_BATCHES)) or (HALO_BATCHES[bi + 1][2] != gi)
            lastpe = None
            if pc < 2:
                ev_tab = batch_tab[bi]
                for c in range(2):
                    hd = pc * 2 + c
                    dst = KTh[:, hd, tile0 * 128: tile0 * 128 + n]
                    lastpe = proj_fm(wv, c * 128, lambda k, hb=hb, n=n: hb[:, k, 0:n], 16, n,
                                     lambda b, ev, dst=dst, n=n, ev_tab=ev_tab, bi=bi: rotary_evac(b, ev, dst, ChB[bi % 2][:, 0:n], ShB[bi % 2][:, 0:n], ev_tab, n),
                                     waits=[evw, evd])
            else:
                cc = (pc - 2) * 256
                for tt in range(ntile):
                    dst = Vh[:, tile0 + tt, cc:cc + 256]
                    lastpe = proj_tm(lambda k, hb=hb, tt=tt: hb[:, k, tt * 128:(tt + 1) * 128], wv, 0, 256, 16,
                                     lambda b, ev, dst=dst: A(lambda a: a.copy(out=dst, in_=bank(b)[:, 0:256]), [ev]),
                                     waits=[evw, evd])
            if pc == 1:
                flush_rot()
                tab_last_use[bi % 2] = rot_free[0]
            if last_of_group:
                R.release(s_, lastpe)
            if pc == 3:
                hb_free[bi % 2] = lastpe

        for tt in range(HALO_BATCHES[0][1]):
            halo_norm(0, tt)
        halo_tables(0)
        NB_H = len(HALO_BATCHES)
        ev_hT_h = [None]

        def own_A(t):
            i, evl = load_xtile(xo[t * 128:(t + 1) * 128, :])
            st = norm_A(xt[i], evl, 0, lambda h, t=t: hT[:, h * 8:(h + 1) * 8, t * 128:(t + 1) * 128], dst_wait=mem_last)
            xt_free[i] = st[0]
            return st

        own_next = [0]
        for bi in range(NB_H):
            nxt = HALO_BATCHES[bi + 1][1] if bi + 1 < NB_H else 0
            for pc in range(4):
                pst = None
                ost = None
                if pc < nxt:
                    pst = halo_norm_A(bi + 1, pc)
                elif bi >= NB_H - 4 and own_next[0] < 8:
                    ost = own_A(own_next[0])
                    own_next[0] += 1
                halo_piece(bi, pc)
                if pst is not None:
                    halo_norm_B(pst)
                if ost is not None:
                    ev_hT_h[0], _ = norm_B(ost)
                if pc == 1 and bi + 1 < NB_H:
                    halo_tables_p1(bi + 1)
                if pc == 3 and bi + 1 < NB_H:
                    halo_tables_p2(bi + 1)
                if pc == 3 and bi == 2:
                    rot_tables(HALO, 512, Co[:, 0:512], So[:, 0:512], None)
                if pc == 3 and bi == 4:
                    ev_tabo_h[0] = rot_tables(HALO + 512, 512, Co[:, 512:1024], So[:, 512:1024], None)
        ev_tabo = ev_tabo_h[0]
        while own_next[0] < 8:
            ev_hT_h[0], _ = norm_B(own_A(own_next[0]))
            own_next[0] += 1
        ev_hT = ev_hT_h[0]
        xt_last = [xt_free[0], xt_free[1]]

        def vtile(ap2, gi, j):
            if gi == 0:
                return ap2[:, j * 128:(j + 1) * 128]
            if gi == 1:
                r, bb = j // 2, j % 2
                return ap2.rearrange("p (i r) -> p r i", r=4)[:, r, bb * 128:(bb + 1) * 128]
            return ap2.rearrange("p (i r) -> p r i", r=16)[:, j, :]

        def gtile(ap2, gi, j):
            return vtile(ap2, gi, j)

        def nat_view(a, gi):
            if gi == 0:
                return a
            return a.rearrange("p (i r) -> p i r", r=(4 if gi == 1 else 16))

        def perm_view(row, gi, th):
            if gi == 0:
                return row[:, th * 512:(th + 1) * 512]
            if gi == 1:
                return row.rearrange("p (r i) -> p i r", r=4)[:, 128 * th:128 * (th + 1), :]
            return row.rearrange("p (r i) -> p i r", r=16)[:, 32 * th:32 * (th + 1), :]

        qn_i = [0]
        qn_free = [None, None]
        vt_free = [None]
        VTs1 = ph[:, 8192:9216]

        def rotary_evac_perm(b, ev_pe, dstrow, gi, th):
            qi = qn_i[0] % 2
            qn_i[0] += 1
            qn = qnb[qi]
            Ct = Co[:, th * 512:(th + 1) * 512]
            St = So[:, th * 512:(th + 1) * 512]
            ev_c = A(lambda a: a.copy(out=qn, in_=bank(b)), [ev_pe, qn_free[qi]])
            A(lambda a: a.copy(out=perm_view(dstrow[32:64], gi, th), in_=nat_view(bank(b)[32:64, :], gi)), [ev_pe])
            ev_c2 = A(lambda a: a.copy(out=perm_view(dstrow[64:128], gi, th), in_=nat_view(bank(b)[64:128, :], gi)), [ev_pe])
            def part2():
                rb = 2 + (rot_i[0] % 2)
                rot_i[0] += 1
                P.wait("pe", bfree[rb])
                ev_r = T(lambda t: t.matmul(bank(rb)[0:32, :], Rsw, qn, start=True, stop=True), waits=[ev_c], sig=True)
                qn_free[qi] = ev_r
                e1 = V(lambda v: v.tensor_tensor(out=t1, in0=bank(rb)[0:32, :], in1=St, op=ALU.mult),
                       [ev_r, ev_tabo, rot_free[0]])
                bfree[rb] = e1
                e2 = V(lambda v: v.tensor_tensor(out=t2, in0=bank(b)[0:32, :], in1=Ct, op=ALU.mult), [ev_c])
                e3 = V(lambda v: v.tensor_tensor(out=perm_view(dstrow[0:32], gi, th), in0=nat_view(t1, gi), in1=nat_view(t2, gi),
                                                 op=ALU.add), [e1, e2, ev_c2])
                rot_free[0] = e3
                bfree[b] = e3
            return part2

        accA = psall[:, 4:6, :].rearrange("p a b -> p (a b)")
        denA = psall[:, 6:8, :].rearrange("p a b -> p (a b)")
        att_done = None
        rdA_free = [None]
        rdm = misc_take(21760, 2048, F32)

        for s in range(4):
            base = s * 1280
            cur = [None, None, None]
            lastpe_piece = [None]

            def get_piece(pcI, cur=cur, base=base, lastpe_piece=lastpe_piece):
                if cur[0] != pcI:
                    if cur[0] is not None:
                        R.release(cur[1][0], lastpe_piece[0])
                    cur[0] = pcI
                    cur[1] = wpiece(w0, 0, 16, base + pcI * 256, 256)
                return cur[1]

            PB = (0, 1, 4, 5, 6, 7)
            for c in range(7):
                s_, wv, evw = get_piece(c // 2)
                cI = c % 2
                for th in range(2):
                    if c < 6:
                        gi = c % 3
                        dst = (QT if c < 3 else KTo)[:, gi, th * 512:(th + 1) * 512]
                        ev = proj_fm(wv, cI * 128, lambda k, th=th: hT[:, k, th * 512:(th + 1) * 512], 16, 512,
                                     lambda b, ev, dst=dst, th=th: rotary_evac(b, ev, dst, Co[:, th * 512:(th + 1) * 512],
                                                                               So[:, th * 512:(th + 1) * 512], ev_tabo, 512),
                                     waits=[evw, ev_hT, att_done], banks=PB)
                    else:
                        dst = qmT[:, th * 512:(th + 1) * 512]
                        ev = proj_fm(wv, cI * 128, lambda k, th=th: hT[:, k, th * 512:(th + 1) * 512], 16, 512,
                                     lambda b, ev, dst=dst: A(lambda a: a.copy(out=dst, in_=bank(b)), [ev]),
                                     waits=[evw, ev_hT, att_done], banks=PB)
                    lastpe_piece[0] = ev
            for gi in range(3):
                c = 7 + gi
                s_, wv, evw = get_piece(c // 2)
                cI = c % 2
                vrow = VTs1
                evv = []
                for th in range(2):
                    def evac_v(b, ev, vrow=vrow, th=th):
                        return A(lambda a: a.copy(out=vrow[:, th * 512:(th + 1) * 512], in_=bank(b)), [ev, vt_free[0]])
                    ev = proj_fm(wv, cI * 128, lambda k, th=th: hT[:, k, th * 512:(th + 1) * 512], 16, 512, evac_v,
                                 waits=[evw, ev_hT, att_done], banks=PB)
                    lastpe_piece[0] = ev
                    evv.append(bfree[PB[(pj[0] - 1) % len(PB)]])
                ntile = 8 if gi < 2 else 16
                tw = 128 if gi < 2 else 64
                for jb in range(ntile // 8):
                    b = 2 + (pj[0] % 2)
                    pj[0] += 1
                    P.wait("pe", [bfree[b], evv])
                    ev = None
                    for jj in range(8):
                        j = jb * 8 + jj
                        ev = T(lambda t, b=b, jj=jj, j=j, vrow=vrow, tw=tw, gi=gi: t.transpose(
                            bank_bf(b)[0:tw, jj * 128:(jj + 1) * 128], vtile(vrow, gi, j), ident), sig=(jj == 7))
                    if gi < 2:
                        dst = Vo[:, gi, :, :]
                    else:
                        dst = Vo3[0:64, jb * 8:(jb + 1) * 8, :]
                    bfree[b] = A(lambda a, b=b, dst=dst, tw=tw: a.copy(
                        out=dst, in_=bank_bf(b)[0:tw, :].rearrange("p (j c) -> p j c", c=128)), [ev])
                    vt_free[0] = ev
            flush_rot()
            R.release(cur[1][0], lastpe_piece[0])
            proj_done = [rot_free[0]] + [bfree[i] for i in range(8)]

            ev_z1 = V(lambda v: v.memset(accA, 0.0), [bfree[4], bfree[5]])
            ev_z2 = V(lambda v: v.memset(denA, 0.0), [bfree[6], bfree[7]])
            P.wait("pe", [ev_z1, ev_z2, proj_done, ev_ov])

            def g_attend(items, mask_ops, nk=128):
                def stage1():
                    i = pv_i[0] % 2
                    pv_i[0] += 1
                    sbk = 2 + i
                    P.wait("pe", bfree[sbk])
                    c = 0
                    offs = []
                    evs = None
                    for n_it, (k_ap, q_ap, nq, pvs) in enumerate(items):
                        evs = T(lambda t, c=c, nq=nq, k_ap=k_ap, q_ap=q_ap: t.matmul(
                            bank(sbk)[0:nk, c:c + nq], k_ap, q_ap, start=True, stop=True), sig=(n_it == len(items) - 1))
                        offs.append(c)
                        c += nq
                    tot = c
                    eve = A(lambda a: a.activation(out=ptS[i][0:nk, 0:tot], in_=bank(sbk)[0:nk, 0:tot],
                                                   func=AF.Exp, scale=128 ** -0.5), [evs, pt_free[i]])
                    bfree[sbk] = eve
                    evm = None
                    for (c0, ncl, in1_ap, inner) in mask_ops:
                        o = pmS[i][0:nk, c0:c0 + ncl].rearrange("p (a b) -> p a b", b=inner)
                        a_in = ptS[i][0:nk, c0:c0 + ncl].rearrange("p (a b) -> p a b", b=inner)
                        evm = V(lambda v, o=o, a_in=a_in, in1_ap=in1_ap: v.tensor_tensor(out=o, in0=a_in, in1=in1_ap, op=ALU.mult),
                                [eve, pm_free[i]])
                    pt_free[i] = evm

                    def stage2():
                        evp = None
                        first = True
                        for (k_ap, q_ap, nq, pvs), off in zip(items, offs):
                            for (co, ncl, v_ap, o_ap, acc_ap, den_ap) in pvs:
                                rhs = pmS[i][0:nk, off + co:off + co + ncl]
                                T(lambda t, v_ap=v_ap, rhs=rhs, acc_ap=acc_ap: t.matmul(
                                    acc_ap, v_ap, rhs, start=False, stop=False, skip_group_check=True),
                                  waits=[evm] if first else None)
                                first = False
                                evp = T(lambda t, o_ap=o_ap, rhs=rhs, den_ap=den_ap: t.matmul(
                                    den_ap, o_ap, rhs, start=False, stop=False, skip_group_check=True), sig=True)
                        pm_free[i] = evp
                        return evp
                    return stage2
                return stage1

            batches = []
            M_op = cmat[:, 128:384]
            M_po = cmat[:, 256:512]

            def g1_item(kt):
                if kt < 0:
                    k_ap, v_ap, o_ap = KTh[:, s, 2560:2688], Vh[:, 20, s * 128:(s + 1) * 128], onesv[:, 20, :]
                else:
                    k_ap, v_ap, o_ap = KTo[:, 0, kt * 128:(kt + 1) * 128], Vo[:, 0, kt, :], ones
                qlo, qhi = max(kt, 0), min(kt + 1, 7)
                nq = (qhi - qlo + 1) * 128
                pvs = [(qi * 128, 128, v_ap, o_ap, accA[:, qt * 128:(qt + 1) * 128], denA[:, qt * 128:(qt + 1) * 128])
                       for qi, qt in enumerate(range(qlo, qhi + 1))]
                return (k_ap, QT[:, 0, qlo * 128:qlo * 128 + nq], nq, pvs)

            batches.append(g_attend([g1_item(-1), g1_item(0)],
                                    [(0, 128, M_prev.unsqueeze(1), 128), (128, 256, M_op.unsqueeze(1), 256)]))
            for kt in (1, 3, 5):
                batches.append(g_attend([g1_item(kt), g1_item(kt + 1)],
                                        [(0, 512, M_op.unsqueeze(1).to_broadcast([128, 2, 256]), 256)]))
            batches.append(g_attend([g1_item(7)], [(0, 128, M_own.unsqueeze(1), 128)]))
            for r in range(4):
                items = []
                for kb in range(-1, 2):
                    if kb < 0:
                        k_ap, v_ap, o_ap = KTh[:, s, 2048 + r * 128:2048 + (r + 1) * 128], Vh[:, 16 + r, s * 128:(s + 1) * 128], onesv[:, 16 + r, :]
                    else:
                        k_ap, v_ap, o_ap = vtile(KTo[:, 1, :], 1, r * 2 + kb), Vo[:, 1, r * 2 + kb, :], ones
                    qlo, qhi = max(kb, 0), min(kb + 1, 1)
                    nq = (qhi - qlo + 1) * 128
                    pvs = [(qi * 128, 128, v_ap, o_ap, gtile(accA, 1, r * 2 + qb), gtile(denA, 1, r * 2 + qb))
                           for qi, qb in enumerate(range(qlo, qhi + 1))]
                    items.append((k_ap, QT[:, 1, :].rearrange("p (i r) -> p r i", r=4)[:, r, qlo * 128:qlo * 128 + nq], nq, pvs))
                batches.append(g_attend(items, [(0, 512, M_po.unsqueeze(1).to_broadcast([128, 2, 256]), 256)]))
            for rb in range(2):
                items = []
                for r in range(rb * 8, rb * 8 + 8):
                    accc = accA.rearrange("p (i r) -> p r i", r=16)[:, r, :]
                    denc = denA.rearrange("p (i r) -> p r i", r=16)[:, r, :]
                    items.append((KTh[:, s, r * 128:(r + 1) * 128], vtile(QT[:, 2, :], 2, r), 64,
                                  [(0, 32, Vh[:, r, s * 128:(s + 1) * 128], onesv[:, r, :], accc[:, 0:32], denc[:, 0:32]),
                                   (32, 32, Vh[:, r, s * 128:(s + 1) * 128], onesv[:, r, :], accc[:, 32:64], denc[:, 32:64])]))
                batches.append(g_attend(items, [(0, 512, M_prev[:, 0:64].unsqueeze(1).to_broadcast([128, 8, 64]), 64)]))
            for rb in range(2):
                items = []
                for r in range(rb * 8, rb * 8 + 8):
                    accc = accA.rearrange("p (i r) -> p r i", r=16)[:, r, :]
                    denc = denA.rearrange("p (i r) -> p r i", r=16)[:, r, :]
                    items.append((vtile(KTo[:, 2, :], 2, r), vtile(QT[:, 2, :], 2, r), 64,
                                  [(0, 32, Vo3[0:64, r, :], ones[0:64, :], accc[:, 0:32], denc[:, 0:32]),
                                   (32, 32, Vo3[0:64, r, :], ones[0:64, :], accc[:, 32:64], denc[:, 32:64])]))
                batches.append(g_attend(items, [(0, 512, M_own[0:64, 0:64].unsqueeze(1).to_broadcast([64, 8, 64]), 64)], nk=64))
            pend = None
            last = None
            for bt in batches:
                s2 = bt()
                if pend is not None:
                    last = pend()
                pend = s2
            last = pend()
            ev1 = recip_act(rden, denA, [last, rdA_free[0]])
            ev2 = V(lambda v, s=s: v.tensor_tensor(out=attnT[:, s, :], in0=accA, in1=rden, op=ALU.mult), [ev1, xt_last])
            rdA_free[0] = ev2
            for b in (4, 5):
                bfree[b] = ev2
            for b in (6, 7):
                bfree[b] = ev1
            att_done = [ev2, mem_attn_multi([(s, qmT[:, th * 512:(th + 1) * 512], attnT[:, 4 + s, th * 512:(th + 1) * 512], [proj_done, xt_last]) for th in range(2)], rdm, pairs=((0, 1), (4, 6)))]

        xl = []
        for t in range(8):
            xl.append(SPD(lambda q, t=t: q.dma_start(out=xs[:, t, :], in_=xo[t * 128:(t + 1) * 128, :]), f"xsl{t}", [att_done]))
        pendF = [None]
        evhF = [None]

        def cb_ffn0norm(t, ev):
            if pendF[0] is not None:
                evhF[0], _ = norm_B(pendF[0])
            pendF[0] = norm_A(xs[:, t, :], ev, 32,
                              lambda h, t=t: hT[:, h * 8:(h + 1) * 8, t * 128:(t + 1) * 128], dst_wait=att_done)
        if stop != "attn0":
            xev = out_proj_t(wo0, 8, lambda k, t: attnT[:, k, t * 128:(t + 1) * 128], list(range(8)), xl, att_done, cb_ffn0norm)
            evhF[0], _ = norm_B(pendF[0])
        else:
            xev = out_proj(wo0, 8, lambda k, t: attnT[:, k, t * 128:(t + 1) * 128], list(range(8)), xl, att_done)
        x_ready = [xev[t] for t in range(8)]
        hT_free[0] = att_done
        xt[0], xt[1] = xt_am[0], xt_am[1]
        xt_free[0] = xt_free[1] = x_ready
        evh_cb = [None]
        if stop != "attn0":
            pendB = [None]

            def cb_l1norm(t, ev, hfree):
                if pendB[0] is not None:
                    evh_cb[0], _ = norm_B(pendB[0])
                pendB[0] = norm_A(xs[:, t, :], ev, 16,
                                  lambda h, t=t: hT[:, h * 8:(h + 1) * 8, t * 128:(t + 1) * 128], dst_wait=hfree)
            x_ready = ffn(0, x_ready, cb_l1norm if (1 in layers) else None, evh_pre=evhF[0])
            if pendB[0] is not None:
                evh_cb[0], _ = norm_B(pendB[0])
            xt_free[0] = xt_free[1] = x_ready
    else:
        evh_cb = [None]
        for t in range(8):
            x_ready[t] = SPD(lambda q, t=t: q.dma_start(out=xs[:, t, :], in_=xo[t * 128:(t + 1) * 128, :]), f"xsl{t}")

    gfin = ph[:, 0:4096].bitcast(F32)
    hTf32 = hT[:].rearrange("p a b -> p (a b)").bitcast(F32)
    ystage = [hTf32[:, i * 2048:(i + 1) * 2048] for i in range(4)]
    yst_free = [None] * 4
    ev_gf_h = [None]
    final_done = [False]
    st_evs = []

    def final_tile(t, ev, hfree):
        ss, sq, rs = stat_col(), stat_col(), stat_col()
        e = A(lambda a: a.activation(out=xnb[:], in_=xs[:, t, :], func=AF.Square, accum_out=ss), [ev, xn_free[0]])
        e = A(lambda a: a.activation(out=sq, in_=ss, func=AF.Sqrt, scale=1.0 / DM, bias=eps_ap), [e])
        e = V(lambda v: v.reciprocal(out=rs, in_=sq), [e])
        i = t % 4
        e = V(lambda v: v.scalar_tensor_tensor(out=ystage[i], in0=xs[:, t, :], scalar=rs, in1=gfin,
                                               op0=ALU.mult, op1=ALU.mult), [e, ev_gf_h[0], yst_free[i], hfree])
        evs = SPD(lambda q: q.dma_start(out=y[t * 128:(t + 1) * 128, :], in_=ystage[i]), f"yst{i}", [e])
        yst_free[i] = evs
        st_evs.append(evs)

    if 1 in layers and stop != "attn0":
        lng = misc_take(0, 6144, F32)
        lnb = misc_take(6144, 6144, F32)
        bsp = misc_take(12288, 6144, F32)
        wsT = misc_take(18432, 3072, BF16).rearrange("p (g t) -> p g t", t=128)
        rden1 = misc_take(21504, 2048, F32)
        vtm = ph[:, 0:6144].rearrange("p (t c) -> p t c", c=1536)
        qm1 = ph[:, 6144:8192].rearrange("p (m t) -> p m t", t=512)
        vgf = ph[:, 8192:11264].bitcast(F32)
        memT1 = ph[:, 0:4096].rearrange("p (k t) -> p k t", t=256)
        e1 = SPD(lambda q: q.dma_start(out=lng, in_=lng_d), "gld4", x_ready)
        e2 = SPD(lambda q: q.dma_start(out=lnb, in_=lnb_d), "gld5")
        e3 = SPD(lambda q: q.dma_start(out=bsp, in_=bsp_d), "gld6")
        e4 = SPD(lambda q: q.dma_start(out=vgf, in_=wst_d), "gld7")
        ev_ws = V(lambda v: v.tensor_tensor(out=wsT, in0=vgf.rearrange("p (g t) -> p g t", t=128),
                                            in1=M_own.unsqueeze(1).to_broadcast([128, 12, 128]), op=ALU.mult), [e4])
        vgb = ph[:, 8192:11264]
        TA = vgb[:, 0:1536]
        TB = vgb[:, 1536:3072]
        bsp_hl = misc[0:64, 6144:7680]
        V(lambda v: v.memset(TA[0:64], 0.0), [ev_ws])
        V(lambda v: v.tensor_copy(out=TA[0:1], in_=bsp[0:1]), [e3])
        eb1 = V(lambda v: v.tensor_copy(out=TB[32:33], in_=bsp[32:33]))
        eb2 = V(lambda v: v.tensor_tensor(out=TA[32:33], in0=bsp[32:33], in1=TB[32:33], op=ALU.subtract), [eb1])
        ev_hl = V(lambda v: v.tensor_copy(out=bsp_hl, in_=TA[0:64]), [eb2])
        ev_tabs = [e1, e2, ev_ws, ev_hl]
        evh = evh_cb[0]
        if evh is None:
            for t in range(8):
                evh, _ = norm_tile(xs[:, t, :], x_ready[t], 16,
                                   lambda h, t=t: hT[:, h * 8:(h + 1) * 8, t * 128:(t + 1) * 128], dst_wait=hT_free[0])
        gTm = am[:].rearrange("p (k t) -> p k t", t=512)
        half_done = [xt_free[0], xt_free[1]]
        pre_evs = []
        for hf in range(2):
            tk0 = hf * 512
            last = None
            for pc in range(2):
                s_, wv, evw = wpiece(w1, 0, 16, 3072 + pc * 256, 256)
                for c in range(2):
                    m = pc * 2 + c
                    last = proj_fm(wv, c * 128, lambda k, tk0=tk0: hT[:, k, tk0:tk0 + 512], 16, 512,
                                   lambda b, ev, m=m: A(lambda a: a.copy(out=qm1[:, m, :], in_=bank(b)), [ev]),
                                   waits=[evw, evh, half_done])
                R.release(s_, last)
            qdone = [bfree[0], bfree[1]]
            if hf == 0:
                mem_kv(1, memT1)
            for cg in range(6):
                pstF = None
                if hf == 1 and cg < 4 and stop != "attn1":
                    pstF = norm_A(xs[:, cg, :], x_ready[cg], 48,
                                  lambda h, cg=cg: hT[:, h * 8:(h + 1) * 8, cg * 128:(cg + 1) * 128], dst_wait=half_done)
                s_, wv, evw = wpiece(w1, 0, 16, 1536 + cg * 256, 256)
                last = None
                for tt in range(4):
                    def evac(b, ev, tt=tt, cg=cg):
                        return A(lambda a: a.activation(out=vtm[:, tt, cg * 256:(cg + 1) * 256], in_=bank(b)[:, 0:256],
                                                        func=AF.Gelu), [ev, half_done])
                    last = proj_tm(lambda k, tt=tt, tk0=tk0: hT[:, k, tk0 + tt * 128: tk0 + (tt + 1) * 128], wv, 0, 256, 16, evac,
                                   waits=[evw, evh], banks=(0, 1, 4, 5))
                R.release(s_, last)
                if pstF is not None:
                    pre_evs.append(norm_B(pstF)[0])
            vdone = [bfree[0], bfree[1], bfree[4], bfree[5]]
            ev_ln_h = [None]

            def ln_tile(tt, vdone=vdone, ev_ln_h=ev_ln_h):
                sm, sq2, mu, var, rs = stat_col(), stat_col(), stat_col(), stat_col(), stat_col()
                ea = A(lambda a: a.activation(out=xnb[:, 0:1536], in_=vtm[:, tt, :], func=AF.Copy, accum_out=sm), [vdone, ev_ws, xn_free[0]])
                eb = A(lambda a: a.activation(out=xnb[:, 0:1536], in_=vtm[:, tt, :], func=AF.Square, accum_out=sq2), [ea])
                e = V(lambda v: v.tensor_scalar(out=mu, in0=sm, scalar1=1.0 / 1536, scalar2=None, op0=ALU.mult), [ea])
                e = V(lambda v: v.tensor_tensor(out=var, in0=mu, in1=mu, op=ALU.mult), [e])
                e = V(lambda v: v.scalar_tensor_tensor(out=var, in0=sq2, scalar=1.0 / 1536, in1=var, op0=ALU.mult, op1=ALU.subtract), [e, eb])
                e = A(lambda a: a.activation(out=var, in_=var, func=AF.Sqrt, bias=lneps_ap), [e])
                e = V(lambda v: v.reciprocal(out=rs, in_=var), [e])
                e = V(lambda v: v.tensor_scalar(out=vgf, in0=vtm[:, tt, :], scalar1=mu, scalar2=rs, op0=ALU.subtract, op1=ALU.mult), [e, eb, ev_ln_h[0]])
                e = V(lambda v: v.tensor_tensor(out=vgf, in0=vgf, in1=lng, op=ALU.mult), [e, ev_tabs])
                ev_ln_h[0] = V(lambda v: v.tensor_tensor(out=vtm[:, tt, :], in0=vgf, in1=lnb, op=ALU.add), [e])

            for pc in range(6):
                s_, wv, evw = wpiece(w1, 0, 16, pc * 256, 256)
                last = None
                for c in range(2):
                    g = pc * 2 + c
                    last = proj_fm(wv, c * 128, lambda k, tk0=tk0: hT[:, k, tk0:tk0 + 512], 16, 512,
                                   lambda b, ev, g=g: A(lambda a: a.activation(out=gTm[:, g, :], in_=bank(b), func=AF.Gelu), [ev, half_done]),
                                   waits=[evw, evh])
                R.release(s_, last)
                if pc < 4:
                    ln_tile(pc)
            ev_ln = ev_ln_h[0]
            udone = [bfree[0], bfree[1]]
            last_ma = mem_attn_multi([(m, qm1[:, m, :], gTm[:, 12 + m, :], [qdone, half_done]) for m in range(4)], rden1)
            ev_gate = None
            for g in range(12):
                b = 2 + (g % 2)
                P.wait("pe", [bfree[b], ev_ln, ev_ws, ev_hl])
                ev = None
                for tt in range(4):
                    ob = bank(b)[:, tt * 128:(tt + 1) * 128]
                    T(lambda t, ob=ob, tt=tt, g=g: t.matmul(ob, vtm[:, tt, g * 128:(g + 1) * 128], wsT[:, g, :],
                                                          start=True, stop=False))
                    ev = T(lambda t, ob=ob, g=g: t.matmul(ob, ones[0:64, :], bsp_hl[:, g * 128:(g + 1) * 128],
                                                         start=False, stop=True), sig=(tt == 3))
                ev_gate = V(lambda v, b=b, g=g: v.tensor_tensor(out=gTm[:, g, :], in0=bank(b), in1=gTm[:, g, :], op=ALU.mult),
                            [ev, udone])
                bfree[b] = ev_gate
            cat_ev = [ev_gate, last_ma]
            toks = [hf * 4 + i for i in range(4)]
            r = out_proj(wo1, 16, lambda k, t: gTm[:, k, (t % 4) * 128:(t % 4 + 1) * 128], toks, x_ready, cat_ev)
            for t in toks:
                x_ready[t] = r[t]
            half_done = [r[t] for t in toks]
            P.wait("pe", half_done)
        hT_free[0] = half_done
        xt_free[0] = xt_free[1] = x_ready
        if stop != "attn1":
            if final:
                ev_gf_h[0] = SPD(lambda q: q.dma_start(out=gfin, in_=gfin_d), "gld8", x_ready)
                x_ready = ffn(1, x_ready, final_tile, pre_tiles=(0, 1, 2, 3) if pre_evs else (), pre_evh=pre_evs)
                final_done[0] = True
            else:
                x_ready = ffn(1, x_ready, pre_tiles=(0, 1, 2, 3) if pre_evs else (), pre_evh=pre_evs)
            xt_free[0] = xt_free[1] = x_ready

    if final and not final_done[0]:
        ev_gf_h[0] = SPD(lambda q: q.dma_start(out=gfin, in_=gfin_d), "gld8", x_ready)
        for t in range(8):
            final_tile(t, x_ready, hT_free[0])
    elif not final:
        for t in range(8):
            evs = SPD(lambda q, t=t: q.dma_start(out=y[t * 128:(t + 1) * 128, :], in_=xs[:, t, :]), "yst0", [x_ready[t]])
            st_evs.append(evs)
    P.wait("sp", st_evs)
    P.build()
    es.close()
    return nc


_NC_CACHE = {}


def _get_nc(layers, final, stop=None):
    key = (tuple(layers), final, stop)
    if key not in _NC_CACHE:
        _NC_CACHE[key] = build(layers, final, stop)
    return _NC_CACHE[key]


def _halo_idx(T0):
    g3 = (T0 - 2048 + 16 * np.arange(128)[None, :] + np.arange(16)[:, None]).reshape(-1)
    g2 = (T0 - 512 + 4 * np.arange(128)[None, :] + np.arange(4)[:, None]).reshape(-1)
    g1 = T0 - 128 + np.arange(128)
    return np.concatenate([g3, g2, g1]).astype(np.int64)


def _consts():
    k = np.arange(128)[:, None]
    q = np.arange(128)[None, :]
    ident = (k == q)
    m_own = (k <= q)
    m_prev = (k >= q)
    m_b3 = ((k // 64) == (q // 64)) & ((k % 64) <= (q % 64))
    ones = np.ones((128, 128), bool)
    m32 = np.arange(32)[None, :]
    rsw = (k < 32) & (k == ((m32 + 16) % 32))
    cmat = np.concatenate([ident, m_own, m_prev, m_own, ones, rsw], axis=1).astype(np.float32)
    half = 16
    inv_freq = (np.float32(500000.0) ** (-(np.arange(half, dtype=np.float32)) / np.float32(half))).astype(np.float32)
    invf = np.zeros((32, 2), np.float32)
    invf[:, 0] = inv_freq[np.arange(32) % 16]
    invf[:, 1] = np.where(np.arange(32) < 16, -1.0, 1.0)
    return np.ascontiguousarray(cmat), invf


def _prep(inp, layers):
    f = lambda a: np.ascontiguousarray(np.asarray(a, dtype=np.float32))
    cmat, invf = _consts()
    gl = [inp["mix_norm"][0], inp["mix_norm"][1], inp["ffn_norm"][0], inp["ffn_norm"][1],
          inp["mem_norm"][0], inp["mem_norm"][1]]
    gT = np.concatenate([np.asarray(g, np.float32).reshape(16, 128).T for g in gl], axis=1)
    common = {
        "mem": f(inp["mem"][0]), "cmat": cmat, "gT": f(gT),
        "gfin": f(np.broadcast_to(np.asarray(inp["final_norm"], np.float32)[None, :], (128, DM))),
        "wkv": f(inp["w_mem_kv"]), "wg": f(inp["w_gate"]), "wu": f(inp["w_up"]), "wd": f(inp["w_down"]),
    }
    if 0 in layers:
        w = np.asarray(inp["attn_w_in"][0], np.float32)
        qc = lambda h: w[:, h * 128:(h + 1) * 128]
        kc = lambda h: w[:, 1536 + h * 128:1536 + (h + 1) * 128]
        vc = lambda h: w[:, 3072 + h * 128:3072 + (h + 1) * 128]
        mc = lambda m: w[:, 4608 + m * 128:4608 + (m + 1) * 128]
        cols = []
        for s in range(4):
            cols += [qc(s), qc(4 + s), qc(8 + s), kc(s), kc(4 + s), kc(8 + s), mc(s), vc(s), vc(4 + s), vc(8 + s)]
        for gi in (2, 1, 0):
            cols += [w[:, 1536 + gi * 512:1536 + (gi + 1) * 512], w[:, 3072 + gi * 512:3072 + (gi + 1) * 512]]
        common["w0"] = np.ascontiguousarray(np.concatenate(cols, axis=1))
        common["wo0"] = f(inp["attn_w_out"][0])
        common["invf"] = invf
    if 1 in layers:
        common["w1"] = f(inp["sgu_w_in"][0])
        common["wo1"] = f(inp["sgu_w_out"][0])
        common["lng"] = f(np.broadcast_to(np.asarray(inp["sgu_ln_g"][0], np.float32)[None, :], (128, 1536)))
        common["lnb"] = f(np.broadcast_to(np.asarray(inp["sgu_ln_b"][0], np.float32)[None, :], (128, 1536)))
        common["bsp"] = f(np.broadcast_to(np.asarray(inp["sgu_b_spatial"][0], np.float32).reshape(1, 1536), (128, 1536)))
        common["wst"] = f(np.asarray(inp["sgu_w_spatial"][0], np.float32).transpose(2, 0, 1).reshape(128, 1536))
    return common


def _run(inp, x2, layers, final, stop=None, ncores=NCORES):
    nc = _get_nc(layers, final, stop)
    common = _prep(inp, layers)
    pos = np.asarray(inp["positions"][0], np.int32)
    in_maps = []
    for c in range(ncores):
        T0 = c * TOK
        m = dict(common)
        m["xo"] = np.ascontiguousarray(x2[T0:T0 + TOK])
        if 0 in layers:
            idx = _halo_idx(T0)
            ok = idx >= 0
            ic = np.clip(idx, 0, None)
            xh = x2[ic].copy()
            xh[~ok] = 0.0
            m["xh"] = xh
            pp = np.concatenate([pos[ic], pos[T0:T0 + TOK]]).astype(np.int32)
            m["posr"] = np.ascontiguousarray(np.broadcast_to(pp[None, :], (32, HALO + TOK)))
            m["valid"] = np.ascontiguousarray(ok.astype(np.float32).reshape(21, 128).T)
        in_maps.append(m)
    res = run_bass_kernel_spmd(nc, in_maps, core_ids=list(range(ncores)))
    return np.concatenate([r["y"] for r in res.results], axis=0)


def kernel(**inp):
    x2 = np.ascontiguousarray(np.asarray(inp["x"], np.float32)[0])
    out = _run(inp, x2, (0, 1), True)
    return out.reshape(1, NCORES * TOK, DM).astype(np.float32)
```

```python
import numpy as np
from contextlib import ExitStack
import concourse.bass as bass
import concourse.mybir as mybir
from concourse.bass_utils import run_bass_kernel_spmd

F32 = mybir.dt.float32
BF16 = mybir.dt.bfloat16
I32 = mybir.dt.int32
AF = mybir.ActivationFunctionType
ALU = mybir.AluOpType

NCORES = 8
TOK = 1024
DM = 2048
FF = 5632
NS = 4
HALO = 2688
ENGS = ("pe", "act", "dve", "pool", "sp")
HALO_BATCHES = [(2 * i, 2, 2) for i in range(8)] + [(16, 2, 1), (18, 2, 1), (20, 1, 0)]
HALO_COL0 = {2: 0, 1: 2048, 0: 2560}


class Prog:
    def __init__(self, nc):
        self.nc = nc
        self.q = {e: [] for e in ENGS}
        self.semcnt = {}
        self.waited = {e: {} for e in ENGS}

    def op(self, eng, fn):
        ent = {"fn": fn, "inc": None}
        self.q[eng].append(ent)
        return ent

    def inc(self, ent, sem, amt=1):
        assert ent["inc"] is None
        self.semcnt[sem] = self.semcnt.get(sem, 0) + amt
        ent["inc"] = (sem, amt)
        return (sem, self.semcnt[sem])

    def wait(self, eng, ev):
        if ev is None:
            return
        if isinstance(ev, list):
            for e in ev:
                self.wait(eng, e)
            return
        sem, val = ev
        w = self.waited[eng]
        if w.get(sem, 0) >= val:
            return
        w[sem] = val
        self.q[eng].append({"wait": (sem, val)})

    def build(self):
        nc = self.nc
        with ExitStack() as es:
            sems = {}
            for name in self.semcnt:
                sems[name] = es.enter_context(nc.semaphore(name))
            block = es.enter_context(nc.Block())

            def replay(engname):
                def body(eng):
                    for ent in self.q[engname]:
                        if "wait" in ent:
                            s, v = ent["wait"]
                            eng.wait_ge(sems[s], v)
                        else:
                            ins = ent["fn"](eng)
                            if ent["inc"] is not None:
                                s, a = ent["inc"]
                                ins.then_inc(sems[s], a)
                return body

            for name, meth in (("sp", block.sync), ("pe", block.tensor), ("act", block.scalar),
                               ("dve", block.vector), ("pool", block.gpsimd)):
                if self.q[name]:
                    meth(replay(name))


def build(layers=(0, 1), final=True, stop=None):
    nc = bass.Bass("TRN2", target_bir_lowering=False)
    P = Prog(nc)

    def din(name, shape, dt=F32):
        return nc.dram_tensor(name, list(shape), dt, kind="ExternalInput").ap()

    xo = din("xo", [TOK, DM])
    if 0 in layers:
        xh = din("xh", [HALO, DM])
        posr = din("posr", [32, HALO + TOK], I32)
        valid = din("valid", [128, 21])
        w0 = din("w0", [DM, 8192])
        wo0 = din("wo0", [1024, DM])
        invf = din("invf", [32, 2])
    if 1 in layers:
        w1 = din("w1", [DM, 3584])
        wo1 = din("wo1", [DM, DM])
        lng_d = din("lng", [128, 1536])
        lnb_d = din("lnb", [128, 1536])
        bsp_d = din("bsp", [128, 1536])
        wst_d = din("wst", [128, 1536])
    memd = din("mem", [256, DM])
    cmat_d = din("cmat", [128, 672])
    gT_d = din("gT", [128, 96])
    gfin_d = din("gfin", [128, DM])
    wkv = din("wkv", [2, DM, 1024])
    wg = din("wg", [2, DM, FF])
    wu = din("wu", [2, DM, FF])
    wd = din("wd", [2, FF, DM])
    y = nc.dram_tensor("y", [TOK, DM], F32, kind="ExternalOutput").ap()

    es = ExitStack()
    sb = lambda name, shape, dt: es.enter_context(nc.sbuf_tensor(name, list(shape), dt))
    xs = sb("xs", [128, 8, DM], F32)
    ring = [sb(f"ring{i}", [128, 4096], BF16) for i in range(NS)]
    hT = sb("hT", [128, 16, TOK], BF16)
    am = sb("am", [128, 8192], BF16)
    mixT = am
    actT = am[:].rearrange("p (a b) -> p a b", b=TOK)
    xt_am = [am[:, i * 4096:(i + 1) * 4096].bitcast(F32) for i in range(2)]
    xt = list(xt_am)
    xn = [sb("xn0", [128, DM], BF16)] * 2
    ph = sb("ph", [128, 11264], BF16)
    cmat = sb("cmat_s", [128, 672], BF16)
    gT = sb("gT_s", [128, 96], F32)
    stat = sb("stat", [128, 64], F32)
    misc = sb("misc", [128, 12 * 1024], BF16)
    ptS = [sb(f"pt{i}", [128, 512], BF16) for i in range(2)]
    pmS = [sb(f"pm{i}", [128, 512], BF16) for i in range(2)]
    memKT = sb("memKT", [128, 4, 256], BF16)
    memV = sb("memV", [128, 2, 512], BF16)
    sgt = ptS
    psall = es.enter_context(nc.psum_tensor("psall", [128, 8, 512], F32))

    ident = cmat[:, 0:128]
    M_own = cmat[:, 128:256]
    M_prev = cmat[:, 256:384]
    M_B3 = cmat[:, 384:512]
    ones = cmat[:, 512:640]
    Rsw = cmat[:, 640:672]

    def bank(b):
        return psall[:, b, :]

    def bank_bf(b):
        return psall[:, b, :].bitcast(BF16)

    bfree = [None] * 8

    def A(fn, waits=None):
        P.wait("act", waits)
        return P.inc(P.op("act", fn), "sA")

    def V(fn, waits=None):
        P.wait("dve", waits)
        return P.inc(P.op("dve", fn), "sV")

    def T(fn, waits=None, sig=False):
        P.wait("pe", waits)
        e = P.op("pe", fn)
        return P.inc(e, "sT") if sig else None

    def SPD(fn, sem, waits=None):
        P.wait("sp", waits)
        return P.inc(P.op("sp", fn), sem, 16)

    def misc_take(off_bytes, nbytes, dt, parts=128):
        a = misc[0:parts, off_bytes // 2:(off_bytes + nbytes) // 2]
        return a if dt == BF16 else a.bitcast(dt)

    class Ring:
        def __init__(self):
            self.free = [None] * NS
            self.n = 0

        def load(self, src3, nk, ncol):
            s = self.n % NS
            self.n += 1
            P.wait("pool", self.free[s])
            view = ring[s][:, 0:nk * ncol].rearrange("p (k c) -> p k c", c=ncol)
            e = P.op("pool", lambda g, view=view, src3=src3: g.dma_start(out=view, in_=src3))
            ev = P.inc(e, f"wld{s}", 16)
            return s, view, ev

        def release(self, s, ev):
            self.free[s] = ev

    R = Ring()

    def wpiece(w2d, row0, nk, col0, ncol):
        src = w2d[row0:row0 + nk * 128, col0:col0 + ncol].rearrange("(k p) c -> p k c", p=128)
        return R.load(src, nk, ncol)

    e = P.op("pool", lambda g: g.dma_start(out=cmat[:], in_=cmat_d))
    ev_c = P.inc(e, "cld", 16)
    ev_g = SPD(lambda q: q.dma_start(out=gT[:], in_=gT_d), "gld1")
    for en in ("pe", "dve", "act"):
        P.wait(en, ev_c)
    P.wait("dve", ev_g)

    eps_t = sb("eps_t", [128, 2], F32)
    eps_ap = eps_t[:, 0:1]
    lneps_ap = eps_t[:, 1:2]
    V(lambda v: v.memset(eps_t[:, 0:1], 1e-6))
    ev_eps = V(lambda v: v.memset(eps_t[:, 1:2], 1e-5))
    P.wait("act", ev_eps)

    stat_i = [0]

    def stat_col():
        i = stat_i[0] % 64
        stat_i[0] += 1
        return stat[:, i:i + 1]

    tp_i = [0]
    xn_free = [None]
    pend_rot = [None]

    def flush_rot():
        if pend_rot[0] is not None:
            f = pend_rot[0]
            pend_rot[0] = None
            f()
    xnb = xn[0]

    def norm_A(src, src_ev, gcol0, dst_of, dst_wait=None):
        ss = stat_col()
        sq = stat_col()
        rs = stat_col()
        ev = A(lambda a: a.activation(out=xnb[:], in_=src, func=AF.Square, accum_out=ss), [src_ev, xn_free[0]])
        ev = A(lambda a: a.activation(out=sq, in_=ss, func=AF.Sqrt, scale=1.0 / DM, bias=eps_ap), [ev])
        ev = V(lambda v: v.reciprocal(out=rs, in_=sq), [ev])
        ev_xn = A(lambda a: a.activation(out=xnb[:], in_=src, func=AF.Copy, scale=rs), [ev])
        return (ev_xn, gcol0, dst_of, dst_wait)

    def norm_B(st):
        ev_xn, gcol0, dst_of, dst_wait = st
        evd = None
        last_pe = None
        for h in range(2):
            b = 6 + (tp_i[0] % 2)
            tp_i[0] += 1
            P.wait("pe", [bfree[b], ev_xn])
            for j in range(8):
                kc = h * 8 + j
                last_pe = T(lambda t, b=b, j=j, kc=kc: t.transpose(
                    bank_bf(b)[:, j * 128:(j + 1) * 128], xnb[:, kc * 128:(kc + 1) * 128], ident), sig=(j == 7))
            flush_rot()
            gb = gT[:, gcol0 + h * 8: gcol0 + h * 8 + 8].unsqueeze(2).to_broadcast([128, 8, 128])
            dst = dst_of(h)
            evd = V(lambda v, b=b, dst=dst, gb=gb: v.tensor_tensor(
                out=dst, in0=bank_bf(b).rearrange("p (a c) -> p a c", c=128), in1=gb, op=ALU.mult),
                [last_pe, dst_wait])
            bfree[b] = evd
        xn_free[0] = last_pe
        return evd, ev_xn

    def norm_tile(src, src_ev, gcol0, dst_of, dst_wait=None):
        return norm_B(norm_A(src, src_ev, gcol0, dst_of, dst_wait))

    xt_free = [None, None]
    xt_n = [0]

    def load_xtile(src_rows):
        i = xt_n[0] % 2
        xt_n[0] += 1
        dst = xt[i]
        ev = SPD(lambda q, dst=dst, src_rows=src_rows: q.dma_start(out=dst, in_=src_rows), f"xld{i}", [xt_free[i]])
        return i, ev

    pj = [0]

    def proj_fm(wview, c0, rhs_of, nk, n, evac, waits=None, banks=(0, 1), oshape=None):
        b = banks[pj[0] % len(banks)]
        pj[0] += 1
        P.wait("pe", [bfree[b], waits])
        out = bank(b)[:, 0:n]
        if oshape is not None:
            out = oshape(out)
        ev = None
        for k in range(nk):
            ev = T(lambda t, k=k: t.matmul(out, wview[:, k, c0:c0 + 128], rhs_of(k),
                                            start=(k == 0), stop=(k == nk - 1)), sig=(k == nk - 1))
        flush_rot()
        r = evac(b, ev)
        if callable(r):
            pend_rot[0] = r
        else:
            bfree[b] = r
        return ev

    def proj_tm(lhs_of, wview, c0, n, nk, evac, waits=None, banks=(4, 5)):
        b = banks[pj[0] % len(banks)]
        pj[0] += 1
        P.wait("pe", [bfree[b], waits])
        ev = None
        for k in range(nk):
            ev = T(lambda t, b=b, k=k: t.matmul(bank(b)[:, 0:n], lhs_of(k), wview[:, k, c0:c0 + n],
                                                 start=(k == 0), stop=(k == nk - 1)), sig=(k == nk - 1))
        flush_rot()
        bfree[b] = evac(b, ev)
        return ev

    def mem_kv(layer, memT):
        evd = None
        for mt in range(2):
            i, evl = load_xtile(memd[mt * 128:(mt + 1) * 128, :])
            evd, evr = norm_tile(xt[i], evl, 64 + 16 * layer,
                                 lambda h, mt=mt: memT[:, h * 8:(h + 1) * 8, mt * 128:(mt + 1) * 128])
            xt_free[i] = evr
        last = None
        for pc in range(4):
            s, wv, evw = wpiece(wkv[layer], 0, 16, pc * 256, 256)
            if pc < 2:
                for c in range(2):
                    m = pc * 2 + c
                    last = proj_fm(wv, c * 128, lambda k: memT[:, k, :], 16, 256,
                                   lambda b, ev, m=m: A(lambda a: a.copy(out=memKT[:, m, :], in_=bank(b)[:, 0:256]), [ev]),
                                   waits=[evw, evd])
            else:
                for mt in range(2):
                    cc = (pc - 2) * 256
                    last = proj_tm(lambda k, mt=mt: memT[:, k, mt * 128:(mt + 1) * 128], wv, 0, 256, 16,
                                   lambda b, ev, mt=mt, cc=cc: A(lambda a: a.copy(out=memV[:, mt, cc:cc + 256], in_=bank(b)[:, 0:256]), [ev]),
                                   waits=[evw, evd])
            R.release(s, last)
        return last

    pt_free = [None, None]
    pm_free = [None, None]

    def recip_act(dst, src, waits):
        e = A(lambda a: a.activation(out=dst, in_=src, func=AF.Ln), waits)
        return A(lambda a: a.activation(out=dst, in_=dst, func=AF.Exp, scale=-1.0), [e])

    pv_i = [0]
    rd_free = [None]

    def mem_attn_multi(groups, rden, pairs=((4, 6), (5, 7))):
        units = []
        for gi_, (m, q_ap, dst, q_ev) in enumerate(groups):
            accb, denb = pairs[gi_ % len(pairs)]
            for mt in range(2):
                units.append((m, q_ap, dst, q_ev, accb, denb, mt))

        def stage1(u):
            m, q_ap, dst, q_ev, accb, denb, mt = u
            i = pv_i[0] % 2
            pv_i[0] += 1
            sbk = 2 + i
            P.wait("pe", [bfree[sbk], q_ev])
            evs = T(lambda t: t.matmul(bank(sbk), memKT[:, m, mt * 128:(mt + 1) * 128], q_ap, start=True, stop=True), sig=True)
            eve = A(lambda a: a.activation(out=ptS[i][:], in_=bank(sbk), func=AF.Exp, scale=128 ** -0.5), [evs, pt_free[i]])
            bfree[sbk] = eve

            def stage2():
                if mt == 0:
                    P.wait("pe", [bfree[accb], bfree[denb]])
                T(lambda t: t.matmul(bank(accb), memV[:, mt, m * 128:(m + 1) * 128], ptS[i][:],
                                     start=(mt == 0), stop=(mt == 1)), waits=[eve])
                evp = T(lambda t: t.matmul(bank(denb), ones, ptS[i][:], start=(mt == 0), stop=(mt == 1)), sig=True)
                pt_free[i] = evp
                if mt == 1:
                    ev1 = recip_act(rden[:, 0:512], bank(denb), [evp, rd_free[0]])
                    ev2 = V(lambda v: v.tensor_tensor(out=dst, in0=bank(accb), in1=rden[:, 0:512], op=ALU.mult), [ev1])
                    rd_free[0] = ev2
                    bfree[accb] = ev2
                    bfree[denb] = ev1
                    return ev2
                return evp
            return stage2

        pend = None
        last = None
        for u in units:
            s2 = stage1(u)
            if pend is not None:
                last = pend()
            pend = s2
        last = pend()
        return last

    hT_free = [None]

    def ffn(layer, x_ready, tile_cb=None, evh_pre=None, pre_tiles=(), pre_evh=None):
        evh = evh_pre
        for t in range(8 if evh_pre is None else 0):
            if t in pre_tiles:
                continue
            evh, _ = norm_tile(xs[:, t, :], x_ready[t], 32 + 16 * layer,
                               lambda h, t=t: hT[:, h * 8:(h + 1) * 8, t * 128:(t + 1) * 128], dst_wait=hT_free[0])
        if pre_evh:
            evh = [evh, pre_evh]
        groups = [(0, 8), (8, 8), (16, 8), (24, 8), (32, 8), (40, 4)]
        xev = list(x_ready)
        act_free = x_ready
        gu_i = 0
        dn_i = 0
        sg_free = [None, None]
        evpu = None
        for (f0, nf) in groups:
            ev_act_last = None
            for pr in range(nf // 2):
                col = (f0 + 2 * pr) * 128
                sg_, wgv, evg = wpiece(wg[layer], 0, 16, col, 256)
                su_, wuv, evu = wpiece(wu[layer], 0, 16, col, 256)
                for c in range(2):
                    fi = 2 * pr + c
                    for th in range(2):
                        bg = 0 + 2 * (gu_i % 2)
                        bu = 1 + 2 * (gu_i % 2)
                        si = gu_i % 2
                        gu_i += 1
                        P.wait("pe", [bfree[bg], bfree[bu], evg, evu, evh])
                        evpg = None
                        for k in range(16):
                            evpg = T(lambda t, bg=bg, k=k, c=c, th=th, wgv=wgv: t.matmul(
                                bank(bg), wgv[:, k, c * 128:(c + 1) * 128], hT[:, k, th * 512:(th + 1) * 512],
                                start=(k == 0), stop=(k == 15)), sig=(k == 15))
                        for k in range(16):
                            evpu = T(lambda t, bu=bu, k=k, c=c, th=th, wuv=wuv: t.matmul(
                                bank(bu), wuv[:, k, c * 128:(c + 1) * 128], hT[:, k, th * 512:(th + 1) * 512],
                                start=(k == 0), stop=(k == 15)), sig=(k == 15))
                        evs = A(lambda a, bg=bg, si=si: a.activation(out=sgt[si][:], in_=bank(bg), func=AF.Silu),
                                [evpg, sg_free[si]])
                        bfree[bg] = evs
                        dst = actT[:, fi, th * 512:(th + 1) * 512]
                        evm = V(lambda v, bu=bu, si=si, dst=dst: v.tensor_tensor(
                            out=dst, in0=bank(bu), in1=sgt[si][:], op=ALU.mult), [evs, evpu, act_free])
                        bfree[bu] = evm
                        sg_free[si] = evm
                        ev_act_last = evm
                R.release(sg_, evpu)
                R.release(su_, evpu)
            lastdn = None
            if tile_cb is not None and f0 == 40:
                pcs = [wpiece(wd[layer], f0 * 128, nf, jp * 1024, 1024) for jp in range(2)]
                lastpe = None
                for t in range(8):
                    for j in range(4):
                        sd_, wdv, evd = pcs[j // 2]
                        jh = j % 2
                        b = 4 + (dn_i % 2)
                        dn_i += 1
                        P.wait("pe", [bfree[b], evd, ev_act_last])
                        for f in range(nf):
                            lastpe = T(lambda tt, b=b, f=f, t=t, wdv=wdv, jh=jh: tt.matmul(
                                bank(b), actT[:, f, t * 128:(t + 1) * 128], wdv[:, f, jh * 512:(jh + 1) * 512],
                                start=(f == 0), stop=(f == nf - 1)), sig=(f == nf - 1))
                        xsl = xs[:, t, j * 512:(j + 1) * 512]
                        eva = V(lambda v, b=b, xsl=xsl: v.tensor_tensor(out=xsl, in0=bank(b), in1=xsl, op=ALU.add),
                                [lastpe, xev[t]])
                        bfree[b] = eva
                        xev[t] = eva
                        lastdn = eva
                    tile_cb(t, xev[t], evpu)
                for jp in range(2):
                    R.release(pcs[jp][0], lastpe)
                act_free = lastdn
                continue
            for j in range(4):
                sd_, wdv, evd = wpiece(wd[layer], f0 * 128, nf, j * 512, 512)
                lastpe = None
                for t in range(8):
                    b = 4 + (dn_i % 2)
                    dn_i += 1
                    P.wait("pe", [bfree[b], evd, ev_act_last])
                    for f in range(nf):
                        lastpe = T(lambda tt, b=b, f=f, t=t, wdv=wdv: tt.matmul(
                            bank(b), actT[:, f, t * 128:(t + 1) * 128], wdv[:, f, :],
                            start=(f == 0), stop=(f == nf - 1)), sig=(f == nf - 1))
                    xsl = xs[:, t, j * 512:(j + 1) * 512]
                    eva = V(lambda v, b=b, xsl=xsl: v.tensor_tensor(out=xsl, in0=bank(b), in1=xsl, op=ALU.add),
                            [lastpe, xev[t]])
                    bfree[b] = eva
                    xev[t] = eva
                    lastdn = eva
                R.release(sd_, lastpe)
            act_free = lastdn
        hT_free[0] = evpu
        return xev

    def out_proj(wout, nk, catT_of, toks, x_evs, cat_ev):
        xev = dict((t, x_evs[t]) for t in toks)
        for j in range(8):
            s_, wv, evw = wpiece(wout, 0, nk, j * 256, 256)
            lastpe = None
            for t in toks:
                def evac(b, ev, t=t, j=j):
                    dst = xs[:, t, j * 256:(j + 1) * 256]
                    e2 = V(lambda v: v.tensor_tensor(out=dst, in0=bank(b)[:, 0:256], in1=dst, op=ALU.add), [ev, xev[t]])
                    xev[t] = e2
                    return e2
                lastpe = proj_tm(lambda k, t=t: catT_of(k, t), wv, 0, 256, nk, evac, waits=[evw, cat_ev],
                                 banks=(0, 1, 2, 3))
            R.release(s_, lastpe)
        return xev

    def out_proj_t(wout, nk, catT_of, toks, x_evs, cat_ev, tile_cb):
        xev = dict((t, x_evs[t]) for t in toks)
        pcs = [wpiece(wout, 0, nk, j * 512, 512) for j in range(4)]
        lastpe = None
        for t in toks:
            for j in range(4):
                s_, wv, evw = pcs[j]

                def evac(b, ev, t=t, j=j):
                    dst = xs[:, t, j * 512:(j + 1) * 512]
                    e2 = V(lambda v: v.tensor_tensor(out=dst, in0=bank(b), in1=dst, op=ALU.add), [ev, xev[t]])
                    xev[t] = e2
                    return e2
                lastpe = proj_tm(lambda k, t=t: catT_of(k, t), wv, 0, 512, nk, evac, waits=[evw, cat_ev], banks=(0, 1, 2, 3))
            tile_cb(t, xev[t])
        for j in range(4):
            R.release(pcs[j][0], lastpe)
        return xev

    x_ready = [None] * 8
    if 0 in layers:
        xsb = xs[:].rearrange("p a b -> p (a b)").bitcast(BF16)
        KTh = xsb[:, 0:10752].rearrange("p (h t) -> p h t", t=HALO)
        Vh = xsb[:, 10752:21504].rearrange("p (t c) -> p t c", c=512)
        QT = xsb[:, 21504:24576].rearrange("p (g t) -> p g t", t=TOK)
        KTo = xsb[:, 24576:27648].rearrange("p (g t) -> p g t", t=TOK)
        Vo = xsb[:, 27648:29696].rearrange("p (g j c) -> p g j c", j=8, c=128)
        Vo3 = xsb[:, 29696:31744].rearrange("p (r c) -> p r c", c=128)
        qmT = xsb[:, 31744:32768]
        xt[0] = xsb[:, 21504:25600].bitcast(F32)
        xt[1] = xsb[:, 25600:29696].bitcast(F32)
        qnb = [ph[:, 7168:7680], ph[:, 7680:8192]]
        attnT = am[:].rearrange("p (k t) -> p k t", t=TOK)
        Co = misc_take(0, 4096, F32, 32)
        So = misc_take(4096, 4096, F32, 32)
        Ch = misc_take(8192, 2048, F32, 32)
        Sh = misc_take(10240, 2048, F32, 32)
        ChB = [Ch, ph[0:32, 9216:10240].bitcast(F32)]
        ShB = [Sh, ph[0:32, 10240:11264].bitcast(F32)]
        tab_last_use = [None, None]
        t1 = misc_take(12288, 2048, F32, 32)
        t2 = misc_take(14336, 2048, F32, 32)
        onesv = misc_take(16384, 5376, BF16).rearrange("p (t c) -> p t c", c=128)
        angb = ph[0:32, 0:1024].bitcast(F32)
        kb_i = ph[0:32, 1024:2048].bitcast(I32)
        kb_f = ph[0:32, 2048:3072].bitcast(F32)
        cmpb = ph[0:32, 3072:4096].bitcast(F32)
        posb = ph[0:32, 4096:5120].bitcast(I32)
        rden = ph[:, 5120:7168].bitcast(F32)
        invf_s = sb("invf_s", [32, 2], F32)
        valid_s = sb("valid_s", [128, 21], F32)
        memT0 = hT[:].rearrange("p a b -> p (a b)")[:, 0:4096].rearrange("p (k t) -> p k t", t=256)

        ev_if = SPD(lambda q: q.dma_start(out=invf_s[:], in_=invf), "gld2")
        ev_vl = SPD(lambda q: q.dma_start(out=valid_s[:], in_=valid), "gld3")

        mem_last = mem_kv(0, memT0)
        ev_ov = V(lambda v: v.tensor_copy(out=onesv, in_=valid_s[:].unsqueeze(2).to_broadcast([128, 21, 128])), [ev_vl])

        TWO_PI = float(2.0 * np.pi)
        C1 = 6.28125
        C2 = float(2.0 * np.pi - 6.28125)
        tab_chain = [None]

        def rot_tables_p1(pcol0, n, Cdst, Sdst, waits):
            ang = angb[:, 0:n]
            ki = kb_i[:, 0:n]
            kf = kb_f[:, 0:n]
            cm = cmpb[:, 0:n]
            evp = SPD(lambda q: q.dma_start(out=posb[:, 0:n], in_=posr[:, pcol0:pcol0 + n]), "pld", [tab_chain[0]])
            ev = V(lambda v: v.tensor_scalar(out=ang, in0=posb[:, 0:n], scalar1=invf_s[:, 0:1], scalar2=None, op0=ALU.mult),
                   [waits, evp, ev_if])
            tab_chain[0] = ev
            for which, dst in ((0, Sdst), (1, Cdst)):
                src = ang
                if which == 1:
                    ev = V(lambda v: v.tensor_scalar(out=cm, in0=ang, scalar1=float(np.pi / 2), scalar2=None, op0=ALU.add), [ev])
                    src = cm
                ev = V(lambda v, src=src: v.tensor_scalar(out=ki, in0=src, scalar1=float(1.0 / TWO_PI), scalar2=None, op0=ALU.mult), [ev])
                ev = V(lambda v: v.tensor_copy(out=kf, in_=ki), [ev])
                ev = V(lambda v, src=src, dst=dst: v.scalar_tensor_tensor(out=dst, in0=kf, scalar=-C1, in1=src, op0=ALU.mult, op1=ALU.add), [ev])
                ev = V(lambda v, dst=dst: v.scalar_tensor_tensor(out=dst, in0=kf, scalar=-C2, in1=dst, op0=ALU.mult, op1=ALU.add), [ev])
                ev = V(lambda v, dst=dst: v.tensor_scalar(out=dst, in0=dst, scalar1=3.14159, scalar2=-3.14159, op0=ALU.min, op1=ALU.max), [ev])
            return (ev, Cdst, Sdst)

        def rot_tables_p2(st):
            ev, Cdst, Sdst = st
            e1 = A(lambda a: a.activation(out=Sdst, in_=Sdst, func=AF.Sin), [ev])
            e2 = A(lambda a: a.activation(out=Cdst, in_=Cdst, func=AF.Sin), [e1])
            e3 = V(lambda v: v.tensor_scalar(out=Sdst, in0=Sdst, scalar1=invf_s[:, 1:2], scalar2=None, op0=ALU.mult), [e1])
            P.wait("dve", e2)
            return [e2, e3]

        def rot_tables(pcol0, n, Cdst, Sdst, waits):
            return rot_tables_p2(rot_tables_p1(pcol0, n, Cdst, Sdst, waits))

        rot_i = [0]
        rot_free = [None]

        def rotary_evac(b, ev_pe, dst, Ct, St, tab_ev, n, shp=None):
            f = shp if shp is not None else (lambda a: a)
            ev_c = A(lambda a: a.copy(out=dst, in_=bank(b)[:, 0:n]), [ev_pe])

            def part2():
                rb = 2 + (rot_i[0] % 2)
                rot_i[0] += 1
                P.wait("pe", bfree[rb])
                ev_r = T(lambda t: t.matmul(bank(rb)[0:32, 0:n], Rsw, dst, start=True, stop=True), waits=[ev_c], sig=True)
                e1 = V(lambda v: v.tensor_tensor(out=f(t1[:, 0:n]), in0=f(bank(rb)[0:32, 0:n]), in1=St, op=ALU.mult),
                       [ev_r, tab_ev, rot_free[0]])
                bfree[rb] = e1
                e2 = V(lambda v: v.tensor_tensor(out=f(t2[:, 0:n]), in0=f(bank(b)[0:32, 0:n]), in1=Ct, op=ALU.mult), [ev_c])
                e3 = V(lambda v: v.tensor_tensor(out=dst[0:32], in0=t1[:, 0:n], in1=t2[:, 0:n], op=ALU.add), [e1, e2])
                rot_free[0] = e3
                bfree[b] = e3
            return part2

        ev_tabo_h = [None]

        hTb = [am[:, i * 4096:(i + 1) * 4096].rearrange("p (k t) -> p k t", t=256) for i in range(2)]
        hb_free = [mem_last, mem_last]
        batch_evd = {}
        batch_tab = {}
        held = {}

        def halo_norm_A(bi, tt):
            tile0, ntile, gi = HALO_BATCHES[bi]
            hb = hTb[bi % 2]
            i, evl = load_xtile(xh[(tile0 + tt) * 128:(tile0 + tt + 1) * 128, :])
            st = norm_A(xt[i], evl, 0,
                        lambda h, tt=tt, hb=hb: hb[:, h * 8:(h + 1) * 8, tt * 128:(tt + 1) * 128],
                        dst_wait=hb_free[bi % 2])
            xt_free[i] = st[0]
            return (bi, st)

        def halo_norm_B(pst):
            bi, st = pst
            evd, evr = norm_B(st)
            batch_evd[bi] = evd

        def halo_norm(bi, tt):
            halo_norm_B(halo_norm_A(bi, tt))

        tab_st = {}

        def halo_tables_p1(bi):
            tile0, ntile, gi = HALO_BATCHES[bi]
            n = ntile * 128
            tab_st[bi] = rot_tables_p1(tile0 * 128, n, ChB[bi % 2][:, 0:n], ShB[bi % 2][:, 0:n], tab_last_use[bi % 2])

        def halo_tables_p2(bi):
            batch_tab[bi] = rot_tables_p2(tab_st[bi])

        def halo_tables(bi):
            halo_tables_p1(bi)
            halo_tables_p2(bi)

        def halo_piece(bi, pc):
            tile0, ntile, gi = HALO_BATCHES[bi]
            hb = hTb[bi % 2]
            n = ntile * 128
            evd = batch_evd[bi]
            kcol = 5120 + (2 - gi) * 1024
            if (gi, pc) not in held:
                held[(gi, pc)] = wpiece(w0, 0, 16, kcol + pc * 256, 256)
            s_, wv, evw = held[(gi, pc)]
            last_of_group = (bi + 1 >= len(HALO_BATCHES)) or (HALO_BATCHES[bi + 1][2] != gi)
            lastpe = None
            if pc < 2:
                ev_tab = batch_tab[bi]
                for c in range(2):
                    hd = pc * 2 + c
                    dst = KTh[:, hd, tile0 * 128: tile0 * 128 + n]
                    lastpe = proj_fm(wv, c * 128, lambda k, hb=hb, n=n: hb[:, k, 0:n], 16, n,
                                     lambda b, ev, dst=dst, n=n, ev_tab=ev_tab, bi=bi: rotary_evac(b, ev, dst, ChB[bi % 2][:, 0:n], ShB[bi % 2][:, 0:n], ev_tab, n),
                                     waits=[evw, evd])
            else:
                cc = (pc - 2) * 256
                for tt in range(ntile):
                    dst = Vh[:, tile0 + tt, cc:cc + 256]
                    lastpe = proj_tm(lambda k, hb=hb, tt=tt: hb[:, k, tt * 128:(tt + 1) * 128], wv, 0, 256, 16,
                                     lambda b, ev, dst=dst: A(lambda a: a.copy(out=dst, in_=bank(b)[:, 0:256]), [ev]),
                                     waits=[evw, evd])
            if pc == 1:
                flush_rot()
                tab_last_use[bi % 2] = rot_free[0]
            if last_of_group:
                R.release(s_, lastpe)
            if pc == 3:
                hb_free[bi % 2] = lastpe

        for tt in range(HALO_BATCHES[0][1]):
            halo_norm(0, tt)
        halo_tables(0)
        NB_H = len(HALO_BATCHES)
        ev_hT_h = [None]

        def own_A(t):
            i, evl = load_xtile(xo[t * 128:(t + 1) * 128, :])
            st = norm_A(xt[i], evl, 0, lambda h, t=t: hT[:, h * 8:(h + 1) * 8, t * 128:(t + 1) * 128], dst_wait=mem_last)
            xt_free[i] = st[0]
            return st

        own_next = [0]
        for bi in range(NB_H):
            nxt = HALO_BATCHES[bi + 1][1] if bi + 1 < NB_H else 0
            for pc in range(4):
                pst = None
                ost = None
                if pc < nxt:
                    pst = halo_norm_A(bi + 1, pc)
                elif bi >= NB_H - 4 and own_next[0] < 8:
                    ost = own_A(own_next[0])
                    own_next[0] += 1
                halo_piece(bi, pc)
                if pst is not None:
                    halo_norm_B(pst)
                if ost is not None:
                    ev_hT_h[0], _ = norm_B(ost)
                if pc == 1 and bi + 1 < NB_H:
                    halo_tables_p1(bi + 1)
                if pc == 3 and bi + 1 < NB_H:
                    halo_tables_p2(bi + 1)
                if pc == 3 and bi == 2:
                    rot_tables(HALO, 512, Co[:, 0:512], So[:, 0:512], None)
                if pc == 3 and bi == 4:
                    ev_tabo_h[0] = rot_tables(HALO + 512, 512, Co[:, 512:1024], So[:, 512:1024], None)
        ev_tabo = ev_tabo_h[0]
        while own_next[0] < 8:
            ev_hT_h[0], _ = norm_B(own_A(own_next[0]))
            own_next[0] += 1
        ev_hT = ev_hT_h[0]
        xt_last = [xt_free[0], xt_free[1]]

        def vtile(ap2, gi, j):
            if gi == 0:
                return ap2[:, j * 128:(j + 1) * 128]
            if gi == 1:
                r, bb = j // 2, j % 2
                return ap2.rearrange("p (i r) -> p r i", r=4)[:, r, bb * 128:(bb + 1) * 128]
            return ap2.rearrange("p (i r) -> p r i", r=16)[:, j, :]

        def gtile(ap2, gi, j):
            return vtile(ap2, gi, j)

        def nat_view(a, gi):
            if gi == 0:
                return a
            return a.rearrange("p (i r) -> p i r", r=(4 if gi == 1 else 16))

        def perm_view(row, gi, th):
            if gi == 0:
                return row[:, th * 512:(th + 1) * 512]
            if gi == 1:
                return row.rearrange("p (r i) -> p i r", r=4)[:, 128 * th:128 * (th + 1), :]
            return row.rearrange("p (r i) -> p i r", r=16)[:, 32 * th:32 * (th + 1), :]

        qn_i = [0]
        qn_free = [None, None]
        vt_free = [None]
        VTs1 = ph[:, 8192:9216]

        def rotary_evac_perm(b, ev_pe, dstrow, gi, th):
            qi = qn_i[0] % 2
            qn_i[0] += 1
            qn = qnb[qi]
            Ct = Co[:, th * 512:(th + 1) * 512]
            St = So[:, th * 512:(th + 1) * 512]
            ev_c = A(lambda a: a.copy(out=qn, in_=bank(b)), [ev_pe, qn_free[qi]])
            A(lambda a: a.copy(out=perm_view(dstrow[32:64], gi, th), in_=nat_view(bank(b)[32:64, :], gi)), [ev_pe])
            ev_c2 = A(lambda a: a.copy(out=perm_view(dstrow[64:128], gi, th), in_=nat_view(bank(b)[64:128, :], gi)), [ev_pe])
            def part2():
                rb = 2 + (rot_i[0] % 2)
                rot_i[0] += 1
                P.wait("pe", bfree[rb])
                ev_r = T(lambda t: t.matmul(bank(rb)[0:32, :], Rsw, qn, start=True, stop=True), waits=[ev_c], sig=True)
                qn_free[qi] = ev_r
                e1 = V(lambda v: v.tensor_tensor(out=t1, in0=bank(rb)[0:32, :], in1=St, op=ALU.mult),
                       [ev_r, ev_tabo, rot_free[0]])
                bfree[rb] = e1
                e2 = V(lambda v: v.tensor_tensor(out=t2, in0=bank(b)[0:32, :], in1=Ct, op=ALU.mult), [ev_c])
                e3 = V(lambda v: v.tensor_tensor(out=perm_view(dstrow[0:32], gi, th), in0=nat_view(t1, gi), in1=nat_view(t2, gi),
                                                 op=ALU.add), [e1, e2, ev_c2])
                rot_free[0] = e3
                bfree[b] = e3
            return part2

        accA = psall[:, 4:6, :].rearrange("p a b -> p (a b)")
        denA = psall[:, 6:8, :].rearrange("p a b -> p (a b)")
        att_done = None
        rdA_free = [None]
        rdm = misc_take(21760, 2048, F32)

        for s in range(4):
            base = s * 1280
            cur = [None, None, None]
            lastpe_piece = [None]

            def get_piece(pcI, cur=cur, base=base, lastpe_piece=lastpe_piece):
                if cur[0] != pcI:
                    if cur[0] is not None:
                        R.release(cur[1][0], lastpe_piece[0])
                    cur[0] = pcI
                    cur[1] = wpiece(w0, 0, 16, base + pcI * 256, 256)
                return cur[1]

            PB = (0, 1, 4, 5, 6, 7)
            for c in range(7):
                s_, wv, evw = get_piece(c // 2)
                cI = c % 2
                for th in range(2):
                    if c < 6:
                        gi = c % 3
                        dst = (QT if c < 3 else KTo)[:, gi, th * 512:(th + 1) * 512]
                        ev = proj_fm(wv, cI * 128, lambda k, th=th: hT[:, k, th * 512:(th + 1) * 512], 16, 512,
                                     lambda b, ev, dst=dst, th=th: rotary_evac(b, ev, dst, Co[:, th * 512:(th + 1) * 512],
                                                                               So[:, th * 512:(th + 1) * 512], ev_tabo, 512),
                                     waits=[evw, ev_hT, att_done], banks=PB)
                    else:
                        dst = qmT[:, th * 512:(th + 1) * 512]
                        ev = proj_fm(wv, cI * 128, lambda k, th=th: hT[:, k, th * 512:(th + 1) * 512], 16, 512,
                                     lambda b, ev, dst=dst: A(lambda a: a.copy(out=dst, in_=bank(b)), [ev]),
                                     waits=[evw, ev_hT, att_done], banks=PB)
                    lastpe_piece[0] = ev
            for gi in range(3):
                c = 7 + gi
                s_, wv, evw = get_piece(c // 2)
                cI = c % 2
                vrow = VTs1
                evv = []
                for th in range(2):
                    def evac_v(b, ev, vrow=vrow, th=th):
                        return A(lambda a: a.copy(out=vrow[:, th * 512:(th + 1) * 512], in_=bank(b)), [ev, vt_free[0]])
                    ev = proj_fm(wv, cI * 128, lambda k, th=th: hT[:, k, th * 512:(th + 1) * 512], 16, 512, evac_v,
                                 waits=[evw, ev_hT, att_done], banks=PB)
                    lastpe_piece[0] = ev
                    evv.append(bfree[PB[(pj[0] - 1) % len(PB)]])
                ntile = 8 if gi < 2 else 16
                tw = 128 if gi < 2 else 64
                for jb in range(ntile // 8):
                    b = 2 + (pj[0] % 2)
                    pj[0] += 1
                    P.wait("pe", [bfree[b], evv])
                    ev = None
                    for jj in range(8):
                        j = jb * 8 + jj
                        ev = T(lambda t, b=b, jj=jj, j=j, vrow=vrow, tw=tw, gi=gi: t.transpose(
                            bank_bf(b)[0:tw, jj * 128:(jj + 1) * 128], vtile(vrow, gi, j), ident), sig=(jj == 7))
                    if gi < 2:
                        dst = Vo[:, gi, :, :]
                    else:
                        dst = Vo3[0:64, jb * 8:(jb + 1) * 8, :]
                    bfree[b] = A(lambda a, b=b, dst=dst, tw=tw: a.copy(
                        out=dst, in_=bank_bf(b)[0:tw, :].rearrange("p (j c) -> p j c", c=128)), [ev])
                    vt_free[0] = ev
            flush_rot()
            R.release(cur[1][0], lastpe_piece[0])
            proj_done = [rot_free[0]] + [bfree[i] for i in range(8)]

            ev_z1 = V(lambda v: v.memset(accA, 0.0), [bfree[4], bfree[5]])
            ev_z2 = V(lambda v: v.memset(denA, 0.0), [bfree[6], bfree[7]])
            P.wait("pe", [ev_z1, ev_z2, proj_done, ev_ov])

            def g_attend(items, mask_ops, nk=128):
                def stage1():
                    i = pv_i[0] % 2
                    pv_i[0] += 1
                    sbk = 2 + i
                    P.wait("pe", bfree[sbk])
                    c = 0
                    offs = []
                    evs = None
                    for n_it, (k_ap, q_ap, nq, pvs) in enumerate(items):
                        evs = T(lambda t, c=c, nq=nq, k_ap=k_ap, q_ap=q_ap: t.matmul(
                            bank(sbk)[0:nk, c:c + nq], k_ap, q_ap, start=True, stop=True), sig=(n_it == len(items) - 1))
                        offs.append(c)
                        c += nq
                    tot = c
                    eve = A(lambda a: a.activation(out=ptS[i][0:nk, 0:tot], in_=bank(sbk)[0:nk, 0:tot],
                                                   func=AF.Exp, scale=128 ** -0.5), [evs, pt_free[i]])
                    bfree[sbk] = eve
                    evm = None
                    for (c0, ncl, in1_ap, inner) in mask_ops:
                        o = pmS[i][0:nk, c0:c0 + ncl].rearrange("p (a b) -> p a b", b=inner)
                        a_in = ptS[i][0:nk, c0:c0 + ncl].rearrange("p (a b) -> p a b", b=inner)
                        evm = V(lambda v, o=o, a_in=a_in, in1_ap=in1_ap: v.tensor_tensor(out=o, in0=a_in, in1=in1_ap, op=ALU.mult),
                                [eve, pm_free[i]])
                    pt_free[i] = evm

                    def stage2():
                        evp = None
                        first = True
                        for (k_ap, q_ap, nq, pvs), off in zip(items, offs):
                            for (co, ncl, v_ap, o_ap, acc_ap, den_ap) in pvs:
                                rhs = pmS[i][0:nk, off + co:off + co + ncl]
                                T(lambda t, v_ap=v_ap, rhs=rhs, acc_ap=acc_ap: t.matmul(
                                    acc_ap, v_ap, rhs, start=False, stop=False, skip_group_check=True),
                                  waits=[evm] if first else None)
                                first = False
                                evp = T(lambda t, o_ap=o_ap, rhs=rhs, den_ap=den_ap: t.matmul(
                                    den_ap, o_ap, rhs, start=False, stop=False, skip_group_check=True), sig=True)
                        pm_free[i] = evp
                        return evp
                    return stage2
                return stage1

            batches = []
            M_op = cmat[:, 128:384]
            M_po = cmat[:, 256:512]

            def g1_item(kt):
                if kt < 0:
                    k_ap, v_ap, o_ap = KTh[:, s, 2560:2688], Vh[:, 20, s * 128:(s + 1) * 128], onesv[:, 20, :]
                else:
                    k_ap, v_ap, o_ap = KTo[:, 0, kt * 128:(kt + 1) * 128], Vo[:, 0, kt, :], ones
                qlo, qhi = max(kt, 0), min(kt + 1, 7)
                nq = (qhi - qlo + 1) * 128
                pvs = [(qi * 128, 128, v_ap, o_ap, accA[:, qt * 128:(qt + 1) * 128], denA[:, qt * 128:(qt + 1) * 128])
                       for qi, qt in enumerate(range(qlo, qhi + 1))]
                return (k_ap, QT[:, 0, qlo * 128:qlo * 128 + nq], nq, pvs)

            batches.append(g_attend([g1_item(-1), g1_item(0)],
                                    [(0, 128, M_prev.unsqueeze(1), 128), (128, 256, M_op.unsqueeze(1), 256)]))
            for kt in (1, 3, 5):
                batches.append(g_attend([g1_item(kt), g1_item(kt + 1)],
                                        [(0, 512, M_op.unsqueeze(1).to_broadcast([128, 2, 256]), 256)]))
            batches.append(g_attend([g1_item(7)], [(0, 128, M_own.unsqueeze(1), 128)]))
            for r in range(4):
                items = []
                for kb in range(-1, 2):
                    if kb < 0:
                        k_ap, v_ap, o_ap = KTh[:, s, 2048 + r * 128:2048 + (r + 1) * 128], Vh[:, 16 + r, s * 128:(s + 1) * 128], onesv[:, 16 + r, :]
                    else:
                        k_ap, v_ap, o_ap = vtile(KTo[:, 1, :], 1, r * 2 + kb), Vo[:, 1, r * 2 + kb, :], ones
                    qlo, qhi = max(kb, 0), min(kb + 1, 1)
                    nq = (qhi - qlo + 1) * 128
                    pvs = [(qi * 128, 128, v_ap, o_ap, gtile(accA, 1, r * 2 + qb), gtile(denA, 1, r * 2 + qb))
                           for qi, qb in enumerate(range(qlo, qhi + 1))]
                    items.append((k_ap, QT[:, 1, :].rearrange("p (i r) -> p r i", r=4)[:, r, qlo * 128:qlo * 128 + nq], nq, pvs))
                batches.append(g_attend(items, [(0, 512, M_po.unsqueeze(1).to_broadcast([128, 2, 256]), 256)]))
            for rb in range(2):
                items = []
                for r in range(rb * 8, rb * 8 + 8):
                    accc = accA.rearrange("p (i r) -> p r i", r=16)[:, r, :]
                    denc = denA.rearrange("p (i r) -> p r i", r=16)[:, r, :]
                    items.append((KTh[:, s, r * 128:(r + 1) * 128], vtile(QT[:, 2, :], 2, r), 64,
                                  [(0, 32, Vh[:, r, s * 128:(s + 1) * 128], onesv[:, r, :], accc[:, 0:32], denc[:, 0:32]),
                                   (32, 32, Vh[:, r, s * 128:(s + 1) * 128], onesv[:, r, :], accc[:, 32:64], denc[:, 32:64])]))
                batches.append(g_attend(items, [(0, 512, M_prev[:, 0:64].unsqueeze(1).to_broadcast([128, 8, 64]), 64)]))
            for rb in range(2):
                items = []
                for r in range(rb * 8, rb * 8 + 8):
                    accc = accA.rearrange("p (i r) -> p r i", r=16)[:, r, :]
                    denc = denA.rearrange("p (i r) -> p r i", r=16)[:, r, :]
                    items.append((vtile(KTo[:, 2, :], 2, r), vtile(QT[:, 2, :], 2, r), 64,
                                  [(0, 32, Vo3[0:64, r, :], ones[0:64, :], accc[:, 0:32], denc[:, 0:32]),
                                   (32, 32, Vo3[0:64, r, :], ones[0:64, :], accc[:, 32:64], denc[:, 32:64])]))
                batches.append(g_attend(items, [(0, 512, M_own[0:64, 0:64].unsqueeze(1).to_broadcast([64, 8, 64]), 64)], nk=64))
            pend = None
            last = None
            for bt in batches:
                s2 = bt()
                if pend is not None:
                    last = pend()
                pend = s2
            last = pend()
            ev1 = recip_act(rden, denA, [last, rdA_free[0]])
            ev2 = V(lambda v, s=s: v.tensor_tensor(out=attnT[:, s, :], in0=accA, in1=rden, op=ALU.mult), [ev1, xt_last])
            rdA_free[0] = ev2
            for b in (4, 5):
                bfree[b] = ev2
            for b in (6, 7):
                bfree[b] = ev1
            att_done = [ev2, mem_attn_multi([(s, qmT[:, th * 512:(th + 1) * 512], attnT[:, 4 + s, th * 512:(th + 1) * 512], [proj_done, xt_last]) for th in range(2)], rdm, pairs=((0, 1), (4, 6)))]

        xl = []
        for t in range(8):
            xl.append(SPD(lambda q, t=t: q.dma_start(out=xs[:, t, :], in_=xo[t * 128:(t + 1) * 128, :]), f"xsl{t}", [att_done]))
        pendF = [None]
        evhF = [None]

        def cb_ffn0norm(t, ev):
            if pendF[0] is not None:
                evhF[0], _ = norm_B(pendF[0])
            pendF[0] = norm_A(xs[:, t, :], ev, 32,
                              lambda h, t=t: hT[:, h * 8:(h + 1) * 8, t * 128:(t + 1) * 128], dst_wait=att_done)
        if stop != "attn0":
            xev = out_proj_t(wo0, 8, lambda k, t: attnT[:, k, t * 128:(t + 1) * 128], list(range(8)), xl, att_done, cb_ffn0norm)
            evhF[0], _ = norm_B(pendF[0])
        else:
            xev = out_proj(wo0, 8, lambda k, t: attnT[:, k, t * 128:(t + 1) * 128], list(range(8)), xl, att_done)
        x_ready = [xev[t] for t in range(8)]
        hT_free[0] = att_done
        xt[0], xt[1] = xt_am[0], xt_am[1]
        xt_free[0] = xt_free[1] = x_ready
        evh_cb = [None]
        if stop != "attn0":
            pendB = [None]

            def cb_l1norm(t, ev, hfree):
                if pendB[0] is not None:
                    evh_cb[0], _ = norm_B(pendB[0])
                pendB[0] = norm_A(xs[:, t, :], ev, 16,
                                  lambda h, t=t: hT[:, h * 8:(h + 1) * 8, t * 128:(t + 1) * 128], dst_wait=hfree)
            x_ready = ffn(0, x_ready, cb_l1norm if (1 in layers) else None, evh_pre=evhF[0])
            if pendB[0] is not None:
                evh_cb[0], _ = norm_B(pendB[0])
            xt_free[0] = xt_free[1] = x_ready
    else:
        evh_cb = [None]
        for t in range(8):
            x_ready[t] = SPD(lambda q, t=t: q.dma_start(out=xs[:, t, :], in_=xo[t * 128:(t + 1) * 128, :]), f"xsl{t}")

    gfin = ph[:, 0:4096].bitcast(F32)
    hTf32 = hT[:].rearrange("p a b -> p (a b)").bitcast(F32)
    ystage = [hTf32[:, i * 2048:(i + 1) * 2048] for i in range(4)]
    yst_free = [None] * 4
    ev_gf_h = [None]
    final_done = [False]
    st_evs = []

    fin_pend = [None]

    def final_flush():
        if fin_pend[0] is not None:
            f = fin_pend[0]
            fin_pend[0] = None
            f()

    def final_tile(t, ev, hfree):
        final_flush()
        ss, sq, rs = stat_col(), stat_col(), stat_col()
        e = A(lambda a: a.activation(out=xnb[:], in_=xs[:, t, :], func=AF.Square, accum_out=ss), [ev, xn_free[0]])
        e_sq = A(lambda a: a.activation(out=sq, in_=ss, func=AF.Sqrt, scale=1.0 / DM, bias=eps_ap), [e])
        i = t % 4

        def part2():
            e1 = V(lambda v: v.reciprocal(out=rs, in_=sq), [e_sq])
            e2 = V(lambda v: v.scalar_tensor_tensor(out=ystage[i], in0=xs[:, t, :], scalar=rs, in1=gfin,
                                                    op0=ALU.mult, op1=ALU.mult), [e1, ev_gf_h[0], yst_free[i], hfree])
            evs = SPD(lambda q: q.dma_start(out=y[t * 128:(t + 1) * 128, :], in_=ystage[i]), f"yst{i}", [e2])
            yst_free[i] = evs
            st_evs.append(evs)
        fin_pend[0] = part2

    if 1 in layers and stop != "attn0":
        lng = misc_take(0, 6144, F32)
        lnb = misc_take(6144, 6144, F32)
        bsp = misc_take(12288, 6144, F32)
        wsT = misc_take(18432, 3072, BF16).rearrange("p (g t) -> p g t", t=128)
        rden1 = misc_take(21504, 2048, F32)
        vtm = ph[:, 0:6144].rearrange("p (t c) -> p t c", c=1536)
        qm1 = ph[:, 6144:8192].rearrange("p (m t) -> p m t", t=512)
        vgf = ph[:, 8192:11264].bitcast(F32)
        memT1 = ph[:, 0:4096].rearrange("p (k t) -> p k t", t=256)
        e1 = SPD(lambda q: q.dma_start(out=lng, in_=lng_d), "gld4", x_ready)
        e2 = SPD(lambda q: q.dma_start(out=lnb, in_=lnb_d), "gld5")
        e3 = SPD(lambda q: q.dma_start(out=bsp, in_=bsp_d), "gld6")
        e4 = SPD(lambda q: q.dma_start(out=vgf, in_=wst_d), "gld7")
        ev_ws = V(lambda v: v.tensor_tensor(out=wsT, in0=vgf.rearrange("p (g t) -> p g t", t=128),
                                            in1=M_own.unsqueeze(1).to_broadcast([128, 12, 128]), op=ALU.mult), [e4])
        vgb = ph[:, 8192:11264]
        TA = vgb[:, 0:1536]
        TB = vgb[:, 1536:3072]
        bsp_hl = misc[0:64, 6144:7680]
        V(lambda v: v.memset(TA[0:64], 0.0), [ev_ws])
        V(lambda v: v.tensor_copy(out=TA[0:1], in_=bsp[0:1]), [e3])
        eb1 = V(lambda v: v.tensor_copy(out=TB[32:33], in_=bsp[32:33]))
        eb2 = V(lambda v: v.tensor_tensor(out=TA[32:33], in0=bsp[32:33], in1=TB[32:33], op=ALU.subtract), [eb1])
        ev_hl = V(lambda v: v.tensor_copy(out=bsp_hl, in_=TA[0:64]), [eb2])
        ev_tabs = [e1, e2, ev_ws, ev_hl]
        evh = evh_cb[0]
        if evh is None:
            for t in range(8):
                evh, _ = norm_tile(xs[:, t, :], x_ready[t], 16,
                                   lambda h, t=t: hT[:, h * 8:(h + 1) * 8, t * 128:(t + 1) * 128], dst_wait=hT_free[0])
        gTm = am[:].rearrange("p (k t) -> p k t", t=512)
        half_done = [xt_free[0], xt_free[1]]
        pre_evs = []
        for hf in range(2):
            tk0 = hf * 512
            last = None
            for pc in range(2):
                s_, wv, evw = wpiece(w1, 0, 16, 3072 + pc * 256, 256)
                for c in range(2):
                    m = pc * 2 + c
                    last = proj_fm(wv, c * 128, lambda k, tk0=tk0: hT[:, k, tk0:tk0 + 512], 16, 512,
                                   lambda b, ev, m=m: A(lambda a: a.copy(out=qm1[:, m, :], in_=bank(b)), [ev]),
                                   waits=[evw, evh, half_done])
                R.release(s_, last)
            qdone = [bfree[0], bfree[1]]
            if hf == 0:
                mem_kv(1, memT1)
            for cg in range(6):
                pstF = None
                if hf == 1 and cg < 4 and stop != "attn1":
                    pstF = norm_A(xs[:, cg, :], x_ready[cg], 48,
                                  lambda h, cg=cg: hT[:, h * 8:(h + 1) * 8, cg * 128:(cg + 1) * 128], dst_wait=half_done)
                s_, wv, evw = wpiece(w1, 0, 16, 1536 + cg * 256, 256)
                last = None
                for tt in range(4):
                    def evac(b, ev, tt=tt, cg=cg):
                        return A(lambda a: a.activation(out=vtm[:, tt, cg * 256:(cg + 1) * 256], in_=bank(b)[:, 0:256],
                                                        func=AF.Gelu), [ev, half_done])
                    last = proj_tm(lambda k, tt=tt, tk0=tk0: hT[:, k, tk0 + tt * 128: tk0 + (tt + 1) * 128], wv, 0, 256, 16, evac,
                                   waits=[evw, evh], banks=(0, 1, 4, 5))
                R.release(s_, last)
                if pstF is not None:
                    pre_evs.append(norm_B(pstF)[0])
            vdone = [bfree[0], bfree[1], bfree[4], bfree[5]]
            ev_ln_h = [None]

            def ln_tile(tt, vdone=vdone, ev_ln_h=ev_ln_h):
                sm, sq2, mu, var, rs = stat_col(), stat_col(), stat_col(), stat_col(), stat_col()
                ea = A(lambda a: a.activation(out=xnb[:, 0:1536], in_=vtm[:, tt, :], func=AF.Copy, accum_out=sm), [vdone, ev_ws, xn_free[0]])
                eb = A(lambda a: a.activation(out=xnb[:, 0:1536], in_=vtm[:, tt, :], func=AF.Square, accum_out=sq2), [ea])
                e = V(lambda v: v.tensor_scalar(out=mu, in0=sm, scalar1=1.0 / 1536, scalar2=None, op0=ALU.mult), [ea])
                e = V(lambda v: v.tensor_tensor(out=var, in0=mu, in1=mu, op=ALU.mult), [e])
                e = V(lambda v: v.scalar_tensor_tensor(out=var, in0=sq2, scalar=1.0 / 1536, in1=var, op0=ALU.mult, op1=ALU.subtract), [e, eb])
                e = A(lambda a: a.activation(out=var, in_=var, func=AF.Sqrt, bias=lneps_ap), [e])
                e = V(lambda v: v.reciprocal(out=rs, in_=var), [e])
                e = V(lambda v: v.tensor_scalar(out=vgf, in0=vtm[:, tt, :], scalar1=mu, scalar2=rs, op0=ALU.subtract, op1=ALU.mult), [e, eb, ev_ln_h[0]])
                e = V(lambda v: v.tensor_tensor(out=vgf, in0=vgf, in1=lng, op=ALU.mult), [e, ev_tabs])
                ev_ln_h[0] = V(lambda v: v.tensor_tensor(out=vtm[:, tt, :], in0=vgf, in1=lnb, op=ALU.add), [e])

            for pc in range(6):
                s_, wv, evw = wpiece(w1, 0, 16, pc * 256, 256)
                last = None
                for c in range(2):
                    g = pc * 2 + c
                    last = proj_fm(wv, c * 128, lambda k, tk0=tk0: hT[:, k, tk0:tk0 + 512], 16, 512,
                                   lambda b, ev, g=g: A(lambda a: a.activation(out=gTm[:, g, :], in_=bank(b), func=AF.Gelu), [ev, half_done]),
                                   waits=[evw, evh])
                R.release(s_, last)
                if pc < 4:
                    ln_tile(pc)
            ev_ln = ev_ln_h[0]
            udone = [bfree[0], bfree[1]]
            last_ma = mem_attn_multi([(m, qm1[:, m, :], gTm[:, 12 + m, :], [qdone, half_done]) for m in range(4)], rden1)
            ev_gate = None
            for g in range(12):
                b = 2 + (g % 2)
                P.wait("pe", [bfree[b], ev_ln, ev_ws, ev_hl])
                ev = None
                for tt in range(4):
                    ob = bank(b)[:, tt * 128:(tt + 1) * 128]
                    T(lambda t, ob=ob, tt=tt, g=g: t.matmul(ob, vtm[:, tt, g * 128:(g + 1) * 128], wsT[:, g, :],
                                                          start=True, stop=False))
                    ev = T(lambda t, ob=ob, g=g: t.matmul(ob, ones[0:64, :], bsp_hl[:, g * 128:(g + 1) * 128],
                                                         start=False, stop=True), sig=(tt == 3))
                ev_gate = V(lambda v, b=b, g=g: v.tensor_tensor(out=gTm[:, g, :], in0=bank(b), in1=gTm[:, g, :], op=ALU.mult),
                            [ev, udone])
                bfree[b] = ev_gate
            cat_ev = [ev_gate, last_ma]
            toks = [hf * 4 + i for i in range(4)]
            r = out_proj(wo1, 16, lambda k, t: gTm[:, k, (t % 4) * 128:(t % 4 + 1) * 128], toks, x_ready, cat_ev)
            for t in toks:
                x_ready[t] = r[t]
            half_done = [r[t] for t in toks]
            P.wait("pe", half_done)
        hT_free[0] = half_done
        xt_free[0] = xt_free[1] = x_ready
        if stop != "attn1":
            if final:
                ev_gf_h[0] = SPD(lambda q: q.dma_start(out=gfin, in_=gfin_d), "gld8", x_ready)
                x_ready = ffn(1, x_ready, final_tile, pre_tiles=(0, 1, 2, 3) if pre_evs else (), pre_evh=pre_evs)
                final_flush()
                final_done[0] = True
            else:
                x_ready = ffn(1, x_ready, pre_tiles=(0, 1, 2, 3) if pre_evs else (), pre_evh=pre_evs)
            xt_free[0] = xt_free[1] = x_ready

    if final and not final_done[0]:
        ev_gf_h[0] = SPD(lambda q: q.dma_start(out=gfin, in_=gfin_d), "gld8", x_ready)
        for t in range(8):
            final_tile(t, x_ready, hT_free[0])
        final_flush()
    elif not final:
        for t in range(8):
            evs = SPD(lambda q, t=t: q.dma_start(out=y[t * 128:(t + 1) * 128, :], in_=xs[:, t, :]), "yst0", [x_ready[t]])
            st_evs.append(evs)
    P.wait("sp", st_evs)
    P.build()
    es.close()
    return nc


_NC_CACHE = {}


def _get_nc(layers, final, stop=None):
    key = (tuple(layers), final, stop)
    if key not in _NC_CACHE:
        _NC_CACHE[key] = build(layers, final, stop)
    return _NC_CACHE[key]


def _halo_idx(T0):
    g3 = (T0 - 2048 + 16 * np.arange(128)[None, :] + np.arange(16)[:, None]).reshape(-1)
    g2 = (T0 - 512 + 4 * np.arange(128)[None, :] + np.arange(4)[:, None]).reshape(-1)
    g1 = T0 - 128 + np.arange(128)
    return np.concatenate([g3, g2, g1]).astype(np.int64)


def _consts():
    k = np.arange(128)[:, None]
    q = np.arange(128)[None, :]
    ident = (k == q)
    m_own = (k <= q)
    m_prev = (k >= q)
    m_b3 = ((k // 64) == (q // 64)) & ((k % 64) <= (q % 64))
    ones = np.ones((128, 128), bool)
    m32 = np.arange(32)[None, :]
    rsw = (k < 32) & (k == ((m32 + 16) % 32))
    cmat = np.concatenate([ident, m_own, m_prev, m_own, ones, rsw], axis=1).astype(np.float32)
    half = 16
    inv_freq = (np.float32(500000.0) ** (-(np.arange(half, dtype=np.float32)) / np.float32(half))).astype(np.float32)
    invf = np.zeros((32, 2), np.float32)
    invf[:, 0] = inv_freq[np.arange(32) % 16]
    invf[:, 1] = np.where(np.arange(32) < 16, -1.0, 1.0)
    return np.ascontiguousarray(cmat), invf


def _prep(inp, layers):
    f = lambda a: np.ascontiguousarray(np.asarray(a, dtype=np.float32))
    cmat, invf = _consts()
    gl = [inp["mix_norm"][0], inp["mix_norm"][1], inp["ffn_norm"][0], inp["ffn_norm"][1],
          inp["mem_norm"][0], inp["mem_norm"][1]]
    gT = np.concatenate([np.asarray(g, np.float32).reshape(16, 128).T for g in gl], axis=1)
    common = {
        "mem": f(inp["mem"][0]), "cmat": cmat, "gT": f(gT),
        "gfin": f(np.broadcast_to(np.asarray(inp["final_norm"], np.float32)[None, :], (128, DM))),
        "wkv": f(inp["w_mem_kv"]), "wg": f(inp["w_gate"]), "wu": f(inp["w_up"]), "wd": f(inp["w_down"]),
    }
    if 0 in layers:
        w = np.asarray(inp["attn_w_in"][0], np.float32)
        qc = lambda h: w[:, h * 128:(h + 1) * 128]
        kc = lambda h: w[:, 1536 + h * 128:1536 + (h + 1) * 128]
        vc = lambda h: w[:, 3072 + h * 128:3072 + (h + 1) * 128]
        mc = lambda m: w[:, 4608 + m * 128:4608 + (m + 1) * 128]
        cols = []
        for s in range(4):
            cols += [qc(s), qc(4 + s), qc(8 + s), kc(s), kc(4 + s), kc(8 + s), mc(s), vc(s), vc(4 + s), vc(8 + s)]
        for gi in (2, 1, 0):
            cols += [w[:, 1536 + gi * 512:1536 + (gi + 1) * 512], w[:, 3072 + gi * 512:3072 + (gi + 1) * 512]]
        common["w0"] = np.ascontiguousarray(np.concatenate(cols, axis=1))
        common["wo0"] = f(inp["attn_w_out"][0])
        common["invf"] = invf
    if 1 in layers:
        common["w1"] = f(inp["sgu_w_in"][0])
        common["wo1"] = f(inp["sgu_w_out"][0])
        common["lng"] = f(np.broadcast_to(np.asarray(inp["sgu_ln_g"][0], np.float32)[None, :], (128, 1536)))
        common["lnb"] = f(np.broadcast_to(np.asarray(inp["sgu_ln_b"][0], np.float32)[None, :], (128, 1536)))
        common["bsp"] = f(np.broadcast_to(np.asarray(inp["sgu_b_spatial"][0], np.float32).reshape(1, 1536), (128, 1536)))
        common["wst"] = f(np.asarray(inp["sgu_w_spatial"][0], np.float32).transpose(2, 0, 1).reshape(128, 1536))
    return common


def _run(inp, x2, layers, final, stop=None, ncores=NCORES):
    nc = _get_nc(layers, final, stop)
    common = _prep(inp, layers)
    pos = np.asarray(inp["positions"][0], np.int32)
    in_maps = []
    for c in range(ncores):
        T0 = c * TOK
        m = dict(common)
        m["xo"] = np.ascontiguousarray(x2[T0:T0 + TOK])
        if 0 in layers:
            idx = _halo_idx(T0)
            ok = idx >= 0
            ic = np.clip(idx, 0, None)
            xh = x2[ic].copy()
            xh[~ok] = 0.0
            m["xh"] = xh
            pp = np.concatenate([pos[ic], pos[T0:T0 + TOK]]).astype(np.int32)
            m["posr"] = np.ascontiguousarray(np.broadcast_to(pp[None, :], (32, HALO + TOK)))
            m["valid"] = np.ascontiguousarray(ok.astype(np.float32).reshape(21, 128).T)
        in_maps.append(m)
    res = run_bass_kernel_spmd(nc, in_maps, core_ids=list(range(ncores)))
    return np.concatenate([r["y"] for r in res.results], axis=0)


def kernel(**inp):
    x2 = np.ascontiguousarray(np.asarray(inp["x"], np.float32)[0])
    out = _run(inp, x2, (0, 1), True)
    return out.reshape(1, NCORES * TOK, DM).astype(np.float32)
```

```python
import numpy as np
from contextlib import ExitStack
import concourse.bass as bass
import concourse.mybir as mybir
from concourse.bass_utils import run_bass_kernel_spmd

F32 = mybir.dt.float32
BF16 = mybir.dt.bfloat16
I32 = mybir.dt.int32
AF = mybir.ActivationFunctionType
ALU = mybir.AluOpType

NCORES = 8
TOK = 1024
DM = 2048
FF = 5632
NS = 4
HALO = 2688
ENGS = ("pe", "act", "dve", "pool", "sp")
HALO_BATCHES = [(2 * i, 2, 2) for i in range(8)] + [(16, 2, 1), (18, 2, 1), (20, 1, 0)]
HALO_COL0 = {2: 0, 1: 2048, 0: 2560}


class Prog:
    def __init__(self, nc):
        self.nc = nc
        self.q = {e: [] for e in ENGS}
        self.semcnt = {}
        self.waited = {e: {} for e in ENGS}

    def op(self, eng, fn):
        ent = {"fn": fn, "inc": None}
        self.q[eng].append(ent)
        return ent

    def inc(self, ent, sem, amt=1):
        assert ent["inc"] is None
        self.semcnt[sem] = self.semcnt.get(sem, 0) + amt
        ent["inc"] = (sem, amt)
        return (sem, self.semcnt[sem])

    def wait(self, eng, ev):
        if ev is None:
            return
        if isinstance(ev, list):
            for e in ev:
                self.wait(eng, e)
            return
        sem, val = ev
        w = self.waited[eng]
        if w.get(sem, 0) >= val:
            return
        w[sem] = val
        self.q[eng].append({"wait": (sem, val)})

    def build(self):
        nc = self.nc
        with ExitStack() as es:
            sems = {}
            for name in self.semcnt:
                sems[name] = es.enter_context(nc.semaphore(name))
            block = es.enter_context(nc.Block())

            def replay(engname):
                def body(eng):
                    for ent in self.q[engname]:
                        if "wait" in ent:
                            s, v = ent["wait"]
                            eng.wait_ge(sems[s], v)
                        else:
                            ins = ent["fn"](eng)
                            if ent["inc"] is not None:
                                s, a = ent["inc"]
                                ins.then_inc(sems[s], a)
                return body

            for name, meth in (("sp", block.sync), ("pe", block.tensor), ("act", block.scalar),
                               ("dve", block.vector), ("pool", block.gpsimd)):
                if self.q[name]:
                    meth(replay(name))


def build(layers=(0, 1), final=True, stop=None):
    nc = bass.Bass("TRN2", target_bir_lowering=False)
    P = Prog(nc)

    def din(name, shape, dt=F32):
        return nc.dram_tensor(name, list(shape), dt, kind="ExternalInput").ap()

    xo = din("xo", [TOK, DM])
    if 0 in layers:
        xh = din("xh", [HALO, DM])
        posr = din("posr", [32, HALO + TOK], I32)
        valid = din("valid", [128, 21])
        w0 = din("w0", [DM, 8192])
        wo0 = din("wo0", [1024, DM])
        invf = din("invf", [32, 2])
    if 1 in layers:
        w1 = din("w1", [DM, 3584])
        wo1 = din("wo1", [DM, DM])
        lng_d = din("lng", [128, 1536])
        lnb_d = din("lnb", [128, 1536])
        bsp_d = din("bsp", [128, 1536])
        wst_d = din("wst", [128, 1536])
    memd = din("mem", [256, DM])
    cmat_d = din("cmat", [128, 672])
    gT_d = din("gT", [128, 96])
    gfin_d = din("gfin", [128, DM])
    wkv = din("wkv", [2, DM, 1024])
    wg = din("wg", [2, DM, FF])
    wu = din("wu", [2, DM, FF])
    wd = din("wd", [2, FF, DM])
    y = nc.dram_tensor("y", [TOK, DM], F32, kind="ExternalOutput").ap()

    es = ExitStack()
    sb = lambda name, shape, dt: es.enter_context(nc.sbuf_tensor(name, list(shape), dt))
    xs = sb("xs", [128, 8, DM], F32)
    ring = [sb(f"ring{i}", [128, 4096], BF16) for i in range(NS)]
    hT = sb("hT", [128, 16, TOK], BF16)
    am = sb("am", [128, 8192], BF16)
    mixT = am
    actT = am[:].rearrange("p (a b) -> p a b", b=TOK)
    xt_am = [am[:, i * 4096:(i + 1) * 4096].bitcast(F32) for i in range(2)]
    xt = list(xt_am)
    xn = [sb("xn0", [128, DM], BF16)] * 2
    ph = sb("ph", [128, 11264], BF16)
    cmat = sb("cmat_s", [128, 672], BF16)
    gT = sb("gT_s", [128, 96], F32)
    stat = sb("stat", [128, 64], F32)
    misc = sb("misc", [128, 12 * 1024], BF16)
    ptS = [sb(f"pt{i}", [128, 512], BF16) for i in range(2)]
    pmS = [sb(f"pm{i}", [128, 512], BF16) for i in range(2)]
    memKT = sb("memKT", [128, 4, 256], BF16)
    memV = sb("memV", [128, 2, 512], BF16)
    sgt = ptS
    psall = es.enter_context(nc.psum_tensor("psall", [128, 8, 512], F32))

    ident = cmat[:, 0:128]
    M_own = cmat[:, 128:256]
    M_prev = cmat[:, 256:384]
    M_B3 = cmat[:, 384:512]
    ones = cmat[:, 512:640]
    Rsw = cmat[:, 640:672]

    def bank(b):
        return psall[:, b, :]

    def bank_bf(b):
        return psall[:, b, :].bitcast(BF16)

    bfree = [None] * 8

    def A(fn, waits=None):
        P.wait("act", waits)
        return P.inc(P.op("act", fn), "sA")

    def V(fn, waits=None):
        P.wait("dve", waits)
        return P.inc(P.op("dve", fn), "sV")

    def T(fn, waits=None, sig=False):
        P.wait("pe", waits)
        e = P.op("pe", fn)
        return P.inc(e, "sT") if sig else None

    def SPD(fn, sem, waits=None):
        P.wait("sp", waits)
        return P.inc(P.op("sp", fn), sem, 16)

    def misc_take(off_bytes, nbytes, dt, parts=128):
        a = misc[0:parts, off_bytes // 2:(off_bytes + nbytes) // 2]
        return a if dt == BF16 else a.bitcast(dt)

    class Ring:
        def __init__(self):
            self.free = [None] * NS
            self.n = 0

        def load(self, src3, nk, ncol):
            s = self.n % NS
            self.n += 1
            P.wait("pool", self.free[s])
            view = ring[s][:, 0:nk * ncol].rearrange("p (k c) -> p k c", c=ncol)
            e = P.op("pool", lambda g, view=view, src3=src3: g.dma_start(out=view, in_=src3))
            ev = P.inc(e, f"wld{s}", 16)
            return s, view, ev

        def release(self, s, ev):
            self.free[s] = ev

    R = Ring()

    def wpiece(w2d, row0, nk, col0, ncol):
        src = w2d[row0:row0 + nk * 128, col0:col0 + ncol].rearrange("(k p) c -> p k c", p=128)
        return R.load(src, nk, ncol)

    e = P.op("pool", lambda g: g.dma_start(out=cmat[:], in_=cmat_d))
    ev_c = P.inc(e, "cld", 16)
    ev_g = SPD(lambda q: q.dma_start(out=gT[:], in_=gT_d), "gld1")
    for en in ("pe", "dve", "act"):
        P.wait(en, ev_c)
    P.wait("dve", ev_g)

    eps_t = sb("eps_t", [128, 2], F32)
    eps_ap = eps_t[:, 0:1]
    lneps_ap = eps_t[:, 1:2]
    V(lambda v: v.memset(eps_t[:, 0:1], 1e-6))
    ev_eps = V(lambda v: v.memset(eps_t[:, 1:2], 1e-5))
    P.wait("act", ev_eps)

    stat_i = [0]

    def stat_col():
        i = stat_i[0] % 64
        stat_i[0] += 1
        return stat[:, i:i + 1]

    tp_i = [0]
    xn_free = [None]
    pend_rot = [None]

    def flush_rot():
        if pend_rot[0] is not None:
            f = pend_rot[0]
            pend_rot[0] = None
            f()
    xnb = xn[0]

    def norm_A(src, src_ev, gcol0, dst_of, dst_wait=None):
        ss = stat_col()
        sq = stat_col()
        rs = stat_col()
        ev = A(lambda a: a.activation(out=xnb[:], in_=src, func=AF.Square, accum_out=ss), [src_ev, xn_free[0]])
        ev = A(lambda a: a.activation(out=sq, in_=ss, func=AF.Ln, scale=1.0 / DM, bias=eps_ap), [ev])
        ev = A(lambda a: a.activation(out=rs, in_=sq, func=AF.Exp, scale=-0.5), [ev])
        ev_xn = A(lambda a: a.activation(out=xnb[:], in_=src, func=AF.Copy, scale=rs), [ev])
        return (ev_xn, gcol0, dst_of, dst_wait)

    def norm_B(st):
        ev_xn, gcol0, dst_of, dst_wait = st
        evd = None
        last_pe = None
        for h in range(2):
            b = 6 + (tp_i[0] % 2)
            tp_i[0] += 1
            P.wait("pe", [bfree[b], ev_xn])
            for j in range(8):
                kc = h * 8 + j
                last_pe = T(lambda t, b=b, j=j, kc=kc: t.transpose(
                    bank_bf(b)[:, j * 128:(j + 1) * 128], xnb[:, kc * 128:(kc + 1) * 128], ident), sig=(j == 7))
            flush_rot()
            gb = gT[:, gcol0 + h * 8: gcol0 + h * 8 + 8].unsqueeze(2).to_broadcast([128, 8, 128])
            dst = dst_of(h)
            evd = V(lambda v, b=b, dst=dst, gb=gb: v.tensor_tensor(
                out=dst, in0=bank_bf(b).rearrange("p (a c) -> p a c", c=128), in1=gb, op=ALU.mult),
                [last_pe, dst_wait])
            bfree[b] = evd
        xn_free[0] = last_pe
        return evd, ev_xn

    def norm_tile(src, src_ev, gcol0, dst_of, dst_wait=None):
        return norm_B(norm_A(src, src_ev, gcol0, dst_of, dst_wait))

    xt_free = [None, None]
    xt_n = [0]

    def load_xtile(src_rows):
        i = xt_n[0] % 2
        xt_n[0] += 1
        dst = xt[i]
        ev = SPD(lambda q, dst=dst, src_rows=src_rows: q.dma_start(out=dst, in_=src_rows), f"xld{i}", [xt_free[i]])
        return i, ev

    pj = [0]

    def proj_fm(wview, c0, rhs_of, nk, n, evac, waits=None, banks=(0, 1), oshape=None):
        b = banks[pj[0] % len(banks)]
        pj[0] += 1
        P.wait("pe", [bfree[b], waits])
        out = bank(b)[:, 0:n]
        if oshape is not None:
            out = oshape(out)
        ev = None
        for k in range(nk):
            ev = T(lambda t, k=k: t.matmul(out, wview[:, k, c0:c0 + 128], rhs_of(k),
                                            start=(k == 0), stop=(k == nk - 1)), sig=(k == nk - 1))
        flush_rot()
        r = evac(b, ev)
        if callable(r):
            pend_rot[0] = r
        else:
            bfree[b] = r
        return ev

    def proj_tm(lhs_of, wview, c0, n, nk, evac, waits=None, banks=(4, 5)):
        b = banks[pj[0] % len(banks)]
        pj[0] += 1
        P.wait("pe", [bfree[b], waits])
        ev = None
        for k in range(nk):
            ev = T(lambda t, b=b, k=k: t.matmul(bank(b)[:, 0:n], lhs_of(k), wview[:, k, c0:c0 + n],
                                                 start=(k == 0), stop=(k == nk - 1)), sig=(k == nk - 1))
        flush_rot()
        bfree[b] = evac(b, ev)
        return ev

    def mem_kv(layer, memT):
        evd = None
        for mt in range(2):
            i, evl = load_xtile(memd[mt * 128:(mt + 1) * 128, :])
            evd, evr = norm_tile(xt[i], evl, 64 + 16 * layer,
                                 lambda h, mt=mt: memT[:, h * 8:(h + 1) * 8, mt * 128:(mt + 1) * 128])
            xt_free[i] = evr
        last = None
        for pc in range(4):
            s, wv, evw = wpiece(wkv[layer], 0, 16, pc * 256, 256)
            if pc < 2:
                for c in range(2):
                    m = pc * 2 + c
                    last = proj_fm(wv, c * 128, lambda k: memT[:, k, :], 16, 256,
                                   lambda b, ev, m=m: A(lambda a: a.copy(out=memKT[:, m, :], in_=bank(b)[:, 0:256]), [ev]),
                                   waits=[evw, evd])
            else:
                for mt in range(2):
                    cc = (pc - 2) * 256
                    last = proj_tm(lambda k, mt=mt: memT[:, k, mt * 128:(mt + 1) * 128], wv, 0, 256, 16,
                                   lambda b, ev, mt=mt, cc=cc: A(lambda a: a.copy(out=memV[:, mt, cc:cc + 256], in_=bank(b)[:, 0:256]), [ev]),
                                   waits=[evw, evd])
            R.release(s, last)
        return last

    pt_free = [None, None]
    pm_free = [None, None]

    def recip_act(dst, src, waits):
        e = A(lambda a: a.activation(out=dst, in_=src, func=AF.Ln), waits)
        return A(lambda a: a.activation(out=dst, in_=dst, func=AF.Exp, scale=-1.0), [e])

    pv_i = [0]
    rd_free = [None]

    def mem_attn_multi(groups, rden, pairs=((4, 6), (5, 7))):
        units = []
        for gi_, (m, q_ap, dst, q_ev) in enumerate(groups):
            accb, denb = pairs[gi_ % len(pairs)]
            for mt in range(2):
                units.append((m, q_ap, dst, q_ev, accb, denb, mt))

        def stage1(u):
            m, q_ap, dst, q_ev, accb, denb, mt = u
            i = pv_i[0] % 2
            pv_i[0] += 1
            sbk = 2 + i
            P.wait("pe", [bfree[sbk], q_ev])
            evs = T(lambda t: t.matmul(bank(sbk), memKT[:, m, mt * 128:(mt + 1) * 128], q_ap, start=True, stop=True), sig=True)
            eve = A(lambda a: a.activation(out=ptS[i][:], in_=bank(sbk), func=AF.Exp, scale=128 ** -0.5), [evs, pt_free[i]])
            bfree[sbk] = eve

            def stage2():
                if mt == 0:
                    P.wait("pe", [bfree[accb], bfree[denb]])
                T(lambda t: t.matmul(bank(accb), memV[:, mt, m * 128:(m + 1) * 128], ptS[i][:],
                                     start=(mt == 0), stop=(mt == 1)), waits=[eve])
                evp = T(lambda t: t.matmul(bank(denb), ones, ptS[i][:], start=(mt == 0), stop=(mt == 1)), sig=True)
                pt_free[i] = evp
                if mt == 1:
                    ev1 = recip_act(rden[:, 0:512], bank(denb), [evp, rd_free[0]])
                    ev2 = V(lambda v: v.tensor_tensor(out=dst, in0=bank(accb), in1=rden[:, 0:512], op=ALU.mult), [ev1])
                    rd_free[0] = ev2
                    bfree[accb] = ev2
                    bfree[denb] = ev1
                    return ev2
                return evp
            return stage2

        pend = None
        last = None
        for u in units:
            s2 = stage1(u)
            if pend is not None:
                last = pend()
            pend = s2
        last = pend()
        return last

    hT_free = [None]

    def ffn(layer, x_ready, tile_cb=None, evh_pre=None, pre_tiles=(), pre_evh=None):
        evh = evh_pre
        for t in range(8 if evh_pre is None else 0):
            if t in pre_tiles:
                continue
            evh, _ = norm_tile(xs[:, t, :], x_ready[t], 32 + 16 * layer,
                               lambda h, t=t: hT[:, h * 8:(h + 1) * 8, t * 128:(t + 1) * 128], dst_wait=hT_free[0])
        if pre_evh:
            evh = [evh, pre_evh]
        groups = [(0, 8), (8, 8), (16, 8), (24, 8), (32, 8), (40, 4)]
        xev = list(x_ready)
        act_free = x_ready
        gu_i = 0
        dn_i = 0
        sg_free = [None, None]
        evpu = None
        for (f0, nf) in groups:
            ev_act_last = None
            for pr in range(nf // 2):
                col = (f0 + 2 * pr) * 128
                sg_, wgv, evg = wpiece(wg[layer], 0, 16, col, 256)
                su_, wuv, evu = wpiece(wu[layer], 0, 16, col, 256)
                for c in range(2):
                    fi = 2 * pr + c
                    for th in range(2):
                        bg = 0 + 2 * (gu_i % 2)
                        bu = 1 + 2 * (gu_i % 2)
                        si = gu_i % 2
                        gu_i += 1
                        P.wait("pe", [bfree[bg], bfree[bu], evg, evu, evh])
                        evpg = None
                        for k in range(16):
                            evpg = T(lambda t, bg=bg, k=k, c=c, th=th, wgv=wgv: t.matmul(
                                bank(bg), wgv[:, k, c * 128:(c + 1) * 128], hT[:, k, th * 512:(th + 1) * 512],
                                start=(k == 0), stop=(k == 15)), sig=(k == 15))
                        for k in range(16):
                            evpu = T(lambda t, bu=bu, k=k, c=c, th=th, wuv=wuv: t.matmul(
                                bank(bu), wuv[:, k, c * 128:(c + 1) * 128], hT[:, k, th * 512:(th + 1) * 512],
                                start=(k == 0), stop=(k == 15)), sig=(k == 15))
                        evs = A(lambda a, bg=bg, si=si: a.activation(out=sgt[si][:], in_=bank(bg), func=AF.Silu),
                                [evpg, sg_free[si]])
                        bfree[bg] = evs
                        dst = actT[:, fi, th * 512:(th + 1) * 512]
                        evm = V(lambda v, bu=bu, si=si, dst=dst: v.tensor_tensor(
                            out=dst, in0=bank(bu), in1=sgt[si][:], op=ALU.mult), [evs, evpu, act_free])
                        bfree[bu] = evm
                        sg_free[si] = evm
                        ev_act_last = evm
                R.release(sg_, evpu)
                R.release(su_, evpu)
            lastdn = None
            if tile_cb is not None and f0 == 40:
                pcs = [wpiece(wd[layer], f0 * 128, nf, jp * 1024, 1024) for jp in range(2)]
                lastpe = None
                for t in range(8):
                    for j in range(4):
                        sd_, wdv, evd = pcs[j // 2]
                        jh = j % 2
                        b = 4 + (dn_i % 2)
                        dn_i += 1
                        P.wait("pe", [bfree[b], evd, ev_act_last])
                        for f in range(nf):
                            lastpe = T(lambda tt, b=b, f=f, t=t, wdv=wdv, jh=jh: tt.matmul(
                                bank(b), actT[:, f, t * 128:(t + 1) * 128], wdv[:, f, jh * 512:(jh + 1) * 512],
                                start=(f == 0), stop=(f == nf - 1)), sig=(f == nf - 1))
                        xsl = xs[:, t, j * 512:(j + 1) * 512]
                        eva = V(lambda v, b=b, xsl=xsl: v.tensor_tensor(out=xsl, in0=bank(b), in1=xsl, op=ALU.add),
                                [lastpe, xev[t]])
                        bfree[b] = eva
                        xev[t] = eva
                        lastdn = eva
                    tile_cb(t, xev[t], evpu)
                for jp in range(2):
                    R.release(pcs[jp][0], lastpe)
                act_free = lastdn
                continue
            for j in range(4):
                sd_, wdv, evd = wpiece(wd[layer], f0 * 128, nf, j * 512, 512)
                lastpe = None
                for t in range(8):
                    b = 4 + (dn_i % 2)
                    dn_i += 1
                    P.wait("pe", [bfree[b], evd, ev_act_last])
                    for f in range(nf):
                        lastpe = T(lambda tt, b=b, f=f, t=t, wdv=wdv: tt.matmul(
                            bank(b), actT[:, f, t * 128:(t + 1) * 128], wdv[:, f, :],
                            start=(f == 0), stop=(f == nf - 1)), sig=(f == nf - 1))
                    xsl = xs[:, t, j * 512:(j + 1) * 512]
                    eva = V(lambda v, b=b, xsl=xsl: v.tensor_tensor(out=xsl, in0=bank(b), in1=xsl, op=ALU.add),
                            [lastpe, xev[t]])
                    bfree[b] = eva
                    xev[t] = eva
                    lastdn = eva
                R.release(sd_, lastpe)
            act_free = lastdn
        hT_free[0] = evpu
        return xev

    def out_proj(wout, nk, catT_of, toks, x_evs, cat_ev):
        xev = dict((t, x_evs[t]) for t in toks)
        for j in range(8):
            s_, wv, evw = wpiece(wout, 0, nk, j * 256, 256)
            lastpe = None
            for t in toks:
                def evac(b, ev, t=t, j=j):
                    dst = xs[:, t, j * 256:(j + 1) * 256]
                    e2 = V(lambda v: v.tensor_tensor(out=dst, in0=bank(b)[:, 0:256], in1=dst, op=ALU.add), [ev, xev[t]])
                    xev[t] = e2
                    return e2
                lastpe = proj_tm(lambda k, t=t: catT_of(k, t), wv, 0, 256, nk, evac, waits=[evw, cat_ev],
                                 banks=(0, 1, 2, 3))
            R.release(s_, lastpe)
        return xev

    def out_proj_t(wout, nk, catT_of, toks, x_evs, cat_ev, tile_cb):
        xev = dict((t, x_evs[t]) for t in toks)
        pcs = [wpiece(wout, 0, nk, j * 512, 512) for j in range(4)]
        lastpe = None
        for t in toks:
            for j in range(4):
                s_, wv, evw = pcs[j]

                def evac(b, ev, t=t, j=j):
                    dst = xs[:, t, j * 512:(j + 1) * 512]
                    e2 = V(lambda v: v.tensor_tensor(out=dst, in0=bank(b), in1=dst, op=ALU.add), [ev, xev[t]])
                    xev[t] = e2
                    return e2
                lastpe = proj_tm(lambda k, t=t: catT_of(k, t), wv, 0, 512, nk, evac, waits=[evw, cat_ev], banks=(0, 1, 2, 3))
            tile_cb(t, xev[t])
        for j in range(4):
            R.release(pcs[j][0], lastpe)
        return xev

    x_ready = [None] * 8
    if 0 in layers:
        xsb = xs[:].rearrange("p a b -> p (a b)").bitcast(BF16)
        KTh = xsb[:, 0:10752].rearrange("p (h t) -> p h t", t=HALO)
        Vh = xsb[:, 10752:21504].rearrange("p (t c) -> p t c", c=512)
        QT = xsb[:, 21504:24576].rearrange("p (g t) -> p g t", t=TOK)
        KTo = xsb[:, 24576:27648].rearrange("p (g t) -> p g t", t=TOK)
        Vo = xsb[:, 27648:29696].rearrange("p (g j c) -> p g j c", j=8, c=128)
        Vo3 = xsb[:, 29696:31744].rearrange("p (r c) -> p r c", c=128)
        qmT = xsb[:, 31744:32768]
        xt[0] = xsb[:, 21504:25600].bitcast(F32)
        xt[1] = xsb[:, 25600:29696].bitcast(F32)
        qnb = [ph[:, 7168:7680], ph[:, 7680:8192]]
        attnT = am[:].rearrange("p (k t) -> p k t", t=TOK)
        Co = misc_take(0, 4096, F32, 32)
        So = misc_take(4096, 4096, F32, 32)
        Ch = misc_take(8192, 2048, F32, 32)
        Sh = misc_take(10240, 2048, F32, 32)
        ChB = [Ch, ph[0:32, 9216:10240].bitcast(F32)]
        ShB = [Sh, ph[0:32, 10240:11264].bitcast(F32)]
        tab_last_use = [None, None]
        t1 = misc_take(12288, 2048, F32, 32)
        t2 = misc_take(14336, 2048, F32, 32)
        onesv = misc_take(16384, 5376, BF16).rearrange("p (t c) -> p t c", c=128)
        angb = ph[0:32, 0:1024].bitcast(F32)
        kb_i = ph[0:32, 1024:2048].bitcast(I32)
        kb_f = ph[0:32, 2048:3072].bitcast(F32)
        cmpb = ph[0:32, 3072:4096].bitcast(F32)
        posb = ph[0:32, 4096:5120].bitcast(I32)
        rden = ph[:, 5120:7168].bitcast(F32)
        invf_s = sb("invf_s", [32, 2], F32)
        valid_s = sb("valid_s", [128, 21], F32)
        memT0 = hT[:].rearrange("p a b -> p (a b)")[:, 0:4096].rearrange("p (k t) -> p k t", t=256)

        ev_if = SPD(lambda q: q.dma_start(out=invf_s[:], in_=invf), "gld2")
        ev_vl = SPD(lambda q: q.dma_start(out=valid_s[:], in_=valid), "gld3")

        mem_last = mem_kv(0, memT0)
        ev_ov = V(lambda v: v.tensor_copy(out=onesv, in_=valid_s[:].unsqueeze(2).to_broadcast([128, 21, 128])), [ev_vl])

        TWO_PI = float(2.0 * np.pi)
        C1 = 6.28125
        C2 = float(2.0 * np.pi - 6.28125)
        tab_chain = [None]

        def rot_tables_p1(pcol0, n, Cdst, Sdst, waits):
            ang = angb[:, 0:n]
            ki = kb_i[:, 0:n]
            kf = kb_f[:, 0:n]
            cm = cmpb[:, 0:n]
            evp = SPD(lambda q: q.dma_start(out=posb[:, 0:n], in_=posr[:, pcol0:pcol0 + n]), "pld", [tab_chain[0]])
            ev = V(lambda v: v.tensor_scalar(out=ang, in0=posb[:, 0:n], scalar1=invf_s[:, 0:1], scalar2=None, op0=ALU.mult),
                   [waits, evp, ev_if])
            tab_chain[0] = ev
            for which, dst in ((0, Sdst), (1, Cdst)):
                src = ang
                if which == 1:
                    ev = V(lambda v: v.tensor_scalar(out=cm, in0=ang, scalar1=float(np.pi / 2), scalar2=None, op0=ALU.add), [ev])
                    src = cm
                ev = V(lambda v, src=src: v.tensor_scalar(out=ki, in0=src, scalar1=float(1.0 / TWO_PI), scalar2=None, op0=ALU.mult), [ev])
                ev = V(lambda v: v.tensor_copy(out=kf, in_=ki), [ev])
                ev = V(lambda v, src=src, dst=dst: v.scalar_tensor_tensor(out=dst, in0=kf, scalar=-C1, in1=src, op0=ALU.mult, op1=ALU.add), [ev])
                ev = V(lambda v, dst=dst: v.scalar_tensor_tensor(out=dst, in0=kf, scalar=-C2, in1=dst, op0=ALU.mult, op1=ALU.add), [ev])
                ev = V(lambda v, dst=dst: v.tensor_scalar(out=dst, in0=dst, scalar1=3.14159, scalar2=-3.14159, op0=ALU.min, op1=ALU.max), [ev])
            return (ev, Cdst, Sdst)

        def rot_tables_p2(st):
            ev, Cdst, Sdst = st
            e1 = A(lambda a: a.activation(out=Sdst, in_=Sdst, func=AF.Sin), [ev])
            e2 = A(lambda a: a.activation(out=Cdst, in_=Cdst, func=AF.Sin), [e1])
            e3 = V(lambda v: v.tensor_scalar(out=Sdst, in0=Sdst, scalar1=invf_s[:, 1:2], scalar2=None, op0=ALU.mult), [e1])
            P.wait("dve", e2)
            return [e2, e3]

        def rot_tables(pcol0, n, Cdst, Sdst, waits):
            return rot_tables_p2(rot_tables_p1(pcol0, n, Cdst, Sdst, waits))

        rot_i = [0]
        rot_free = [None]

        def rotary_evac(b, ev_pe, dst, Ct, St, tab_ev, n, shp=None):
            f = shp if shp is not None else (lambda a: a)
            ev_c = A(lambda a: a.copy(out=dst, in_=bank(b)[:, 0:n]), [ev_pe])

            def part2():
                rb = 2 + (rot_i[0] % 2)
                rot_i[0] += 1
                P.wait("pe", bfree[rb])
                ev_r = T(lambda t: t.matmul(bank(rb)[0:32, 0:n], Rsw, dst, start=True, stop=True), waits=[ev_c], sig=True)
                e1 = V(lambda v: v.tensor_tensor(out=f(t1[:, 0:n]), in0=f(bank(rb)[0:32, 0:n]), in1=St, op=ALU.mult),
                       [ev_r, tab_ev, rot_free[0]])
                bfree[rb] = e1
                e2 = V(lambda v: v.tensor_tensor(out=f(t2[:, 0:n]), in0=f(bank(b)[0:32, 0:n]), in1=Ct, op=ALU.mult), [ev_c])
                e3 = V(lambda v: v.tensor_tensor(out=dst[0:32], in0=t1[:, 0:n], in1=t2[:, 0:n], op=ALU.add), [e1, e2])
                rot_free[0] = e3
                bfree[b] = e3
            return part2

        ev_tabo_h = [None]

        hTb = [am[:, i * 4096:(i + 1) * 4096].rearrange("p (k t) -> p k t", t=256) for i in range(2)]
        hb_free = [mem_last, mem_last]
        batch_evd = {}
        batch_tab = {}
        held = {}

        def halo_norm_A(bi, tt):
            tile0, ntile, gi = HALO_BATCHES[bi]
            hb = hTb[bi % 2]
            i, evl = load_xtile(xh[(tile0 + tt) * 128:(tile0 + tt + 1) * 128, :])
            st = norm_A(xt[i], evl, 0,
                        lambda h, tt=tt, hb=hb: hb[:, h * 8:(h + 1) * 8, tt * 128:(tt + 1) * 128],
                        dst_wait=hb_free[bi % 2])
            xt_free[i] = st[0]
            return (bi, st)

        def halo_norm_B(pst):
            bi, st = pst
            evd, evr = norm_B(st)
            batch_evd[bi] = evd

        def halo_norm(bi, tt):
            halo_norm_B(halo_norm_A(bi, tt))

        tab_st = {}

        def halo_tables_p1(bi):
            tile0, ntile, gi = HALO_BATCHES[bi]
            n = ntile * 128
            tab_st[bi] = rot_tables_p1(tile0 * 128, n, ChB[bi % 2][:, 0:n], ShB[bi % 2][:, 0:n], tab_last_use[bi % 2])

        def halo_tables_p2(bi):
            batch_tab[bi] = rot_tables_p2(tab_st[bi])

        def halo_tables(bi):
            halo_tables_p1(bi)
            halo_tables_p2(bi)

        def halo_piece(bi, pc):
            tile0, ntile, gi = HALO_BATCHES[bi]
            hb = hTb[bi % 2]
            n = ntile * 128
            evd = batch_evd[bi]
            kcol = 5120 + (2 - gi) * 1024
            if (gi, pc) not in held:
                held[(gi, pc)] = wpiece(w0, 0, 16, kcol + pc * 256, 256)
            s_, wv, evw = held[(gi, pc)]
            last_of_group = (bi + 1 >= len(HALO_BATCHES)) or (HALO_BATCHES[bi + 1][2] != gi)
            lastpe = None
            if pc < 2:
                ev_tab = batch_tab[bi]
                for c in range(2):
                    hd = pc * 2 + c
                    dst = KTh[:, hd, tile0 * 128: tile0 * 128 + n]
                    lastpe = proj_fm(wv, c * 128, lambda k, hb=hb, n=n: hb[:, k, 0:n], 16, n,
                                     lambda b, ev, dst=dst, n=n, ev_tab=ev_tab, bi=bi: rotary_evac(b, ev, dst, ChB[bi % 2][:, 0:n], ShB[bi % 2][:, 0:n], ev_tab, n),
                                     waits=[evw, evd])
            else:
                cc = (pc - 2) * 256
                for tt in range(ntile):
                    dst = Vh[:, tile0 + tt, cc:cc + 256]
                    lastpe = proj_tm(lambda k, hb=hb, tt=tt: hb[:, k, tt * 128:(tt + 1) * 128], wv, 0, 256, 16,
                                     lambda b, ev, dst=dst: A(lambda a: a.copy(out=dst, in_=bank(b)[:, 0:256]), [ev]),
                                     waits=[evw, evd])
            if pc == 1:
                flush_rot()
                tab_last_use[bi % 2] = rot_free[0]
            if last_of_group:
                R.release(s_, lastpe)
            if pc == 3:
                hb_free[bi % 2] = lastpe

        for tt in range(HALO_BATCHES[0][1]):
            halo_norm(0, tt)
        halo_tables(0)
        NB_H = len(HALO_BATCHES)
        ev_hT_h = [None]

        def own_A(t):
            i, evl = load_xtile(xo[t * 128:(t + 1) * 128, :])
            st = norm_A(xt[i], evl, 0, lambda h, t=t: hT[:, h * 8:(h + 1) * 8, t * 128:(t + 1) * 128], dst_wait=mem_last)
            xt_free[i] = st[0]
            return st

        own_next = [0]
        for bi in range(NB_H):
            nxt = HALO_BATCHES[bi + 1][1] if bi + 1 < NB_H else 0
            for pc in range(4):
                pst = None
                ost = None
                if pc < nxt:
                    pst = halo_norm_A(bi + 1, pc)
                elif bi >= NB_H - 4 and own_next[0] < 8:
                    ost = own_A(own_next[0])
                    own_next[0] += 1
                halo_piece(bi, pc)
                if pst is not None:
                    halo_norm_B(pst)
                if ost is not None:
                    ev_hT_h[0], _ = norm_B(ost)
                if pc == 1 and bi + 1 < NB_H:
                    halo_tables_p1(bi + 1)
                if pc == 3 and bi + 1 < NB_H:
                    halo_tables_p2(bi + 1)
                if pc == 3 and bi == 2:
                    rot_tables(HALO, 512, Co[:, 0:512], So[:, 0:512], None)
                if pc == 3 and bi == 4:
                    ev_tabo_h[0] = rot_tables(HALO + 512, 512, Co[:, 512:1024], So[:, 512:1024], None)
        ev_tabo = ev_tabo_h[0]
        while own_next[0] < 8:
            ev_hT_h[0], _ = norm_B(own_A(own_next[0]))
            own_next[0] += 1
        ev_hT = ev_hT_h[0]
        xt_last = [xt_free[0], xt_free[1]]

        def vtile(ap2, gi, j):
            if gi == 0:
                return ap2[:, j * 128:(j + 1) * 128]
            if gi == 1:
                r, bb = j // 2, j % 2
                return ap2.rearrange("p (i r) -> p r i", r=4)[:, r, bb * 128:(bb + 1) * 128]
            return ap2.rearrange("p (i r) -> p r i", r=16)[:, j, :]

        def gtile(ap2, gi, j):
            return vtile(ap2, gi, j)

        def nat_view(a, gi):
            if gi == 0:
                return a
            return a.rearrange("p (i r) -> p i r", r=(4 if gi == 1 else 16))

        def perm_view(row, gi, th):
            if gi == 0:
                return row[:, th * 512:(th + 1) * 512]
            if gi == 1:
                return row.rearrange("p (r i) -> p i r", r=4)[:, 128 * th:128 * (th + 1), :]
            return row.rearrange("p (r i) -> p i r", r=16)[:, 32 * th:32 * (th + 1), :]

        qn_i = [0]
        qn_free = [None, None]
        vt_free = [None]
        VTs1 = ph[:, 8192:9216]

        def rotary_evac_perm(b, ev_pe, dstrow, gi, th):
            qi = qn_i[0] % 2
            qn_i[0] += 1
            qn = qnb[qi]
            Ct = Co[:, th * 512:(th + 1) * 512]
            St = So[:, th * 512:(th + 1) * 512]
            ev_c = A(lambda a: a.copy(out=qn, in_=bank(b)), [ev_pe, qn_free[qi]])
            A(lambda a: a.copy(out=perm_view(dstrow[32:64], gi, th), in_=nat_view(bank(b)[32:64, :], gi)), [ev_pe])
            ev_c2 = A(lambda a: a.copy(out=perm_view(dstrow[64:128], gi, th), in_=nat_view(bank(b)[64:128, :], gi)), [ev_pe])
            def part2():
                rb = 2 + (rot_i[0] % 2)
                rot_i[0] += 1
                P.wait("pe", bfree[rb])
                ev_r = T(lambda t: t.matmul(bank(rb)[0:32, :], Rsw, qn, start=True, stop=True), waits=[ev_c], sig=True)
                qn_free[qi] = ev_r
                e1 = V(lambda v: v.tensor_tensor(out=t1, in0=bank(rb)[0:32, :], in1=St, op=ALU.mult),
                       [ev_r, ev_tabo, rot_free[0]])
                bfree[rb] = e1
                e2 = V(lambda v: v.tensor_tensor(out=t2, in0=bank(b)[0:32, :], in1=Ct, op=ALU.mult), [ev_c])
                e3 = V(lambda v: v.tensor_tensor(out=perm_view(dstrow[0:32], gi, th), in0=nat_view(t1, gi), in1=nat_view(t2, gi),
                                                 op=ALU.add), [e1, e2, ev_c2])
                rot_free[0] = e3
                bfree[b] = e3
            return part2

        accA = psall[:, 4:6, :].rearrange("p a b -> p (a b)")
        denA = psall[:, 6:8, :].rearrange("p a b -> p (a b)")
        att_done = None
        rdA_free = [None]
        rdm = misc_take(21760, 2048, F32)

        for s in range(4):
            base = s * 1280
            cur = [None, None, None]
            lastpe_piece = [None]

            def get_piece(pcI, cur=cur, base=base, lastpe_piece=lastpe_piece):
                if cur[0] != pcI:
                    if cur[0] is not None:
                        R.release(cur[1][0], lastpe_piece[0])
                    cur[0] = pcI
                    cur[1] = wpiece(w0, 0, 16, base + pcI * 256, 256)
                return cur[1]

            PB = (0, 1, 4, 5, 6, 7)
            for c in range(7):
                s_, wv, evw = get_piece(c // 2)
                cI = c % 2
                for th in range(2):
                    if c < 6:
                        gi = c % 3
                        dst = (QT if c < 3 else KTo)[:, gi, th * 512:(th + 1) * 512]
                        ev = proj_fm(wv, cI * 128, lambda k, th=th: hT[:, k, th * 512:(th + 1) * 512], 16, 512,
                                     lambda b, ev, dst=dst, th=th: rotary_evac(b, ev, dst, Co[:, th * 512:(th + 1) * 512],
                                                                               So[:, th * 512:(th + 1) * 512], ev_tabo, 512),
                                     waits=[evw, ev_hT, att_done], banks=PB)
                    else:
                        dst = qmT[:, th * 512:(th + 1) * 512]
                        ev = proj_fm(wv, cI * 128, lambda k, th=th: hT[:, k, th * 512:(th + 1) * 512], 16, 512,
                                     lambda b, ev, dst=dst: A(lambda a: a.copy(out=dst, in_=bank(b)), [ev]),
                                     waits=[evw, ev_hT, att_done], banks=PB)
                    lastpe_piece[0] = ev
            for gi in range(3):
                c = 7 + gi
                s_, wv, evw = get_piece(c // 2)
                cI = c % 2
                vrow = VTs1
                evv = []
                for th in range(2):
                    def evac_v(b, ev, vrow=vrow, th=th):
                        return A(lambda a: a.copy(out=vrow[:, th * 512:(th + 1) * 512], in_=bank(b)), [ev, vt_free[0]])
                    ev = proj_fm(wv, cI * 128, lambda k, th=th: hT[:, k, th * 512:(th + 1) * 512], 16, 512, evac_v,
                                 waits=[evw, ev_hT, att_done], banks=PB)
                    lastpe_piece[0] = ev
                    evv.append(bfree[PB[(pj[0] - 1) % len(PB)]])
                ntile = 8 if gi < 2 else 16
                tw = 128 if gi < 2 else 64
                for jb in range(ntile // 8):
                    b = 2 + (pj[0] % 2)
                    pj[0] += 1
                    P.wait("pe", [bfree[b], evv])
                    ev = None
                    for jj in range(8):
                        j = jb * 8 + jj
                        ev = T(lambda t, b=b, jj=jj, j=j, vrow=vrow, tw=tw, gi=gi: t.transpose(
                            bank_bf(b)[0:tw, jj * 128:(jj + 1) * 128], vtile(vrow, gi, j), ident), sig=(jj == 7))
                    if gi < 2:
                        dst = Vo[:, gi, :, :]
                    else:
                        dst = Vo3[0:64, jb * 8:(jb + 1) * 8, :]
                    bfree[b] = A(lambda a, b=b, dst=dst, tw=tw: a.copy(
                        out=dst, in_=bank_bf(b)[0:tw, :].rearrange("p (j c) -> p j c", c=128)), [ev])
                    vt_free[0] = ev
            flush_rot()
            R.release(cur[1][0], lastpe_piece[0])
            proj_done = [rot_free[0]] + [bfree[i] for i in range(8)]

            ev_z1 = V(lambda v: v.memset(accA, 0.0), [bfree[4], bfree[5]])
            ev_z2 = V(lambda v: v.memset(denA, 0.0), [bfree[6], bfree[7]])
            P.wait("pe", [ev_z1, ev_z2, proj_done, ev_ov])

            def g_attend(items, mask_ops, nk=128):
                def stage1():
                    i = pv_i[0] % 2
                    pv_i[0] += 1
                    sbk = 2 + i
                    P.wait("pe", bfree[sbk])
                    c = 0
                    offs = []
                    evs = None
                    for n_it, (k_ap, q_ap, nq, pvs) in enumerate(items):
                        evs = T(lambda t, c=c, nq=nq, k_ap=k_ap, q_ap=q_ap: t.matmul(
                            bank(sbk)[0:nk, c:c + nq], k_ap, q_ap, start=True, stop=True), sig=(n_it == len(items) - 1))
                        offs.append(c)
                        c += nq
                    tot = c
                    eve = A(lambda a: a.activation(out=ptS[i][0:nk, 0:tot], in_=bank(sbk)[0:nk, 0:tot],
                                                   func=AF.Exp, scale=128 ** -0.5), [evs, pt_free[i]])
                    bfree[sbk] = eve
                    evm = None
                    for (c0, ncl, in1_ap, inner) in mask_ops:
                        o = pmS[i][0:nk, c0:c0 + ncl].rearrange("p (a b) -> p a b", b=inner)
                        a_in = ptS[i][0:nk, c0:c0 + ncl].rearrange("p (a b) -> p a b", b=inner)
                        evm = V(lambda v, o=o, a_in=a_in, in1_ap=in1_ap: v.tensor_tensor(out=o, in0=a_in, in1=in1_ap, op=ALU.mult),
                                [eve, pm_free[i]])
                    pt_free[i] = evm

                    def stage2():
                        evp = None
                        first = True
                        for (k_ap, q_ap, nq, pvs), off in zip(items, offs):
                            for (co, ncl, v_ap, o_ap, acc_ap, den_ap) in pvs:
                                rhs = pmS[i][0:nk, off + co:off + co + ncl]
                                T(lambda t, v_ap=v_ap, rhs=rhs, acc_ap=acc_ap: t.matmul(
                                    acc_ap, v_ap, rhs, start=False, stop=False, skip_group_check=True),
                                  waits=[evm] if first else None)
                                first = False
                                evp = T(lambda t, o_ap=o_ap, rhs=rhs, den_ap=den_ap: t.matmul(
                                    den_ap, o_ap, rhs, start=False, stop=False, skip_group_check=True), sig=True)
                        pm_free[i] = evp
                        return evp
                    return stage2
                return stage1

            batches = []
            M_op = cmat[:, 128:384]
            M_po = cmat[:, 256:512]

            def g1_item(kt):
                if kt < 0:
                    k_ap, v_ap, o_ap = KTh[:, s, 2560:2688], Vh[:, 20, s * 128:(s + 1) * 128], onesv[:, 20, :]
                else:
                    k_ap, v_ap, o_ap = KTo[:, 0, kt * 128:(kt + 1) * 128], Vo[:, 0, kt, :], ones
                qlo, qhi = max(kt, 0), min(kt + 1, 7)
                nq = (qhi - qlo + 1) * 128
                pvs = [(qi * 128, 128, v_ap, o_ap, accA[:, qt * 128:(qt + 1) * 128], denA[:, qt * 128:(qt + 1) * 128])
                       for qi, qt in enumerate(range(qlo, qhi + 1))]
                return (k_ap, QT[:, 0, qlo * 128:qlo * 128 + nq], nq, pvs)

            batches.append(g_attend([g1_item(-1), g1_item(0)],
                                    [(0, 128, M_prev.unsqueeze(1), 128), (128, 256, M_op.unsqueeze(1), 256)]))
            for kt in (1, 3, 5):
                batches.append(g_attend([g1_item(kt), g1_item(kt + 1)],
                                        [(0, 512, M_op.unsqueeze(1).to_broadcast([128, 2, 256]), 256)]))
            batches.append(g_attend([g1_item(7)], [(0, 128, M_own.unsqueeze(1), 128)]))
            for r in range(4):
                items = []
                for kb in range(-1, 2):
                    if kb < 0:
                        k_ap, v_ap, o_ap = KTh[:, s, 2048 + r * 128:2048 + (r + 1) * 128], Vh[:, 16 + r, s * 128:(s + 1) * 128], onesv[:, 16 + r, :]
                    else:
                        k_ap, v_ap, o_ap = vtile(KTo[:, 1, :], 1, r * 2 + kb), Vo[:, 1, r * 2 + kb, :], ones
                    qlo, qhi = max(kb, 0), min(kb + 1, 1)
                    nq = (qhi - qlo + 1) * 128
                    pvs = [(qi * 128, 128, v_ap, o_ap, gtile(accA, 1, r * 2 + qb), gtile(denA, 1, r * 2 + qb))
                           for qi, qb in enumerate(range(qlo, qhi + 1))]
                    items.append((k_ap, QT[:, 1, :].rearrange("p (i r) -> p r i", r=4)[:, r, qlo * 128:qlo * 128 + nq], nq, pvs))
                batches.append(g_attend(items, [(0, 512, M_po.unsqueeze(1).to_broadcast([128, 2, 256]), 256)]))
            for rb in range(2):
                items = []
                for r in range(rb * 8, rb * 8 + 8):
                    accc = accA.rearrange("p (i r) -> p r i", r=16)[:, r, :]
                    denc = denA.rearrange("p (i r) -> p r i", r=16)[:, r, :]
                    items.append((KTh[:, s, r * 128:(r + 1) * 128], vtile(QT[:, 2, :], 2, r), 64,
                                  [(0, 32, Vh[:, r, s * 128:(s + 1) * 128], onesv[:, r, :], accc[:, 0:32], denc[:, 0:32]),
                                   (32, 32, Vh[:, r, s * 128:(s + 1) * 128], onesv[:, r, :], accc[:, 32:64], denc[:, 32:64])]))
                batches.append(g_attend(items, [(0, 512, M_prev[:, 0:64].unsqueeze(1).to_broadcast([128, 8, 64]), 64)]))
            for rb in range(2):
                items = []
                for r in range(rb * 8, rb * 8 + 8):
                    accc = accA.rearrange("p (i r) -> p r i", r=16)[:, r, :]
                    denc = denA.rearrange("p (i r) -> p r i", r=16)[:, r, :]
                    items.append((vtile(KTo[:, 2, :], 2, r), vtile(QT[:, 2, :], 2, r), 64,
                                  [(0, 32, Vo3[0:64, r, :], ones[0:64, :], accc[:, 0:32], denc[:, 0:32]),
                                   (32, 32, Vo3[0:64, r, :], ones[0:64, :], accc[:, 32:64], denc[:, 32:64])]))
                batches.append(g_attend(items, [(0, 512, M_own[0:64, 0:64].unsqueeze(1).to_broadcast([64, 8, 64]), 64)], nk=64))
            pend = None
            last = None
            for bt in batches:
                s2 = bt()
                if pend is not None:
                    last = pend()
                pend = s2
            last = pend()
            ev1 = recip_act(rden, denA, [last, rdA_free[0]])
            ev2 = V(lambda v, s=s: v.tensor_tensor(out=attnT[:, s, :], in0=accA, in1=rden, op=ALU.mult), [ev1, xt_last])
            rdA_free[0] = ev2
            for b in (4, 5):
                bfree[b] = ev2
            for b in (6, 7):
                bfree[b] = ev1
            att_done = [ev2, mem_attn_multi([(s, qmT[:, th * 512:(th + 1) * 512], attnT[:, 4 + s, th * 512:(th + 1) * 512], [proj_done, xt_last]) for th in range(2)], rdm, pairs=((0, 1), (4, 6)))]

        xl = []
        for t in range(8):
            xl.append(SPD(lambda q, t=t: q.dma_start(out=xs[:, t, :], in_=xo[t * 128:(t + 1) * 128, :]), f"xsl{t}", [att_done]))
        pendF = [None]
        evhF = [None]

        def cb_ffn0norm(t, ev):
            if pendF[0] is not None:
                evhF[0], _ = norm_B(pendF[0])
            pendF[0] = norm_A(xs[:, t, :], ev, 32,
                              lambda h, t=t: hT[:, h * 8:(h + 1) * 8, t * 128:(t + 1) * 128], dst_wait=att_done)
        if stop != "attn0":
            xev = out_proj_t(wo0, 8, lambda k, t: attnT[:, k, t * 128:(t + 1) * 128], list(range(8)), xl, att_done, cb_ffn0norm)
            evhF[0], _ = norm_B(pendF[0])
        else:
            xev = out_proj(wo0, 8, lambda k, t: attnT[:, k, t * 128:(t + 1) * 128], list(range(8)), xl, att_done)
        x_ready = [xev[t] for t in range(8)]
        hT_free[0] = att_done
        xt[0], xt[1] = xt_am[0], xt_am[1]
        xt_free[0] = xt_free[1] = x_ready
        evh_cb = [None]
        if stop != "attn0":
            pendB = [None]

            def cb_l1norm(t, ev, hfree):
                if pendB[0] is not None:
                    evh_cb[0], _ = norm_B(pendB[0])
                pendB[0] = norm_A(xs[:, t, :], ev, 16,
                                  lambda h, t=t: hT[:, h * 8:(h + 1) * 8, t * 128:(t + 1) * 128], dst_wait=hfree)
            x_ready = ffn(0, x_ready, cb_l1norm if (1 in layers) else None, evh_pre=evhF[0])
            if pendB[0] is not None:
                evh_cb[0], _ = norm_B(pendB[0])
            xt_free[0] = xt_free[1] = x_ready
    else:
        evh_cb = [None]
        for t in range(8):
            x_ready[t] = SPD(lambda q, t=t: q.dma_start(out=xs[:, t, :], in_=xo[t * 128:(t + 1) * 128, :]), f"xsl{t}")

    gfin = ph[:, 0:4096].bitcast(F32)
    hTf32 = hT[:].rearrange("p a b -> p (a b)").bitcast(F32)
    ystage = [hTf32[:, i * 2048:(i + 1) * 2048] for i in range(4)]
    yst_free = [None] * 4
    ev_gf_h = [None]
    final_done = [False]
    st_evs = []

    fin_pend = [None]

    def final_flush():
        if fin_pend[0] is not None:
            f = fin_pend[0]
            fin_pend[0] = None
            f()

    def final_tile(t, ev, hfree):
        final_flush()
        ss, sq, rs = stat_col(), stat_col(), stat_col()
        e = A(lambda a: a.activation(out=xnb[:], in_=xs[:, t, :], func=AF.Square, accum_out=ss), [ev, xn_free[0]])
        e_ln = A(lambda a: a.activation(out=sq, in_=ss, func=AF.Ln, scale=1.0 / DM, bias=eps_ap), [e])
        e_sq = A(lambda a: a.activation(out=rs, in_=sq, func=AF.Exp, scale=-0.5), [e_ln])
        i = t % 4

        def part2():
            e1 = e_sq
            e2 = V(lambda v: v.scalar_tensor_tensor(out=ystage[i], in0=xs[:, t, :], scalar=rs, in1=gfin,
                                                    op0=ALU.mult, op1=ALU.mult), [e1, ev_gf_h[0], yst_free[i], hfree])
            evs = SPD(lambda q: q.dma_start(out=y[t * 128:(t + 1) * 128, :], in_=ystage[i]), f"yst{i}", [e2])
            yst_free[i] = evs
            st_evs.append(evs)
        fin_pend[0] = part2

    if 1 in layers and stop != "attn0":
        lng = misc_take(0, 6144, F32)
        lnb = misc_take(6144, 6144, F32)
        bsp = misc_take(12288, 6144, F32)
        wsT = misc_take(18432, 3072, BF16).rearrange("p (g t) -> p g t", t=128)
        rden1 = misc_take(21504, 2048, F32)
        vtm = ph[:, 0:6144].rearrange("p (t c) -> p t c", c=1536)
        qm1 = ph[:, 6144:8192].rearrange("p (m t) -> p m t", t=512)
        vgf = ph[:, 8192:11264].bitcast(F32)
        memT1 = ph[:, 0:4096].rearrange("p (k t) -> p k t", t=256)
        e1 = SPD(lambda q: q.dma_start(out=lng, in_=lng_d), "gld4", x_ready)
        e2 = SPD(lambda q: q.dma_start(out=lnb, in_=lnb_d), "gld5")
        e3 = SPD(lambda q: q.dma_start(out=bsp, in_=bsp_d), "gld6")
        e4 = SPD(lambda q: q.dma_start(out=vgf, in_=wst_d), "gld7")
        ev_ws = V(lambda v: v.tensor_tensor(out=wsT, in0=vgf.rearrange("p (g t) -> p g t", t=128),
                                            in1=M_own.unsqueeze(1).to_broadcast([128, 12, 128]), op=ALU.mult), [e4])
        vgb = ph[:, 8192:11264]
        TA = vgb[:, 0:1536]
        TB = vgb[:, 1536:3072]
        bsp_hl = misc[0:64, 6144:7680]
        V(lambda v: v.memset(TA[0:64], 0.0), [ev_ws])
        V(lambda v: v.tensor_copy(out=TA[0:1], in_=bsp[0:1]), [e3])
        eb1 = V(lambda v: v.tensor_copy(out=TB[32:33], in_=bsp[32:33]))
        eb2 = V(lambda v: v.tensor_tensor(out=TA[32:33], in0=bsp[32:33], in1=TB[32:33], op=ALU.subtract), [eb1])
        ev_hl = V(lambda v: v.tensor_copy(out=bsp_hl, in_=TA[0:64]), [eb2])
        ev_tabs = [e1, e2, ev_ws, ev_hl]
        evh = evh_cb[0]
        if evh is None:
            for t in range(8):
                evh, _ = norm_tile(xs[:, t, :], x_ready[t], 16,
                                   lambda h, t=t: hT[:, h * 8:(h + 1) * 8, t * 128:(t + 1) * 128], dst_wait=hT_free[0])
        gTm = am[:].rearrange("p (k t) -> p k t", t=512)
        half_done = [xt_free[0], xt_free[1]]
        pre_evs = []
        for hf in range(2):
            tk0 = hf * 512
            last = None
            for pc in range(2):
                s_, wv, evw = wpiece(w1, 0, 16, 3072 + pc * 256, 256)
                for c in range(2):
                    m = pc * 2 + c
                    last = proj_fm(wv, c * 128, lambda k, tk0=tk0: hT[:, k, tk0:tk0 + 512], 16, 512,
                                   lambda b, ev, m=m: A(lambda a: a.copy(out=qm1[:, m, :], in_=bank(b)), [ev]),
                                   waits=[evw, evh, half_done])
                R.release(s_, last)
            qdone = [bfree[0], bfree[1]]
            if hf == 0:
                mem_kv(1, memT1)
            for cg in range(6):
                pstF = None
                if hf == 1 and cg < 4 and stop != "attn1":
                    pstF = norm_A(xs[:, cg, :], x_ready[cg], 48,
                                  lambda h, cg=cg: hT[:, h * 8:(h + 1) * 8, cg * 128:(cg + 1) * 128], dst_wait=half_done)
                s_, wv, evw = wpiece(w1, 0, 16, 1536 + cg * 256, 256)
                last = None
                for tt in range(4):
                    def evac(b, ev, tt=tt, cg=cg):
                        return A(lambda a: a.activation(out=vtm[:, tt, cg * 256:(cg + 1) * 256], in_=bank(b)[:, 0:256],
                                                        func=AF.Gelu), [ev, half_done])
                    last = proj_tm(lambda k, tt=tt, tk0=tk0: hT[:, k, tk0 + tt * 128: tk0 + (tt + 1) * 128], wv, 0, 256, 16, evac,
                                   waits=[evw, evh], banks=(0, 1, 4, 5))
                R.release(s_, last)
                if pstF is not None:
                    pre_evs.append(norm_B(pstF)[0])
            vdone = [bfree[0], bfree[1], bfree[4], bfree[5]]
            ev_ln_h = [None]

            def ln_tile(tt, vdone=vdone, ev_ln_h=ev_ln_h):
                sm, sq2, mu, var, rs = stat_col(), stat_col(), stat_col(), stat_col(), stat_col()
                ea = A(lambda a: a.activation(out=xnb[:, 0:1536], in_=vtm[:, tt, :], func=AF.Copy, accum_out=sm), [vdone, ev_ws, xn_free[0]])
                eb = A(lambda a: a.activation(out=xnb[:, 0:1536], in_=vtm[:, tt, :], func=AF.Square, accum_out=sq2), [ea])
                e = V(lambda v: v.tensor_scalar(out=mu, in0=sm, scalar1=1.0 / 1536, scalar2=None, op0=ALU.mult), [ea])
                e = V(lambda v: v.tensor_tensor(out=var, in0=mu, in1=mu, op=ALU.mult), [e])
                e = V(lambda v: v.scalar_tensor_tensor(out=var, in0=sq2, scalar=1.0 / 1536, in1=var, op0=ALU.mult, op1=ALU.subtract), [e, eb])
                e = A(lambda a: a.activation(out=var, in_=var, func=AF.Sqrt, bias=lneps_ap), [e])
                e = V(lambda v: v.reciprocal(out=rs, in_=var), [e])
                e = V(lambda v: v.tensor_scalar(out=vgf, in0=vtm[:, tt, :], scalar1=mu, scalar2=rs, op0=ALU.subtract, op1=ALU.mult), [e, eb, ev_ln_h[0]])
                e = V(lambda v: v.tensor_tensor(out=vgf, in0=vgf, in1=lng, op=ALU.mult), [e, ev_tabs])
                ev_ln_h[0] = V(lambda v: v.tensor_tensor(out=vtm[:, tt, :], in0=vgf, in1=lnb, op=ALU.add), [e])

            for pc in range(6):
                s_, wv, evw = wpiece(w1, 0, 16, pc * 256, 256)
                last = None
                for c in range(2):
                    g = pc * 2 + c
                    last = proj_fm(wv, c * 128, lambda k, tk0=tk0: hT[:, k, tk0:tk0 + 512], 16, 512,
                                   lambda b, ev, g=g: A(lambda a: a.activation(out=gTm[:, g, :], in_=bank(b), func=AF.Gelu), [ev, half_done]),
                                   waits=[evw, evh])
                R.release(s_, last)
                if pc < 4:
                    ln_tile(pc)
            ev_ln = ev_ln_h[0]
            udone = [bfree[0], bfree[1]]
            last_ma = mem_attn_multi([(m, qm1[:, m, :], gTm[:, 12 + m, :], [qdone, half_done]) for m in range(4)], rden1)
            ev_gate = None
            for g in range(12):
                b = 2 + (g % 2)
                P.wait("pe", [bfree[b], ev_ln, ev_ws, ev_hl])
                ev = None
                for tt in range(4):
                    ob = bank(b)[:, tt * 128:(tt + 1) * 128]
                    T(lambda t, ob=ob, tt=tt, g=g: t.matmul(ob, vtm[:, tt, g * 128:(g + 1) * 128], wsT[:, g, :],
                                                          start=True, stop=False))
                    ev = T(lambda t, ob=ob, g=g: t.matmul(ob, ones[0:64, :], bsp_hl[:, g * 128:(g + 1) * 128],
                                                         start=False, stop=True), sig=(tt == 3))
                ev_gate = V(lambda v, b=b, g=g: v.tensor_tensor(out=gTm[:, g, :], in0=bank(b), in1=gTm[:, g, :], op=ALU.mult),
                            [ev, udone])
                bfree[b] = ev_gate
            cat_ev = [ev_gate, last_ma]
            toks = [hf * 4 + i for i in range(4)]
            r = out_proj(wo1, 16, lambda k, t: gTm[:, k, (t % 4) * 128:(t % 4 + 1) * 128], toks, x_ready, cat_ev)
            for t in toks:
                x_ready[t] = r[t]
            half_done = [r[t] for t in toks]
            P.wait("pe", half_done)
        hT_free[0] = half_done
        xt_free[0] = xt_free[1] = x_ready
        if stop != "attn1":
            if final:
                ev_gf_h[0] = SPD(lambda q: q.dma_start(out=gfin, in_=gfin_d), "gld8", x_ready)
                x_ready = ffn(1, x_ready, final_tile, pre_tiles=(0, 1, 2, 3) if pre_evs else (), pre_evh=pre_evs)
                final_flush()
                final_done[0] = True
            else:
                x_ready = ffn(1, x_ready, pre_tiles=(0, 1, 2, 3) if pre_evs else (), pre_evh=pre_evs)
            xt_free[0] = xt_free[1] = x_ready

    if final and not final_done[0]:
        ev_gf_h[0] = SPD(lambda q: q.dma_start(out=gfin, in_=gfin_d), "gld8", x_ready)
        for t in range(8):
            final_tile(t, x_ready, hT_free[0])
        final_flush()
    elif not final:
        for t in range(8):
            evs = SPD(lambda q, t=t: q.dma_start(out=y[t * 128:(t + 1) * 128, :], in_=xs[:, t, :]), "yst0", [x_ready[t]])
            st_evs.append(evs)
    P.wait("sp", st_evs)
    P.build()
    es.close()
    return nc


_NC_CACHE = {}


def _get_nc(layers, final, stop=None):
    key = (tuple(layers), final, stop)
    if key not in _NC_CACHE:
        _NC_CACHE[key] = build(layers, final, stop)
    return _NC_CACHE[key]


def _halo_idx(T0):
    g3 = (T0 - 2048 + 16 * np.arange(128)[None, :] + np.arange(16)[:, None]).reshape(-1)
    g2 = (T0 - 512 + 4 * np.arange(128)[None, :] + np.arange(4)[:, None]).reshape(-1)
    g1 = T0 - 128 + np.arange(128)
    return np.concatenate([g3, g2, g1]).astype(np.int64)


def _consts():
    k = np.arange(128)[:, None]
    q = np.arange(128)[None, :]
    ident = (k == q)
    m_own = (k <= q)
    m_prev = (k >= q)
    m_b3 = ((k // 64) == (q // 64)) & ((k % 64) <= (q % 64))
    ones = np.ones((128, 128), bool)
    m32 = np.arange(32)[None, :]
    rsw = (k < 32) & (k == ((m32 + 16) % 32))
    cmat = np.concatenate([ident, m_own, m_prev, m_own, ones, rsw], axis=1).astype(np.float32)
    half = 16
    inv_freq = (np.float32(500000.0) ** (-(np.arange(half, dtype=np.float32)) / np.float32(half))).astype(np.float32)
    invf = np.zeros((32, 2), np.float32)
    invf[:, 0] = inv_freq[np.arange(32) % 16]
    invf[:, 1] = np.where(np.arange(32) < 16, -1.0, 1.0)
    return np.ascontiguousarray(cmat), invf


def _prep(inp, layers):
    f = lambda a: np.ascontiguousarray(np.asarray(a, dtype=np.float32))
    cmat, invf = _consts()
    gl = [inp["mix_norm"][0], inp["mix_norm"][1], inp["ffn_norm"][0], inp["ffn_norm"][1],
          inp["mem_norm"][0], inp["mem_norm"][1]]
    gT = np.concatenate([np.asarray(g, np.float32).reshape(16, 128).T for g in gl], axis=1)
    common = {
        "mem": f(inp["mem"][0]), "cmat": cmat, "gT": f(gT),
        "gfin": f(np.broadcast_to(np.asarray(inp["final_norm"], np.float32)[None, :], (128, DM))),
        "wkv": f(inp["w_mem_kv"]), "wg": f(inp["w_gate"]), "wu": f(inp["w_up"]), "wd": f(inp["w_down"]),
    }
    if 0 in layers:
        w = np.asarray(inp["attn_w_in"][0], np.float32)
        qc = lambda h: w[:, h * 128:(h + 1) * 128]
        kc = lambda h: w[:, 1536 + h * 128:1536 + (h + 1) * 128]
        vc = lambda h: w[:, 3072 + h * 128:3072 + (h + 1) * 128]
        mc = lambda m: w[:, 4608 + m * 128:4608 + (m + 1) * 128]
        cols = []
        for s in range(4):
            cols += [qc(s), qc(4 + s), qc(8 + s), kc(s), kc(4 + s), kc(8 + s), mc(s), vc(s), vc(4 + s), vc(8 + s)]
        for gi in (2, 1, 0):
            cols += [w[:, 1536 + gi * 512:1536 + (gi + 1) * 512], w[:, 3072 + gi * 512:3072 + (gi + 1) * 512]]
        common["w0"] = np.ascontiguousarray(np.concatenate(cols, axis=1))
        common["wo0"] = f(inp["attn_w_out"][0])
        common["invf"] = invf
    if 1 in layers:
        common["w1"] = f(inp["sgu_w_in"][0])
        common["wo1"] = f(inp["sgu_w_out"][0])
        common["lng"] = f(np.broadcast_to(np.asarray(inp["sgu_ln_g"][0], np.float32)[None, :], (128, 1536)))
        common["lnb"] = f(np.broadcast_to(np.asarray(inp["sgu_ln_b"][0], np.float32)[None, :], (128, 1536)))
        common["bsp"] = f(np.broadcast_to(np.asarray(inp["sgu_b_spatial"][0], np.float32).reshape(1, 1536), (128, 1536)))
        common["wst"] = f(np.asarray(inp["sgu_w_spatial"][0], np.float32).transpose(2, 0, 1).reshape(128, 1536))
    return common


def _run(inp, x2, layers, final, stop=None, ncores=NCORES):
    nc = _get_nc(layers, final, stop)
    common = _prep(inp, layers)
    pos = np.asarray(inp["positions"][0], np.int32)
    in_maps = []
    for c in range(ncores):
        T0 = c * TOK
        m = dict(common)
        m["xo"] = np.ascontiguousarray(x2[T0:T0 + TOK])
        if 0 in layers:
            idx = _halo_idx(T0)
            ok = idx >= 0
            ic = np.clip(idx, 0, None)
            xh = x2[ic].copy()
            xh[~ok] = 0.0
            m["xh"] = xh
            pp = np.concatenate([pos[ic], pos[T0:T0 + TOK]]).astype(np.int32)
            m["posr"] = np.ascontiguousarray(np.broadcast_to(pp[None, :], (32, HALO + TOK)))
            m["valid"] = np.ascontiguousarray(ok.astype(np.float32).reshape(21, 128).T)
        in_maps.append(m)
    res = run_bass_kernel_spmd(nc, in_maps, core_ids=list(range(ncores)))
    return np.concatenate([r["y"] for r in res.results], axis=0)


def kernel(**inp):
    x2 = np.ascontiguousarray(np.asarray(inp["x"], np.float32)[0])
    out = _run(inp, x2, (0, 1), True)
    return out.reshape(1, NCORES * TOK, DM).astype(np.float32)
```
